# Optimizing a Trainium2 kernel written in Bass

```python
import math
import jax, jax.numpy as jnp
from jax import lax
import numpy as np

D_MODEL = 1024
BATCH = 16
SEQ = 2048
DEPTH = 1

CHUNK = 64
Q_BLOCK = 128
NORM_EPS = 1e-6
NEG_INF = -1e30

DIFF_HEAD_DIM = 64
DIFF_HEADS = D_MODEL // (2 * DIFF_HEAD_DIM)
DIFF_V_DIM = 2 * DIFF_HEAD_DIM
DIFF_ROT_DIM = DIFF_HEAD_DIM // 4
ROPE_THETA = 500000.0

MLA_HEADS = 8
MLA_NOPE_DIM = 64
MLA_ROPE_DIM = 32
MLA_V_DIM = 64
MLA_Q_LORA = 3 * D_MODEL // 8
MLA_KV_LORA = D_MODEL // 4
MLA_ROPE_THETA = 10000.0

FFN_HIDDEN = ((8 * D_MODEL + 3 * 256 - 1) // (3 * 256)) * 256

PLE_DIM = 256

IN_SPLITS = (
    2 * DIFF_HEADS * DIFF_HEAD_DIM,
    2 * DIFF_HEADS * DIFF_HEAD_DIM,
    DIFF_HEADS * DIFF_V_DIM,
    MLA_Q_LORA,
    MLA_KV_LORA,
    MLA_ROPE_DIM,
    D_MODEL,
    D_MODEL,
)
IN_COLS = sum(IN_SPLITS)
IN_SPLIT_POINTS = [int(c) for c in np.cumsum(IN_SPLITS)[:-1]]

kernel_name = "hybrid_diffattn_mla_gated_block"


def rmsnorm(x, g):
    xf = x.astype(jnp.float32)
    y = xf * lax.rsqrt(jnp.mean(xf * xf, axis=-1, keepdims=True) + NORM_EPS)
    return (y * g.astype(jnp.float32)).astype(x.dtype)


def apply_rope(x, rot_dim, theta):
    seq = x.shape[1]
    half = rot_dim // 2
    pos = jnp.arange(seq, dtype=jnp.float32)
    inv_freq = theta ** (-(jnp.arange(0, rot_dim, 2, dtype=jnp.float32) / rot_dim))
    ang = pos[:, None] * inv_freq[None, :]
    cos = jnp.cos(ang)[None, :, None, :]
    sin = jnp.sin(ang)[None, :, None, :]
    xr = x[..., :rot_dim].astype(jnp.float32)
    x1, x2 = xr[..., :half], xr[..., half:]
    rot = jnp.concatenate([x1 * cos - x2 * sin, x2 * cos + x1 * sin], axis=-1).astype(x.dtype)
    return jnp.concatenate([rot, x[..., rot_dim:]], axis=-1)


def chunk_mask(q_start, seq):
    q_chunk = (q_start + jnp.arange(Q_BLOCK)) // CHUNK
    k_chunk = jnp.arange(seq) // CHUNK
    return k_chunk[None, :] <= q_chunk[:, None]


def masked_softmax(scores, mask):
    s = jnp.where(mask, scores.astype(jnp.float32), NEG_INF)
    return jax.nn.softmax(s, axis=-1)


def blocks_to_seq(o):
    nb, b, h, qb, d = o.shape
    return jnp.transpose(o, (1, 0, 3, 2, 4)).reshape(b, nb * qb, h, d)


def diff_attention(q1, q2, k1, k2, v, lam):
    seq = q1.shape[2]
    scale = DIFF_HEAD_DIM ** -0.5

    def block(i):
        s0 = i * Q_BLOCK
        mask = chunk_mask(s0, seq)
        qb1 = lax.dynamic_slice_in_dim(q1, s0, Q_BLOCK, axis=2)
        qb2 = lax.dynamic_slice_in_dim(q2, s0, Q_BLOCK, axis=2)
        a1 = masked_softmax(jnp.einsum('bhqd,bhkd->bhqk', qb1, k1).astype(jnp.float32) * scale, mask)
        a2 = masked_softmax(jnp.einsum('bhqd,bhkd->bhqk', qb2, k2).astype(jnp.float32) * scale, mask)
        w = (a1 - lam * a2).astype(v.dtype)
        return jnp.einsum('bhqk,bhkd->bhqd', w, v)

    return blocks_to_seq(lax.map(block, jnp.arange(seq // Q_BLOCK)))


def mla_attention(q_nope, q_rope, k_nope, k_rope, v):
    seq = q_nope.shape[2]
    scale = (MLA_NOPE_DIM + MLA_ROPE_DIM) ** -0.5

    def block(i):
        s0 = i * Q_BLOCK
        mask = chunk_mask(s0, seq)
        qn = lax.dynamic_slice_in_dim(q_nope, s0, Q_BLOCK, axis=2)
        qr = lax.dynamic_slice_in_dim(q_rope, s0, Q_BLOCK, axis=2)
        s = (jnp.einsum('bhqd,bhkd->bhqk', qn, k_nope).astype(jnp.float32)
             + jnp.einsum('bhqr,bkr->bhqk', qr, k_rope).astype(jnp.float32)) * scale
        a = masked_softmax(s, mask).astype(v.dtype)
        return jnp.einsum('bhqk,bhkd->bhqd', a, v)

    return blocks_to_seq(lax.map(block, jnp.arange(seq // Q_BLOCK)))


def setup_inputs(seed: int = 0) -> dict:
    key = jax.random.key(seed)
    ks = jax.random.split(key, 32)
    f32 = jnp.float32

    def nrm(k, shape, scale):
        return jax.random.normal(k, shape, f32) * scale

    def gain(k, shape):
        return 1.0 + 0.05 * jax.random.normal(k, shape, f32)

    L, D = DEPTH, D_MODEL
    return {
        "x": nrm(ks[0], (BATCH, SEQ, D), 1.0),
        "p": nrm(ks[1], (DEPTH, BATCH, SEQ, PLE_DIM), 1.0),
        "attn_norm": gain(ks[2], (L, D)),
        "w_in": nrm(ks[3], (L, D, IN_COLS), D ** -0.5),
        "b_gate": nrm(ks[4], (L, 2, D), 0.1),
        "lam_q1": nrm(ks[5], (L, DIFF_HEAD_DIM), 0.1),
        "lam_k1": nrm(ks[6], (L, DIFF_HEAD_DIM), 0.1),
        "lam_q2": nrm(ks[7], (L, DIFF_HEAD_DIM), 0.1),
        "lam_k2": nrm(ks[8], (L, DIFF_HEAD_DIM), 0.1),
        "diff_subln": gain(ks[9], (L, DIFF_V_DIM)),
        "w_o_diff": nrm(ks[10], (L, DIFF_HEADS * DIFF_V_DIM, D), (DIFF_HEADS * DIFF_V_DIM) ** -0.5),
        "q_norm": gain(ks[11], (L, MLA_Q_LORA)),
        "w_uq": nrm(ks[12], (L, MLA_Q_LORA, MLA_HEADS * (MLA_NOPE_DIM + MLA_ROPE_DIM)), MLA_Q_LORA ** -0.5),
        "kv_norm": gain(ks[13], (L, MLA_KV_LORA)),
        "w_ukv": nrm(ks[14], (L, MLA_KV_LORA, MLA_HEADS * (MLA_NOPE_DIM + MLA_V_DIM)), MLA_KV_LORA ** -0.5),
        "w_o_mla": nrm(ks[15], (L, MLA_HEADS * MLA_V_DIM, D), (MLA_HEADS * MLA_V_DIM) ** -0.5),
        "w_out": nrm(ks[16], (L, D, D), D ** -0.5),
        "ffn_norm": gain(ks[17], (L, D)),
        "w_ffn_gate": nrm(ks[18], (L, D, FFN_HIDDEN), D ** -0.5),
        "w_ffn_up": nrm(ks[19], (L, D, FFN_HIDDEN), D ** -0.5),
        "w_ffn_down": nrm(ks[20], (L, FFN_HIDDEN, D), FFN_HIDDEN ** -0.5),
        "ple_norm": gain(ks[21], (L, D)),
        "w_ple_gate": nrm(ks[22], (L, D, D), D ** -0.5),
        "b_ple_gate": nrm(ks[23], (L, D), 0.1),
        "w_ple": nrm(ks[24], (L, PLE_DIM, D), PLE_DIM ** -0.5),
        "final_norm": gain(ks[25], (D,)),
    }


def reference(x, p, attn_norm, w_in, b_gate, lam_q1, lam_k1, lam_q2, lam_k2, diff_subln,
              w_o_diff, q_norm, w_uq, kv_norm, w_ukv, w_o_mla, w_out, ffn_norm,
              w_ffn_gate, w_ffn_up, w_ffn_down, ple_norm, w_ple_gate, b_ple_gate, w_ple,
              final_norm):
    B, S, _ = x.shape
    for i in range(DEPTH):
        h = rmsnorm(x, attn_norm[i])
        proj = jnp.einsum('bsd,dc->bsc', h, w_in[i])
        dq, dk, dv, cq, ckv, kr, ga, gb = jnp.split(proj, IN_SPLIT_POINTS, axis=-1)

        dq = apply_rope(dq.reshape(B, S, 2 * DIFF_HEADS, DIFF_HEAD_DIM), DIFF_ROT_DIM, ROPE_THETA)
        dk = apply_rope(dk.reshape(B, S, 2 * DIFF_HEADS, DIFF_HEAD_DIM), DIFF_ROT_DIM, ROPE_THETA)
        dq = jnp.transpose(dq.reshape(B, S, DIFF_HEADS, 2, DIFF_HEAD_DIM), (3, 0, 2, 1, 4))
        dk = jnp.transpose(dk.reshape(B, S, DIFF_HEADS, 2, DIFF_HEAD_DIM), (3, 0, 2, 1, 4))
        dv = jnp.transpose(dv.reshape(B, S, DIFF_HEADS, DIFF_V_DIM), (0, 2, 1, 3))
        lam_init = 0.8 - 0.6 * math.exp(-0.3 * i)
        lam = (jnp.exp(jnp.sum(lam_q1[i].astype(jnp.float32) * lam_k1[i].astype(jnp.float32)))
               - jnp.exp(jnp.sum(lam_q2[i].astype(jnp.float32) * lam_k2[i].astype(jnp.float32)))
               + lam_init)
        od = diff_attention(dq[0], dq[1], dk[0], dk[1], dv, lam)
        od = rmsnorm(od, diff_subln[i]) * (1.0 - lam_init)
        out_a = jnp.einsum('bsc,cd->bsd', od.reshape(B, S, DIFF_HEADS * DIFF_V_DIM), w_o_diff[i])

        q = jnp.einsum('bsr,rc->bsc', rmsnorm(cq, q_norm[i]), w_uq[i])
        q = q.reshape(B, S, MLA_HEADS, MLA_NOPE_DIM + MLA_ROPE_DIM)
        q_nope = jnp.transpose(q[..., :MLA_NOPE_DIM], (0, 2, 1, 3))
        q_rope = jnp.transpose(apply_rope(q[..., MLA_NOPE_DIM:], MLA_ROPE_DIM, MLA_ROPE_THETA), (0, 2, 1, 3))
        kv = jnp.einsum('bsr,rc->bsc', rmsnorm(ckv, kv_norm[i]), w_ukv[i])
        kv = kv.reshape(B, S, MLA_HEADS, MLA_NOPE_DIM + MLA_V_DIM)
        k_nope = jnp.transpose(kv[..., :MLA_NOPE_DIM], (0, 2, 1, 3))
        v_mla = jnp.transpose(kv[..., MLA_NOPE_DIM:], (0, 2, 1, 3))
        k_rope = apply_rope(kr[:, :, None, :], MLA_ROPE_DIM, MLA_ROPE_THETA)[:, :, 0, :]
        om = mla_attention(q_nope, q_rope, k_nope, k_rope, v_mla)
        out_b = jnp.einsum('bsc,cd->bsd', om.reshape(B, S, MLA_HEADS * MLA_V_DIM), w_o_mla[i])

        merged = jax.nn.sigmoid(ga + b_gate[i, 0]) * out_a + jax.nn.sigmoid(gb + b_gate[i, 1]) * out_b
        x = x + jnp.einsum('bsd,de->bse', merged, w_out[i])

        h = rmsnorm(x, ffn_norm[i])
        hid = jax.nn.silu(jnp.einsum('bsd,df->bsf', h, w_ffn_gate[i])) * jnp.einsum('bsd,df->bsf', h, w_ffn_up[i])
        x = x + jnp.einsum('bsf,fd->bsd', hid, w_ffn_down[i])

        h = rmsnorm(x, ple_norm[i])
        gate = jax.nn.sigmoid(jnp.einsum('bsd,de->bse', h, w_ple_gate[i]) + b_ple_gate[i])
        x = x + jnp.einsum('bsp,pd->bsd', p[i], w_ple[i]) * gate

    return rmsnorm(x, final_norm)
```

```python
import math
import os
from contextlib import ExitStack

import numpy as np
import ml_dtypes

import concourse.bass as bass
import concourse.mybir as mybir
from concourse.bass_utils import run_bass_kernel_spmd

F32 = mybir.dt.float32
BF16 = mybir.dt.bfloat16
AF = mybir.ActivationFunctionType
ALU = mybir.AluOpType
AX = mybir.AxisListType

NCORES = 8
SEQ_PER_CORE = 2
SQ = 2048
D = 1024
NT = 16
NB = 4
FF = 2816
NF = 22
EPS = 1e-6
Q_OFF, K_OFF, V_OFF, CQ_OFF, CKV_OFF, KR_OFF, GA_OFF, GB_OFF = 0, 1024, 2048, 3072, 3456, 3712, 3744, 4768
IN_COLS = 5792
WS = 6144
ARENA_BYTES = 207872
LAM_INIT = 0.8 - 0.6 * math.exp(0.0)

ENGS = ("pe", "act", "dve", "pool", "sp")


class Buf:
    __slots__ = ("name", "w", "r", "init", "psum")

    def __init__(self, name, init=(), psum=False):
        self.name = name
        self.w = None
        self.r = []
        self.init = list(init)
        self.psum = psum


class DmaSem:
    __slots__ = ("h", "count")

    def __init__(self, h):
        self.h = h
        self.count = 0


class Op:
    __slots__ = ("eng", "emit", "deps", "dsem", "dval", "ndma", "sig", "sigval", "pos")

    def __init__(self, eng, emit):
        self.eng = eng
        self.emit = emit
        self.deps = []
        self.dsem = None
        self.dval = 0
        self.ndma = 0
        self.sig = False
        self.sigval = 0


class Sched:
    def __init__(self):
        self.streams = {e: [] for e in ENGS}
        self.last_compute = {e: None for e in ENGS}
        self.stores = []

    def fence(self):
        return [o for o in self.last_compute.values() if o is not None] + list(self.stores)

    def add(self, eng, emit, reads=(), writes=(), dsem=None, ndma=1, store=False):
        op = Op(eng, emit)
        deps = {}

        def dep(o, raw):
            if o is None:
                return
            if o.dsem is None and o.eng == eng:
                if eng == "pe":
                    return
            deps[id(o)] = o

        for b in reads:
            dep(b.w, True)
            for o in b.init:
                dep(o, True)
            if b.psum:
                for o in b.r:
                    if o.eng != eng:
                        dep(o, True)
        for b in writes:
            dep(b.w, False)
            for o in b.r:
                dep(o, False)
            for o in b.init:
                dep(o, True)
            b.init = []
        latest = {}
        dl = []
        for o in deps.values():
            if o.dsem is not None:
                dl.append(o)
            elif o.eng not in latest or latest[o.eng].pos < o.pos:
                latest[o.eng] = o
        op.deps = dl + list(latest.values())
        for b in reads:
            b.r.append(op)
        for b in writes:
            b.w = op
            b.r = []
        if dsem is not None:
            op.dsem = dsem
            op.ndma = ndma
            dsem.count += 16 * ndma
            op.dval = dsem.count
            if store:
                self.stores.append(op)
        else:
            self.last_compute[eng] = op
        op.pos = len(self.streams[eng])
        self.streams[eng].append(op)
        return op

    def emit_all(self, block, esem):
        for e in ENGS:
            for op in self.streams[e]:
                for d in op.deps:
                    if d.dsem is None:
                        d.sig = True
        for e in ENGS:
            c = 0
            for op in self.streams[e]:
                if op.sig:
                    c += 1
                    op.sigval = c
        streams = self.streams

        def run(eng_name, eng):
            waited = {}
            for op in streams[eng_name]:
                for d in op.deps:
                    if d.dsem is not None:
                        key, h, v = id(d.dsem), d.dsem.h, d.dval
                    else:
                        key, h, v = d.eng, esem[d.eng], d.sigval
                    if waited.get(key, 0) >= v:
                        continue
                    waited[key] = v
                    eng.wait_ge(h, v)
                res = op.emit(eng)
                if op.dsem is not None:
                    if not isinstance(res, (list, tuple)):
                        res = [res]
                    assert len(res) == op.ndma
                    for r in res:
                        r.then_inc(op.dsem.h, 16)
                elif op.sig:
                    if isinstance(res, (list, tuple)):
                        res = res[-1]
                    res.then_inc(esem[eng_name], 1)

        @block.tensor
        def _(eng):
            run("pe", eng)

        @block.scalar
        def _(eng):
            run("act", eng)

        @block.vector
        def _(eng):
            run("dve", eng)

        @block.gpsimd
        def _(eng):
            run("pool", eng)

        @block.sync
        def _(eng):
            run("sp", eng)


class Rot:
    def __init__(self, items):
        self.items = items
        self.i = 0

    def next(self):
        it = self.items[self.i % len(self.items)]
        self.i += 1
        return it


class _Stop(Exception):
    pass


def build_program(nseq=SEQ_PER_CORE, stage=99):
    nc = bass.Bass("TRN2", target_bir_lowering=False)
    dbg_outs = {}

    def din(name, shape, dt=F32):
        return nc.dram_tensor(name, list(shape), dt, kind="ExternalInput").ap()

    x_d = din("x", [nseq, SQ, D])
    p_d = din("p", [nseq, SQ, 256])
    y_d = nc.dram_tensor("y", [nseq, SQ, D], F32, kind="ExternalOutput").ap()
    w_in_d = din("w_in", [D, IN_COLS])
    w_od_d = din("w_o_diff", [1024, D])
    w_uq_d = din("w_uq", [384, 768])
    w_kn_d = din("w_ukv_kn", [256, 512])
    w_v_d = din("w_ukv_v", [256, 512])
    w_om_d = din("w_o_mla", [512, D])
    w_out_d = din("w_out", [D, D])
    w_g_d = din("w_ffn_gate", [D, FF])
    w_u_d = din("w_ffn_up", [D, FF])
    w_d_d = din("w_ffn_down", [FF, D])
    w_pg_d = din("w_ple_gate", [D, D])
    w_pl_d = din("w_ple", [256, D])
    g_attn_d = din("attn_norm", [D])
    g_ffn_d = din("ffn_norm", [D])
    g_ple_d = din("ple_norm", [D])
    g_fin_d = din("final_norm", [D])
    b_ple_d = din("b_ple_gate", [D])
    lam_d = [din(n, [64]) for n in ("lam_q1", "lam_k1", "lam_q2", "lam_k2")]
    vecs_d = din("vecs", [128, 24])
    ident_d = din("ident", [128, 128], BF16)
    pd_d = din("perm_diff", [128, 128], BF16)
    pm_d = din("perm_mla", [128, 128], BF16)
    tab_d = din("rope_tabs", [4, 128, SQ])

    def wview(w):
        return w.rearrange("(c p) n -> p c n", p=128)

    w_in_v = wview(w_in_d)

    S = Sched()
    G = ExitStack()
    with G:
        arena = G.enter_context(nc.sbuf_tensor("arena", [128, ARENA_BYTES // 2], BF16))
        abase = nc.lookup_mloc(arena).addr
        free_list = [[abase, abase + ARENA_BYTES]]
        _uid = [0]

        def a_alloc(nbytes):
            nbytes = (nbytes + 63) // 64 * 64
            for iv in free_list:
                if iv[1] - iv[0] >= nbytes:
                    off = iv[0]
                    iv[0] += nbytes
                    if iv[0] == iv[1]:
                        free_list.remove(iv)
                    return off, nbytes
            raise RuntimeError("SBUF arena full: need %d, free %s" % (nbytes, free_list))

        def a_free(off, nbytes):
            free_list.append([off, off + nbytes])
            free_list.sort()
            i = 0
            while i + 1 < len(free_list):
                if free_list[i][1] == free_list[i + 1][0]:
                    free_list[i][1] = free_list[i + 1][1]
                    del free_list[i + 1]
                else:
                    i += 1

        def sbt(es, name, shape, dt):
            n = 1
            for d in shape[1:]:
                n *= d
            nbytes = n * (4 if dt == F32 else 2)
            off, nb = a_alloc(nbytes)
            _uid[0] += 1
            t = nc.alloc_sbuf_tensor_at("%s_%d" % (name, _uid[0]), list(shape), dt, offset=off)
            es.callback(a_free, off, nb)
            return t

        def sem(name):
            return G.enter_context(nc.semaphore(name))

        esem = {e: sem("s_" + e) for e in ("pe", "act", "dve", "pool")}
        _ds = [0]

        def dsem():
            _ds[0] += 1
            return DmaSem(sem("d%d" % _ds[0]))

        accs = []
        for i in range(2):
            t = G.enter_context(nc.psum_tensor("acc%d" % i, [128, 1024], F32))
            accs.append((t, Buf("acc%d" % i, psum=True)))
        banks = []
        for i in range(4):
            t = G.enter_context(nc.psum_tensor("bank%d" % i, [128, 512], F32))
            banks.append((t, Buf("bank%d" % i, psum=True)))
        allbanks = []
        for (t, b) in accs:
            allbanks.append((t[:, 0:512], b))
        for (t, b) in banks:
            allbanks.append((t[:], b))
        gen_banks = Rot([allbanks[0], allbanks[1], (banks[0][0][:], banks[0][1]), (banks[1][0][:], banks[1][1]),
                         (banks[2][0][:], banks[2][1])])
        trp_t, trp_b = banks[3]
        trp_bf = trp_t[:].bitcast(BF16)

        ident = sbt(G, "ident", [128, 128], BF16)
        ones = sbt(G, "ones", [128, 128], BF16)
        pdm = sbt(G, "pdm", [128, 128], BF16)
        pmm = sbt(G, "pmm", [128, 128], BF16)
        vecs = sbt(G, "vecs_s", [128, 24], F32)
        lamt = sbt(G, "lamt", [128, 4, 64], F32)
        lsm = sbt(G, "lsm", [128, 8], F32)
        Bconst = Buf("const")
        Blam = Buf("lam")
        dconst = dsem()
        S.add("sp", lambda e: [e.dma_start(out=ident[:], in_=ident_d), e.dma_start(out=pdm[:], in_=pd_d),
                               e.dma_start(out=pmm[:], in_=pm_d), e.dma_start(out=vecs[:], in_=vecs_d)]
              + [e.dma_start(out=lamt[:, i, :], in_=lam_d[i].partition_broadcast(128)) for i in range(4)],
              writes=[Bconst, Blam], dsem=dconst, ndma=8)
        S.add("dve", lambda e: e.memset(ones[:], 1.0), writes=[Bconst])
        mhalf = sbt(G, "mhalf", [128, 1], F32)
        S.add("dve", lambda e: e.memset(mhalf[:], -0.5), writes=[Bconst])

        def rstd_pool(st, sbf):
            S.add("dve", lambda e: e.tensor_scalar(out=st[:, 1:2], in0=st[:, 0:1], scalar1=1.0 / D, scalar2=EPS,
                                                   op0=ALU.mult, op1=ALU.add), reads=[sbf], writes=[sbf])
            S.add("pool", lambda e: e.tensor_tensor(out=st[:, 2:3], in0=st[:, 1:2], in1=mhalf[:], op=ALU.pow),
                  reads=[sbf, Bconst], writes=[sbf])
        lprod = sbt(G, "lprod", [128, 2, 64], F32)
        S.add("dve", lambda e: e.tensor_tensor(out=lprod[:, 0, :], in0=lamt[:, 0, :], in1=lamt[:, 1, :], op=ALU.mult),
              reads=[Blam], writes=[Blam])
        S.add("dve", lambda e: e.tensor_tensor(out=lprod[:, 1, :], in0=lamt[:, 2, :], in1=lamt[:, 3, :], op=ALU.mult),
              reads=[Blam], writes=[Blam])
        S.add("dve", lambda e: e.tensor_reduce(out=lsm[:, 0:2], in_=lprod[:], axis=AX.X, op=ALU.add),
              reads=[Blam], writes=[Blam])
        S.add("act", lambda e: e.activation(out=lsm[:, 2:4], in_=lsm[:, 0:2], func=AF.Exp), reads=[Blam], writes=[Blam])
        S.add("dve", lambda e: e.tensor_tensor(out=lsm[:, 4:5], in0=lsm[:, 3:4], in1=lsm[:, 2:3], op=ALU.subtract),
              reads=[Blam], writes=[Blam])
        S.add("dve", lambda e: e.tensor_scalar(out=lsm[:, 5:6], in0=lsm[:, 4:5], scalar1=-LAM_INIT, scalar2=None,
                                               op0=ALU.add), reads=[Blam], writes=[Blam])
        neglam = lsm[:, 5:6]

        wslots = []
        for i in range(3):
            t = sbt(G, "wslot%d" % i, [128, WS], BF16)
            wslots.append((t, Buf("wslot%d" % i), dsem()))
        wrot = Rot(wslots)

        def wload(parts):
            t, b, ds = wrot.next()
            views = []
            off = 0
            pairs = []
            for part in parts:
                src_, C, N = part[0], part[1], part[2]
                v = t[:, off:off + C * N].rearrange("p (c n) -> p c n", c=C)
                views.append(v)
                if len(part) > 3:
                    n = src_.shape[2]
                    flat = t[:, off:off + C * N]
                    S.add("dve", lambda e, flat=flat: e.memset(flat, 0.0), writes=[b])
                    pairs.append((v[:, :, part[3]:part[3] + n], src_))
                else:
                    pairs.append((v, src_))
                off += C * N
            assert off <= WS
            S.add("pool", lambda e: [e.dma_start(out=v, in_=s) for (v, s) in pairs], writes=[b], dsem=ds,
                  ndma=len(pairs))
            return views, b

        stat = sbt(G, "stat", [128, 64], F32)
        stat_rot = Rot([(stat[:, i * 4:(i + 1) * 4], Buf("stat%d" % i)) for i in range(16)])

        _nsems = {}

        def nsem(key):
            if key not in _nsems:
                _nsems[key] = dsem()
            return _nsems[key]

        def dump(name, ap, shape, dt, bufs):
            d = nc.dram_tensor("dbg_" + name, list(shape), dt, kind="ExternalOutput").ap()
            dbg_outs[name] = d
            S.add("sp", lambda e: e.dma_start(out=d, in_=ap), reads=bufs, dsem=nsem("dbg_" + name), store=True)

        def mm(out, lhsT, rhs, start, stop, reads, writes, **kw):
            return S.add("pe", lambda e: e.matmul(out, lhsT=lhsT, rhs=rhs, start=start, stop=stop, **kw),
                         reads=reads, writes=writes)

        def tr(out, in_, reads, writes):
            return S.add("pe", lambda e: e.transpose(out=out, in_=in_, identity=ident[:]), reads=list(reads) + [Bconst],
                         writes=writes)

        def act(out, in_, func, reads, writes, **kw):
            return S.add("act", lambda e: e.activation(out=out, in_=in_, func=func, **kw), reads=reads, writes=writes)

        def dve(fn, reads, writes):
            return S.add("dve", fn, reads=reads, writes=writes)

        _alt = [0]

        def copy_alt(out, in_, reads, writes):
            _alt[0] += 1
            if _alt[0] % 2:
                return act(out, in_, AF.Copy, reads, writes)
            return dve(lambda e: e.tensor_copy(out=out, in_=in_), reads, writes)

        def norm_T(es, tag, get_x, g_d, dstT, dstB, external=False):
            gbc = sbt(es, tag + "_gbc", [128, D], F32)
            Bg = Buf(tag + "_gbc", S.fence())
            dg = nsem("gbc_" + tag)
            S.add("sp", lambda e: e.dma_start(out=gbc[:], in_=g_d.partition_broadcast(128)), writes=[Bg], dsem=dg)
            junk = sbt(es, tag + "_junk", [128, D], BF16)
            Bj = Buf(tag + "_junk", S.fence())
            hns = Rot([(sbt(es, tag + "_hn%d" % i, [128, D], BF16), Buf(tag + "_hn%d" % i, S.fence())) for i in range(3)])
            trps = Rot([(trp_bf, trp_b), (banks[2][0][:].bitcast(BF16), banks[2][1])])
            def stage1a(t):
                xa, xb = get_x(t)
                st, sbf = stat_rot.next()
                act(junk[:], xa, AF.Square, [xb], [Bj, sbf], accum_out=st[:, 0:1])
                rstd_pool(st, sbf)
                return xa, xb, st, sbf

            def stage1b(xa, xb, st, sbf):
                hn, hb = hns.next()
                dve(lambda e: e.scalar_tensor_tensor(out=hn[:], in0=xa, scalar=st[:, 2:3], in1=gbc[:],
                                                     op0=ALU.mult, op1=ALU.mult), [xb, sbf, Bg], [hb])
                return hn, hb

            def stage2(t, hn, hb):
                tb_, tbB = trps.next()
                for c in range(8):
                    tr(tb_[:, c * 128:(c + 1) * 128], hn[:, c * 128:(c + 1) * 128], [hb], [tbB])
                copy_alt(dstT[:, :, t * 128:(t + 1) * 128], tb_.rearrange("p (c t) -> p c t", c=8), [tbB],
                         [dstB[t // 4]])

            if external:
                return stage1a, stage1b, stage2
            norm_drive(stage1a, stage1b, stage2)

        def norm_drive(s1a, s1b, s2, extra=None):
            q = [s1b(*s1a(0)), s1b(*s1a(1))]
            for t in range(NT):
                sa = s1a(t + 2) if t + 2 < NT else None
                if extra is not None:
                    extra(t)
                s2(t, *q.pop(0))
                if sa is not None:
                    q.append(s1b(*sa))

        def rope(Aps, Ab, r0, r1, perm, Ct, St, Btab, tb, dst, dstB, tmp):
            (qbf, qbfB), (t1, t1B), (t2, t2B), (Bps, BpB) = tmp
            cs = slice(tb * 512, (tb + 1) * 512)
            p0, p1 = (0, 128) if r0 > 0 else (r0, r1)
            lvl = int(os.environ.get("KROPE", "9"))
            act(qbf[p0:p1, :], Aps[p0:p1, :], AF.Copy, [Ab], [qbfB])
            if lvl >= 2:
                mm(Bps[p0:p1, :], perm[p0:p1, p0:p1], qbf[p0:p1, :], True, True, [qbfB, Bconst], [BpB])
            if lvl >= 3:
                dve(lambda e: e.tensor_tensor(out=t1[r0:r1, :], in0=Aps[r0:r1, :], in1=Ct[r0:r1, cs], op=ALU.mult),
                    [Ab] + ([] if os.environ.get("KNOTAB") else [Btab]), [t1B])
            if lvl >= 4:
                dve(lambda e: e.tensor_tensor(out=t2[r0:r1, :], in0=Bps[r0:r1, :], in1=St[r0:r1, cs], op=ALU.mult),
                    [BpB, Btab], [t2B])
            if lvl >= 5:
                dve(lambda e: e.tensor_tensor(out=dst[r0:r1, :], in0=t1[r0:r1, :], in1=t2[r0:r1, :], op=ALU.add),
                    [t1B, t2B], [dstB])

        sc_banks = Rot([banks[0], banks[1]])
        NDUMMY = int(os.environ.get("KDUMMY", "0"))
        att_banks = Rot([banks[0], banks[1], banks[2]] if NDUMMY == 0 else [banks[0], banks[1]])
        LOOK = 2 if NDUMMY == 0 else 1
        INTERLEAVE = os.environ.get("KNOINT", "") == ""

        def attn_steps(KT, KB, QT, QB, r0, r1, Vfn, VB, dv, scale, qt, acc, accB, pts, after):
            steps = []
            nk = 4 * qt + 4
            first = {0: True, 1: True}
            accv = acc[:].rearrange("p (i n) -> p i n", i=4)
            for kt in range(nk):
                j = kt - 4 * qt
                q0 = 128 * j if j > 0 else 0
                sct, scb = att_banks.next()
                pt, ptb = pts.next()

                def s_fn(kt=kt, q0=q0, sct=sct, scb=scb):
                    mm(sct[:, q0:512], KT[r0:r1, kt * 128:(kt + 1) * 128], QT[r0:r1, qt * 512 + q0:(qt + 1) * 512],
                       True, True, [KB, QB], [scb])

                avs = []
                for i in range(max(j, 0), 4):
                    bk = i // 2
                    avs.append((i, first[bk]))
                    first[bk] = False

                def rest_fn(kt=kt, j=j, q0=q0, sct=sct, scb=scb, pt=pt, ptb=ptb, avs=avs, last=(kt == nk - 1)):
                    for dmy in range(NDUMMY):
                        mm(banks[2][0][:, :], ones[:], QT[:, 0:512], True, True, [Bconst, QB], [banks[2][1]])
                    act(pt[:, q0:512], sct[:, q0:512], AF.Exp, [scb], [ptb], scale=scale)
                    if j >= 0:
                        dve(lambda e: e.memset(pt[64:128, 128 * j:128 * j + 64], 0.0), [], [ptb])
                    if os.environ.get("KAV", "") == "dense":
                        vv = Vfn(kt)
                        for bki in range(2):
                            mm(acc[0:dv, bki * 512 + q0:(bki + 1) * 512], vv[:, 0:dv] if bki == 0 else ones[:, 0:dv],
                               pt[:, q0:512], kt == 0, True, [ptb, VB, Bconst], [accB], skip_group_check=True)
                    else:
                      for (i, st) in avs:
                        mm(accv[:, i, 0:dv + 1], pt[:, i * 128:(i + 1) * 128], Vfn(kt), st, True, [ptb, VB], [accB],
                           skip_group_check=True)
                    if last:
                        return after()
                    return None

                steps.append((s_fn, rest_fn))
            return steps

        def run_steps(steps, side=()):
            side = list(side)
            busy = [False]

            def run_side():
                fn, b = side.pop(0)
                fn()
                busy[0] = b

            if not steps:
                while side:
                    run_side()
                return
            per = -(-len(side) // len(steps)) if side else 0
            pending = []
            for i in range(min(LOOK, len(steps))):
                steps[i][0]()
            for i, (s_fn, rest_fn) in enumerate(steps):
                if i + LOOK < len(steps):
                    steps[i + LOOK][0]()
                while pending and pending[0][0] <= i and not busy[0]:
                    pending.pop(0)[1]()
                d = rest_fn()
                if d is not None:
                    pending.append((i + d[0], d[1]))
                for _ in range(per):
                    if side:
                        run_side()
            while side:
                run_side()
            for (_, fn) in pending:
                fn()

        def one_seq(s):
          try:
              ABC = ExitStack()
              with ABC:
                  hT = sbt(ABC, "hT", [128, 8, SQ], BF16)
                  hTB = [Buf("hT%d" % i, S.fence()) for i in range(4)]
                  odT = sbt(ABC, "odT", [128, 8, SQ], BF16)
                  omT = sbt(ABC, "omT", [128, 4, SQ], BF16)
                  omTB = [Buf("omT%d" % i, S.fence()) for i in range(4)]
                  with ExitStack() as A:
                      xsl = Rot([(sbt(A, "xsA%d" % i, [128, D], F32), Buf("xsA%d" % i, S.fence()), nsem("xsA%d" % i))
                                 for i in range(6)])

                      def get_x(t, s=s, xsl=xsl):
                          xt, xb, ds = xsl.next()
                          S.add("sp", lambda e: e.dma_start(out=xt[:], in_=x_d[s, t * 128:(t + 1) * 128, :]), writes=[xb],
                                dsem=ds)
                          return xt[:], xb

                      norm_T(A, "nA", get_x, g_attn_d, hT, hTB)
                  if stage == 1:
                      dump("hT", hT[:], [128, 8, SQ], BF16, hTB)
                      raise _Stop()

                  with ExitStack() as Bx:
                      Ct = sbt(Bx, "Ct", [128, SQ], F32)
                      St = sbt(Bx, "St", [128, SQ], F32)
                      Btab = Buf("tab", S.fence())
                      dtab = nsem("tab")
                      S.add("sp", lambda e: [e.dma_start(out=Ct[:], in_=tab_d[0]), e.dma_start(out=St[:], in_=tab_d[1])],
                            writes=[Btab], dsem=dtab, ndma=2)
                      Vbuf = sbt(Bx, "Vbuf", [128, 8704], BF16)
                      QK = [(sbt(Bx, "QT%d" % i, [128, SQ], BF16), Buf("QT%d" % i, S.fence()),
                             sbt(Bx, "KT%d" % i, [128, SQ], BF16), Buf("KT%d" % i, S.fence())) for i in range(2)]
                      K2s = [(sbt(Bx, "K2_%d" % i, [128, SQ], BF16), Buf("K2_%d" % i, S.fence())) for i in range(2)]
                      pts = Rot([(sbt(Bx, "pt%d" % i, [128, 512], BF16), Buf("pt%d" % i, S.fence())) for i in range(4)])
                      f0 = S.fence()
                      rtmp = ((sbt(Bx, "qbf", [128, 512], BF16), Buf("qbf", f0)),
                              (sbt(Bx, "rt1", [128, 512], F32), Buf("rt1", f0)),
                              (sbt(Bx, "rt2", [128, 512], F32), Buf("rt2", f0)),
                              (banks[2][0], banks[2][1]))
                      o1n = sbt(Bx, "o1n", [128, 4, 128], F32)
                      o1nB = Buf("o1n", f0)
                      otmp = sbt(Bx, "otmp", [128, 4, 128], F32)
                      otmpB = Buf("otmp", f0)
                      odf = sbt(Bx, "odf", [128, 4, 128], F32)
                      odfB = Buf("odf", f0)
                      osq = sbt(Bx, "osq", [128, 4, 128], F32)
                      osqB = Buf("osq", f0)
                      odn = sbt(Bx, "odn", [128, 4, 128], BF16)
                      odnB = Buf("odn", f0)
                      omn = sbt(Bx, "omn", [128, 4, 64], BF16)
                      omnB = Buf("omn", f0)

                      latT = odT
                      latB = [Buf("lat%d" % i, f0) for i in range(4)]
                      latf = [Vbuf[:, j * 1024:(j + 1) * 1024].bitcast(F32) for j in range(5)]
                      sqs = [Vbuf[:, 5120 + j * 512:5120 + (j + 1) * 512] for j in range(5)]
                      ltB = Buf("lattmp", f0)
                      rsq = sbt(Bx, "rsq", [128, 512], F32)
                      rsqB = Buf("rsq", f0)
                      rstd = sbt(Bx, "rstdl", [128, 512], F32)
                      rstdB = Buf("rstdl", f0)
                      (wl, wkr), wlB = wload([(w_in_v[:, :, CQ_OFF:CQ_OFF + 640], 8, 640),
                                                  (w_in_v[:, :, KR_OFF:KR_OFF + 32], 8, 128, 64) if os.environ.get("KDBG", "") != "krplain"
                                                  else (w_in_v[:, :, KR_OFF - 96:KR_OFF + 32], 8, 128)])
                      for tb in range(NB):
                          cs = slice(tb * 512, (tb + 1) * 512)
                          for j in range(6):
                              bk, bb = sc_banks.next()
                              if j < 5:
                                  c0 = j * 128
                                  for c in range(8):
                                      mm(bk[:, :], wl[:, c, c0:c0 + 128], hT[:, c, cs], c == 0, c == 7, [wlB, hTB[tb]], [bb])
                                  act(latf[j], bk[:, :], AF.Copy, [bb], [ltB])
                                  act(sqs[j], bk[:, :], AF.Square, [bb], [ltB])
                              elif os.environ.get("KDBG", "") != "skipkr":
                                  for c in range(8):
                                      mm(bk[:, :], wkr[:, c, :], hT[:, c, cs], c == 0, c == 7, [wlB, hTB[tb]], [bb])
                                  rope(bk, bb, 0, 128, pmm, Ct, St, Btab, tb, latT[:, 5, cs], latB[tb], rtmp)
                          for (js, nrm, vcol) in (((0, 1, 2), 384.0, 8 + 8), ((3, 4), 256.0, 8 + 8 + 3)):
                              rk, rb = banks[2]
                              for n, j in enumerate(js):
                                  mm(rk[:, :], ones[:], sqs[j], n == 0, n == len(js) - 1, [ltB, Bconst], [rb])
                              dve(lambda e, nrm=nrm: e.tensor_scalar(out=rsq[:], in0=rk[:, :], scalar1=1.0 / nrm, scalar2=EPS,
                                                                   op0=ALU.mult, op1=ALU.add), [rb], [rsqB])
                              S.add("pool", lambda e: e.tensor_tensor(out=rstd[:], in0=rsq[:],
                                                                      in1=mhalf[:].broadcast_to([128, 512]), op=ALU.pow),
                                    reads=[rsqB, Bconst], writes=[rstdB])
                              for n, j in enumerate(js):
                                  dve(lambda e, j=j, n=n, vcol=vcol, cs=cs: e.scalar_tensor_tensor(
                                      out=latT[:, j, cs], in0=latf[j], scalar=vecs[:, vcol + n:vcol + n + 1], in1=rstd[:],
                                      op0=ALU.mult, op1=ALU.mult), [ltB, rstdB, Bconst], [latB[tb]])
                      if stage == 2.1:
                          dump("latT", odT[:, 0:5, :], [128, 5, SQ], BF16, latB)
                          if os.environ.get("KDBG", "") not in ("skipkr", "nodumpkr"):
                              dump("kr", odT[64:96, 5, :], [32, SQ], BF16, latB)
                          raise _Stop()
                      (wuq, wkn, wv), wmB = wload([(wview(w_uq_d), 3, 768), (wview(w_kn_d), 2, 512), (wview(w_v_d), 2, 512)])
                      VB = Buf("V", S.fence() + [ltB.w] + list(ltB.r))
                      Vm = Vbuf[:, 0:16 * 8 * 66].rearrange("p (t h d) -> p t h d", t=16, h=8)
                      dve(lambda e: e.memset(Vm[:, :, :, 64:65], 1.0), [], [VB])
                      for t in range(NT):
                          bk, bb = sc_banks.next()
                          for c in range(2):
                              mm(bk[:, :], latT[:, 3 + c, t * 128:(t + 1) * 128], wv[:, c, :], c == 0, c == 1,
                                 [latB[t // 4], wmB], [bb])
                          copy_alt(Vm[:, t, :, 0:64], bk[:, :].rearrange("p (h d) -> p h d", h=8), [bb], [VB])
                      trp_f = trp_t[:, :]

                      def diff_proj_tasks(h):
                          (wq, wk), wqB = wload([(w_in_v[:, :, Q_OFF + h * 128:Q_OFF + (h + 1) * 128], 8, 128),
                                                 (w_in_v[:, :, K_OFF + h * 128:K_OFF + (h + 1) * 128], 8, 128)])
                          QTt, QB_, KTt, KB_ = QK[h % 2]
                          K2t, K2B = K2s[h % 2]
                          (qbf, qbfB), (t1, t1B), (t2, t2B), _ = rtmp
                          tasks = []
                          if h < 2:
                              dve(lambda e: e.memset(KTt[64:128, :], 0.0), [], [KB_])
                              dve(lambda e: e.memset(K2t[0:64, :], 0.0), [], [K2B])
                          for tb in range(NB):
                              cs = slice(tb * 512, (tb + 1) * 512)
                              for (wx, dT, dB) in ((wk, None, None), (wq, QTt, QB_)):
                                  def pm(c, wx=wx, cs=cs, tb=tb):
                                      mm(trp_f, wx[:, c, :], hT[:, c, cs], c == 0, c == 7, [wqB, hTB[tb]], [trp_b])

                                  def p1b(cs=cs):
                                      dve(lambda e: e.tensor_copy(out=qbf[:], in_=trp_f), [trp_b], [qbfB])
                                      dve(lambda e: e.tensor_tensor(out=t1[:], in0=trp_f, in1=Ct[:, cs], op=ALU.mult),
                                          [trp_b, Btab], [t1B])

                                  def p2(dT=dT, dB=dB, cs=cs):
                                      b2, b2B = trp_t, trp_b
                                      mm(b2[:, :], pdm[:, :], qbf[:], True, True, [qbfB, Bconst], [b2B])
                                      dve(lambda e: e.tensor_tensor(out=t2[:], in0=b2[:, :], in1=St[:, cs], op=ALU.mult),
                                          [b2B, Btab], [t2B])
                                      if dT is None:
                                          dve(lambda e: e.tensor_tensor(out=KTt[0:64, cs], in0=t1[0:64, :], in1=t2[0:64, :],
                                                                        op=ALU.add), [t1B, t2B], [KB_])
                                          dve(lambda e: e.tensor_tensor(out=K2t[64:128, cs], in0=t1[64:128, :],
                                                                        in1=t2[64:128, :], op=ALU.add), [t1B, t2B], [K2B])
                                      else:
                                          dve(lambda e: e.tensor_tensor(out=dT[:, cs], in0=t1[:], in1=t2[:], op=ALU.add),
                                              [t1B, t2B], [dB])

                                  tasks += [((lambda c=c, pm=pm: pm(c)), True) for c in range(8)]
                                  tasks += [(p1b, False), (p2, False)]
                          return tasks

                      sc_m = 96.0 ** -0.5
                      def mla_proj_tasks(h):
                          QTt, QB_, KTt, KB_ = QK[h % 2]
                          (qbf, qbfB), (t1, t1B), (t2, t2B), _ = rtmp
                          tasks = []
                          for tb in range(NB):
                              cs = slice(tb * 512, (tb + 1) * 512)

                              def k1(cs=cs, tb=tb):
                                  for c in range(2):
                                      mm(trp_f[0:64, :], wkn[:, c, h * 64:(h + 1) * 64], latT[:, 3 + c, cs], c == 0, c == 1,
                                         [wmB, latB[tb]], [trp_b])

                              def k2(cs=cs, tb=tb):
                                  dve(lambda e: e.tensor_copy(out=KTt[0:64, cs], in_=trp_f[0:64, :]), [trp_b], [KB_])
                                  dve(lambda e: e.tensor_copy(out=KTt[64:96, cs], in_=latT[64:96, 5, cs]), [latB[tb]], [KB_])

                              def q1(cs=cs, tb=tb):
                                  for c in range(3):
                                      mm(trp_f[0:96, :], wuq[:, c, h * 96:(h + 1) * 96], latT[:, c, cs], c == 0, c == 2,
                                         [wmB, latB[tb]], [trp_b])

                              def q2(cs=cs):
                                  dve(lambda e: e.tensor_copy(out=qbf[0:96, :], in_=trp_f[0:96, :]), [trp_b], [qbfB])
                                  dve(lambda e: e.tensor_tensor(out=t1[0:96, :], in0=trp_f[0:96, :], in1=Ct[0:96, cs],
                                                                op=ALU.mult), [trp_b, Btab], [t1B])

                              def q3(cs=cs):
                                  mm(trp_f[0:96, :], pmm[0:96, 0:96], qbf[0:96, :], True, True, [qbfB, Bconst], [trp_b])
                                  dve(lambda e: e.tensor_tensor(out=t2[0:96, :], in0=trp_f[0:96, :], in1=St[0:96, cs],
                                                                op=ALU.mult), [trp_b, Btab], [t2B])
                                  dve(lambda e: e.tensor_tensor(out=QTt[0:96, cs], in0=t1[0:96, :], in1=t2[0:96, :],
                                                                op=ALU.add), [t1B, t2B], [QB_])

                              tasks += [(k1, True), (k2, False), (q1, True), (q2, False), (q3, False)]
                          return tasks

                      for fn, _b in mla_proj_tasks(0):
                          fn()
                      for h in range(8):
                          QTt, QB_, KTt, KB_ = QK[h % 2]
                          steps = []
                          for qt in range(NB):
                              acc, accB = accs[(h * 4 + qt) % 2]

                              def after(h=h, qt=qt, acc=acc, accB=accB):
                                  accv = acc[:].rearrange("p (i n) -> p i n", i=4)
                                  st, sbf = stat_rot.next()
                                  dve(lambda e: e.reciprocal(out=st[:, 0:4], in_=accv[:, :, 64]), [accB], [sbf])
                                  dve(lambda e: e.tensor_tensor(out=omn[:], in0=accv[:, :, 0:64],
                                                                in1=st[:, 0:4].unsqueeze(2).broadcast_to([128, 4, 64]),
                                                                op=ALU.mult), [accB, sbf], [omnB])
                                  ro = 64 * (h % 2)

                                  def later():
                                      for i in range(4):
                                          tr(trp_bf[ro:ro + 64, i * 128:(i + 1) * 128], omn[:, i, :], [omnB], [trp_b])
                                      dve(lambda e: e.tensor_copy(out=omT[ro:ro + 64, h // 2, qt * 512:(qt + 1) * 512],
                                                                  in_=trp_bf[ro:ro + 64, 0:512]), [trp_b], [omTB[qt]])

                                  return (3, later)

                              steps += attn_steps(KTt, KB_, QTt, QB_, 0, 96, lambda kt, h=h: Vm[:, kt, h, 0:65], VB, 64, sc_m,
                                                  qt, acc, accB, pts, after)
                          side = mla_proj_tasks(h + 1) if h + 1 < 8 else []
                          if h == 7 and INTERLEAVE:
                              S.add("sp", lambda e: [e.dma_start(out=Ct[:], in_=tab_d[2]),
                                                     e.dma_start(out=St[:], in_=tab_d[3])],
                                    writes=[Btab], dsem=dtab, ndma=2)
                              side = diff_proj_tasks(0)
                          run_steps(steps, side)

                      if stage == 2.2:
                          dump("omT", omT[:], [128, 4, SQ], BF16, omTB)
                          raise _Stop()
                      if not INTERLEAVE:
                          S.add("sp", lambda e: [e.dma_start(out=Ct[:], in_=tab_d[2]), e.dma_start(out=St[:], in_=tab_d[3])],
                                writes=[Btab], dsem=dtab, ndma=2)
                      f1 = S.fence()
                      odTB = [Buf("odT%d" % i, f1) for i in range(4)]
                      Vd = Vbuf[:, 0:16 * 4 * 130].rearrange("p (t h d) -> p t h d", t=16, h=4)
                      sc_d = 64.0 ** -0.5
                      for g in range(2):
                          (wvd,), wvB = wload([(w_in_v[:, :, V_OFF + g * 512:V_OFF + (g + 1) * 512], 8, 512)])
                          dve(lambda e: e.memset(Vd[:, :, :, 128:129], 1.0), [], [VB])
                          for t in range(NT):
                              bk, bb = sc_banks.next()
                              for c in range(8):
                                  mm(bk[:, :], hT[:, c, t * 128:(t + 1) * 128], wvd[:, c, :], c == 0, c == 7,
                                     [hTB[t // 4], wvB], [bb])
                              copy_alt(Vd[:, t, :, 0:128], bk[:, :].rearrange("p (h d) -> p h d", h=4), [bb], [VB])
                          for hh in range(4):
                              h = g * 4 + hh
                              QTt, QB_, KTt, KB_ = QK[h % 2]
                              if not INTERLEAVE:
                                  for fn, _b in diff_proj_tasks(h):
                                      fn()
                              steps = []
                              for qt in range(NB):
                                  a1, a1B = accs[0]
                                  a2, a2B = accs[1]

                                  def after1(a1=a1, a1B=a1B):
                                      accv = a1[:].rearrange("p (i n) -> p i n", i=4)
                                      st, sbf = stat_rot.next()
                                      dve(lambda e: e.reciprocal(out=st[:, 0:4], in_=accv[:, :, 128]), [a1B], [sbf])
                                      dve(lambda e: e.tensor_tensor(out=o1n[:], in0=accv[:, :, 0:128],
                                                                    in1=st[:, 0:4].unsqueeze(2).broadcast_to([128, 4, 128]),
                                                                    op=ALU.mult), [a1B, sbf], [o1nB])

                                  def after2(h=h, qt=qt, a2=a2, a2B=a2B):
                                      accv = a2[:].rearrange("p (i n) -> p i n", i=4)
                                      st, sbf = stat_rot.next()
                                      st2, sbf2 = stat_rot.next()
                                      dve(lambda e: e.reciprocal(out=st[:, 0:4], in_=accv[:, :, 128]), [a2B], [sbf])
                                      dve(lambda e: e.tensor_scalar(out=st2[:, 0:4], in0=st[:, 0:4], scalar1=neglam,
                                                                    scalar2=None, op0=ALU.mult), [sbf, Blam], [sbf2])
                                      dve(lambda e: e.tensor_tensor(out=otmp[:], in0=accv[:, :, 0:128],
                                                                    in1=st2[:, 0:4].unsqueeze(2).broadcast_to([128, 4, 128]),
                                                                    op=ALU.mult), [a2B, sbf2], [otmpB])
                                      dve(lambda e: e.tensor_tensor(out=odf[:], in0=o1n[:], in1=otmp[:], op=ALU.add),
                                          [o1nB, otmpB], [odfB])
                                      dve(lambda e: e.tensor_tensor(out=osq[:], in0=odf[:], in1=odf[:], op=ALU.mult),
                                          [odfB], [osqB])
                                      st3, sbf3 = stat_rot.next()
                                      st4, sbf4 = stat_rot.next()
                                      st5, sbf5 = stat_rot.next()
                                      dve(lambda e: e.tensor_reduce(out=st3[:, 0:4], in_=osq[:], axis=AX.X, op=ALU.add),
                                          [osqB], [sbf3])
                                      dve(lambda e: e.tensor_scalar(out=st4[:, 0:4], in0=st3[:, 0:4], scalar1=1.0 / 128,
                                                                    scalar2=EPS, op0=ALU.mult, op1=ALU.add), [sbf3], [sbf4])
                                      act(st5[:, 0:4], st4[:, 0:4], AF.Ln, [sbf4], [sbf5])
                                      act(st3[:, 0:4], st5[:, 0:4], AF.Exp, [sbf5], [sbf3], scale=-0.5)
                                      dve(lambda e: e.scalar_tensor_tensor(
                                          out=odn[:], in0=odf[:], scalar=1.0 - LAM_INIT,
                                          in1=st3[:, 0:4].unsqueeze(2).broadcast_to([128, 4, 128]),
                                          op0=ALU.mult, op1=ALU.mult), [odfB, sbf3], [odnB])
                                      def later():
                                          for i in range(4):
                                              tr(trp_bf[:, i * 128:(i + 1) * 128], odn[:, i, :], [odnB], [trp_b])
                                          dve(lambda e: e.tensor_scalar(out=odT[:, h, qt * 512:(qt + 1) * 512],
                                                                        in0=trp_bf[:, 0:512], scalar1=vecs[:, 21:22],
                                                                        scalar2=None, op0=ALU.mult),
                                              [trp_b, Bconst], [odTB[qt]])

                                      return (8, later)

                                  K2t, K2B = K2s[h % 2]
                                  steps += attn_steps(KTt, KB_, QTt, QB_, 0, 128, lambda kt, hh=hh: Vd[:, kt, hh, 0:129], VB,
                                                      128, sc_d, qt, a1, a1B, pts, after1)
                                  steps += attn_steps(K2t, K2B, QTt, QB_, 0, 128, lambda kt, hh=hh: Vd[:, kt, hh, 0:129], VB,
                                                      128, sc_d, qt, a2, a2B, pts, after2)
                              run_steps(steps, diff_proj_tasks(h + 1) if (INTERLEAVE and h + 1 < 8) else [])

                  if stage == 2.3:
                      dump("odT", odT[:], [128, 8, SQ], BF16, odTB)
                      raise _Stop()
                  CD = ExitStack()
                  CD.__enter__()
                  mgT = sbt(CD, "mgT", [128, 8, SQ], BF16)
                  mgB = [Buf("mgT%d" % i, S.fence()) for i in range(4)]
                  with ExitStack() as C:
                      fC = S.fence()
                      tmps = Rot([tuple((sbt(C, "mc%d_%d" % (k, i), [128, 512], F32), Buf("mc%d_%d" % (k, i), fC))
                                        for k in range(4)) for i in range(2)])
                      for j in range(8):
                          (wga, wgb, wod, wom), wcB = wload([
                              (w_in_v[:, :, GA_OFF + j * 128:GA_OFF + (j + 1) * 128], 8, 128),
                              (w_in_v[:, :, GB_OFF + j * 128:GB_OFF + (j + 1) * 128], 8, 128),
                              (wview(w_od_d)[:, :, j * 128:(j + 1) * 128], 8, 128),
                              (wview(w_om_d)[:, :, j * 128:(j + 1) * 128], 4, 128)])
                          for tb in range(NB):
                              cs = slice(tb * 512, (tb + 1) * 512)
                              (sa, saB), (sb_, sbB), (m1, m1B), (m2, m2B) = tmps.next()
                              ga, gaB = gen_banks.next()
                              for c in range(8):
                                  mm(ga, wga[:, c, :], hT[:, c, cs], c == 0, c == 7, [wcB, hTB[tb]], [gaB])
                              act(sa[:], ga, AF.Sigmoid, [gaB, Bconst], [saB], bias=vecs[:, j:j + 1])
                              gb, gbB = gen_banks.next()
                              for c in range(8):
                                  mm(gb, wgb[:, c, :], hT[:, c, cs], c == 0, c == 7, [wcB, hTB[tb]], [gbB])
                              act(sb_[:], gb, AF.Sigmoid, [gbB, Bconst], [sbB], bias=vecs[:, 8 + j:8 + j + 1])
                              oa, oaB = gen_banks.next()
                              for c in range(8):
                                  mm(oa, wod[:, c, :], odT[:, c, cs], c == 0, c == 7, [wcB, odTB[tb]], [oaB])
                              dve(lambda e, m1=m1, sa=sa, oa=oa: e.tensor_tensor(out=m1[:], in0=oa, in1=sa[:], op=ALU.mult),
                                  [oaB, saB], [m1B])
                              ob, obB = gen_banks.next()
                              for c in range(4):
                                  mm(ob, wom[:, c, :], omT[:, c, cs], c == 0, c == 3, [wcB, omTB[tb]], [obB])
                              dve(lambda e, m2=m2, sb_=sb_, ob=ob: e.tensor_tensor(out=m2[:], in0=ob, in1=sb_[:], op=ALU.mult),
                                  [obB, sbB], [m2B])
                              dve(lambda e, m1=m1, m2=m2, j=j, cs=cs: e.tensor_tensor(out=mgT[:, j, cs], in0=m1[:], in1=m2[:],
                                                                                    op=ALU.add), [m1B, m2B], [mgB[tb]])
              if stage == 3:
                  dump("mgT", mgT[:], [128, 8, SQ], BF16, mgB)
                  raise _Stop()
              DG = ExitStack()
              DG.__enter__()
              x1 = sbt(DG, "x1", [128, NT, D], F32)
              fD = S.fence()
              x1B = [Buf("x1_%d" % i, fD) for i in range(NT)]
              EG = ExitStack()
              EG.__enter__()
              h2T = sbt(EG, "h2T", [128, 8, SQ], BF16)
              h2B = [Buf("h2T%d" % i, S.fence()) for i in range(4)]
              with ExitStack() as Dp:
                  xsl = Rot([(sbt(Dp, "xsD%d" % i, [128, D], F32), Buf("xsD%d" % i, fD), nsem("xsD%d" % i)) for i in range(2)])
                  nE1a, nE1b, nE2 = norm_T(Dp, "nE", lambda t: (x1[:, t, :], x1B[t]), g_ffn_d, h2T, h2B, external=True)
                  wo = []
                  for nh in range(2):
                      (wv_,), wb_ = wload([(wview(w_out_d)[:, :, nh * 512:(nh + 1) * 512], 8, 512)])
                      wo.append((wv_, wb_))
                  qE = []
                  for t in range(NT):
                      xt, xb, ds = xsl.next()
                      S.add("sp", lambda e, xt=xt, t=t, s=s: e.dma_start(out=xt[:], in_=x_d[s, t * 128:(t + 1) * 128, :]),
                            writes=[xb], dsem=ds)
                      for nh in range(2):
                          bk, bb = gen_banks.next()
                          for c in range(8):
                              mm(bk, mgT[:, c, t * 128:(t + 1) * 128], wo[nh][0][:, c, :], c == 0, c == 7,
                                 [mgB[t // 4], wo[nh][1]], [bb])
                          dve(lambda e, bk=bk, xt=xt, t=t, nh=nh: e.tensor_tensor(
                              out=x1[:, t, nh * 512:(nh + 1) * 512], in0=bk, in1=xt[:, nh * 512:(nh + 1) * 512], op=ALU.add),
                              [bb, xb], [x1B[t]])
                      sa = nE1a(t)
                      if len(qE) >= 2:
                          nE2(t - 2, *qE.pop(0))
                      qE.append(nE1b(*sa))
                  nE2(NT - 2, *qE.pop(0))
                  nE2(NT - 1, *qE.pop(0))
              CD.__exit__(None, None, None)
              if stage == 4:
                  dump("x1", x1[:], [128, NT, D], F32, x1B)
                  raise _Stop()
              with ExitStack() as Fp:
                  fF = S.fence()
                  hid = sbt(Fp, "hid", [128, NF, 1024], BF16)
                  hidB = [Buf("hid%d" % i, fF) for i in range(2)]
                  sgs = Rot([(sbt(Fp, "sg%d" % i, [128, 512], F32), Buf("sg%d" % i, fF)) for i in range(3)])
                  for half in range(2):
                      for fb in range(11):
                          (wg, wu), wfB = wload([(wview(w_g_d)[:, :, fb * 256:(fb + 1) * 256], 8, 256),
                                                 (wview(w_u_d)[:, :, fb * 256:(fb + 1) * 256], 8, 256)])
                          for fc in range(2):
                              f = fb * 2 + fc
                              for tbh in range(2):
                                  tb = half * 2 + tbh
                                  cs = slice(tb * 512, (tb + 1) * 512)
                                  gk, gkB = gen_banks.next()
                                  for c in range(8):
                                      mm(gk, wg[:, c, fc * 128:(fc + 1) * 128], h2T[:, c, cs], c == 0, c == 7, [wfB, h2B[tb]],
                                         [gkB])
                                  sg, sgB = sgs.next()
                                  act(sg[:], gk, AF.Silu, [gkB], [sgB])
                                  uk, ukB = gen_banks.next()
                                  for c in range(8):
                                      mm(uk, wu[:, c, fc * 128:(fc + 1) * 128], h2T[:, c, cs], c == 0, c == 7, [wfB, h2B[tb]],
                                         [ukB])
                                  dve(lambda e, sg=sg, uk=uk, f=f, tbh=tbh: e.tensor_tensor(
                                      out=hid[:, f, tbh * 512:(tbh + 1) * 512], in0=uk, in1=sg[:], op=ALU.mult),
                                      [ukB, sgB], [hidB[tbh]])
                      wdv = w_d_d.rearrange("(f p) n -> p f n", p=128)
                      for nq in range(4):
                          (wd,), wdB = wload([(wdv[:, :, nq * 256:(nq + 1) * 256], NF, 256)])
                          for tl in range(8):
                              t = half * 8 + tl
                              bk, bb = gen_banks.next()
                              for f in range(NF):
                                  mm(bk[:, 0:256], hid[:, f, tl * 128:(tl + 1) * 128], wd[:, f, :], f == 0, f == NF - 1,
                                     [hidB[tl // 4], wdB], [bb])
                              dve(lambda e, bk=bk, t=t, nq=nq: e.tensor_tensor(
                                  out=x1[:, t, nq * 256:(nq + 1) * 256], in0=bk[:, 0:256], in1=x1[:, t, nq * 256:(nq + 1) * 256],
                                  op=ALU.add), [bb, x1B[t]], [x1B[t]])
              if stage == 5:
                  dump("x2", x1[:], [128, NT, D], F32, x1B)
                  raise _Stop()
              with ExitStack() as Gp:
                  nG1a, nG1b, nG2 = norm_T(Gp, "nG", lambda t: (x1[:, t, :], x1B[t]), g_ple_d, h2T, h2B, external=True)
                  fG = S.fence()
                  pT = sbt(Gp, "pT", [128, 2, SQ], BF16)
                  pTB = [Buf("pT%d" % i, fG) for i in range(4)]
                  pin = Rot([(sbt(Gp, "pin%d" % i, [128, 256], F32), Buf("pin%d" % i, fG), nsem("pin%d" % i)) for i in range(2)])
                  pbf = Rot([(sbt(Gp, "pbf%d" % i, [128, 256], BF16), Buf("pbf%d" % i, fG)) for i in range(2)])
                  gfb = sbt(Gp, "gfb", [128, D], F32)
                  Bbb = Buf("bbc", fG)
                  S.add("sp", lambda e: e.dma_start(out=gfb[:], in_=g_fin_d.partition_broadcast(128)),
                        writes=[Bbb], dsem=nsem("bbc"))
                  bpl = sbt(Gp, "bpl", [128, D], BF16)
                  e0 = sbt(Gp, "e0", [128, 128], BF16)
                  Bbp = Buf("bpl", fG)
                  dve(lambda e: e.memset(bpl[:], 0.0), [], [Bbp])
                  dve(lambda e: e.memset(e0[:], 0.0), [], [Bbp])
                  dve(lambda e: e.memset(e0[0:1, :], 1.0), [], [Bbp])
                  S.add("pool", lambda e: e.dma_start(out=bpl[0:1, :], in_=b_ple_d.rearrange("(o n) -> o n", o=1)),
                        writes=[Bbp], dsem=nsem("bpl"))
                  pk_bank = allbanks[0]

                  def p_tile(t):
                      pt_, pb_, ds = pin.next()
                      S.add("sp", lambda e: e.dma_start(out=pt_[:], in_=p_d[s, t * 128:(t + 1) * 128, :]),
                            writes=[pb_], dsem=ds)
                      pf, pfB = pbf.next()
                      dve(lambda e: e.tensor_copy(out=pf[:], in_=pt_[:]), [pb_], [pfB])
                      pbk = pk_bank[0].bitcast(BF16)
                      for c in range(2):
                          tr(pbk[:, c * 128:(c + 1) * 128], pf[:, c * 128:(c + 1) * 128], [pfB], [pk_bank[1]])
                      copy_alt(pT[:, :, t * 128:(t + 1) * 128], pbk[:, 0:256].rearrange("p (c t) -> p c t", c=2),
                               [pk_bank[1]], [pTB[t // 4]])

                  norm_drive(nG1a, nG1b, nG2, extra=p_tile)
                  wpg = []
                  for nh in range(2):
                      (wv_,), wb_ = wload([(wview(w_pg_d)[:, :, nh * 512:(nh + 1) * 512], 8, 512)])
                      wpg.append((wv_, wb_))
                  (wpl,), wplB = wload([(wview(w_pl_d), 2, 1024)])
                  gtm = Rot([tuple((sbt(Gp, "gt%d_%d" % (k, i), [128, 512], F32), Buf("gt%d_%d" % (k, i), fG))
                                   for k in range(3)) for i in range(2)])
                  junk = sbt(Gp, "junkG", [128, D], BF16)
                  Bj = Buf("junkG", fG)
                  ysl = Rot([(sbt(Gp, "ys%d" % i, [128, D], F32), Buf("ys%d" % i, fG), nsem("ys%d" % i)) for i in range(2)])
                  def final_norm(t):
                      st, sbf = stat_rot.next()
                      act(junk[:], x1[:, t, :], AF.Square, [x1B[t]], [Bj, sbf], accum_out=st[:, 0:1])
                      rstd_pool(st, sbf)
                      yt, yb, ysem = ysl.next()
                      dve(lambda e: e.scalar_tensor_tensor(out=yt[:], in0=x1[:, t, :], scalar=st[:, 2:3], in1=gfb[:],
                                                           op0=ALU.mult, op1=ALU.mult), [x1B[t], sbf, Bbb], [yb])
                      S.add("sp", lambda e: e.dma_start(out=y_d[s, t * 128:(t + 1) * 128, :], in_=yt[:]),
                            reads=[yb], dsem=ysem, store=True)

                  for t in range(NT):
                      ts_ = slice(t * 128, (t + 1) * 128)
                      for nh in range(2):
                          ns = slice(nh * 512, (nh + 1) * 512)
                          (g1, g1B), (g2, g2B), (g3, g3B) = gtm.next()
                          bk, bb = gen_banks.next()
                          for c in range(8):
                              mm(bk, h2T[:, c, ts_], wpg[nh][0][:, c, :], c == 0, False, [h2B[t // 4], wpg[nh][1]], [bb])
                          mm(bk, e0[:], bpl[:, ns], False, True, [Bbp], [bb])
                          act(g2[:], bk, AF.Sigmoid, [bb], [g2B])
                          pk, pkB = gen_banks.next()
                          for c in range(2):
                              mm(pk, pT[:, c, ts_], wpl[:, c, ns], c == 0, c == 1, [pTB[t // 4], wplB], [pkB])
                          dve(lambda e, g3=g3, g2=g2, pk=pk: e.tensor_tensor(out=g3[:], in0=pk, in1=g2[:], op=ALU.mult),
                              [pkB, g2B], [g3B])
                          dve(lambda e, g3=g3, t=t, ns=ns: e.tensor_tensor(out=x1[:, t, ns], in0=x1[:, t, ns], in1=g3[:],
                                                                          op=ALU.add), [g3B, x1B[t]], [x1B[t]])
                      if t > 0:
                          final_norm(t - 1)
                  final_norm(NT - 1)
              EG.__exit__(None, None, None)
              DG.__exit__(None, None, None)
          except _Stop:
            pass

        for s_ in range(nseq):
            one_seq(s_)

        fin = S.add("sp", lambda e: e.nop(), reads=[])
        fin.deps = list(S.stores)
        with nc.Block() as block:
            S.emit_all(block, esem)
    return nc


def _rope_tables():
    pos = np.arange(SQ, dtype=np.float32)
    tabs = np.zeros((4, 128, SQ), np.float32)
    tabs[0] = 1.0
    tabs[2] = 1.0
    inv = (np.float32(10000.0) ** (-(np.arange(0, 32, 2, dtype=np.float32) / np.float32(32)))).astype(np.float32)
    ang = (pos[:, None] * inv[None, :]).astype(np.float32)
    c, sn = np.cos(ang).astype(np.float32).T, np.sin(ang).astype(np.float32).T
    tabs[0, 64:80], tabs[0, 80:96] = c, c
    tabs[1, 64:80], tabs[1, 80:96] = sn, sn
    inv = (np.float32(500000.0) ** (-(np.arange(0, 16, 2, dtype=np.float32) / np.float32(16)))).astype(np.float32)
    ang = (pos[:, None] * inv[None, :]).astype(np.float32)
    c, sn = np.cos(ang).astype(np.float32).T, np.sin(ang).astype(np.float32).T
    for r0 in (0, 64):
        tabs[2, r0:r0 + 8], tabs[2, r0 + 8:r0 + 16] = c, c
        tabs[3, r0:r0 + 8], tabs[3, r0 + 8:r0 + 16] = sn, sn
    return tabs


def _perm_mats():
    pd = np.zeros((128, 128), np.float32)
    for r0 in (0, 64):
        for r in range(8):
            pd[r0 + r + 8, r0 + r] = -1.0
            pd[r0 + r, r0 + r + 8] = 1.0
    pm = np.zeros((128, 128), np.float32)
    for r in range(16):
        pm[64 + r + 16, 64 + r] = -1.0
        pm[64 + r, 64 + r + 16] = 1.0
    return pd.astype(ml_dtypes.bfloat16), pm.astype(ml_dtypes.bfloat16)


_CACHE = {}


def _get_nc():
    if "nc" not in _CACHE:
        _CACHE["nc"] = build_program()
    return _CACHE["nc"]


def kernel(x, p, attn_norm, w_in, b_gate, lam_q1, lam_k1, lam_q2, lam_k2, diff_subln, w_o_diff, q_norm, w_uq,
           kv_norm, w_ukv, w_o_mla, w_out, ffn_norm, w_ffn_gate, w_ffn_up, w_ffn_down, ple_norm, w_ple_gate,
           b_ple_gate, w_ple, final_norm):
    f = lambda a: np.ascontiguousarray(np.asarray(a, dtype=np.float32))
    x = f(x)
    p = f(p)[0]
    B = x.shape[0]
    nseq = B // NCORES
    w_ukv_ = f(w_ukv)[0].reshape(256, 8, 2, 64)
    vecs = np.zeros((128, 24), np.float32)
    bg = f(b_gate)[0]
    vecs[:, 0:8] = bg[0].reshape(8, 128).T
    vecs[:, 8:16] = bg[1].reshape(8, 128).T
    vecs[:, 16:19] = f(q_norm)[0].reshape(3, 128).T
    vecs[:, 19:21] = f(kv_norm)[0].reshape(2, 128).T
    vecs[:, 21] = f(diff_subln)[0]
    pd, pm = _perm_mats()
    shared = {
        "w_in": f(w_in)[0], "w_o_diff": f(w_o_diff)[0], "w_uq": f(w_uq)[0],
        "w_ukv_kn": np.ascontiguousarray(w_ukv_[:, :, 0, :].reshape(256, 512)),
        "w_ukv_v": np.ascontiguousarray(w_ukv_[:, :, 1, :].reshape(256, 512)),
        "w_o_mla": f(w_o_mla)[0], "w_out": f(w_out)[0], "w_ffn_gate": f(w_ffn_gate)[0], "w_ffn_up": f(w_ffn_up)[0],
        "w_ffn_down": f(w_ffn_down)[0], "w_ple_gate": f(w_ple_gate)[0], "w_ple": f(w_ple)[0],
        "attn_norm": f(attn_norm)[0], "ffn_norm": f(ffn_norm)[0], "ple_norm": f(ple_norm)[0],
        "final_norm": f(final_norm), "b_ple_gate": f(b_ple_gate)[0],
        "lam_q1": f(lam_q1)[0], "lam_k1": f(lam_k1)[0], "lam_q2": f(lam_q2)[0], "lam_k2": f(lam_k2)[0],
        "vecs": vecs, "ident": np.eye(128, dtype=np.float32).astype(ml_dtypes.bfloat16),
        "perm_diff": pd, "perm_mla": pm, "rope_tabs": _rope_tables(),
    }
    nc = _get_nc()
    in_maps = []
    for c in range(NCORES):
        m = dict(shared)
        m["x"] = x[c * nseq:(c + 1) * nseq]
        m["p"] = p[c * nseq:(c + 1) * nseq]
        in_maps.append(m)
    res = run_bass_kernel_spmd(nc, in_maps, core_ids=list(range(NCORES)))
    return np.concatenate([r["y"] for r in res.results], axis=0).astype(np.float32)
```

```python
import math
import os
from contextlib import ExitStack

import numpy as np
import ml_dtypes

import concourse.bass as bass
import concourse.mybir as mybir
from concourse.bass_utils import run_bass_kernel_spmd

F32 = mybir.dt.float32
BF16 = mybir.dt.bfloat16
AF = mybir.ActivationFunctionType
ALU = mybir.AluOpType
AX = mybir.AxisListType

NCORES = 8
SEQ_PER_CORE = 2
SQ = 2048
D = 1024
NT = 16
NB = 4
FF = 2816
NF = 22
EPS = 1e-6
Q_OFF, K_OFF, V_OFF, CQ_OFF, CKV_OFF, KR_OFF, GA_OFF, GB_OFF = 0, 1024, 2048, 3072, 3456, 3712, 3744, 4768
IN_COLS = 5792
WS = 6144
ARENA_BYTES = 207872
LAM_INIT = 0.8 - 0.6 * math.exp(0.0)

ENGS = ("pe", "act", "dve", "pool", "sp")


class Buf:
    __slots__ = ("name", "w", "r", "init", "psum")

    def __init__(self, name, init=(), psum=False):
        self.name = name
        self.w = None
        self.r = []
        self.init = list(init)
        self.psum = psum


class DmaSem:
    __slots__ = ("h", "count")

    def __init__(self, h):
        self.h = h
        self.count = 0


class Op:
    __slots__ = ("eng", "emit", "deps", "dsem", "dval", "ndma", "sig", "sigval", "pos")

    def __init__(self, eng, emit):
        self.eng = eng
        self.emit = emit
        self.deps = []
        self.dsem = None
        self.dval = 0
        self.ndma = 0
        self.sig = False
        self.sigval = 0


class Sched:
    def __init__(self):
        self.streams = {e: [] for e in ENGS}
        self.last_compute = {e: None for e in ENGS}
        self.stores = []

    def fence(self):
        return [o for o in self.last_compute.values() if o is not None] + list(self.stores)

    def add(self, eng, emit, reads=(), writes=(), dsem=None, ndma=1, store=False):
        op = Op(eng, emit)
        deps = {}

        def dep(o, raw):
            if o is None:
                return
            if o.dsem is None and o.eng == eng:
                if eng == "pe":
                    return
            deps[id(o)] = o

        for b in reads:
            dep(b.w, True)
            for o in b.init:
                dep(o, True)
            if b.psum:
                for o in b.r:
                    if o.eng != eng:
                        dep(o, True)
        for b in writes:
            dep(b.w, False)
            for o in b.r:
                dep(o, False)
            for o in b.init:
                dep(o, True)
            b.init = []
        latest = {}
        dl = []
        for o in deps.values():
            if o.dsem is not None:
                dl.append(o)
            elif o.eng not in latest or latest[o.eng].pos < o.pos:
                latest[o.eng] = o
        op.deps = dl + list(latest.values())
        for b in reads:
            b.r.append(op)
        for b in writes:
            b.w = op
            b.r = []
        if dsem is not None:
            op.dsem = dsem
            op.ndma = ndma
            dsem.count += 16 * ndma
            op.dval = dsem.count
            if store:
                self.stores.append(op)
        else:
            self.last_compute[eng] = op
        op.pos = len(self.streams[eng])
        self.streams[eng].append(op)
        return op

    def emit_all(self, block, esem):
        for e in ENGS:
            for op in self.streams[e]:
                for d in op.deps:
                    if d.dsem is None:
                        d.sig = True
        for e in ENGS:
            c = 0
            for op in self.streams[e]:
                if op.sig:
                    c += 1
                    op.sigval = c
        streams = self.streams

        def run(eng_name, eng):
            waited = {}
            for op in streams[eng_name]:
                for d in op.deps:
                    if d.dsem is not None:
                        key, h, v = id(d.dsem), d.dsem.h, d.dval
                    else:
                        key, h, v = d.eng, esem[d.eng], d.sigval
                    if waited.get(key, 0) >= v:
                        continue
                    waited[key] = v
                    eng.wait_ge(h, v)
                res = op.emit(eng)
                if op.dsem is not None:
                    if not isinstance(res, (list, tuple)):
                        res = [res]
                    assert len(res) == op.ndma
                    for r in res:
                        r.then_inc(op.dsem.h, 16)
                elif op.sig:
                    if isinstance(res, (list, tuple)):
                        res = res[-1]
                    res.then_inc(esem[eng_name], 1)

        @block.tensor
        def _(eng):
            run("pe", eng)

        @block.scalar
        def _(eng):
            run("act", eng)

        @block.vector
        def _(eng):
            run("dve", eng)

        @block.gpsimd
        def _(eng):
            run("pool", eng)

        @block.sync
        def _(eng):
            run("sp", eng)


class Rot:
    def __init__(self, items):
        self.items = items
        self.i = 0

    def next(self):
        it = self.items[self.i % len(self.items)]
        self.i += 1
        return it


class _Stop(Exception):
    pass


def build_program(nseq=SEQ_PER_CORE, stage=99):
    nc = bass.Bass("TRN2", target_bir_lowering=False)
    dbg_outs = {}

    def din(name, shape, dt=F32):
        return nc.dram_tensor(name, list(shape), dt, kind="ExternalInput").ap()

    x_d = din("x", [nseq, SQ, D])
    p_d = din("p", [nseq, SQ, 256])
    y_d = nc.dram_tensor("y", [nseq, SQ, D], F32, kind="ExternalOutput").ap()
    w_in_d = din("w_in", [D, IN_COLS])
    w_od_d = din("w_o_diff", [1024, D])
    w_uq_d = din("w_uq", [384, 768])
    w_kn_d = din("w_ukv_kn", [256, 512])
    w_v_d = din("w_ukv_v", [256, 512])
    w_om_d = din("w_o_mla", [512, D])
    w_out_d = din("w_out", [D, D])
    w_g_d = din("w_ffn_gate", [D, FF])
    w_u_d = din("w_ffn_up", [D, FF])
    w_d_d = din("w_ffn_down", [FF, D])
    w_pg_d = din("w_ple_gate", [D, D])
    w_pl_d = din("w_ple", [256, D])
    g_attn_d = din("attn_norm", [D])
    g_ffn_d = din("ffn_norm", [D])
    g_ple_d = din("ple_norm", [D])
    g_fin_d = din("final_norm", [D])
    b_ple_d = din("b_ple_gate", [D])
    lam_d = [din(n, [64]) for n in ("lam_q1", "lam_k1", "lam_q2", "lam_k2")]
    vecs_d = din("vecs", [128, 24])
    ident_d = din("ident", [128, 128], BF16)
    pd_d = din("perm_diff", [128, 128], BF16)
    pm_d = din("perm_mla", [128, 128], BF16)
    tab_d = din("rope_tabs", [4, 128, SQ])

    def wview(w):
        return w.rearrange("(c p) n -> p c n", p=128)

    w_in_v = wview(w_in_d)

    S = Sched()
    G = ExitStack()
    with G:
        arena = G.enter_context(nc.sbuf_tensor("arena", [128, ARENA_BYTES // 2], BF16))
        abase = nc.lookup_mloc(arena).addr
        free_list = [[abase, abase + ARENA_BYTES]]
        _uid = [0]

        def a_alloc(nbytes):
            nbytes = (nbytes + 63) // 64 * 64
            for iv in free_list:
                if iv[1] - iv[0] >= nbytes:
                    off = iv[0]
                    iv[0] += nbytes
                    if iv[0] == iv[1]:
                        free_list.remove(iv)
                    return off, nbytes
            raise RuntimeError("SBUF arena full: need %d, free %s" % (nbytes, free_list))

        def a_free(off, nbytes):
            free_list.append([off, off + nbytes])
            free_list.sort()
            i = 0
            while i + 1 < len(free_list):
                if free_list[i][1] == free_list[i + 1][0]:
                    free_list[i][1] = free_list[i + 1][1]
                    del free_list[i + 1]
                else:
                    i += 1

        def sbt(es, name, shape, dt):
            n = 1
            for d in shape[1:]:
                n *= d
            nbytes = n * (4 if dt == F32 else 2)
            off, nb = a_alloc(nbytes)
            _uid[0] += 1
            t = nc.alloc_sbuf_tensor_at("%s_%d" % (name, _uid[0]), list(shape), dt, offset=off)
            es.callback(a_free, off, nb)
            return t

        def sem(name):
            return G.enter_context(nc.semaphore(name))

        esem = {e: sem("s_" + e) for e in ("pe", "act", "dve", "pool")}
        _ds = [0]

        def dsem():
            _ds[0] += 1
            return DmaSem(sem("d%d" % _ds[0]))

        accs = []
        for i in range(2):
            t = G.enter_context(nc.psum_tensor("acc%d" % i, [128, 1024], F32))
            accs.append((t, Buf("acc%d" % i, psum=True)))
        banks = []
        for i in range(4):
            t = G.enter_context(nc.psum_tensor("bank%d" % i, [128, 512], F32))
            banks.append((t, Buf("bank%d" % i, psum=True)))
        allbanks = []
        for (t, b) in accs:
            allbanks.append((t[:, 0:512], b))
        for (t, b) in banks:
            allbanks.append((t[:], b))
        gen_banks = Rot([allbanks[0], allbanks[1], (banks[0][0][:], banks[0][1]), (banks[1][0][:], banks[1][1]),
                         (banks[2][0][:], banks[2][1])])
        trp_t, trp_b = banks[3]
        trp_bf = trp_t[:].bitcast(BF16)

        ident = sbt(G, "ident", [128, 128], BF16)
        ones = sbt(G, "ones", [128, 128], BF16)
        pdm = sbt(G, "pdm", [128, 128], BF16)
        pmm = sbt(G, "pmm", [128, 128], BF16)
        vecs = sbt(G, "vecs_s", [128, 24], F32)
        lamt = sbt(G, "lamt", [128, 4, 64], F32)
        lsm = sbt(G, "lsm", [128, 8], F32)
        Bconst = Buf("const")
        Blam = Buf("lam")
        dconst = dsem()
        S.add("sp", lambda e: [e.dma_start(out=ident[:], in_=ident_d), e.dma_start(out=pdm[:], in_=pd_d),
                               e.dma_start(out=pmm[:], in_=pm_d), e.dma_start(out=vecs[:], in_=vecs_d)]
              + [e.dma_start(out=lamt[:, i, :], in_=lam_d[i].partition_broadcast(128)) for i in range(4)],
              writes=[Bconst, Blam], dsem=dconst, ndma=8)
        S.add("dve", lambda e: e.memset(ones[:], 1.0), writes=[Bconst])
        mhalf = sbt(G, "mhalf", [128, 1], F32)
        S.add("dve", lambda e: e.memset(mhalf[:], -0.5), writes=[Bconst])

        def rstd_pool(st, sbf):
            S.add("dve", lambda e: e.tensor_scalar(out=st[:, 1:2], in0=st[:, 0:1], scalar1=1.0 / D, scalar2=EPS,
                                                   op0=ALU.mult, op1=ALU.add), reads=[sbf], writes=[sbf])
            S.add("pool", lambda e: e.tensor_tensor(out=st[:, 2:3], in0=st[:, 1:2], in1=mhalf[:], op=ALU.pow),
                  reads=[sbf, Bconst], writes=[sbf])
        lprod = sbt(G, "lprod", [128, 2, 64], F32)
        S.add("dve", lambda e: e.tensor_tensor(out=lprod[:, 0, :], in0=lamt[:, 0, :], in1=lamt[:, 1, :], op=ALU.mult),
              reads=[Blam], writes=[Blam])
        S.add("dve", lambda e: e.tensor_tensor(out=lprod[:, 1, :], in0=lamt[:, 2, :], in1=lamt[:, 3, :], op=ALU.mult),
              reads=[Blam], writes=[Blam])
        S.add("dve", lambda e: e.tensor_reduce(out=lsm[:, 0:2], in_=lprod[:], axis=AX.X, op=ALU.add),
              reads=[Blam], writes=[Blam])
        S.add("act", lambda e: e.activation(out=lsm[:, 2:4], in_=lsm[:, 0:2], func=AF.Exp), reads=[Blam], writes=[Blam])
        S.add("dve", lambda e: e.tensor_tensor(out=lsm[:, 4:5], in0=lsm[:, 3:4], in1=lsm[:, 2:3], op=ALU.subtract),
              reads=[Blam], writes=[Blam])
        S.add("dve", lambda e: e.tensor_scalar(out=lsm[:, 5:6], in0=lsm[:, 4:5], scalar1=-LAM_INIT, scalar2=None,
                                               op0=ALU.add), reads=[Blam], writes=[Blam])
        neglam = lsm[:, 5:6]

        wslots = []
        for i in range(3):
            t = sbt(G, "wslot%d" % i, [128, WS], BF16)
            wslots.append((t, Buf("wslot%d" % i), dsem()))
        wrot = Rot(wslots)

        def wload(parts):
            t, b, ds = wrot.next()
            views = []
            off = 0
            pairs = []
            for part in parts:
                src_, C, N = part[0], part[1], part[2]
                v = t[:, off:off + C * N].rearrange("p (c n) -> p c n", c=C)
                views.append(v)
                if len(part) > 3:
                    n = src_.shape[2]
                    flat = t[:, off:off + C * N]
                    S.add("dve", lambda e, flat=flat: e.memset(flat, 0.0), writes=[b])
                    pairs.append((v[:, :, part[3]:part[3] + n], src_))
                else:
                    pairs.append((v, src_))
                off += C * N
            assert off <= WS
            S.add("pool", lambda e: [e.dma_start(out=v, in_=s) for (v, s) in pairs], writes=[b], dsem=ds,
                  ndma=len(pairs))
            return views, b

        stat = sbt(G, "stat", [128, 64], F32)
        stat_rot = Rot([(stat[:, i * 4:(i + 1) * 4], Buf("stat%d" % i)) for i in range(16)])

        _nsems = {}

        def nsem(key):
            if key not in _nsems:
                _nsems[key] = dsem()
            return _nsems[key]

        def dump(name, ap, shape, dt, bufs):
            d = nc.dram_tensor("dbg_" + name, list(shape), dt, kind="ExternalOutput").ap()
            dbg_outs[name] = d
            S.add("sp", lambda e: e.dma_start(out=d, in_=ap), reads=bufs, dsem=nsem("dbg_" + name), store=True)

        def mm(out, lhsT, rhs, start, stop, reads, writes, **kw):
            return S.add("pe", lambda e: e.matmul(out, lhsT=lhsT, rhs=rhs, start=start, stop=stop, **kw),
                         reads=reads, writes=writes)

        def tr(out, in_, reads, writes):
            return S.add("pe", lambda e: e.transpose(out=out, in_=in_, identity=ident[:]), reads=list(reads) + [Bconst],
                         writes=writes)

        def act(out, in_, func, reads, writes, **kw):
            return S.add("act", lambda e: e.activation(out=out, in_=in_, func=func, **kw), reads=reads, writes=writes)

        def dve(fn, reads, writes):
            return S.add("dve", fn, reads=reads, writes=writes)

        _alt = [0]

        def copy_alt(out, in_, reads, writes):
            _alt[0] += 1
            if _alt[0] % 2:
                return act(out, in_, AF.Copy, reads, writes)
            return dve(lambda e: e.tensor_copy(out=out, in_=in_), reads, writes)

        def norm_T(es, tag, get_x, g_d, dstT, dstB, external=False):
            gbc = sbt(es, tag + "_gbc", [128, D], F32)
            Bg = Buf(tag + "_gbc", S.fence())
            dg = nsem("gbc_" + tag)
            S.add("sp", lambda e: e.dma_start(out=gbc[:], in_=g_d.partition_broadcast(128)), writes=[Bg], dsem=dg)
            junk = sbt(es, tag + "_junk", [128, D], BF16)
            Bj = Buf(tag + "_junk", S.fence())
            hns = Rot([(sbt(es, tag + "_hn%d" % i, [128, D], BF16), Buf(tag + "_hn%d" % i, S.fence())) for i in range(3)])
            trps = Rot([(trp_bf, trp_b), (banks[2][0][:].bitcast(BF16), banks[2][1])])
            def stage1a(t):
                xa, xb = get_x(t)
                st, sbf = stat_rot.next()
                act(junk[:], xa, AF.Square, [xb], [Bj, sbf], accum_out=st[:, 0:1])
                rstd_pool(st, sbf)
                return xa, xb, st, sbf

            def stage1b(xa, xb, st, sbf):
                hn, hb = hns.next()
                dve(lambda e: e.scalar_tensor_tensor(out=hn[:], in0=xa, scalar=st[:, 2:3], in1=gbc[:],
                                                     op0=ALU.mult, op1=ALU.mult), [xb, sbf, Bg], [hb])
                return hn, hb

            def stage2(t, hn, hb):
                tb_, tbB = trps.next()
                for c in range(8):
                    tr(tb_[:, c * 128:(c + 1) * 128], hn[:, c * 128:(c + 1) * 128], [hb], [tbB])
                copy_alt(dstT[:, :, t * 128:(t + 1) * 128], tb_.rearrange("p (c t) -> p c t", c=8), [tbB],
                         [dstB[t // 4]])

            if external:
                return stage1a, stage1b, stage2
            norm_drive(stage1a, stage1b, stage2)

        def norm_drive(s1a, s1b, s2, extra=None):
            q = [s1b(*s1a(0)), s1b(*s1a(1))]
            for t in range(NT):
                sa = s1a(t + 2) if t + 2 < NT else None
                if extra is not None:
                    extra(t)
                s2(t, *q.pop(0))
                if sa is not None:
                    q.append(s1b(*sa))

        def rope(Aps, Ab, r0, r1, perm, Ct, St, Btab, tb, dst, dstB, tmp):
            (qbf, qbfB), (t1, t1B), (t2, t2B), (Bps, BpB) = tmp
            cs = slice(tb * 512, (tb + 1) * 512)
            p0, p1 = (0, 128) if r0 > 0 else (r0, r1)
            lvl = int(os.environ.get("KROPE", "9"))
            act(qbf[p0:p1, :], Aps[p0:p1, :], AF.Copy, [Ab], [qbfB])
            if lvl >= 2:
                mm(Bps[p0:p1, :], perm[p0:p1, p0:p1], qbf[p0:p1, :], True, True, [qbfB, Bconst], [BpB])
            if lvl >= 3:
                dve(lambda e: e.tensor_tensor(out=t1[r0:r1, :], in0=Aps[r0:r1, :], in1=Ct[r0:r1, cs], op=ALU.mult),
                    [Ab] + ([] if os.environ.get("KNOTAB") else [Btab]), [t1B])
            if lvl >= 4:
                dve(lambda e: e.tensor_tensor(out=t2[r0:r1, :], in0=Bps[r0:r1, :], in1=St[r0:r1, cs], op=ALU.mult),
                    [BpB, Btab], [t2B])
            if lvl >= 5:
                dve(lambda e: e.tensor_tensor(out=dst[r0:r1, :], in0=t1[r0:r1, :], in1=t2[r0:r1, :], op=ALU.add),
                    [t1B, t2B], [dstB])

        sc_banks = Rot([banks[0], banks[1]])
        NDUMMY = int(os.environ.get("KDUMMY", "0"))
        att_banks = Rot([banks[0], banks[1], banks[2]] if NDUMMY == 0 else [banks[0], banks[1]])
        LOOK = 2 if NDUMMY == 0 else 1
        INTERLEAVE = os.environ.get("KNOINT", "") == ""

        def attn_steps(KT, KB, QT, QB, r0, r1, Vfn, VB, dv, scale, qt, acc, accB, pts, after):
            steps = []
            nk = 4 * qt + 4
            first = {0: True, 1: True}
            accv = acc[:].rearrange("p (i n) -> p i n", i=4)
            for kt in range(nk):
                j = kt - 4 * qt
                q0 = 128 * j if j > 0 else 0
                sct, scb = att_banks.next()
                pt, ptb = pts.next()

                def s_fn(kt=kt, q0=q0, sct=sct, scb=scb):
                    mm(sct[:, q0:512], KT[r0:r1, kt * 128:(kt + 1) * 128], QT[r0:r1, qt * 512 + q0:(qt + 1) * 512],
                       True, True, [KB, QB], [scb])

                avs = []
                for i in range(max(j, 0), 4):
                    bk = i // 2
                    avs.append((i, first[bk]))
                    first[bk] = False

                def rest_fn(kt=kt, j=j, q0=q0, sct=sct, scb=scb, pt=pt, ptb=ptb, avs=avs, last=(kt == nk - 1)):
                    for dmy in range(NDUMMY):
                        mm(banks[2][0][:, :], ones[:], QT[:, 0:512], True, True, [Bconst, QB], [banks[2][1]])
                    act(pt[:, q0:512], sct[:, q0:512], AF.Exp, [scb], [ptb], scale=scale)
                    if j >= 0:
                        dve(lambda e: e.memset(pt[64:128, 128 * j:128 * j + 64], 0.0), [], [ptb])
                    if os.environ.get("KAV", "") == "dense":
                        vv = Vfn(kt)
                        for bki in range(2):
                            mm(acc[0:dv, bki * 512 + q0:(bki + 1) * 512], vv[:, 0:dv] if bki == 0 else ones[:, 0:dv],
                               pt[:, q0:512], kt == 0, True, [ptb, VB, Bconst], [accB], skip_group_check=True)
                    else:
                      for (i, st) in avs:
                        mm(accv[:, i, 0:dv + 1], pt[:, i * 128:(i + 1) * 128], Vfn(kt), st, True, [ptb, VB], [accB],
                           skip_group_check=True)
                    if last:
                        return after()
                    return None

                steps.append((s_fn, rest_fn))
            return steps

        def run_steps(steps, side=()):
            side = list(side)
            busy = [False]

            def run_side():
                fn, b = side.pop(0)
                fn()
                busy[0] = b

            if not steps:
                while side:
                    run_side()
                return
            per = -(-len(side) // len(steps)) if side else 0
            pending = []
            for i in range(min(LOOK, len(steps))):
                steps[i][0]()
            for i, (s_fn, rest_fn) in enumerate(steps):
                if i + LOOK < len(steps):
                    steps[i + LOOK][0]()
                while pending and pending[0][0] <= i and not busy[0]:
                    pending.pop(0)[1]()
                d = rest_fn()
                if d is not None:
                    pending.append((i + d[0], d[1]))
                for _ in range(per):
                    if side:
                        run_side()
            while side:
                run_side()
            for (_, fn) in pending:
                fn()

        def one_seq(s):
          try:
              ABC = ExitStack()
              with ABC:
                  hT = sbt(ABC, "hT", [128, 8, SQ], BF16)
                  hTB = [Buf("hT%d" % i, S.fence()) for i in range(4)]
                  odT = sbt(ABC, "odT", [128, 8, SQ], BF16)
                  omT = sbt(ABC, "omT", [128, 4, SQ], BF16)
                  omTB = [Buf("omT%d" % i, S.fence()) for i in range(4)]
                  with ExitStack() as A:
                      xsl = Rot([(sbt(A, "xsA%d" % i, [128, D], F32), Buf("xsA%d" % i, S.fence()), nsem("xsA%d" % i))
                                 for i in range(6)])

                      def get_x(t, s=s, xsl=xsl):
                          xt, xb, ds = xsl.next()
                          S.add("sp", lambda e: e.dma_start(out=xt[:], in_=x_d[s, t * 128:(t + 1) * 128, :]), writes=[xb],
                                dsem=ds)
                          return xt[:], xb

                      norm_T(A, "nA", get_x, g_attn_d, hT, hTB)
                  if stage == 1:
                      dump("hT", hT[:], [128, 8, SQ], BF16, hTB)
                      raise _Stop()

                  with ExitStack() as Bx:
                      Ct = sbt(Bx, "Ct", [128, SQ], F32)
                      St = sbt(Bx, "St", [128, SQ], F32)
                      Btab = Buf("tab", S.fence())
                      dtab = nsem("tab")
                      S.add("sp", lambda e: [e.dma_start(out=Ct[:], in_=tab_d[0]), e.dma_start(out=St[:], in_=tab_d[1])],
                            writes=[Btab], dsem=dtab, ndma=2)
                      Vbuf = sbt(Bx, "Vbuf", [128, 8704], BF16)
                      QK = [(sbt(Bx, "QT%d" % i, [128, SQ], BF16), Buf("QT%d" % i, S.fence()),
                             sbt(Bx, "KT%d" % i, [128, SQ], BF16), Buf("KT%d" % i, S.fence())) for i in range(2)]
                      K2s = [(sbt(Bx, "K2_%d" % i, [128, SQ], BF16), Buf("K2_%d" % i, S.fence())) for i in range(2)]
                      pts = Rot([(sbt(Bx, "pt%d" % i, [128, 512], BF16), Buf("pt%d" % i, S.fence())) for i in range(4)])
                      f0 = S.fence()
                      rtmp = ((sbt(Bx, "qbf", [128, 512], BF16), Buf("qbf", f0)),
                              (sbt(Bx, "rt1", [128, 512], F32), Buf("rt1", f0)),
                              (sbt(Bx, "rt2", [128, 512], F32), Buf("rt2", f0)),
                              (banks[2][0], banks[2][1]))
                      o1n = sbt(Bx, "o1n", [128, 4, 128], F32)
                      o1nB = Buf("o1n", f0)
                      otmp = sbt(Bx, "otmp", [128, 4, 128], F32)
                      otmpB = Buf("otmp", f0)
                      odf = sbt(Bx, "odf", [128, 4, 128], F32)
                      odfB = Buf("odf", f0)
                      osq = sbt(Bx, "osq", [128, 4, 128], F32)
                      osqB = Buf("osq", f0)
                      odn = sbt(Bx, "odn", [128, 4, 128], BF16)
                      odnB = Buf("odn", f0)
                      omn = sbt(Bx, "omn", [128, 4, 64], BF16)
                      omnB = Buf("omn", f0)

                      latT = odT
                      latB = [Buf("lat%d" % i, f0) for i in range(4)]
                      latf = [Vbuf[:, j * 1024:(j + 1) * 1024].bitcast(F32) for j in range(5)]
                      sqs = [Vbuf[:, 5120 + j * 512:5120 + (j + 1) * 512] for j in range(5)]
                      ltB = Buf("lattmp", f0)
                      rsq = sbt(Bx, "rsq", [128, 512], F32)
                      rsqB = Buf("rsq", f0)
                      rstd = sbt(Bx, "rstdl", [128, 512], F32)
                      rstdB = Buf("rstdl", f0)
                      (wl, wkr), wlB = wload([(w_in_v[:, :, CQ_OFF:CQ_OFF + 640], 8, 640),
                                                  (w_in_v[:, :, KR_OFF:KR_OFF + 32], 8, 128, 64) if os.environ.get("KDBG", "") != "krplain"
                                                  else (w_in_v[:, :, KR_OFF - 96:KR_OFF + 32], 8, 128)])
                      for tb in range(NB):
                          cs = slice(tb * 512, (tb + 1) * 512)
                          for j in range(6):
                              bk, bb = sc_banks.next()
                              if j < 5:
                                  c0 = j * 128
                                  for c in range(8):
                                      mm(bk[:, :], wl[:, c, c0:c0 + 128], hT[:, c, cs], c == 0, c == 7, [wlB, hTB[tb]], [bb])
                                  act(latf[j], bk[:, :], AF.Copy, [bb], [ltB])
                                  act(sqs[j], bk[:, :], AF.Square, [bb], [ltB])
                              elif os.environ.get("KDBG", "") != "skipkr":
                                  for c in range(8):
                                      mm(bk[:, :], wkr[:, c, :], hT[:, c, cs], c == 0, c == 7, [wlB, hTB[tb]], [bb])
                                  rope(bk, bb, 0, 128, pmm, Ct, St, Btab, tb, latT[:, 5, cs], latB[tb], rtmp)
                          for (js, nrm, vcol) in (((0, 1, 2), 384.0, 8 + 8), ((3, 4), 256.0, 8 + 8 + 3)):
                              rk, rb = banks[2]
                              for n, j in enumerate(js):
                                  mm(rk[:, :], ones[:], sqs[j], n == 0, n == len(js) - 1, [ltB, Bconst], [rb])
                              act(rsq[:], rk[:, :], AF.Sqrt, [rb], [rsqB], scale=1.0 / nrm, bias=EPS)
                              dve(lambda e: e.reciprocal(out=rstd[:], in_=rsq[:]), [rsqB], [rstdB])
                              for n, j in enumerate(js):
                                  dve(lambda e, j=j, n=n, vcol=vcol, cs=cs: e.scalar_tensor_tensor(
                                      out=latT[:, j, cs], in0=latf[j], scalar=vecs[:, vcol + n:vcol + n + 1], in1=rstd[:],
                                      op0=ALU.mult, op1=ALU.mult), [ltB, rstdB, Bconst], [latB[tb]])
                      if stage == 2.1:
                          dump("latT", odT[:, 0:5, :], [128, 5, SQ], BF16, latB)
                          if os.environ.get("KDBG", "") not in ("skipkr", "nodumpkr"):
                              dump("kr", odT[64:96, 5, :], [32, SQ], BF16, latB)
                          raise _Stop()
                      (wuq, wkn, wv), wmB = wload([(wview(w_uq_d), 3, 768), (wview(w_kn_d), 2, 512), (wview(w_v_d), 2, 512)])
                      VB = Buf("V", S.fence() + [ltB.w] + list(ltB.r))
                      Vm = Vbuf[:, 0:16 * 8 * 66].rearrange("p (t h d) -> p t h d", t=16, h=8)
                      dve(lambda e: e.memset(Vm[:, :, :, 64:65], 1.0), [], [VB])
                      for t in range(NT):
                          bk, bb = sc_banks.next()
                          for c in range(2):
                              mm(bk[:, :], latT[:, 3 + c, t * 128:(t + 1) * 128], wv[:, c, :], c == 0, c == 1,
                                 [latB[t // 4], wmB], [bb])
                          copy_alt(Vm[:, t, :, 0:64], bk[:, :].rearrange("p (h d) -> p h d", h=8), [bb], [VB])
                      trp_f = trp_t[:, :]

                      def diff_proj_tasks(h):
                          (wq, wk), wqB = wload([(w_in_v[:, :, Q_OFF + h * 128:Q_OFF + (h + 1) * 128], 8, 128),
                                                 (w_in_v[:, :, K_OFF + h * 128:K_OFF + (h + 1) * 128], 8, 128)])
                          QTt, QB_, KTt, KB_ = QK[h % 2]
                          K2t, K2B = K2s[h % 2]
                          (qbf, qbfB), (t1, t1B), (t2, t2B), _ = rtmp
                          tasks = []
                          if h < 2:
                              dve(lambda e: e.memset(KTt[64:128, :], 0.0), [], [KB_])
                              dve(lambda e: e.memset(K2t[0:64, :], 0.0), [], [K2B])
                          for tb in range(NB):
                              cs = slice(tb * 512, (tb + 1) * 512)
                              for (wx, dT, dB) in ((wk, None, None), (wq, QTt, QB_)):
                                  def pm(c, wx=wx, cs=cs, tb=tb):
                                      mm(trp_f, wx[:, c, :], hT[:, c, cs], c == 0, c == 7, [wqB, hTB[tb]], [trp_b])

                                  def p1b(cs=cs):
                                      dve(lambda e: e.tensor_copy(out=qbf[:], in_=trp_f), [trp_b], [qbfB])
                                      dve(lambda e: e.tensor_tensor(out=t1[:], in0=trp_f, in1=Ct[:, cs], op=ALU.mult),
                                          [trp_b, Btab], [t1B])

                                  def p2(dT=dT, dB=dB, cs=cs):
                                      b2, b2B = trp_t, trp_b
                                      mm(b2[:, :], pdm[:, :], qbf[:], True, True, [qbfB, Bconst], [b2B])
                                      dve(lambda e: e.tensor_tensor(out=t2[:], in0=b2[:, :], in1=St[:, cs], op=ALU.mult),
                                          [b2B, Btab], [t2B])
                                      if dT is None:
                                          dve(lambda e: e.tensor_tensor(out=KTt[0:64, cs], in0=t1[0:64, :], in1=t2[0:64, :],
                                                                        op=ALU.add), [t1B, t2B], [KB_])
                                          dve(lambda e: e.tensor_tensor(out=K2t[64:128, cs], in0=t1[64:128, :],
                                                                        in1=t2[64:128, :], op=ALU.add), [t1B, t2B], [K2B])
                                      else:
                                          dve(lambda e: e.tensor_tensor(out=dT[:, cs], in0=t1[:], in1=t2[:], op=ALU.add),
                                              [t1B, t2B], [dB])

                                  tasks += [((lambda c=c, pm=pm: pm(c)), True) for c in range(8)]
                                  tasks += [(p1b, False), (p2, False)]
                          return tasks

                      sc_m = 96.0 ** -0.5
                      def mla_proj_tasks(h):
                          QTt, QB_, KTt, KB_ = QK[h % 2]
                          (qbf, qbfB), (t1, t1B), (t2, t2B), _ = rtmp
                          tasks = []
                          for tb in range(NB):
                              cs = slice(tb * 512, (tb + 1) * 512)

                              def k1(cs=cs, tb=tb):
                                  for c in range(2):
                                      mm(trp_f[0:64, :], wkn[:, c, h * 64:(h + 1) * 64], latT[:, 3 + c, cs], c == 0, c == 1,
                                         [wmB, latB[tb]], [trp_b])

                              def k2(cs=cs, tb=tb):
                                  dve(lambda e: e.tensor_copy(out=KTt[0:64, cs], in_=trp_f[0:64, :]), [trp_b], [KB_])
                                  dve(lambda e: e.tensor_copy(out=KTt[64:96, cs], in_=latT[64:96, 5, cs]), [latB[tb]], [KB_])

                              def q1(cs=cs, tb=tb):
                                  for c in range(3):
                                      mm(trp_f[0:96, :], wuq[:, c, h * 96:(h + 1) * 96], latT[:, c, cs], c == 0, c == 2,
                                         [wmB, latB[tb]], [trp_b])

                              def q2(cs=cs):
                                  dve(lambda e: e.tensor_copy(out=qbf[0:96, :], in_=trp_f[0:96, :]), [trp_b], [qbfB])
                                  dve(lambda e: e.tensor_tensor(out=t1[0:96, :], in0=trp_f[0:96, :], in1=Ct[0:96, cs],
                                                                op=ALU.mult), [trp_b, Btab], [t1B])

                              def q3(cs=cs):
                                  mm(trp_f[0:96, :], pmm[0:96, 0:96], qbf[0:96, :], True, True, [qbfB, Bconst], [trp_b])
                                  dve(lambda e: e.tensor_tensor(out=t2[0:96, :], in0=trp_f[0:96, :], in1=St[0:96, cs],
                                                                op=ALU.mult), [trp_b, Btab], [t2B])
                                  dve(lambda e: e.tensor_tensor(out=QTt[0:96, cs], in0=t1[0:96, :], in1=t2[0:96, :],
                                                                op=ALU.add), [t1B, t2B], [QB_])

                              tasks += [(k1, True), (k2, False), (q1, True), (q2, False), (q3, False)]
                          return tasks

                      for fn, _b in mla_proj_tasks(0):
                          fn()
                      for h in range(8):
                          QTt, QB_, KTt, KB_ = QK[h % 2]
                          steps = []
                          for qt in range(NB):
                              acc, accB = accs[(h * 4 + qt) % 2]

                              def after(h=h, qt=qt, acc=acc, accB=accB):
                                  accv = acc[:].rearrange("p (i n) -> p i n", i=4)
                                  st, sbf = stat_rot.next()
                                  dve(lambda e: e.reciprocal(out=st[:, 0:4], in_=accv[:, :, 64]), [accB], [sbf])
                                  dve(lambda e: e.tensor_tensor(out=omn[:], in0=accv[:, :, 0:64],
                                                                in1=st[:, 0:4].unsqueeze(2).broadcast_to([128, 4, 64]),
                                                                op=ALU.mult), [accB, sbf], [omnB])
                                  ro = 64 * (h % 2)

                                  def later():
                                      for i in range(4):
                                          tr(trp_bf[ro:ro + 64, i * 128:(i + 1) * 128], omn[:, i, :], [omnB], [trp_b])
                                      dve(lambda e: e.tensor_copy(out=omT[ro:ro + 64, h // 2, qt * 512:(qt + 1) * 512],
                                                                  in_=trp_bf[ro:ro + 64, 0:512]), [trp_b], [omTB[qt]])

                                  return (3, later)

                              steps += attn_steps(KTt, KB_, QTt, QB_, 0, 96, lambda kt, h=h: Vm[:, kt, h, 0:65], VB, 64, sc_m,
                                                  qt, acc, accB, pts, after)
                          side = mla_proj_tasks(h + 1) if h + 1 < 8 else []
                          if h == 7 and INTERLEAVE:
                              S.add("sp", lambda e: [e.dma_start(out=Ct[:], in_=tab_d[2]),
                                                     e.dma_start(out=St[:], in_=tab_d[3])],
                                    writes=[Btab], dsem=dtab, ndma=2)
                              side = diff_proj_tasks(0)
                          run_steps(steps, side)

                      if stage == 2.2:
                          dump("omT", omT[:], [128, 4, SQ], BF16, omTB)
                          raise _Stop()
                      if not INTERLEAVE:
                          S.add("sp", lambda e: [e.dma_start(out=Ct[:], in_=tab_d[2]), e.dma_start(out=St[:], in_=tab_d[3])],
                                writes=[Btab], dsem=dtab, ndma=2)
                      f1 = S.fence()
                      odTB = [Buf("odT%d" % i, f1) for i in range(4)]
                      Vd = Vbuf[:, 0:16 * 4 * 130].rearrange("p (t h d) -> p t h d", t=16, h=4)
                      sc_d = 64.0 ** -0.5
                      for g in range(2):
                          (wvd,), wvB = wload([(w_in_v[:, :, V_OFF + g * 512:V_OFF + (g + 1) * 512], 8, 512)])
                          dve(lambda e: e.memset(Vd[:, :, :, 128:129], 1.0), [], [VB])
                          for t in range(NT):
                              bk, bb = sc_banks.next()
                              for c in range(8):
                                  mm(bk[:, :], hT[:, c, t * 128:(t + 1) * 128], wvd[:, c, :], c == 0, c == 7,
                                     [hTB[t // 4], wvB], [bb])
                              copy_alt(Vd[:, t, :, 0:128], bk[:, :].rearrange("p (h d) -> p h d", h=4), [bb], [VB])
                          for hh in range(4):
                              h = g * 4 + hh
                              QTt, QB_, KTt, KB_ = QK[h % 2]
                              if not INTERLEAVE:
                                  for fn, _b in diff_proj_tasks(h):
                                      fn()
                              steps = []
                              for qt in range(NB):
                                  a1, a1B = accs[0]
                                  a2, a2B = accs[1]

                                  def after1(a1=a1, a1B=a1B):
                                      accv = a1[:].rearrange("p (i n) -> p i n", i=4)
                                      st, sbf = stat_rot.next()
                                      dve(lambda e: e.reciprocal(out=st[:, 0:4], in_=accv[:, :, 128]), [a1B], [sbf])
                                      dve(lambda e: e.tensor_tensor(out=o1n[:], in0=accv[:, :, 0:128],
                                                                    in1=st[:, 0:4].unsqueeze(2).broadcast_to([128, 4, 128]),
                                                                    op=ALU.mult), [a1B, sbf], [o1nB])

                                  def after2(h=h, qt=qt, a2=a2, a2B=a2B):
                                      accv = a2[:].rearrange("p (i n) -> p i n", i=4)
                                      st, sbf = stat_rot.next()
                                      st2, sbf2 = stat_rot.next()
                                      dve(lambda e: e.reciprocal(out=st[:, 0:4], in_=accv[:, :, 128]), [a2B], [sbf])
                                      dve(lambda e: e.tensor_scalar(out=st2[:, 0:4], in0=st[:, 0:4], scalar1=neglam,
                                                                    scalar2=None, op0=ALU.mult), [sbf, Blam], [sbf2])
                                      dve(lambda e: e.tensor_tensor(out=otmp[:], in0=accv[:, :, 0:128],
                                                                    in1=st2[:, 0:4].unsqueeze(2).broadcast_to([128, 4, 128]),
                                                                    op=ALU.mult), [a2B, sbf2], [otmpB])
                                      dve(lambda e: e.tensor_tensor(out=odf[:], in0=o1n[:], in1=otmp[:], op=ALU.add),
                                          [o1nB, otmpB], [odfB])
                                      dve(lambda e: e.tensor_tensor(out=osq[:], in0=odf[:], in1=odf[:], op=ALU.mult),
                                          [odfB], [osqB])
                                      st3, sbf3 = stat_rot.next()
                                      st4, sbf4 = stat_rot.next()
                                      st5, sbf5 = stat_rot.next()
                                      dve(lambda e: e.tensor_reduce(out=st3[:, 0:4], in_=osq[:], axis=AX.X, op=ALU.add),
                                          [osqB], [sbf3])
                                      dve(lambda e: e.tensor_scalar(out=st4[:, 0:4], in0=st3[:, 0:4], scalar1=1.0 / 128,
                                                                    scalar2=EPS, op0=ALU.mult, op1=ALU.add), [sbf3], [sbf4])
                                      act(st5[:, 0:4], st4[:, 0:4], AF.Ln, [sbf4], [sbf5])
                                      act(st3[:, 0:4], st5[:, 0:4], AF.Exp, [sbf5], [sbf3], scale=-0.5)
                                      dve(lambda e: e.scalar_tensor_tensor(
                                          out=odn[:], in0=odf[:], scalar=1.0 - LAM_INIT,
                                          in1=st3[:, 0:4].unsqueeze(2).broadcast_to([128, 4, 128]),
                                          op0=ALU.mult, op1=ALU.mult), [odfB, sbf3], [odnB])
                                      def later():
                                          for i in range(4):
                                              tr(trp_bf[:, i * 128:(i + 1) * 128], odn[:, i, :], [odnB], [trp_b])
                                          dve(lambda e: e.tensor_scalar(out=odT[:, h, qt * 512:(qt + 1) * 512],
                                                                        in0=trp_bf[:, 0:512], scalar1=vecs[:, 21:22],
                                                                        scalar2=None, op0=ALU.mult),
                                              [trp_b, Bconst], [odTB[qt]])

                                      return (8, later)

                                  K2t, K2B = K2s[h % 2]
                                  steps += attn_steps(KTt, KB_, QTt, QB_, 0, 128, lambda kt, hh=hh: Vd[:, kt, hh, 0:129], VB,
                                                      128, sc_d, qt, a1, a1B, pts, after1)
                                  steps += attn_steps(K2t, K2B, QTt, QB_, 0, 128, lambda kt, hh=hh: Vd[:, kt, hh, 0:129], VB,
                                                      128, sc_d, qt, a2, a2B, pts, after2)
                              run_steps(steps, diff_proj_tasks(h + 1) if (INTERLEAVE and h + 1 < 8) else [])

                  if stage == 2.3:
                      dump("odT", odT[:], [128, 8, SQ], BF16, odTB)
                      raise _Stop()
                  CD = ExitStack()
                  CD.__enter__()
                  mgT = sbt(CD, "mgT", [128, 8, SQ], BF16)
                  mgB = [Buf("mgT%d" % i, S.fence()) for i in range(4)]
                  with ExitStack() as C:
                      fC = S.fence()
                      tmps = Rot([tuple((sbt(C, "mc%d_%d" % (k, i), [128, 512], F32), Buf("mc%d_%d" % (k, i), fC))
                                        for k in range(4)) for i in range(2)])
                      for j in range(8):
                          (wga, wgb, wod, wom), wcB = wload([
                              (w_in_v[:, :, GA_OFF + j * 128:GA_OFF + (j + 1) * 128], 8, 128),
                              (w_in_v[:, :, GB_OFF + j * 128:GB_OFF + (j + 1) * 128], 8, 128),
                              (wview(w_od_d)[:, :, j * 128:(j + 1) * 128], 8, 128),
                              (wview(w_om_d)[:, :, j * 128:(j + 1) * 128], 4, 128)])
                          for tb in range(NB):
                              cs = slice(tb * 512, (tb + 1) * 512)
                              (sa, saB), (sb_, sbB), (m1, m1B), (m2, m2B) = tmps.next()
                              ga, gaB = gen_banks.next()
                              for c in range(8):
                                  mm(ga, wga[:, c, :], hT[:, c, cs], c == 0, c == 7, [wcB, hTB[tb]], [gaB])
                              act(sa[:], ga, AF.Sigmoid, [gaB, Bconst], [saB], bias=vecs[:, j:j + 1])
                              gb, gbB = gen_banks.next()
                              for c in range(8):
                                  mm(gb, wgb[:, c, :], hT[:, c, cs], c == 0, c == 7, [wcB, hTB[tb]], [gbB])
                              act(sb_[:], gb, AF.Sigmoid, [gbB, Bconst], [sbB], bias=vecs[:, 8 + j:8 + j + 1])
                              oa, oaB = gen_banks.next()
                              for c in range(8):
                                  mm(oa, wod[:, c, :], odT[:, c, cs], c == 0, c == 7, [wcB, odTB[tb]], [oaB])
                              dve(lambda e, m1=m1, sa=sa, oa=oa: e.tensor_tensor(out=m1[:], in0=oa, in1=sa[:], op=ALU.mult),
                                  [oaB, saB], [m1B])
                              ob, obB = gen_banks.next()
                              for c in range(4):
                                  mm(ob, wom[:, c, :], omT[:, c, cs], c == 0, c == 3, [wcB, omTB[tb]], [obB])
                              dve(lambda e, m2=m2, sb_=sb_, ob=ob: e.tensor_tensor(out=m2[:], in0=ob, in1=sb_[:], op=ALU.mult),
                                  [obB, sbB], [m2B])
                              dve(lambda e, m1=m1, m2=m2, j=j, cs=cs: e.tensor_tensor(out=mgT[:, j, cs], in0=m1[:], in1=m2[:],
                                                                                    op=ALU.add), [m1B, m2B], [mgB[tb]])
              if stage == 3:
                  dump("mgT", mgT[:], [128, 8, SQ], BF16, mgB)
                  raise _Stop()
              DG = ExitStack()
              DG.__enter__()
              x1 = sbt(DG, "x1", [128, NT, D], F32)
              fD = S.fence()
              x1B = [Buf("x1_%d" % i, fD) for i in range(NT)]
              EG = ExitStack()
              EG.__enter__()
              h2T = sbt(EG, "h2T", [128, 8, SQ], BF16)
              h2B = [Buf("h2T%d" % i, S.fence()) for i in range(4)]
              with ExitStack() as Dp:
                  xsl = Rot([(sbt(Dp, "xsD%d" % i, [128, D], F32), Buf("xsD%d" % i, fD), nsem("xsD%d" % i)) for i in range(2)])
                  nE1a, nE1b, nE2 = norm_T(Dp, "nE", lambda t: (x1[:, t, :], x1B[t]), g_ffn_d, h2T, h2B, external=True)
                  wo = []
                  for nh in range(2):
                      (wv_,), wb_ = wload([(wview(w_out_d)[:, :, nh * 512:(nh + 1) * 512], 8, 512)])
                      wo.append((wv_, wb_))
                  qE = []
                  for t in range(NT):
                      xt, xb, ds = xsl.next()
                      S.add("sp", lambda e, xt=xt, t=t, s=s: e.dma_start(out=xt[:], in_=x_d[s, t * 128:(t + 1) * 128, :]),
                            writes=[xb], dsem=ds)
                      for nh in range(2):
                          bk, bb = gen_banks.next()
                          for c in range(8):
                              mm(bk, mgT[:, c, t * 128:(t + 1) * 128], wo[nh][0][:, c, :], c == 0, c == 7,
                                 [mgB[t // 4], wo[nh][1]], [bb])
                          dve(lambda e, bk=bk, xt=xt, t=t, nh=nh: e.tensor_tensor(
                              out=x1[:, t, nh * 512:(nh + 1) * 512], in0=bk, in1=xt[:, nh * 512:(nh + 1) * 512], op=ALU.add),
                              [bb, xb], [x1B[t]])
                      sa = nE1a(t)
                      if len(qE) >= 2:
                          nE2(t - 2, *qE.pop(0))
                      qE.append(nE1b(*sa))
                  nE2(NT - 2, *qE.pop(0))
                  nE2(NT - 1, *qE.pop(0))
              CD.__exit__(None, None, None)
              if stage == 4:
                  dump("x1", x1[:], [128, NT, D], F32, x1B)
                  raise _Stop()
              with ExitStack() as Fp:
                  fF = S.fence()
                  hid = sbt(Fp, "hid", [128, NF, 1024], BF16)
                  hidB = [Buf("hid%d" % i, fF) for i in range(2)]
                  sgs = Rot([(sbt(Fp, "sg%d" % i, [128, 512], F32), Buf("sg%d" % i, fF)) for i in range(3)])
                  for half in range(2):
                      for fb in range(11):
                          (wg, wu), wfB = wload([(wview(w_g_d)[:, :, fb * 256:(fb + 1) * 256], 8, 256),
                                                 (wview(w_u_d)[:, :, fb * 256:(fb + 1) * 256], 8, 256)])
                          for fc in range(2):
                              f = fb * 2 + fc
                              for tbh in range(2):
                                  tb = half * 2 + tbh
                                  cs = slice(tb * 512, (tb + 1) * 512)
                                  gk, gkB = gen_banks.next()
                                  for c in range(8):
                                      mm(gk, wg[:, c, fc * 128:(fc + 1) * 128], h2T[:, c, cs], c == 0, c == 7, [wfB, h2B[tb]],
                                         [gkB])
                                  sg, sgB = sgs.next()
                                  act(sg[:], gk, AF.Silu, [gkB], [sgB])
                                  uk, ukB = gen_banks.next()
                                  for c in range(8):
                                      mm(uk, wu[:, c, fc * 128:(fc + 1) * 128], h2T[:, c, cs], c == 0, c == 7, [wfB, h2B[tb]],
                                         [ukB])
                                  dve(lambda e, sg=sg, uk=uk, f=f, tbh=tbh: e.tensor_tensor(
                                      out=hid[:, f, tbh * 512:(tbh + 1) * 512], in0=uk, in1=sg[:], op=ALU.mult),
                                      [ukB, sgB], [hidB[tbh]])
                      wdv = w_d_d.rearrange("(f p) n -> p f n", p=128)
                      for nq in range(4):
                          (wd,), wdB = wload([(wdv[:, :, nq * 256:(nq + 1) * 256], NF, 256)])
                          for tl in range(8):
                              t = half * 8 + tl
                              bk, bb = gen_banks.next()
                              for f in range(NF):
                                  mm(bk[:, 0:256], hid[:, f, tl * 128:(tl + 1) * 128], wd[:, f, :], f == 0, f == NF - 1,
                                     [hidB[tl // 4], wdB], [bb])
                              dve(lambda e, bk=bk, t=t, nq=nq: e.tensor_tensor(
                                  out=x1[:, t, nq * 256:(nq + 1) * 256], in0=bk[:, 0:256], in1=x1[:, t, nq * 256:(nq + 1) * 256],
                                  op=ALU.add), [bb, x1B[t]], [x1B[t]])
              if stage == 5:
                  dump("x2", x1[:], [128, NT, D], F32, x1B)
                  raise _Stop()
              with ExitStack() as Gp:
                  nG1a, nG1b, nG2 = norm_T(Gp, "nG", lambda t: (x1[:, t, :], x1B[t]), g_ple_d, h2T, h2B, external=True)
                  fG = S.fence()
                  pT = sbt(Gp, "pT", [128, 2, SQ], BF16)
                  pTB = [Buf("pT%d" % i, fG) for i in range(4)]
                  pin = Rot([(sbt(Gp, "pin%d" % i, [128, 256], F32), Buf("pin%d" % i, fG), nsem("pin%d" % i)) for i in range(2)])
                  pbf = Rot([(sbt(Gp, "pbf%d" % i, [128, 256], BF16), Buf("pbf%d" % i, fG)) for i in range(2)])
                  gfb = sbt(Gp, "gfb", [128, D], F32)
                  Bbb = Buf("bbc", fG)
                  S.add("sp", lambda e: e.dma_start(out=gfb[:], in_=g_fin_d.partition_broadcast(128)),
                        writes=[Bbb], dsem=nsem("bbc"))
                  bpl = sbt(Gp, "bpl", [128, D], BF16)
                  e0 = sbt(Gp, "e0", [128, 128], BF16)
                  Bbp = Buf("bpl", fG)
                  dve(lambda e: e.memset(bpl[:], 0.0), [], [Bbp])
                  dve(lambda e: e.memset(e0[:], 0.0), [], [Bbp])
                  dve(lambda e: e.memset(e0[0:1, :], 1.0), [], [Bbp])
                  S.add("pool", lambda e: e.dma_start(out=bpl[0:1, :], in_=b_ple_d.rearrange("(o n) -> o n", o=1)),
                        writes=[Bbp], dsem=nsem("bpl"))
                  pk_bank = allbanks[0]

                  def p_tile(t):
                      pt_, pb_, ds = pin.next()
                      S.add("sp", lambda e: e.dma_start(out=pt_[:], in_=p_d[s, t * 128:(t + 1) * 128, :]),
                            writes=[pb_], dsem=ds)
                      pf, pfB = pbf.next()
                      dve(lambda e: e.tensor_copy(out=pf[:], in_=pt_[:]), [pb_], [pfB])
                      pbk = pk_bank[0].bitcast(BF16)
                      for c in range(2):
                          tr(pbk[:, c * 128:(c + 1) * 128], pf[:, c * 128:(c + 1) * 128], [pfB], [pk_bank[1]])
                      copy_alt(pT[:, :, t * 128:(t + 1) * 128], pbk[:, 0:256].rearrange("p (c t) -> p c t", c=2),
                               [pk_bank[1]], [pTB[t // 4]])

                  norm_drive(nG1a, nG1b, nG2, extra=p_tile)
                  wpg = []
                  for nh in range(2):
                      (wv_,), wb_ = wload([(wview(w_pg_d)[:, :, nh * 512:(nh + 1) * 512], 8, 512)])
                      wpg.append((wv_, wb_))
                  (wpl,), wplB = wload([(wview(w_pl_d), 2, 1024)])
                  gtm = Rot([tuple((sbt(Gp, "gt%d_%d" % (k, i), [128, 512], F32), Buf("gt%d_%d" % (k, i), fG))
                                   for k in range(3)) for i in range(2)])
                  junk = sbt(Gp, "junkG", [128, D], BF16)
                  Bj = Buf("junkG", fG)
                  ysl = Rot([(sbt(Gp, "ys%d" % i, [128, D], F32), Buf("ys%d" % i, fG), nsem("ys%d" % i)) for i in range(2)])
                  def final_norm(t):
                      st, sbf = stat_rot.next()
                      act(junk[:], x1[:, t, :], AF.Square, [x1B[t]], [Bj, sbf], accum_out=st[:, 0:1])
                      rstd_pool(st, sbf)
                      yt, yb, ysem = ysl.next()
                      dve(lambda e: e.scalar_tensor_tensor(out=yt[:], in0=x1[:, t, :], scalar=st[:, 2:3], in1=gfb[:],
                                                           op0=ALU.mult, op1=ALU.mult), [x1B[t], sbf, Bbb], [yb])
                      S.add("sp", lambda e: e.dma_start(out=y_d[s, t * 128:(t + 1) * 128, :], in_=yt[:]),
                            reads=[yb], dsem=ysem, store=True)

                  for t in range(NT):
                      ts_ = slice(t * 128, (t + 1) * 128)
                      for nh in range(2):
                          ns = slice(nh * 512, (nh + 1) * 512)
                          (g1, g1B), (g2, g2B), (g3, g3B) = gtm.next()
                          bk, bb = gen_banks.next()
                          for c in range(8):
                              mm(bk, h2T[:, c, ts_], wpg[nh][0][:, c, :], c == 0, False, [h2B[t // 4], wpg[nh][1]], [bb])
                          mm(bk, e0[:], bpl[:, ns], False, True, [Bbp], [bb])
                          act(g2[:], bk, AF.Sigmoid, [bb], [g2B])
                          pk, pkB = gen_banks.next()
                          for c in range(2):
                              mm(pk, pT[:, c, ts_], wpl[:, c, ns], c == 0, c == 1, [pTB[t // 4], wplB], [pkB])
                          dve(lambda e, g3=g3, g2=g2, pk=pk: e.tensor_tensor(out=g3[:], in0=pk, in1=g2[:], op=ALU.mult),
                              [pkB, g2B], [g3B])
                          dve(lambda e, g3=g3, t=t, ns=ns: e.tensor_tensor(out=x1[:, t, ns], in0=x1[:, t, ns], in1=g3[:],
                                                                          op=ALU.add), [g3B, x1B[t]], [x1B[t]])
                      if t > 0:
                          final_norm(t - 1)
                  final_norm(NT - 1)
              EG.__exit__(None, None, None)
              DG.__exit__(None, None, None)
          except _Stop:
            pass

        for s_ in range(nseq):
            one_seq(s_)

        fin = S.add("sp", lambda e: e.nop(), reads=[])
        fin.deps = list(S.stores)
        with nc.Block() as block:
            S.emit_all(block, esem)
    return nc


def _rope_tables():
    pos = np.arange(SQ, dtype=np.float32)
    tabs = np.zeros((4, 128, SQ), np.float32)
    tabs[0] = 1.0
    tabs[2] = 1.0
    inv = (np.float32(10000.0) ** (-(np.arange(0, 32, 2, dtype=np.float32) / np.float32(32)))).astype(np.float32)
    ang = (pos[:, None] * inv[None, :]).astype(np.float32)
    c, sn = np.cos(ang).astype(np.float32).T, np.sin(ang).astype(np.float32).T
    tabs[0, 64:80], tabs[0, 80:96] = c, c
    tabs[1, 64:80], tabs[1, 80:96] = sn, sn
    inv = (np.float32(500000.0) ** (-(np.arange(0, 16, 2, dtype=np.float32) / np.float32(16)))).astype(np.float32)
    ang = (pos[:, None] * inv[None, :]).astype(np.float32)
    c, sn = np.cos(ang).astype(np.float32).T, np.sin(ang).astype(np.float32).T
    for r0 in (0, 64):
        tabs[2, r0:r0 + 8], tabs[2, r0 + 8:r0 + 16] = c, c
        tabs[3, r0:r0 + 8], tabs[3, r0 + 8:r0 + 16] = sn, sn
    return tabs


def _perm_mats():
    pd = np.zeros((128, 128), np.float32)
    for r0 in (0, 64):
        for r in range(8):
            pd[r0 + r + 8, r0 + r] = -1.0
            pd[r0 + r, r0 + r + 8] = 1.0
    pm = np.zeros((128, 128), np.float32)
    for r in range(16):
        pm[64 + r + 16, 64 + r] = -1.0
        pm[64 + r, 64 + r + 16] = 1.0
    return pd.astype(ml_dtypes.bfloat16), pm.astype(ml_dtypes.bfloat16)


_CACHE = {}


def _get_nc():
    if "nc" not in _CACHE:
        _CACHE["nc"] = build_program()
    return _CACHE["nc"]


def kernel(x, p, attn_norm, w_in, b_gate, lam_q1, lam_k1, lam_q2, lam_k2, diff_subln, w_o_diff, q_norm, w_uq,
           kv_norm, w_ukv, w_o_mla, w_out, ffn_norm, w_ffn_gate, w_ffn_up, w_ffn_down, ple_norm, w_ple_gate,
           b_ple_gate, w_ple, final_norm):
    f = lambda a: np.ascontiguousarray(np.asarray(a, dtype=np.float32))
    x = f(x)
    p = f(p)[0]
    B = x.shape[0]
    nseq = B // NCORES
    w_ukv_ = f(w_ukv)[0].reshape(256, 8, 2, 64)
    vecs = np.zeros((128, 24), np.float32)
    bg = f(b_gate)[0]
    vecs[:, 0:8] = bg[0].reshape(8, 128).T
    vecs[:, 8:16] = bg[1].reshape(8, 128).T
    vecs[:, 16:19] = f(q_norm)[0].reshape(3, 128).T
    vecs[:, 19:21] = f(kv_norm)[0].reshape(2, 128).T
    vecs[:, 21] = f(diff_subln)[0]
    pd, pm = _perm_mats()
    shared = {
        "w_in": f(w_in)[0], "w_o_diff": f(w_o_diff)[0], "w_uq": f(w_uq)[0],
        "w_ukv_kn": np.ascontiguousarray(w_ukv_[:, :, 0, :].reshape(256, 512)),
        "w_ukv_v": np.ascontiguousarray(w_ukv_[:, :, 1, :].reshape(256, 512)),
        "w_o_mla": f(w_o_mla)[0], "w_out": f(w_out)[0], "w_ffn_gate": f(w_ffn_gate)[0], "w_ffn_up": f(w_ffn_up)[0],
        "w_ffn_down": f(w_ffn_down)[0], "w_ple_gate": f(w_ple_gate)[0], "w_ple": f(w_ple)[0],
        "attn_norm": f(attn_norm)[0], "ffn_norm": f(ffn_norm)[0], "ple_norm": f(ple_norm)[0],
        "final_norm": f(final_norm), "b_ple_gate": f(b_ple_gate)[0],
        "lam_q1": f(lam_q1)[0], "lam_k1": f(lam_k1)[0], "lam_q2": f(lam_q2)[0], "lam_k2": f(lam_k2)[0],
        "vecs": vecs, "ident": np.eye(128, dtype=np.float32).astype(ml_dtypes.bfloat16),
        "perm_diff": pd, "perm_mla": pm, "rope_tabs": _rope_tables(),
    }
    nc = _get_nc()
    in_maps = []
    for c in range(NCORES):
        m = dict(shared)
        m["x"] = x[c * nseq:(c + 1) * nseq]
        m["p"] = p[c * nseq:(c + 1) * nseq]
        in_maps.append(m)
    res = run_bass_kernel_spmd(nc, in_maps, core_ids=list(range(NCORES)))
    return np.concatenate([r["y"] for r in res.results], axis=0).astype(np.float32)
```

```python
import math
import os
from contextlib import ExitStack

import numpy as np
import ml_dtypes

import concourse.bass as bass
import concourse.mybir as mybir
from concourse.bass_utils import run_bass_kernel_spmd

F32 = mybir.dt.float32
BF16 = mybir.dt.bfloat16
AF = mybir.ActivationFunctionType
ALU = mybir.AluOpType
AX = mybir.AxisListType

NCORES = 8
SEQ_PER_CORE = 2
SQ = 2048
D = 1024
NT = 16
NB = 4
FF = 2816
NF = 22
EPS = 1e-6
Q_OFF, K_OFF, V_OFF, CQ_OFF, CKV_OFF, KR_OFF, GA_OFF, GB_OFF = 0, 1024, 2048, 3072, 3456, 3712, 3744, 4768
IN_COLS = 5792
WS = 6144
ARENA_BYTES = 207872
LAM_INIT = 0.8 - 0.6 * math.exp(0.0)

ENGS = ("pe", "act", "dve", "pool", "sp")


class Buf:
    __slots__ = ("name", "w", "r", "init", "psum")

    def __init__(self, name, init=(), psum=False):
        self.name = name
        self.w = None
        self.r = []
        self.init = list(init)
        self.psum = psum


class DmaSem:
    __slots__ = ("h", "count")

    def __init__(self, h):
        self.h = h
        self.count = 0


class Op:
    __slots__ = ("eng", "emit", "deps", "dsem", "dval", "ndma", "sig", "sigval", "pos")

    def __init__(self, eng, emit):
        self.eng = eng
        self.emit = emit
        self.deps = []
        self.dsem = None
        self.dval = 0
        self.ndma = 0
        self.sig = False
        self.sigval = 0


class Sched:
    def __init__(self):
        self.streams = {e: [] for e in ENGS}
        self.last_compute = {e: None for e in ENGS}
        self.stores = []

    def fence(self):
        return [o for o in self.last_compute.values() if o is not None] + list(self.stores)

    def add(self, eng, emit, reads=(), writes=(), dsem=None, ndma=1, store=False):
        op = Op(eng, emit)
        deps = {}

        def dep(o, raw):
            if o is None:
                return
            if o.dsem is None and o.eng == eng:
                if eng == "pe":
                    return
            deps[id(o)] = o

        for b in reads:
            dep(b.w, True)
            for o in b.init:
                dep(o, True)
            if b.psum:
                for o in b.r:
                    if o.eng != eng:
                        dep(o, True)
        for b in writes:
            dep(b.w, False)
            for o in b.r:
                dep(o, False)
            for o in b.init:
                dep(o, True)
            b.init = []
        latest = {}
        dl = []
        for o in deps.values():
            if o.dsem is not None:
                dl.append(o)
            elif o.eng not in latest or latest[o.eng].pos < o.pos:
                latest[o.eng] = o
        op.deps = dl + list(latest.values())
        for b in reads:
            b.r.append(op)
        for b in writes:
            b.w = op
            b.r = []
        if dsem is not None:
            op.dsem = dsem
            op.ndma = ndma
            dsem.count += 16 * ndma
            op.dval = dsem.count
            if store:
                self.stores.append(op)
        else:
            self.last_compute[eng] = op
        op.pos = len(self.streams[eng])
        self.streams[eng].append(op)
        return op

    def emit_all(self, block, esem):
        for e in ENGS:
            for op in self.streams[e]:
                for d in op.deps:
                    if d.dsem is None:
                        d.sig = True
        for e in ENGS:
            c = 0
            for op in self.streams[e]:
                if op.sig:
                    c += 1
                    op.sigval = c
        streams = self.streams

        def run(eng_name, eng):
            waited = {}
            for op in streams[eng_name]:
                for d in op.deps:
                    if d.dsem is not None:
                        key, h, v = id(d.dsem), d.dsem.h, d.dval
                    else:
                        key, h, v = d.eng, esem[d.eng], d.sigval
                    if waited.get(key, 0) >= v:
                        continue
                    waited[key] = v
                    eng.wait_ge(h, v)
                res = op.emit(eng)
                if op.dsem is not None:
                    if not isinstance(res, (list, tuple)):
                        res = [res]
                    assert len(res) == op.ndma
                    for r in res:
                        r.then_inc(op.dsem.h, 16)
                elif op.sig:
                    if isinstance(res, (list, tuple)):
                        res = res[-1]
                    res.then_inc(esem[eng_name], 1)

        @block.tensor
        def _(eng):
            run("pe", eng)

        @block.scalar
        def _(eng):
            run("act", eng)

        @block.vector
        def _(eng):
            run("dve", eng)

        @block.gpsimd
        def _(eng):
            run("pool", eng)

        @block.sync
        def _(eng):
            run("sp", eng)


class Rot:
    def __init__(self, items):
        self.items = items
        self.i = 0

    def next(self):
        it = self.items[self.i % len(self.items)]
        self.i += 1
        return it


class _Stop(Exception):
    pass


def build_program(nseq=SEQ_PER_CORE, stage=99):
    nc = bass.Bass("TRN2", target_bir_lowering=False)
    dbg_outs = {}

    def din(name, shape, dt=F32):
        return nc.dram_tensor(name, list(shape), dt, kind="ExternalInput").ap()

    x_d = din("x", [nseq, SQ, D])
    p_d = din("p", [nseq, SQ, 256])
    y_d = nc.dram_tensor("y", [nseq, SQ, D], F32, kind="ExternalOutput").ap()
    w_in_d = din("w_in", [D, IN_COLS])
    w_od_d = din("w_o_diff", [1024, D])
    w_uq_d = din("w_uq", [384, 768])
    w_kn_d = din("w_ukv_kn", [256, 512])
    w_v_d = din("w_ukv_v", [256, 512])
    w_om_d = din("w_o_mla", [512, D])
    w_out_d = din("w_out", [D, D])
    w_g_d = din("w_ffn_gate", [D, FF])
    w_u_d = din("w_ffn_up", [D, FF])
    w_d_d = din("w_ffn_down", [FF, D])
    w_pg_d = din("w_ple_gate", [D, D])
    w_pl_d = din("w_ple", [256, D])
    g_attn_d = din("attn_norm", [D])
    g_ffn_d = din("ffn_norm", [D])
    g_ple_d = din("ple_norm", [D])
    g_fin_d = din("final_norm", [D])
    b_ple_d = din("b_ple_gate", [D])
    lam_d = [din(n, [64]) for n in ("lam_q1", "lam_k1", "lam_q2", "lam_k2")]
    vecs_d = din("vecs", [128, 24])
    ident_d = din("ident", [128, 128], BF16)
    pd_d = din("perm_diff", [128, 128], BF16)
    pm_d = din("perm_mla", [128, 128], BF16)
    tab_d = din("rope_tabs", [4, 128, SQ])

    def wview(w):
        return w.rearrange("(c p) n -> p c n", p=128)

    w_in_v = wview(w_in_d)

    S = Sched()
    G = ExitStack()
    with G:
        arena = G.enter_context(nc.sbuf_tensor("arena", [128, ARENA_BYTES // 2], BF16))
        abase = nc.lookup_mloc(arena).addr
        free_list = [[abase, abase + ARENA_BYTES]]
        _uid = [0]

        def a_alloc(nbytes):
            nbytes = (nbytes + 63) // 64 * 64
            for iv in free_list:
                if iv[1] - iv[0] >= nbytes:
                    off = iv[0]
                    iv[0] += nbytes
                    if iv[0] == iv[1]:
                        free_list.remove(iv)
                    return off, nbytes
            raise RuntimeError("SBUF arena full: need %d, free %s" % (nbytes, free_list))

        def a_free(off, nbytes):
            free_list.append([off, off + nbytes])
            free_list.sort()
            i = 0
            while i + 1 < len(free_list):
                if free_list[i][1] == free_list[i + 1][0]:
                    free_list[i][1] = free_list[i + 1][1]
                    del free_list[i + 1]
                else:
                    i += 1

        def sbt(es, name, shape, dt):
            n = 1
            for d in shape[1:]:
                n *= d
            nbytes = n * (4 if dt == F32 else 2)
            off, nb = a_alloc(nbytes)
            _uid[0] += 1
            t = nc.alloc_sbuf_tensor_at("%s_%d" % (name, _uid[0]), list(shape), dt, offset=off)
            es.callback(a_free, off, nb)
            return t

        def sem(name):
            return G.enter_context(nc.semaphore(name))

        esem = {e: sem("s_" + e) for e in ("pe", "act", "dve", "pool")}
        _ds = [0]

        def dsem():
            _ds[0] += 1
            return DmaSem(sem("d%d" % _ds[0]))

        accs = []
        for i in range(2):
            t = G.enter_context(nc.psum_tensor("acc%d" % i, [128, 1024], F32))
            accs.append((t, Buf("acc%d" % i, psum=True)))
        banks = []
        for i in range(4):
            t = G.enter_context(nc.psum_tensor("bank%d" % i, [128, 512], F32))
            banks.append((t, Buf("bank%d" % i, psum=True)))
        allbanks = []
        for (t, b) in accs:
            allbanks.append((t[:, 0:512], b))
        for (t, b) in banks:
            allbanks.append((t[:], b))
        gen_banks = Rot([allbanks[0], allbanks[1], (banks[0][0][:], banks[0][1]), (banks[1][0][:], banks[1][1]),
                         (banks[2][0][:], banks[2][1])])
        trp_t, trp_b = banks[3]
        trp_bf = trp_t[:].bitcast(BF16)

        ident = sbt(G, "ident", [128, 128], BF16)
        ones = sbt(G, "ones", [128, 128], BF16)
        pdm = sbt(G, "pdm", [128, 128], BF16)
        pmm = sbt(G, "pmm", [128, 128], BF16)
        vecs = sbt(G, "vecs_s", [128, 24], F32)
        lamt = sbt(G, "lamt", [128, 4, 64], F32)
        lsm = sbt(G, "lsm", [128, 8], F32)
        Bconst = Buf("const")
        Blam = Buf("lam")
        dconst = dsem()
        S.add("sp", lambda e: [e.dma_start(out=ident[:], in_=ident_d), e.dma_start(out=pdm[:], in_=pd_d),
                               e.dma_start(out=pmm[:], in_=pm_d), e.dma_start(out=vecs[:], in_=vecs_d)]
              + [e.dma_start(out=lamt[:, i, :], in_=lam_d[i].partition_broadcast(128)) for i in range(4)],
              writes=[Bconst, Blam], dsem=dconst, ndma=8)
        S.add("dve", lambda e: e.memset(ones[:], 1.0), writes=[Bconst])
        mhalf = sbt(G, "mhalf", [128, 1], F32)
        S.add("dve", lambda e: e.memset(mhalf[:], -0.5), writes=[Bconst])

        def rstd_pool(st, sbf):
            S.add("dve", lambda e: e.tensor_scalar(out=st[:, 1:2], in0=st[:, 0:1], scalar1=1.0 / D, scalar2=EPS,
                                                   op0=ALU.mult, op1=ALU.add), reads=[sbf], writes=[sbf])
            S.add("pool", lambda e: e.tensor_tensor(out=st[:, 2:3], in0=st[:, 1:2], in1=mhalf[:], op=ALU.pow),
                  reads=[sbf, Bconst], writes=[sbf])
        lprod = sbt(G, "lprod", [128, 2, 64], F32)
        S.add("dve", lambda e: e.tensor_tensor(out=lprod[:, 0, :], in0=lamt[:, 0, :], in1=lamt[:, 1, :], op=ALU.mult),
              reads=[Blam], writes=[Blam])
        S.add("dve", lambda e: e.tensor_tensor(out=lprod[:, 1, :], in0=lamt[:, 2, :], in1=lamt[:, 3, :], op=ALU.mult),
              reads=[Blam], writes=[Blam])
        S.add("dve", lambda e: e.tensor_reduce(out=lsm[:, 0:2], in_=lprod[:], axis=AX.X, op=ALU.add),
              reads=[Blam], writes=[Blam])
        S.add("act", lambda e: e.activation(out=lsm[:, 2:4], in_=lsm[:, 0:2], func=AF.Exp), reads=[Blam], writes=[Blam])
        S.add("dve", lambda e: e.tensor_tensor(out=lsm[:, 4:5], in0=lsm[:, 3:4], in1=lsm[:, 2:3], op=ALU.subtract),
              reads=[Blam], writes=[Blam])
        S.add("dve", lambda e: e.tensor_scalar(out=lsm[:, 5:6], in0=lsm[:, 4:5], scalar1=-LAM_INIT, scalar2=None,
                                               op0=ALU.add), reads=[Blam], writes=[Blam])
        neglam = lsm[:, 5:6]

        wslots = []
        for i in range(3):
            t = sbt(G, "wslot%d" % i, [128, WS], BF16)
            wslots.append((t, Buf("wslot%d" % i), dsem()))
        wrot = Rot(wslots)

        def wload(parts):
            t, b, ds = wrot.next()
            views = []
            off = 0
            pairs = []
            for part in parts:
                src_, C, N = part[0], part[1], part[2]
                v = t[:, off:off + C * N].rearrange("p (c n) -> p c n", c=C)
                views.append(v)
                if len(part) > 3:
                    n = src_.shape[2]
                    flat = t[:, off:off + C * N]
                    S.add("dve", lambda e, flat=flat: e.memset(flat, 0.0), writes=[b])
                    pairs.append((v[:, :, part[3]:part[3] + n], src_))
                else:
                    pairs.append((v, src_))
                off += C * N
            assert off <= WS
            S.add("pool", lambda e: [e.dma_start(out=v, in_=s) for (v, s) in pairs], writes=[b], dsem=ds,
                  ndma=len(pairs))
            return views, b

        stat = sbt(G, "stat", [128, 64], F32)
        stat_rot = Rot([(stat[:, i * 4:(i + 1) * 4], Buf("stat%d" % i)) for i in range(16)])

        _nsems = {}

        def nsem(key):
            if key not in _nsems:
                _nsems[key] = dsem()
            return _nsems[key]

        def dump(name, ap, shape, dt, bufs):
            d = nc.dram_tensor("dbg_" + name, list(shape), dt, kind="ExternalOutput").ap()
            dbg_outs[name] = d
            S.add("sp", lambda e: e.dma_start(out=d, in_=ap), reads=bufs, dsem=nsem("dbg_" + name), store=True)

        def mm(out, lhsT, rhs, start, stop, reads, writes, **kw):
            return S.add("pe", lambda e: e.matmul(out, lhsT=lhsT, rhs=rhs, start=start, stop=stop, **kw),
                         reads=reads, writes=writes)

        def tr(out, in_, reads, writes):
            return S.add("pe", lambda e: e.transpose(out=out, in_=in_, identity=ident[:]), reads=list(reads) + [Bconst],
                         writes=writes)

        def act(out, in_, func, reads, writes, **kw):
            return S.add("act", lambda e: e.activation(out=out, in_=in_, func=func, **kw), reads=reads, writes=writes)

        def dve(fn, reads, writes):
            return S.add("dve", fn, reads=reads, writes=writes)

        _alt = [0]

        def copy_alt(out, in_, reads, writes):
            _alt[0] += 1
            if _alt[0] % 2:
                return act(out, in_, AF.Copy, reads, writes)
            return dve(lambda e: e.tensor_copy(out=out, in_=in_), reads, writes)

        def norm_T(es, tag, get_x, g_d, dstT, dstB, external=False):
            gbc = sbt(es, tag + "_gbc", [128, D], F32)
            Bg = Buf(tag + "_gbc", S.fence())
            dg = nsem("gbc_" + tag)
            S.add("sp", lambda e: e.dma_start(out=gbc[:], in_=g_d.partition_broadcast(128)), writes=[Bg], dsem=dg)
            junk = sbt(es, tag + "_junk", [128, D], BF16)
            Bj = Buf(tag + "_junk", S.fence())
            hns = Rot([(sbt(es, tag + "_hn%d" % i, [128, D], BF16), Buf(tag + "_hn%d" % i, S.fence())) for i in range(3)])
            trps = Rot([(trp_bf, trp_b), (banks[2][0][:].bitcast(BF16), banks[2][1])])
            def stage1a(t):
                xa, xb = get_x(t)
                st, sbf = stat_rot.next()
                act(junk[:], xa, AF.Square, [xb], [Bj, sbf], accum_out=st[:, 0:1])
                rstd_pool(st, sbf)
                return xa, xb, st, sbf

            def stage1b(xa, xb, st, sbf):
                hn, hb = hns.next()
                dve(lambda e: e.scalar_tensor_tensor(out=hn[:], in0=xa, scalar=st[:, 2:3], in1=gbc[:],
                                                     op0=ALU.mult, op1=ALU.mult), [xb, sbf, Bg], [hb])
                return hn, hb

            def stage2(t, hn, hb):
                tb_, tbB = trps.next()
                for c in range(8):
                    tr(tb_[:, c * 128:(c + 1) * 128], hn[:, c * 128:(c + 1) * 128], [hb], [tbB])
                copy_alt(dstT[:, :, t * 128:(t + 1) * 128], tb_.rearrange("p (c t) -> p c t", c=8), [tbB],
                         [dstB[t // 4]])

            if external:
                return stage1a, stage1b, stage2
            norm_drive(stage1a, stage1b, stage2)

        def norm_drive(s1a, s1b, s2, extra=None):
            q = [s1b(*s1a(0)), s1b(*s1a(1))]
            for t in range(NT):
                sa = s1a(t + 2) if t + 2 < NT else None
                if extra is not None:
                    extra(t)
                s2(t, *q.pop(0))
                if sa is not None:
                    q.append(s1b(*sa))

        def rope(Aps, Ab, r0, r1, perm, Ct, St, Btab, tb, dst, dstB, tmp):
            (qbf, qbfB), (t1, t1B), (t2, t2B), (Bps, BpB) = tmp
            cs = slice(tb * 512, (tb + 1) * 512)
            p0, p1 = (0, 128) if r0 > 0 else (r0, r1)
            lvl = int(os.environ.get("KROPE", "9"))
            act(qbf[p0:p1, :], Aps[p0:p1, :], AF.Copy, [Ab], [qbfB])
            if lvl >= 2:
                mm(Bps[p0:p1, :], perm[p0:p1, p0:p1], qbf[p0:p1, :], True, True, [qbfB, Bconst], [BpB])
            if lvl >= 3:
                dve(lambda e: e.tensor_tensor(out=t1[r0:r1, :], in0=Aps[r0:r1, :], in1=Ct[r0:r1, cs], op=ALU.mult),
                    [Ab] + ([] if os.environ.get("KNOTAB") else [Btab]), [t1B])
            if lvl >= 4:
                dve(lambda e: e.tensor_tensor(out=t2[r0:r1, :], in0=Bps[r0:r1, :], in1=St[r0:r1, cs], op=ALU.mult),
                    [BpB, Btab], [t2B])
            if lvl >= 5:
                dve(lambda e: e.tensor_tensor(out=dst[r0:r1, :], in0=t1[r0:r1, :], in1=t2[r0:r1, :], op=ALU.add),
                    [t1B, t2B], [dstB])

        sc_banks = Rot([banks[0], banks[1]])
        NDUMMY = int(os.environ.get("KDUMMY", "0"))
        att_banks = Rot([banks[0], banks[1], banks[2]] if NDUMMY == 0 else [banks[0], banks[1]])
        LOOK = 2 if NDUMMY == 0 else 1
        INTERLEAVE = os.environ.get("KNOINT", "") == ""

        def attn_steps(KT, KB, QT, QB, r0, r1, Vfn, VB, dv, scale, qt, acc, accB, pts, after):
            steps = []
            nk = 4 * qt + 4
            first = {0: True, 1: True}
            accv = acc[:].rearrange("p (i n) -> p i n", i=4)
            for kt in range(nk):
                j = kt - 4 * qt
                q0 = 128 * j if j > 0 else 0
                sct, scb = att_banks.next()
                pt, ptb = pts.next()

                def s_fn(kt=kt, q0=q0, sct=sct, scb=scb):
                    mm(sct[:, q0:512], KT[r0:r1, kt * 128:(kt + 1) * 128], QT[r0:r1, qt * 512 + q0:(qt + 1) * 512],
                       True, True, [KB, QB], [scb])

                avs = []
                for i in range(max(j, 0), 4):
                    bk = i // 2
                    avs.append((i, first[bk]))
                    first[bk] = False

                def rest_fn(kt=kt, j=j, q0=q0, sct=sct, scb=scb, pt=pt, ptb=ptb, avs=avs, last=(kt == nk - 1)):
                    for dmy in range(NDUMMY):
                        mm(banks[2][0][:, :], ones[:], QT[:, 0:512], True, True, [Bconst, QB], [banks[2][1]])
                    act(pt[:, q0:512], sct[:, q0:512], AF.Exp, [scb], [ptb], scale=scale)
                    if j >= 0:
                        dve(lambda e: e.memset(pt[64:128, 128 * j:128 * j + 64], 0.0), [], [ptb])
                    if os.environ.get("KAV", "") == "dense":
                        vv = Vfn(kt)
                        for bki in range(2):
                            mm(acc[0:dv, bki * 512 + q0:(bki + 1) * 512], vv[:, 0:dv] if bki == 0 else ones[:, 0:dv],
                               pt[:, q0:512], kt == 0, True, [ptb, VB, Bconst], [accB], skip_group_check=True)
                    else:
                      for (i, st) in avs:
                        mm(accv[:, i, 0:dv + 1], pt[:, i * 128:(i + 1) * 128], Vfn(kt), st, True, [ptb, VB], [accB],
                           skip_group_check=True)
                    if last:
                        return after()
                    return None

                steps.append((s_fn, rest_fn))
            return steps

        def run_steps(steps, side=()):
            side = list(side)
            busy = [False]

            def run_side():
                fn, b = side.pop(0)
                fn()
                busy[0] = b

            if not steps:
                while side:
                    run_side()
                return
            per = -(-len(side) // len(steps)) if side else 0
            pending = []
            for i in range(min(LOOK, len(steps))):
                steps[i][0]()
            for i, (s_fn, rest_fn) in enumerate(steps):
                if i + LOOK < len(steps):
                    steps[i + LOOK][0]()
                while pending and pending[0][0] <= i and not busy[0]:
                    pending.pop(0)[1]()
                d = rest_fn()
                if d is not None:
                    pending.append((i + d[0], d[1]))
                for _ in range(per):
                    if side:
                        run_side()
            while side:
                run_side()
            for (_, fn) in pending:
                fn()

        def one_seq(s):
          try:
              ABC = ExitStack()
              with ABC:
                  hT = sbt(ABC, "hT", [128, 8, SQ], BF16)
                  hTB = [Buf("hT%d" % i, S.fence()) for i in range(4)]
                  odT = sbt(ABC, "odT", [128, 8, SQ], BF16)
                  omT = sbt(ABC, "omT", [128, 4, SQ], BF16)
                  omTB = [Buf("omT%d" % i, S.fence()) for i in range(4)]
                  with ExitStack() as A:
                      xsl = Rot([(sbt(A, "xsA%d" % i, [128, D], F32), Buf("xsA%d" % i, S.fence()), nsem("xsA%d" % i))
                                 for i in range(6)])

                      def get_x(t, s=s, xsl=xsl):
                          xt, xb, ds = xsl.next()
                          S.add("sp", lambda e: e.dma_start(out=xt[:], in_=x_d[s, t * 128:(t + 1) * 128, :]), writes=[xb],
                                dsem=ds)
                          return xt[:], xb

                      norm_T(A, "nA", get_x, g_attn_d, hT, hTB)
                  if stage == 1:
                      dump("hT", hT[:], [128, 8, SQ], BF16, hTB)
                      raise _Stop()

                  with ExitStack() as Bx:
                      Ct = sbt(Bx, "Ct", [128, SQ], F32)
                      St = sbt(Bx, "St", [128, SQ], F32)
                      Btab = Buf("tab", S.fence())
                      dtab = nsem("tab")
                      S.add("sp", lambda e: [e.dma_start(out=Ct[:], in_=tab_d[0]), e.dma_start(out=St[:], in_=tab_d[1])],
                            writes=[Btab], dsem=dtab, ndma=2)
                      Vbuf = sbt(Bx, "Vbuf", [128, 8704], BF16)
                      QK = [(sbt(Bx, "QT%d" % i, [128, SQ], BF16), Buf("QT%d" % i, S.fence()),
                             sbt(Bx, "KT%d" % i, [128, SQ], BF16), Buf("KT%d" % i, S.fence())) for i in range(2)]
                      K2s = [(sbt(Bx, "K2_%d" % i, [128, SQ], BF16), Buf("K2_%d" % i, S.fence())) for i in range(2)]
                      pts = Rot([(sbt(Bx, "pt%d" % i, [128, 512], BF16), Buf("pt%d" % i, S.fence())) for i in range(4)])
                      f0 = S.fence()
                      rtmp = ((sbt(Bx, "qbf", [128, 512], BF16), Buf("qbf", f0)),
                              (sbt(Bx, "rt1", [128, 512], F32), Buf("rt1", f0)),
                              (sbt(Bx, "rt2", [128, 512], F32), Buf("rt2", f0)),
                              (banks[2][0], banks[2][1]))
                      o1n = sbt(Bx, "o1n", [128, 4, 128], F32)
                      o1nB = Buf("o1n", f0)
                      otmp = sbt(Bx, "otmp", [128, 4, 128], F32)
                      otmpB = Buf("otmp", f0)
                      odf = sbt(Bx, "odf", [128, 4, 128], F32)
                      odfB = Buf("odf", f0)
                      osq = sbt(Bx, "osq", [128, 4, 128], F32)
                      osqB = Buf("osq", f0)
                      odn = sbt(Bx, "odn", [128, 4, 128], BF16)
                      odnB = Buf("odn", f0)
                      omn = sbt(Bx, "omn", [128, 4, 64], BF16)
                      omnB = Buf("omn", f0)

                      latT = odT
                      latB = [Buf("lat%d" % i, f0) for i in range(4)]
                      latf = [Vbuf[:, j * 1024:(j + 1) * 1024].bitcast(F32) for j in range(5)]
                      sqs = [Vbuf[:, 5120 + j * 512:5120 + (j + 1) * 512] for j in range(5)]
                      ltBs = [Buf("lattmp%d" % j, f0) for j in range(5)]
                      rsq = sbt(Bx, "rsq", [128, 512], F32)
                      rsqB = Buf("rsq", f0)
                      rstd = sbt(Bx, "rstdl", [128, 512], F32)
                      rstdB = Buf("rstdl", f0)
                      (wl, wkr), wlB = wload([(w_in_v[:, :, CQ_OFF:CQ_OFF + 640], 8, 640),
                                                  (w_in_v[:, :, KR_OFF:KR_OFF + 32], 8, 128, 64) if os.environ.get("KDBG", "") != "krplain"
                                                  else (w_in_v[:, :, KR_OFF - 96:KR_OFF + 32], 8, 128)])
                      for tb in range(NB):
                          cs = slice(tb * 512, (tb + 1) * 512)
                          for j in range(6):
                              bk, bb = sc_banks.next()
                              if j < 5:
                                  c0 = j * 128
                                  for c in range(8):
                                      mm(bk[:, :], wl[:, c, c0:c0 + 128], hT[:, c, cs], c == 0, c == 7, [wlB, hTB[tb]], [bb])
                                  act(latf[j], bk[:, :], AF.Copy, [bb], [ltBs[j]])
                                  act(sqs[j], bk[:, :], AF.Square, [bb], [ltBs[j]])
                              elif os.environ.get("KDBG", "") != "skipkr":
                                  for c in range(8):
                                      mm(bk[:, :], wkr[:, c, :], hT[:, c, cs], c == 0, c == 7, [wlB, hTB[tb]], [bb])
                                  rope(bk, bb, 0, 128, pmm, Ct, St, Btab, tb, latT[:, 5, cs], latB[tb], rtmp)
                          for (js, nrm, vcol) in (((0, 1, 2), 384.0, 8 + 8), ((3, 4), 256.0, 8 + 8 + 3)):
                              rk, rb = banks[2]
                              for n, j in enumerate(js):
                                  mm(rk[:, :], ones[:], sqs[j], n == 0, n == len(js) - 1, [ltBs[j], Bconst], [rb])
                              act(rsq[:], rk[:, :], AF.Sqrt, [rb], [rsqB], scale=1.0 / nrm, bias=EPS)
                              dve(lambda e: e.reciprocal(out=rstd[:], in_=rsq[:]), [rsqB], [rstdB])
                              for n, j in enumerate(js):
                                  dve(lambda e, j=j, n=n, vcol=vcol, cs=cs: e.scalar_tensor_tensor(
                                      out=latT[:, j, cs], in0=latf[j], scalar=vecs[:, vcol + n:vcol + n + 1], in1=rstd[:],
                                      op0=ALU.mult, op1=ALU.mult), [ltBs[j], rstdB, Bconst], [latB[tb]])
                      if stage == 2.1:
                          dump("latT", odT[:, 0:5, :], [128, 5, SQ], BF16, latB)
                          if os.environ.get("KDBG", "") not in ("skipkr", "nodumpkr"):
                              dump("kr", odT[64:96, 5, :], [32, SQ], BF16, latB)
                          raise _Stop()
                      (wuq, wkn, wv), wmB = wload([(wview(w_uq_d), 3, 768), (wview(w_kn_d), 2, 512), (wview(w_v_d), 2, 512)])
                      VB = Buf("V", S.fence() + [b.w for b in ltBs] + [o for b in ltBs for o in b.r])
                      Vm = Vbuf[:, 0:16 * 8 * 66].rearrange("p (t h d) -> p t h d", t=16, h=8)
                      dve(lambda e: e.memset(Vm[:, :, :, 64:65], 1.0), [], [VB])
                      for t in range(NT):
                          bk, bb = sc_banks.next()
                          for c in range(2):
                              mm(bk[:, :], latT[:, 3 + c, t * 128:(t + 1) * 128], wv[:, c, :], c == 0, c == 1,
                                 [latB[t // 4], wmB], [bb])
                          copy_alt(Vm[:, t, :, 0:64], bk[:, :].rearrange("p (h d) -> p h d", h=8), [bb], [VB])
                      trp_f = trp_t[:, :]

                      def diff_proj_tasks(h):
                          (wq, wk), wqB = wload([(w_in_v[:, :, Q_OFF + h * 128:Q_OFF + (h + 1) * 128], 8, 128),
                                                 (w_in_v[:, :, K_OFF + h * 128:K_OFF + (h + 1) * 128], 8, 128)])
                          QTt, QB_, KTt, KB_ = QK[h % 2]
                          K2t, K2B = K2s[h % 2]
                          (qbf, qbfB), (t1, t1B), (t2, t2B), _ = rtmp
                          tasks = []
                          if h < 2:
                              dve(lambda e: e.memset(KTt[64:128, :], 0.0), [], [KB_])
                              dve(lambda e: e.memset(K2t[0:64, :], 0.0), [], [K2B])
                          for tb in range(NB):
                              cs = slice(tb * 512, (tb + 1) * 512)
                              for (wx, dT, dB) in ((wk, None, None), (wq, QTt, QB_)):
                                  def pm(c, wx=wx, cs=cs, tb=tb):
                                      mm(trp_f, wx[:, c, :], hT[:, c, cs], c == 0, c == 7, [wqB, hTB[tb]], [trp_b])

                                  def p1b(cs=cs):
                                      dve(lambda e: e.tensor_copy(out=qbf[:], in_=trp_f), [trp_b], [qbfB])
                                      dve(lambda e: e.tensor_tensor(out=t1[:], in0=trp_f, in1=Ct[:, cs], op=ALU.mult),
                                          [trp_b, Btab], [t1B])

                                  def p2(dT=dT, dB=dB, cs=cs):
                                      b2, b2B = trp_t, trp_b
                                      mm(b2[:, :], pdm[:, :], qbf[:], True, True, [qbfB, Bconst], [b2B])
                                      dve(lambda e: e.tensor_tensor(out=t2[:], in0=b2[:, :], in1=St[:, cs], op=ALU.mult),
                                          [b2B, Btab], [t2B])
                                      if dT is None:
                                          dve(lambda e: e.tensor_tensor(out=KTt[0:64, cs], in0=t1[0:64, :], in1=t2[0:64, :],
                                                                        op=ALU.add), [t1B, t2B], [KB_])
                                          dve(lambda e: e.tensor_tensor(out=K2t[64:128, cs], in0=t1[64:128, :],
                                                                        in1=t2[64:128, :], op=ALU.add), [t1B, t2B], [K2B])
                                      else:
                                          dve(lambda e: e.tensor_tensor(out=dT[:, cs], in0=t1[:], in1=t2[:], op=ALU.add),
                                              [t1B, t2B], [dB])

                                  tasks += [((lambda c=c, pm=pm: pm(c)), True) for c in range(8)]
                                  tasks += [(p1b, False), (p2, False)]
                          return tasks

                      sc_m = 96.0 ** -0.5
                      for h in range(8):
                          QTt, QB_, KTt, KB_ = QK[h % 2]
                          for tb in range(NB):
                              cs = slice(tb * 512, (tb + 1) * 512)
                              bk, bb = sc_banks.next()
                              for c in range(2):
                                  mm(bk[0:64, :], wkn[:, c, h * 64:(h + 1) * 64], latT[:, 3 + c, cs], c == 0, c == 1,
                                     [wmB, latB[tb]], [bb])
                              copy_alt(KTt[0:64, cs], bk[0:64, :], [bb], [KB_])
                              dve(lambda e, KTt=KTt, cs=cs: e.tensor_copy(out=KTt[64:96, cs], in_=latT[64:96, 5, cs]),
                                  [latB[tb]], [KB_])
                              bk, bb = sc_banks.next()
                              for c in range(3):
                                  mm(bk[0:96, :], wuq[:, c, h * 96:(h + 1) * 96], latT[:, c, cs], c == 0, c == 2,
                                     [wmB, latB[tb]], [bb])
                              rope(bk, bb, 0, 96, pmm, Ct, St, Btab, tb, QTt[:, cs], QB_, rtmp)
                          steps = []
                          for qt in range(NB):
                              acc, accB = accs[(h * 4 + qt) % 2]

                              def after(h=h, qt=qt, acc=acc, accB=accB):
                                  accv = acc[:].rearrange("p (i n) -> p i n", i=4)
                                  st, sbf = stat_rot.next()
                                  dve(lambda e: e.reciprocal(out=st[:, 0:4], in_=accv[:, :, 64]), [accB], [sbf])
                                  dve(lambda e: e.tensor_tensor(out=omn[:], in0=accv[:, :, 0:64],
                                                                in1=st[:, 0:4].unsqueeze(2).broadcast_to([128, 4, 64]),
                                                                op=ALU.mult), [accB, sbf], [omnB])
                                  ro = 64 * (h % 2)

                                  def later():
                                      for i in range(4):
                                          tr(trp_bf[ro:ro + 64, i * 128:(i + 1) * 128], omn[:, i, :], [omnB], [trp_b])
                                      dve(lambda e: e.tensor_copy(out=omT[ro:ro + 64, h // 2, qt * 512:(qt + 1) * 512],
                                                                  in_=trp_bf[ro:ro + 64, 0:512]), [trp_b], [omTB[qt]])

                                  return (3, later)

                              steps += attn_steps(KTt, KB_, QTt, QB_, 0, 96, lambda kt, h=h: Vm[:, kt, h, 0:65], VB, 64, sc_m,
                                                  qt, acc, accB, pts, after)
                          side = []
                          if h == 7 and INTERLEAVE:
                              S.add("sp", lambda e: [e.dma_start(out=Ct[:], in_=tab_d[2]),
                                                     e.dma_start(out=St[:], in_=tab_d[3])],
                                    writes=[Btab], dsem=dtab, ndma=2)
                              side = diff_proj_tasks(0)
                          run_steps(steps, side)

                      if stage == 2.2:
                          dump("omT", omT[:], [128, 4, SQ], BF16, omTB)
                          raise _Stop()
                      if not INTERLEAVE:
                          S.add("sp", lambda e: [e.dma_start(out=Ct[:], in_=tab_d[2]), e.dma_start(out=St[:], in_=tab_d[3])],
                                writes=[Btab], dsem=dtab, ndma=2)
                      f1 = S.fence()
                      odTB = [Buf("odT%d" % i, f1) for i in range(4)]
                      Vd = Vbuf[:, 0:16 * 4 * 130].rearrange("p (t h d) -> p t h d", t=16, h=4)
                      sc_d = 64.0 ** -0.5
                      for g in range(2):
                          (wvd,), wvB = wload([(w_in_v[:, :, V_OFF + g * 512:V_OFF + (g + 1) * 512], 8, 512)])
                          dve(lambda e: e.memset(Vd[:, :, :, 128:129], 1.0), [], [VB])
                          for t in range(NT):
                              bk, bb = sc_banks.next()
                              for c in range(8):
                                  mm(bk[:, :], hT[:, c, t * 128:(t + 1) * 128], wvd[:, c, :], c == 0, c == 7,
                                     [hTB[t // 4], wvB], [bb])
                              copy_alt(Vd[:, t, :, 0:128], bk[:, :].rearrange("p (h d) -> p h d", h=4), [bb], [VB])
                          for hh in range(4):
                              h = g * 4 + hh
                              QTt, QB_, KTt, KB_ = QK[h % 2]
                              if not INTERLEAVE:
                                  for fn, _b in diff_proj_tasks(h):
                                      fn()
                              steps = []
                              for qt in range(NB):
                                  a1, a1B = accs[0]
                                  a2, a2B = accs[1]

                                  def after1(a1=a1, a1B=a1B):
                                      accv = a1[:].rearrange("p (i n) -> p i n", i=4)
                                      st, sbf = stat_rot.next()
                                      dve(lambda e: e.reciprocal(out=st[:, 0:4], in_=accv[:, :, 128]), [a1B], [sbf])
                                      dve(lambda e: e.tensor_tensor(out=o1n[:], in0=accv[:, :, 0:128],
                                                                    in1=st[:, 0:4].unsqueeze(2).broadcast_to([128, 4, 128]),
                                                                    op=ALU.mult), [a1B, sbf], [o1nB])

                                  def after2(h=h, qt=qt, a2=a2, a2B=a2B):
                                      accv = a2[:].rearrange("p (i n) -> p i n", i=4)
                                      st, sbf = stat_rot.next()
                                      st2, sbf2 = stat_rot.next()
                                      dve(lambda e: e.reciprocal(out=st[:, 0:4], in_=accv[:, :, 128]), [a2B], [sbf])
                                      dve(lambda e: e.tensor_scalar(out=st2[:, 0:4], in0=st[:, 0:4], scalar1=neglam,
                                                                    scalar2=None, op0=ALU.mult), [sbf, Blam], [sbf2])
                                      dve(lambda e: e.tensor_tensor(out=otmp[:], in0=accv[:, :, 0:128],
                                                                    in1=st2[:, 0:4].unsqueeze(2).broadcast_to([128, 4, 128]),
                                                                    op=ALU.mult), [a2B, sbf2], [otmpB])
                                      dve(lambda e: e.tensor_tensor(out=odf[:], in0=o1n[:], in1=otmp[:], op=ALU.add),
                                          [o1nB, otmpB], [odfB])
                                      dve(lambda e: e.tensor_tensor(out=osq[:], in0=odf[:], in1=odf[:], op=ALU.mult),
                                          [odfB], [osqB])
                                      st3, sbf3 = stat_rot.next()
                                      st4, sbf4 = stat_rot.next()
                                      st5, sbf5 = stat_rot.next()
                                      dve(lambda e: e.tensor_reduce(out=st3[:, 0:4], in_=osq[:], axis=AX.X, op=ALU.add),
                                          [osqB], [sbf3])
                                      dve(lambda e: e.tensor_scalar(out=st4[:, 0:4], in0=st3[:, 0:4], scalar1=1.0 / 128,
                                                                    scalar2=EPS, op0=ALU.mult, op1=ALU.add), [sbf3], [sbf4])
                                      act(st5[:, 0:4], st4[:, 0:4], AF.Ln, [sbf4], [sbf5])
                                      act(st3[:, 0:4], st5[:, 0:4], AF.Exp, [sbf5], [sbf3], scale=-0.5)
                                      dve(lambda e: e.scalar_tensor_tensor(
                                          out=odn[:], in0=odf[:], scalar=1.0 - LAM_INIT,
                                          in1=st3[:, 0:4].unsqueeze(2).broadcast_to([128, 4, 128]),
                                          op0=ALU.mult, op1=ALU.mult), [odfB, sbf3], [odnB])
                                      def later():
                                          for i in range(4):
                                              tr(trp_bf[:, i * 128:(i + 1) * 128], odn[:, i, :], [odnB], [trp_b])
                                          dve(lambda e: e.tensor_scalar(out=odT[:, h, qt * 512:(qt + 1) * 512],
                                                                        in0=trp_bf[:, 0:512], scalar1=vecs[:, 21:22],
                                                                        scalar2=None, op0=ALU.mult),
                                              [trp_b, Bconst], [odTB[qt]])

                                      return (8, later)

                                  K2t, K2B = K2s[h % 2]
                                  steps += attn_steps(KTt, KB_, QTt, QB_, 0, 128, lambda kt, hh=hh: Vd[:, kt, hh, 0:129], VB,
                                                      128, sc_d, qt, a1, a1B, pts, after1)
                                  steps += attn_steps(K2t, K2B, QTt, QB_, 0, 128, lambda kt, hh=hh: Vd[:, kt, hh, 0:129], VB,
                                                      128, sc_d, qt, a2, a2B, pts, after2)
                              run_steps(steps, diff_proj_tasks(h + 1) if (INTERLEAVE and h + 1 < 8) else [])

                  if stage == 2.3:
                      dump("odT", odT[:], [128, 8, SQ], BF16, odTB)
                      raise _Stop()
                  CD = ExitStack()
                  CD.__enter__()
                  mgT = sbt(CD, "mgT", [128, 8, SQ], BF16)
                  mgB = [Buf("mgT%d" % i, S.fence()) for i in range(4)]
                  with ExitStack() as C:
                      fC = S.fence()
                      tmps = Rot([tuple((sbt(C, "mc%d_%d" % (k, i), [128, 512], F32), Buf("mc%d_%d" % (k, i), fC))
                                        for k in range(4)) for i in range(2)])
                      for j in range(8):
                          (wga, wgb, wod, wom), wcB = wload([
                              (w_in_v[:, :, GA_OFF + j * 128:GA_OFF + (j + 1) * 128], 8, 128),
                              (w_in_v[:, :, GB_OFF + j * 128:GB_OFF + (j + 1) * 128], 8, 128),
                              (wview(w_od_d)[:, :, j * 128:(j + 1) * 128], 8, 128),
                              (wview(w_om_d)[:, :, j * 128:(j + 1) * 128], 4, 128)])
                          for tb in range(NB):
                              cs = slice(tb * 512, (tb + 1) * 512)
                              (sa, saB), (sb_, sbB), (m1, m1B), (m2, m2B) = tmps.next()
                              ga, gaB = gen_banks.next()
                              for c in range(8):
                                  mm(ga, wga[:, c, :], hT[:, c, cs], c == 0, c == 7, [wcB, hTB[tb]], [gaB])
                              act(sa[:], ga, AF.Sigmoid, [gaB, Bconst], [saB], bias=vecs[:, j:j + 1])
                              gb, gbB = gen_banks.next()
                              for c in range(8):
                                  mm(gb, wgb[:, c, :], hT[:, c, cs], c == 0, c == 7, [wcB, hTB[tb]], [gbB])
                              act(sb_[:], gb, AF.Sigmoid, [gbB, Bconst], [sbB], bias=vecs[:, 8 + j:8 + j + 1])
                              oa, oaB = gen_banks.next()
                              for c in range(8):
                                  mm(oa, wod[:, c, :], odT[:, c, cs], c == 0, c == 7, [wcB, odTB[tb]], [oaB])
                              dve(lambda e, m1=m1, sa=sa, oa=oa: e.tensor_tensor(out=m1[:], in0=oa, in1=sa[:], op=ALU.mult),
                                  [oaB, saB], [m1B])
                              ob, obB = gen_banks.next()
                              for c in range(4):
                                  mm(ob, wom[:, c, :], omT[:, c, cs], c == 0, c == 3, [wcB, omTB[tb]], [obB])
                              dve(lambda e, m2=m2, sb_=sb_, ob=ob: e.tensor_tensor(out=m2[:], in0=ob, in1=sb_[:], op=ALU.mult),
                                  [obB, sbB], [m2B])
                              dve(lambda e, m1=m1, m2=m2, j=j, cs=cs: e.tensor_tensor(out=mgT[:, j, cs], in0=m1[:], in1=m2[:],
                                                                                    op=ALU.add), [m1B, m2B], [mgB[tb]])
              if stage == 3:
                  dump("mgT", mgT[:], [128, 8, SQ], BF16, mgB)
                  raise _Stop()
              DG = ExitStack()
              DG.__enter__()
              x1 = sbt(DG, "x1", [128, NT, D], F32)
              fD = S.fence()
              x1B = [Buf("x1_%d" % i, fD) for i in range(NT)]
              EG = ExitStack()
              EG.__enter__()
              h2T = sbt(EG, "h2T", [128, 8, SQ], BF16)
              h2B = [Buf("h2T%d" % i, S.fence()) for i in range(4)]
              with ExitStack() as Dp:
                  xsl = Rot([(sbt(Dp, "xsD%d" % i, [128, D], F32), Buf("xsD%d" % i, fD), nsem("xsD%d" % i)) for i in range(2)])
                  nE1a, nE1b, nE2 = norm_T(Dp, "nE", lambda t: (x1[:, t, :], x1B[t]), g_ffn_d, h2T, h2B, external=True)
                  wo = []
                  for nh in range(2):
                      (wv_,), wb_ = wload([(wview(w_out_d)[:, :, nh * 512:(nh + 1) * 512], 8, 512)])
                      wo.append((wv_, wb_))
                  qE = []
                  for t in range(NT):
                      xt, xb, ds = xsl.next()
                      S.add("sp", lambda e, xt=xt, t=t, s=s: e.dma_start(out=xt[:], in_=x_d[s, t * 128:(t + 1) * 128, :]),
                            writes=[xb], dsem=ds)
                      for nh in range(2):
                          bk, bb = gen_banks.next()
                          for c in range(8):
                              mm(bk, mgT[:, c, t * 128:(t + 1) * 128], wo[nh][0][:, c, :], c == 0, c == 7,
                                 [mgB[t // 4], wo[nh][1]], [bb])
                          dve(lambda e, bk=bk, xt=xt, t=t, nh=nh: e.tensor_tensor(
                              out=x1[:, t, nh * 512:(nh + 1) * 512], in0=bk, in1=xt[:, nh * 512:(nh + 1) * 512], op=ALU.add),
                              [bb, xb], [x1B[t]])
                      sa = nE1a(t)
                      if len(qE) >= 2:
                          nE2(t - 2, *qE.pop(0))
                      qE.append(nE1b(*sa))
                  nE2(NT - 2, *qE.pop(0))
                  nE2(NT - 1, *qE.pop(0))
              CD.__exit__(None, None, None)
              if stage == 4:
                  dump("x1", x1[:], [128, NT, D], F32, x1B)
                  raise _Stop()
              with ExitStack() as Fp:
                  fF = S.fence()
                  hid = sbt(Fp, "hid", [128, NF, 1024], BF16)
                  hidB = [Buf("hid%d" % i, fF) for i in range(2)]
                  sgs = Rot([(sbt(Fp, "sg%d" % i, [128, 512], F32), Buf("sg%d" % i, fF)) for i in range(3)])
                  for half in range(2):
                      for fb in range(11):
                          (wg, wu), wfB = wload([(wview(w_g_d)[:, :, fb * 256:(fb + 1) * 256], 8, 256),
                                                 (wview(w_u_d)[:, :, fb * 256:(fb + 1) * 256], 8, 256)])
                          for fc in range(2):
                              f = fb * 2 + fc
                              for tbh in range(2):
                                  tb = half * 2 + tbh
                                  cs = slice(tb * 512, (tb + 1) * 512)
                                  gk, gkB = gen_banks.next()
                                  for c in range(8):
                                      mm(gk, wg[:, c, fc * 128:(fc + 1) * 128], h2T[:, c, cs], c == 0, c == 7, [wfB, h2B[tb]],
                                         [gkB])
                                  sg, sgB = sgs.next()
                                  act(sg[:], gk, AF.Silu, [gkB], [sgB])
                                  uk, ukB = gen_banks.next()
                                  for c in range(8):
                                      mm(uk, wu[:, c, fc * 128:(fc + 1) * 128], h2T[:, c, cs], c == 0, c == 7, [wfB, h2B[tb]],
                                         [ukB])
                                  dve(lambda e, sg=sg, uk=uk, f=f, tbh=tbh: e.tensor_tensor(
                                      out=hid[:, f, tbh * 512:(tbh + 1) * 512], in0=uk, in1=sg[:], op=ALU.mult),
                                      [ukB, sgB], [hidB[tbh]])
                      wdv = w_d_d.rearrange("(f p) n -> p f n", p=128)
                      for nq in range(4):
                          (wd,), wdB = wload([(wdv[:, :, nq * 256:(nq + 1) * 256], NF, 256)])
                          for tl in range(8):
                              t = half * 8 + tl
                              bk, bb = gen_banks.next()
                              for f in range(NF):
                                  mm(bk[:, 0:256], hid[:, f, tl * 128:(tl + 1) * 128], wd[:, f, :], f == 0, f == NF - 1,
                                     [hidB[tl // 4], wdB], [bb])
                              dve(lambda e, bk=bk, t=t, nq=nq: e.tensor_tensor(
                                  out=x1[:, t, nq * 256:(nq + 1) * 256], in0=bk[:, 0:256], in1=x1[:, t, nq * 256:(nq + 1) * 256],
                                  op=ALU.add), [bb, x1B[t]], [x1B[t]])
              if stage == 5:
                  dump("x2", x1[:], [128, NT, D], F32, x1B)
                  raise _Stop()
              with ExitStack() as Gp:
                  nG1a, nG1b, nG2 = norm_T(Gp, "nG", lambda t: (x1[:, t, :], x1B[t]), g_ple_d, h2T, h2B, external=True)
                  fG = S.fence()
                  pT = sbt(Gp, "pT", [128, 2, SQ], BF16)
                  pTB = [Buf("pT%d" % i, fG) for i in range(4)]
                  pin = Rot([(sbt(Gp, "pin%d" % i, [128, 256], F32), Buf("pin%d" % i, fG), nsem("pin%d" % i)) for i in range(2)])
                  pbf = Rot([(sbt(Gp, "pbf%d" % i, [128, 256], BF16), Buf("pbf%d" % i, fG)) for i in range(2)])
                  gfb = sbt(Gp, "gfb", [128, D], F32)
                  Bbb = Buf("bbc", fG)
                  S.add("sp", lambda e: e.dma_start(out=gfb[:], in_=g_fin_d.partition_broadcast(128)),
                        writes=[Bbb], dsem=nsem("bbc"))
                  bpl = sbt(Gp, "bpl", [128, D], BF16)
                  e0 = sbt(Gp, "e0", [128, 128], BF16)
                  Bbp = Buf("bpl", fG)
                  dve(lambda e: e.memset(bpl[:], 0.0), [], [Bbp])
                  dve(lambda e: e.memset(e0[:], 0.0), [], [Bbp])
                  dve(lambda e: e.memset(e0[0:1, :], 1.0), [], [Bbp])
                  S.add("pool", lambda e: e.dma_start(out=bpl[0:1, :], in_=b_ple_d.rearrange("(o n) -> o n", o=1)),
                        writes=[Bbp], dsem=nsem("bpl"))
                  pk_bank = allbanks[0]

                  def p_tile(t):
                      pt_, pb_, ds = pin.next()
                      S.add("sp", lambda e: e.dma_start(out=pt_[:], in_=p_d[s, t * 128:(t + 1) * 128, :]),
                            writes=[pb_], dsem=ds)
                      pf, pfB = pbf.next()
                      dve(lambda e: e.tensor_copy(out=pf[:], in_=pt_[:]), [pb_], [pfB])
                      pbk = pk_bank[0].bitcast(BF16)
                      for c in range(2):
                          tr(pbk[:, c * 128:(c + 1) * 128], pf[:, c * 128:(c + 1) * 128], [pfB], [pk_bank[1]])
                      copy_alt(pT[:, :, t * 128:(t + 1) * 128], pbk[:, 0:256].rearrange("p (c t) -> p c t", c=2),
                               [pk_bank[1]], [pTB[t // 4]])

                  norm_drive(nG1a, nG1b, nG2, extra=p_tile)
                  wpg = []
                  for nh in range(2):
                      (wv_,), wb_ = wload([(wview(w_pg_d)[:, :, nh * 512:(nh + 1) * 512], 8, 512)])
                      wpg.append((wv_, wb_))
                  (wpl,), wplB = wload([(wview(w_pl_d), 2, 1024)])
                  gtm = Rot([tuple((sbt(Gp, "gt%d_%d" % (k, i), [128, 512], F32), Buf("gt%d_%d" % (k, i), fG))
                                   for k in range(3)) for i in range(2)])
                  junk = sbt(Gp, "junkG", [128, D], BF16)
                  Bj = Buf("junkG", fG)
                  ysl = Rot([(sbt(Gp, "ys%d" % i, [128, D], F32), Buf("ys%d" % i, fG), nsem("ys%d" % i)) for i in range(2)])
                  def final_norm(t):
                      st, sbf = stat_rot.next()
                      act(junk[:], x1[:, t, :], AF.Square, [x1B[t]], [Bj, sbf], accum_out=st[:, 0:1])
                      rstd_pool(st, sbf)
                      yt, yb, ysem = ysl.next()
                      dve(lambda e: e.scalar_tensor_tensor(out=yt[:], in0=x1[:, t, :], scalar=st[:, 2:3], in1=gfb[:],
                                                           op0=ALU.mult, op1=ALU.mult), [x1B[t], sbf, Bbb], [yb])
                      S.add("sp", lambda e: e.dma_start(out=y_d[s, t * 128:(t + 1) * 128, :], in_=yt[:]),
                            reads=[yb], dsem=ysem, store=True)

                  for t in range(NT):
                      ts_ = slice(t * 128, (t + 1) * 128)
                      for nh in range(2):
                          ns = slice(nh * 512, (nh + 1) * 512)
                          (g1, g1B), (g2, g2B), (g3, g3B) = gtm.next()
                          bk, bb = gen_banks.next()
                          for c in range(8):
                              mm(bk, h2T[:, c, ts_], wpg[nh][0][:, c, :], c == 0, False, [h2B[t // 4], wpg[nh][1]], [bb])
                          mm(bk, e0[:], bpl[:, ns], False, True, [Bbp], [bb])
                          act(g2[:], bk, AF.Sigmoid, [bb], [g2B])
                          pk, pkB = gen_banks.next()
                          for c in range(2):
                              mm(pk, pT[:, c, ts_], wpl[:, c, ns], c == 0, c == 1, [pTB[t // 4], wplB], [pkB])
                          dve(lambda e, g3=g3, g2=g2, pk=pk: e.tensor_tensor(out=g3[:], in0=pk, in1=g2[:], op=ALU.mult),
                              [pkB, g2B], [g3B])
                          dve(lambda e, g3=g3, t=t, ns=ns: e.tensor_tensor(out=x1[:, t, ns], in0=x1[:, t, ns], in1=g3[:],
                                                                          op=ALU.add), [g3B, x1B[t]], [x1B[t]])
                      if t > 0:
                          final_norm(t - 1)
                  final_norm(NT - 1)
              EG.__exit__(None, None, None)
              DG.__exit__(None, None, None)
          except _Stop:
            pass

        for s_ in range(nseq):
            one_seq(s_)

        fin = S.add("sp", lambda e: e.nop(), reads=[])
        fin.deps = list(S.stores)
        with nc.Block() as block:
            S.emit_all(block, esem)
    return nc


def _rope_tables():
    pos = np.arange(SQ, dtype=np.float32)
    tabs = np.zeros((4, 128, SQ), np.float32)
    tabs[0] = 1.0
    tabs[2] = 1.0
    inv = (np.float32(10000.0) ** (-(np.arange(0, 32, 2, dtype=np.float32) / np.float32(32)))).astype(np.float32)
    ang = (pos[:, None] * inv[None, :]).astype(np.float32)
    c, sn = np.cos(ang).astype(np.float32).T, np.sin(ang).astype(np.float32).T
    tabs[0, 64:80], tabs[0, 80:96] = c, c
    tabs[1, 64:80], tabs[1, 80:96] = sn, sn
    inv = (np.float32(500000.0) ** (-(np.arange(0, 16, 2, dtype=np.float32) / np.float32(16)))).astype(np.float32)
    ang = (pos[:, None] * inv[None, :]).astype(np.float32)
    c, sn = np.cos(ang).astype(np.float32).T, np.sin(ang).astype(np.float32).T
    for r0 in (0, 64):
        tabs[2, r0:r0 + 8], tabs[2, r0 + 8:r0 + 16] = c, c
        tabs[3, r0:r0 + 8], tabs[3, r0 + 8:r0 + 16] = sn, sn
    return tabs


def _perm_mats():
    pd = np.zeros((128, 128), np.float32)
    for r0 in (0, 64):
        for r in range(8):
            pd[r0 + r + 8, r0 + r] = -1.0
            pd[r0 + r, r0 + r + 8] = 1.0
    pm = np.zeros((128, 128), np.float32)
    for r in range(16):
        pm[64 + r + 16, 64 + r] = -1.0
        pm[64 + r, 64 + r + 16] = 1.0
    return pd.astype(ml_dtypes.bfloat16), pm.astype(ml_dtypes.bfloat16)


_CACHE = {}


def _get_nc():
    if "nc" not in _CACHE:
        _CACHE["nc"] = build_program()
    return _CACHE["nc"]


def kernel(x, p, attn_norm, w_in, b_gate, lam_q1, lam_k1, lam_q2, lam_k2, diff_subln, w_o_diff, q_norm, w_uq,
           kv_norm, w_ukv, w_o_mla, w_out, ffn_norm, w_ffn_gate, w_ffn_up, w_ffn_down, ple_norm, w_ple_gate,
           b_ple_gate, w_ple, final_norm):
    f = lambda a: np.ascontiguousarray(np.asarray(a, dtype=np.float32))
    x = f(x)
    p = f(p)[0]
    B = x.shape[0]
    nseq = B // NCORES
    w_ukv_ = f(w_ukv)[0].reshape(256, 8, 2, 64)
    vecs = np.zeros((128, 24), np.float32)
    bg = f(b_gate)[0]
    vecs[:, 0:8] = bg[0].reshape(8, 128).T
    vecs[:, 8:16] = bg[1].reshape(8, 128).T
    vecs[:, 16:19] = f(q_norm)[0].reshape(3, 128).T
    vecs[:, 19:21] = f(kv_norm)[0].reshape(2, 128).T
    vecs[:, 21] = f(diff_subln)[0]
    pd, pm = _perm_mats()
    shared = {
        "w_in": f(w_in)[0], "w_o_diff": f(w_o_diff)[0], "w_uq": f(w_uq)[0],
        "w_ukv_kn": np.ascontiguousarray(w_ukv_[:, :, 0, :].reshape(256, 512)),
        "w_ukv_v": np.ascontiguousarray(w_ukv_[:, :, 1, :].reshape(256, 512)),
        "w_o_mla": f(w_o_mla)[0], "w_out": f(w_out)[0], "w_ffn_gate": f(w_ffn_gate)[0], "w_ffn_up": f(w_ffn_up)[0],
        "w_ffn_down": f(w_ffn_down)[0], "w_ple_gate": f(w_ple_gate)[0], "w_ple": f(w_ple)[0],
        "attn_norm": f(attn_norm)[0], "ffn_norm": f(ffn_norm)[0], "ple_norm": f(ple_norm)[0],
        "final_norm": f(final_norm), "b_ple_gate": f(b_ple_gate)[0],
        "lam_q1": f(lam_q1)[0], "lam_k1": f(lam_k1)[0], "lam_q2": f(lam_q2)[0], "lam_k2": f(lam_k2)[0],
        "vecs": vecs, "ident": np.eye(128, dtype=np.float32).astype(ml_dtypes.bfloat16),
        "perm_diff": pd, "perm_mla": pm, "rope_tabs": _rope_tables(),
    }
    nc = _get_nc()
    in_maps = []
    for c in range(NCORES):
        m = dict(shared)
        m["x"] = x[c * nseq:(c + 1) * nseq]
        m["p"] = p[c * nseq:(c + 1) * nseq]
        in_maps.append(m)
    res = run_bass_kernel_spmd(nc, in_maps, core_ids=list(range(NCORES)))
    return np.concatenate([r["y"] for r in res.results], axis=0).astype(np.float32)
```

```python
import math
import os
from contextlib import ExitStack

import numpy as np
import ml_dtypes

import concourse.bass as bass
import concourse.mybir as mybir
from concourse.bass_utils import run_bass_kernel_spmd

F32 = mybir.dt.float32
BF16 = mybir.dt.bfloat16
AF = mybir.ActivationFunctionType
ALU = mybir.AluOpType
AX = mybir.AxisListType

NCORES = 8
SEQ_PER_CORE = 2
SQ = 2048
D = 1024
NT = 16
NB = 4
FF = 2816
NF = 22
EPS = 1e-6
Q_OFF, K_OFF, V_OFF, CQ_OFF, CKV_OFF, KR_OFF, GA_OFF, GB_OFF = 0, 1024, 2048, 3072, 3456, 3712, 3744, 4768
IN_COLS = 5792
WS = 6144
ARENA_BYTES = 207872
LAM_INIT = 0.8 - 0.6 * math.exp(0.0)

ENGS = ("pe", "act", "dve", "pool", "sp")


class Buf:
    __slots__ = ("name", "w", "r", "init", "psum")

    def __init__(self, name, init=(), psum=False):
        self.name = name
        self.w = None
        self.r = []
        self.init = list(init)
        self.psum = psum


class DmaSem:
    __slots__ = ("h", "count")

    def __init__(self, h):
        self.h = h
        self.count = 0


class Op:
    __slots__ = ("eng", "emit", "deps", "dsem", "dval", "ndma", "sig", "sigval", "pos")

    def __init__(self, eng, emit):
        self.eng = eng
        self.emit = emit
        self.deps = []
        self.dsem = None
        self.dval = 0
        self.ndma = 0
        self.sig = False
        self.sigval = 0


class Sched:
    def __init__(self):
        self.streams = {e: [] for e in ENGS}
        self.last_compute = {e: None for e in ENGS}
        self.stores = []

    def fence(self):
        return [o for o in self.last_compute.values() if o is not None] + list(self.stores)

    def add(self, eng, emit, reads=(), writes=(), dsem=None, ndma=1, store=False):
        op = Op(eng, emit)
        deps = {}

        def dep(o, raw):
            if o is None:
                return
            if o.dsem is None and o.eng == eng:
                if eng == "pe":
                    return
            deps[id(o)] = o

        for b in reads:
            dep(b.w, True)
            for o in b.init:
                dep(o, True)
            if b.psum:
                for o in b.r:
                    if o.eng != eng:
                        dep(o, True)
        for b in writes:
            dep(b.w, False)
            for o in b.r:
                dep(o, False)
            for o in b.init:
                dep(o, True)
            b.init = []
        latest = {}
        dl = []
        for o in deps.values():
            if o.dsem is not None:
                dl.append(o)
            elif o.eng not in latest or latest[o.eng].pos < o.pos:
                latest[o.eng] = o
        op.deps = dl + list(latest.values())
        for b in reads:
            b.r.append(op)
        for b in writes:
            b.w = op
            b.r = []
        if dsem is not None:
            op.dsem = dsem
            op.ndma = ndma
            dsem.count += 16 * ndma
            op.dval = dsem.count
            if store:
                self.stores.append(op)
        else:
            self.last_compute[eng] = op
        op.pos = len(self.streams[eng])
        self.streams[eng].append(op)
        return op

    def emit_all(self, block, esem):
        for e in ENGS:
            for op in self.streams[e]:
                for d in op.deps:
                    if d.dsem is None:
                        d.sig = True
        for e in ENGS:
            c = 0
            for op in self.streams[e]:
                if op.sig:
                    c += 1
                    op.sigval = c
        streams = self.streams

        def run(eng_name, eng):
            waited = {}
            for op in streams[eng_name]:
                for d in op.deps:
                    if d.dsem is not None:
                        key, h, v = id(d.dsem), d.dsem.h, d.dval
                    else:
                        key, h, v = d.eng, esem[d.eng], d.sigval
                    if waited.get(key, 0) >= v:
                        continue
                    waited[key] = v
                    eng.wait_ge(h, v)
                res = op.emit(eng)
                if op.dsem is not None:
                    if not isinstance(res, (list, tuple)):
                        res = [res]
                    assert len(res) == op.ndma
                    for r in res:
                        r.then_inc(op.dsem.h, 16)
                elif op.sig:
                    if isinstance(res, (list, tuple)):
                        res = res[-1]
                    res.then_inc(esem[eng_name], 1)

        @block.tensor
        def _(eng):
            run("pe", eng)

        @block.scalar
        def _(eng):
            run("act", eng)

        @block.vector
        def _(eng):
            run("dve", eng)

        @block.gpsimd
        def _(eng):
            run("pool", eng)

        @block.sync
        def _(eng):
            run("sp", eng)


class Rot:
    def __init__(self, items):
        self.items = items
        self.i = 0

    def next(self):
        it = self.items[self.i % len(self.items)]
        self.i += 1
        return it


class _Stop(Exception):
    pass


def build_program(nseq=SEQ_PER_CORE, stage=99):
    nc = bass.Bass("TRN2", target_bir_lowering=False)
    dbg_outs = {}

    def din(name, shape, dt=F32):
        return nc.dram_tensor(name, list(shape), dt, kind="ExternalInput").ap()

    x_d = din("x", [nseq, SQ, D])
    p_d = din("p", [nseq, SQ, 256])
    y_d = nc.dram_tensor("y", [nseq, SQ, D], F32, kind="ExternalOutput").ap()
    w_in_d = din("w_in", [D, IN_COLS])
    w_od_d = din("w_o_diff", [1024, D])
    w_uq_d = din("w_uq", [384, 768])
    w_kn_d = din("w_ukv_kn", [256, 512])
    w_v_d = din("w_ukv_v", [256, 512])
    w_om_d = din("w_o_mla", [512, D])
    w_out_d = din("w_out", [D, D])
    w_g_d = din("w_ffn_gate", [D, FF])
    w_u_d = din("w_ffn_up", [D, FF])
    w_d_d = din("w_ffn_down", [FF, D])
    w_pg_d = din("w_ple_gate", [D, D])
    w_pl_d = din("w_ple", [256, D])
    g_attn_d = din("attn_norm", [D])
    g_ffn_d = din("ffn_norm", [D])
    g_ple_d = din("ple_norm", [D])
    g_fin_d = din("final_norm", [D])
    b_ple_d = din("b_ple_gate", [D])
    lam_d = [din(n, [64]) for n in ("lam_q1", "lam_k1", "lam_q2", "lam_k2")]
    vecs_d = din("vecs", [128, 24])
    ident_d = din("ident", [128, 128], BF16)
    pd_d = din("perm_diff", [128, 128], BF16)
    pm_d = din("perm_mla", [128, 128], BF16)
    tab_d = din("rope_tabs", [4, 128, SQ])

    def wview(w):
        return w.rearrange("(c p) n -> p c n", p=128)

    w_in_v = wview(w_in_d)

    S = Sched()
    G = ExitStack()
    with G:
        arena = G.enter_context(nc.sbuf_tensor("arena", [128, ARENA_BYTES // 2], BF16))
        abase = nc.lookup_mloc(arena).addr
        free_list = [[abase, abase + ARENA_BYTES]]
        _uid = [0]

        def a_alloc(nbytes):
            nbytes = (nbytes + 63) // 64 * 64
            for iv in free_list:
                if iv[1] - iv[0] >= nbytes:
                    off = iv[0]
                    iv[0] += nbytes
                    if iv[0] == iv[1]:
                        free_list.remove(iv)
                    return off, nbytes
            raise RuntimeError("SBUF arena full: need %d, free %s" % (nbytes, free_list))

        def a_free(off, nbytes):
            free_list.append([off, off + nbytes])
            free_list.sort()
            i = 0
            while i + 1 < len(free_list):
                if free_list[i][1] == free_list[i + 1][0]:
                    free_list[i][1] = free_list[i + 1][1]
                    del free_list[i + 1]
                else:
                    i += 1

        def sbt(es, name, shape, dt):
            n = 1
            for d in shape[1:]:
                n *= d
            nbytes = n * (4 if dt == F32 else 2)
            off, nb = a_alloc(nbytes)
            _uid[0] += 1
            t = nc.alloc_sbuf_tensor_at("%s_%d" % (name, _uid[0]), list(shape), dt, offset=off)
            es.callback(a_free, off, nb)
            return t

        def sem(name):
            return G.enter_context(nc.semaphore(name))

        esem = {e: sem("s_" + e) for e in ("pe", "act", "dve", "pool")}
        _ds = [0]

        def dsem():
            _ds[0] += 1
            return DmaSem(sem("d%d" % _ds[0]))

        accs = []
        for i in range(2):
            t = G.enter_context(nc.psum_tensor("acc%d" % i, [128, 1024], F32))
            accs.append((t, Buf("acc%d" % i, psum=True)))
        banks = []
        for i in range(4):
            t = G.enter_context(nc.psum_tensor("bank%d" % i, [128, 512], F32))
            banks.append((t, Buf("bank%d" % i, psum=True)))
        allbanks = []
        for (t, b) in accs:
            allbanks.append((t[:, 0:512], b))
        for (t, b) in banks:
            allbanks.append((t[:], b))
        gen_banks = Rot([allbanks[0], allbanks[1], (banks[0][0][:], banks[0][1]), (banks[1][0][:], banks[1][1]),
                         (banks[2][0][:], banks[2][1])])
        trp_t, trp_b = banks[3]
        trp_bf = trp_t[:].bitcast(BF16)

        ident = sbt(G, "ident", [128, 128], BF16)
        ones = sbt(G, "ones", [128, 128], BF16)
        pdm = sbt(G, "pdm", [128, 128], BF16)
        pmm = sbt(G, "pmm", [128, 128], BF16)
        vecs = sbt(G, "vecs_s", [128, 24], F32)
        lamt = sbt(G, "lamt", [128, 4, 64], F32)
        lsm = sbt(G, "lsm", [128, 8], F32)
        Bconst = Buf("const")
        Blam = Buf("lam")
        dconst = dsem()
        S.add("sp", lambda e: [e.dma_start(out=ident[:], in_=ident_d), e.dma_start(out=pdm[:], in_=pd_d),
                               e.dma_start(out=pmm[:], in_=pm_d), e.dma_start(out=vecs[:], in_=vecs_d)]
              + [e.dma_start(out=lamt[:, i, :], in_=lam_d[i].partition_broadcast(128)) for i in range(4)],
              writes=[Bconst, Blam], dsem=dconst, ndma=8)
        S.add("dve", lambda e: e.memset(ones[:], 1.0), writes=[Bconst])
        mhalf = sbt(G, "mhalf", [128, 1], F32)
        S.add("dve", lambda e: e.memset(mhalf[:], -0.5), writes=[Bconst])

        def rstd_pool(st, sbf):
            S.add("dve", lambda e: e.tensor_scalar(out=st[:, 1:2], in0=st[:, 0:1], scalar1=1.0 / D, scalar2=EPS,
                                                   op0=ALU.mult, op1=ALU.add), reads=[sbf], writes=[sbf])
            S.add("pool", lambda e: e.tensor_tensor(out=st[:, 2:3], in0=st[:, 1:2], in1=mhalf[:], op=ALU.pow),
                  reads=[sbf, Bconst], writes=[sbf])
        lprod = sbt(G, "lprod", [128, 2, 64], F32)
        S.add("dve", lambda e: e.tensor_tensor(out=lprod[:, 0, :], in0=lamt[:, 0, :], in1=lamt[:, 1, :], op=ALU.mult),
              reads=[Blam], writes=[Blam])
        S.add("dve", lambda e: e.tensor_tensor(out=lprod[:, 1, :], in0=lamt[:, 2, :], in1=lamt[:, 3, :], op=ALU.mult),
              reads=[Blam], writes=[Blam])
        S.add("dve", lambda e: e.tensor_reduce(out=lsm[:, 0:2], in_=lprod[:], axis=AX.X, op=ALU.add),
              reads=[Blam], writes=[Blam])
        S.add("act", lambda e: e.activation(out=lsm[:, 2:4], in_=lsm[:, 0:2], func=AF.Exp), reads=[Blam], writes=[Blam])
        S.add("dve", lambda e: e.tensor_tensor(out=lsm[:, 4:5], in0=lsm[:, 3:4], in1=lsm[:, 2:3], op=ALU.subtract),
              reads=[Blam], writes=[Blam])
        S.add("dve", lambda e: e.tensor_scalar(out=lsm[:, 5:6], in0=lsm[:, 4:5], scalar1=-LAM_INIT, scalar2=None,
                                               op0=ALU.add), reads=[Blam], writes=[Blam])
        neglam = lsm[:, 5:6]

        wslots = []
        for i in range(3):
            t = sbt(G, "wslot%d" % i, [128, WS], BF16)
            wslots.append((t, Buf("wslot%d" % i), dsem()))
        wrot = Rot(wslots)

        def wload(parts):
            t, b, ds = wrot.next()
            views = []
            off = 0
            pairs = []
            for part in parts:
                src_, C, N = part[0], part[1], part[2]
                v = t[:, off:off + C * N].rearrange("p (c n) -> p c n", c=C)
                views.append(v)
                if len(part) > 3:
                    n = src_.shape[2]
                    flat = t[:, off:off + C * N]
                    S.add("dve", lambda e, flat=flat: e.memset(flat, 0.0), writes=[b])
                    pairs.append((v[:, :, part[3]:part[3] + n], src_))
                else:
                    pairs.append((v, src_))
                off += C * N
            assert off <= WS
            S.add("pool", lambda e: [e.dma_start(out=v, in_=s) for (v, s) in pairs], writes=[b], dsem=ds,
                  ndma=len(pairs))
            return views, b

        stat = sbt(G, "stat", [128, 64], F32)
        stat_rot = Rot([(stat[:, i * 4:(i + 1) * 4], Buf("stat%d" % i)) for i in range(16)])

        _nsems = {}

        def nsem(key):
            if key not in _nsems:
                _nsems[key] = dsem()
            return _nsems[key]

        def dump(name, ap, shape, dt, bufs):
            d = nc.dram_tensor("dbg_" + name, list(shape), dt, kind="ExternalOutput").ap()
            dbg_outs[name] = d
            S.add("sp", lambda e: e.dma_start(out=d, in_=ap), reads=bufs, dsem=nsem("dbg_" + name), store=True)

        def mm(out, lhsT, rhs, start, stop, reads, writes, **kw):
            return S.add("pe", lambda e: e.matmul(out, lhsT=lhsT, rhs=rhs, start=start, stop=stop, **kw),
                         reads=reads, writes=writes)

        def tr(out, in_, reads, writes):
            return S.add("pe", lambda e: e.transpose(out=out, in_=in_, identity=ident[:]), reads=list(reads) + [Bconst],
                         writes=writes)

        def act(out, in_, func, reads, writes, **kw):
            return S.add("act", lambda e: e.activation(out=out, in_=in_, func=func, **kw), reads=reads, writes=writes)

        def dve(fn, reads, writes):
            return S.add("dve", fn, reads=reads, writes=writes)

        _alt = [0]

        def copy_alt(out, in_, reads, writes):
            _alt[0] += 1
            if _alt[0] % 2:
                return act(out, in_, AF.Copy, reads, writes)
            return dve(lambda e: e.tensor_copy(out=out, in_=in_), reads, writes)

        def norm_T(es, tag, get_x, g_d, dstT, dstB, external=False):
            gbc = sbt(es, tag + "_gbc", [128, D], F32)
            Bg = Buf(tag + "_gbc", S.fence())
            dg = nsem("gbc_" + tag)
            S.add("sp", lambda e: e.dma_start(out=gbc[:], in_=g_d.partition_broadcast(128)), writes=[Bg], dsem=dg)
            junk = sbt(es, tag + "_junk", [128, D], BF16)
            Bj = Buf(tag + "_junk", S.fence())
            hns = Rot([(sbt(es, tag + "_hn%d" % i, [128, D], BF16), Buf(tag + "_hn%d" % i, S.fence())) for i in range(3)])
            trps = Rot([(trp_bf, trp_b), (banks[2][0][:].bitcast(BF16), banks[2][1])])
            def stage1a(t):
                xa, xb = get_x(t)
                st, sbf = stat_rot.next()
                act(junk[:], xa, AF.Square, [xb], [Bj, sbf], accum_out=st[:, 0:1])
                rstd_pool(st, sbf)
                return xa, xb, st, sbf

            def stage1b(xa, xb, st, sbf):
                hn, hb = hns.next()
                dve(lambda e: e.scalar_tensor_tensor(out=hn[:], in0=xa, scalar=st[:, 2:3], in1=gbc[:],
                                                     op0=ALU.mult, op1=ALU.mult), [xb, sbf, Bg], [hb])
                return hn, hb

            def stage2(t, hn, hb):
                tb_, tbB = trps.next()
                for c in range(8):
                    tr(tb_[:, c * 128:(c + 1) * 128], hn[:, c * 128:(c + 1) * 128], [hb], [tbB])
                copy_alt(dstT[:, :, t * 128:(t + 1) * 128], tb_.rearrange("p (c t) -> p c t", c=8), [tbB],
                         [dstB[t // 4]])

            if external:
                return stage1a, stage1b, stage2
            norm_drive(stage1a, stage1b, stage2)

        def norm_drive(s1a, s1b, s2, extra=None):
            q = [s1b(*s1a(0)), s1b(*s1a(1))]
            for t in range(NT):
                sa = s1a(t + 2) if t + 2 < NT else None
                if extra is not None:
                    extra(t)
                s2(t, *q.pop(0))
                if sa is not None:
                    q.append(s1b(*sa))

        def rope(Aps, Ab, r0, r1, perm, Ct, St, Btab, tb, dst, dstB, tmp):
            (qbf, qbfB), (t1, t1B), (t2, t2B), (Bps, BpB) = tmp
            cs = slice(tb * 512, (tb + 1) * 512)
            p0, p1 = (0, 128) if r0 > 0 else (r0, r1)
            lvl = int(os.environ.get("KROPE", "9"))
            act(qbf[p0:p1, :], Aps[p0:p1, :], AF.Copy, [Ab], [qbfB])
            if lvl >= 2:
                mm(Bps[p0:p1, :], perm[p0:p1, p0:p1], qbf[p0:p1, :], True, True, [qbfB, Bconst], [BpB])
            if lvl >= 3:
                dve(lambda e: e.tensor_tensor(out=t1[r0:r1, :], in0=Aps[r0:r1, :], in1=Ct[r0:r1, cs], op=ALU.mult),
                    [Ab] + ([] if os.environ.get("KNOTAB") else [Btab]), [t1B])
            if lvl >= 4:
                dve(lambda e: e.tensor_tensor(out=t2[r0:r1, :], in0=Bps[r0:r1, :], in1=St[r0:r1, cs], op=ALU.mult),
                    [BpB, Btab], [t2B])
            if lvl >= 5:
                dve(lambda e: e.tensor_tensor(out=dst[r0:r1, :], in0=t1[r0:r1, :], in1=t2[r0:r1, :], op=ALU.add),
                    [t1B, t2B], [dstB])

        sc_banks = Rot([banks[0], banks[1]])
        NDUMMY = int(os.environ.get("KDUMMY", "0"))
        att_banks = Rot([banks[0], banks[1], banks[2]] if NDUMMY == 0 else [banks[0], banks[1]])
        LOOK = 2 if NDUMMY == 0 else 1
        INTERLEAVE = os.environ.get("KNOINT", "") == ""

        def attn_steps(KT, KB, QT, QB, r0, r1, Vfn, VB, dv, scale, qt, acc, accB, pts, after):
            steps = []
            nk = 4 * qt + 4
            first = {0: True, 1: True}
            accv = acc[:].rearrange("p (i n) -> p i n", i=4)
            for kt in range(nk):
                j = kt - 4 * qt
                q0 = 128 * j if j > 0 else 0
                sct, scb = att_banks.next()
                pt, ptb = pts.next()

                def s_fn(kt=kt, q0=q0, sct=sct, scb=scb):
                    mm(sct[:, q0:512], KT[r0:r1, kt * 128:(kt + 1) * 128], QT[r0:r1, qt * 512 + q0:(qt + 1) * 512],
                       True, True, [KB[kt // 4], QB[qt]], [scb])

                avs = []
                for i in range(max(j, 0), 4):
                    bk = i // 2
                    avs.append((i, first[bk]))
                    first[bk] = False

                def rest_fn(kt=kt, j=j, q0=q0, sct=sct, scb=scb, pt=pt, ptb=ptb, avs=avs, last=(kt == nk - 1)):
                    for dmy in range(NDUMMY):
                        mm(banks[2][0][:, :], ones[:], QT[:, 0:512], True, True, [Bconst, QB], [banks[2][1]])
                    act(pt[:, q0:512], sct[:, q0:512], AF.Exp, [scb], [ptb], scale=scale)
                    if j >= 0:
                        dve(lambda e: e.memset(pt[64:128, 128 * j:128 * j + 64], 0.0), [], [ptb])
                    if os.environ.get("KAV", "") == "dense":
                        vv = Vfn(kt)
                        for bki in range(2):
                            mm(acc[0:dv, bki * 512 + q0:(bki + 1) * 512], vv[:, 0:dv] if bki == 0 else ones[:, 0:dv],
                               pt[:, q0:512], kt == 0, True, [ptb, VB, Bconst], [accB], skip_group_check=True)
                    else:
                      for (i, st) in avs:
                        mm(accv[:, i, 0:dv + 1], pt[:, i * 128:(i + 1) * 128], Vfn(kt), st, True, [ptb, VB], [accB],
                           skip_group_check=True)
                    if last:
                        return after()
                    return None

                steps.append((s_fn, rest_fn))
            return steps

        def run_steps(steps, side=()):
            side = list(side)
            busy = [False]

            def run_side():
                fn, b = side.pop(0)
                fn()
                busy[0] = b

            if not steps:
                while side:
                    run_side()
                return
            per = -(-len(side) // len(steps)) if side else 0
            pending = []
            for i in range(min(LOOK, len(steps))):
                steps[i][0]()
            for i, (s_fn, rest_fn) in enumerate(steps):
                if i + LOOK < len(steps):
                    steps[i + LOOK][0]()
                while pending and pending[0][0] <= i and not busy[0]:
                    pending.pop(0)[1]()
                d = rest_fn()
                if d is not None:
                    pending.append((i + d[0], d[1]))
                for _ in range(per):
                    if side:
                        run_side()
            while side:
                run_side()
            for (_, fn) in pending:
                fn()

        def one_seq(s):
          try:
              ABC = ExitStack()
              with ABC:
                  hT = sbt(ABC, "hT", [128, 8, SQ], BF16)
                  hTB = [Buf("hT%d" % i, S.fence()) for i in range(4)]
                  odT = sbt(ABC, "odT", [128, 8, SQ], BF16)
                  omT = sbt(ABC, "omT", [128, 4, SQ], BF16)
                  omTB = [Buf("omT%d" % i, S.fence()) for i in range(4)]
                  with ExitStack() as A:
                      xsl = Rot([(sbt(A, "xsA%d" % i, [128, D], F32), Buf("xsA%d" % i, S.fence()), nsem("xsA%d" % i))
                                 for i in range(6)])

                      def get_x(t, s=s, xsl=xsl):
                          xt, xb, ds = xsl.next()
                          S.add("sp", lambda e: e.dma_start(out=xt[:], in_=x_d[s, t * 128:(t + 1) * 128, :]), writes=[xb],
                                dsem=ds)
                          return xt[:], xb

                      norm_T(A, "nA", get_x, g_attn_d, hT, hTB)
                  if stage == 1:
                      dump("hT", hT[:], [128, 8, SQ], BF16, hTB)
                      raise _Stop()

                  with ExitStack() as Bx:
                      Ct = sbt(Bx, "Ct", [128, SQ], F32)
                      St = sbt(Bx, "St", [128, SQ], F32)
                      Btab = Buf("tab", S.fence())
                      dtab = nsem("tab")
                      S.add("sp", lambda e: [e.dma_start(out=Ct[:], in_=tab_d[0]), e.dma_start(out=St[:], in_=tab_d[1])],
                            writes=[Btab], dsem=dtab, ndma=2)
                      Vbuf = sbt(Bx, "Vbuf", [128, 8704], BF16)
                      QK = [(sbt(Bx, "QT%d" % i, [128, SQ], BF16), [Buf("QT%d_%d" % (i, k), S.fence()) for k in range(4)],
                             sbt(Bx, "KT%d" % i, [128, SQ], BF16), [Buf("KT%d_%d" % (i, k), S.fence()) for k in range(4)])
                            for i in range(2)]
                      K2s = [(sbt(Bx, "K2_%d" % i, [128, SQ], BF16), [Buf("K2_%d_%d" % (i, k), S.fence()) for k in range(4)])
                             for i in range(2)]
                      pts = Rot([(sbt(Bx, "pt%d" % i, [128, 512], BF16), Buf("pt%d" % i, S.fence())) for i in range(4)])
                      f0 = S.fence()
                      rtmp = ((sbt(Bx, "qbf", [128, 512], BF16), Buf("qbf", f0)),
                              (sbt(Bx, "rt1", [128, 512], F32), Buf("rt1", f0)),
                              (sbt(Bx, "rt2", [128, 512], F32), Buf("rt2", f0)),
                              (banks[2][0], banks[2][1]))
                      o1n = sbt(Bx, "o1n", [128, 4, 128], F32)
                      o1nB = Buf("o1n", f0)
                      otmp = sbt(Bx, "otmp", [128, 4, 128], F32)
                      otmpB = Buf("otmp", f0)
                      odf = sbt(Bx, "odf", [128, 4, 128], F32)
                      odfB = Buf("odf", f0)
                      osq = sbt(Bx, "osq", [128, 4, 128], F32)
                      osqB = Buf("osq", f0)
                      odn = sbt(Bx, "odn", [128, 4, 128], BF16)
                      odnB = Buf("odn", f0)
                      omn = sbt(Bx, "omn", [128, 4, 64], BF16)
                      omnB = Buf("omn", f0)

                      latT = odT
                      latB = [Buf("lat%d" % i, f0) for i in range(4)]
                      latf = [Vbuf[:, j * 1024:(j + 1) * 1024].bitcast(F32) for j in range(5)]
                      sqs = [Vbuf[:, 5120 + j * 512:5120 + (j + 1) * 512] for j in range(5)]
                      ltBs = [Buf("lattmp%d" % j, f0) for j in range(5)]
                      rsq = sbt(Bx, "rsq", [128, 512], F32)
                      rsqB = Buf("rsq", f0)
                      rstd = sbt(Bx, "rstdl", [128, 512], F32)
                      rstdB = Buf("rstdl", f0)
                      (wl, wkr), wlB = wload([(w_in_v[:, :, CQ_OFF:CQ_OFF + 640], 8, 640),
                                                  (w_in_v[:, :, KR_OFF:KR_OFF + 32], 8, 128, 64) if os.environ.get("KDBG", "") != "krplain"
                                                  else (w_in_v[:, :, KR_OFF - 96:KR_OFF + 32], 8, 128)])
                      for tb in range(NB):
                          cs = slice(tb * 512, (tb + 1) * 512)
                          for j in range(6):
                              bk, bb = sc_banks.next()
                              if j < 5:
                                  c0 = j * 128
                                  for c in range(8):
                                      mm(bk[:, :], wl[:, c, c0:c0 + 128], hT[:, c, cs], c == 0, c == 7, [wlB, hTB[tb]], [bb])
                                  act(latf[j], bk[:, :], AF.Copy, [bb], [ltBs[j]])
                                  act(sqs[j], bk[:, :], AF.Square, [bb], [ltBs[j]])
                              elif os.environ.get("KDBG", "") != "skipkr":
                                  for c in range(8):
                                      mm(bk[:, :], wkr[:, c, :], hT[:, c, cs], c == 0, c == 7, [wlB, hTB[tb]], [bb])
                                  rope(bk, bb, 0, 128, pmm, Ct, St, Btab, tb, latT[:, 5, cs], latB[tb], rtmp)
                          for (js, nrm, vcol) in (((0, 1, 2), 384.0, 8 + 8), ((3, 4), 256.0, 8 + 8 + 3)):
                              rk, rb = banks[2]
                              for n, j in enumerate(js):
                                  mm(rk[:, :], ones[:], sqs[j], n == 0, n == len(js) - 1, [ltBs[j], Bconst], [rb])
                              act(rsq[:], rk[:, :], AF.Sqrt, [rb], [rsqB], scale=1.0 / nrm, bias=EPS)
                              dve(lambda e: e.reciprocal(out=rstd[:], in_=rsq[:]), [rsqB], [rstdB])
                              for n, j in enumerate(js):
                                  dve(lambda e, j=j, n=n, vcol=vcol, cs=cs: e.scalar_tensor_tensor(
                                      out=latT[:, j, cs], in0=latf[j], scalar=vecs[:, vcol + n:vcol + n + 1], in1=rstd[:],
                                      op0=ALU.mult, op1=ALU.mult), [ltBs[j], rstdB, Bconst], [latB[tb]])
                      if stage == 2.1:
                          dump("latT", odT[:, 0:5, :], [128, 5, SQ], BF16, latB)
                          if os.environ.get("KDBG", "") not in ("skipkr", "nodumpkr"):
                              dump("kr", odT[64:96, 5, :], [32, SQ], BF16, latB)
                          raise _Stop()
                      (wuq, wkn, wv), wmB = wload([(wview(w_uq_d), 3, 768), (wview(w_kn_d), 2, 512), (wview(w_v_d), 2, 512)])
                      VB = Buf("V", S.fence() + [b.w for b in ltBs] + [o for b in ltBs for o in b.r])
                      Vm = Vbuf[:, 0:16 * 8 * 66].rearrange("p (t h d) -> p t h d", t=16, h=8)
                      dve(lambda e: e.memset(Vm[:, :, :, 64:65], 1.0), [], [VB])
                      for t in range(NT):
                          bk, bb = sc_banks.next()
                          for c in range(2):
                              mm(bk[:, :], latT[:, 3 + c, t * 128:(t + 1) * 128], wv[:, c, :], c == 0, c == 1,
                                 [latB[t // 4], wmB], [bb])
                          copy_alt(Vm[:, t, :, 0:64], bk[:, :].rearrange("p (h d) -> p h d", h=8), [bb], [VB])
                      trp_f = trp_t[:, :]

                      def diff_proj_tasks(h):
                          (wq, wk), wqB = wload([(w_in_v[:, :, Q_OFF + h * 128:Q_OFF + (h + 1) * 128], 8, 128),
                                                 (w_in_v[:, :, K_OFF + h * 128:K_OFF + (h + 1) * 128], 8, 128)])
                          QTt, QB_, KTt, KB_ = QK[h % 2]
                          K2t, K2B = K2s[h % 2]
                          (qbf, qbfB), (t1, t1B), (t2, t2B), _ = rtmp
                          tasks = []
                          if h < 2:
                              dve(lambda e: e.memset(KTt[64:128, :], 0.0), [], KB_)
                              dve(lambda e: e.memset(K2t[0:64, :], 0.0), [], K2B)
                          for tb in range(NB):
                              cs = slice(tb * 512, (tb + 1) * 512)
                              for (wx, dT, dB) in ((wk, None, None), (wq, QTt, QB_[tb])):
                                  def pm(c, wx=wx, cs=cs, tb=tb):
                                      mm(trp_f, wx[:, c, :], hT[:, c, cs], c == 0, c == 7, [wqB, hTB[tb]], [trp_b])

                                  def p1b(cs=cs):
                                      dve(lambda e: e.tensor_copy(out=qbf[:], in_=trp_f), [trp_b], [qbfB])
                                      dve(lambda e: e.tensor_tensor(out=t1[:], in0=trp_f, in1=Ct[:, cs], op=ALU.mult),
                                          [trp_b, Btab], [t1B])

                                  def p2(dT=dT, dB=dB, cs=cs, tb=tb):
                                      b2, b2B = trp_t, trp_b
                                      mm(b2[:, :], pdm[:, :], qbf[:], True, True, [qbfB, Bconst], [b2B])
                                      dve(lambda e: e.tensor_tensor(out=t2[:], in0=b2[:, :], in1=St[:, cs], op=ALU.mult),
                                          [b2B, Btab], [t2B])
                                      if dT is None:
                                          dve(lambda e: e.tensor_tensor(out=KTt[0:64, cs], in0=t1[0:64, :], in1=t2[0:64, :],
                                                                        op=ALU.add), [t1B, t2B], [KB_[tb]])
                                          dve(lambda e: e.tensor_tensor(out=K2t[64:128, cs], in0=t1[64:128, :],
                                                                        in1=t2[64:128, :], op=ALU.add), [t1B, t2B], [K2B[tb]])
                                      else:
                                          dve(lambda e: e.tensor_tensor(out=dT[:, cs], in0=t1[:], in1=t2[:], op=ALU.add),
                                              [t1B, t2B], [dB])

                                  tasks += [((lambda c=c, pm=pm: pm(c)), True) for c in range(8)]
                                  tasks += [(p1b, False), (p2, False)]
                          return tasks

                      sc_m = 96.0 ** -0.5
                      for h in range(8):
                          QTt, QB_, KTt, KB_ = QK[h % 2]
                          for tb in range(NB):
                              cs = slice(tb * 512, (tb + 1) * 512)
                              bk, bb = sc_banks.next()
                              for c in range(2):
                                  mm(bk[0:64, :], wkn[:, c, h * 64:(h + 1) * 64], latT[:, 3 + c, cs], c == 0, c == 1,
                                     [wmB, latB[tb]], [bb])
                              copy_alt(KTt[0:64, cs], bk[0:64, :], [bb], [KB_[tb]])
                              dve(lambda e, KTt=KTt, cs=cs: e.tensor_copy(out=KTt[64:96, cs], in_=latT[64:96, 5, cs]),
                                  [latB[tb]], [KB_[tb]])
                              bk, bb = sc_banks.next()
                              for c in range(3):
                                  mm(bk[0:96, :], wuq[:, c, h * 96:(h + 1) * 96], latT[:, c, cs], c == 0, c == 2,
                                     [wmB, latB[tb]], [bb])
                              rope(bk, bb, 0, 96, pmm, Ct, St, Btab, tb, QTt[:, cs], QB_[tb], rtmp)
                          steps = []
                          for qt in range(NB):
                              acc, accB = accs[(h * 4 + qt) % 2]

                              def after(h=h, qt=qt, acc=acc, accB=accB):
                                  accv = acc[:].rearrange("p (i n) -> p i n", i=4)
                                  st, sbf = stat_rot.next()
                                  dve(lambda e: e.reciprocal(out=st[:, 0:4], in_=accv[:, :, 64]), [accB], [sbf])
                                  dve(lambda e: e.tensor_tensor(out=omn[:], in0=accv[:, :, 0:64],
                                                                in1=st[:, 0:4].unsqueeze(2).broadcast_to([128, 4, 64]),
                                                                op=ALU.mult), [accB, sbf], [omnB])
                                  ro = 64 * (h % 2)

                                  def later():
                                      for i in range(4):
                                          tr(trp_bf[ro:ro + 64, i * 128:(i + 1) * 128], omn[:, i, :], [omnB], [trp_b])
                                      dve(lambda e: e.tensor_copy(out=omT[ro:ro + 64, h // 2, qt * 512:(qt + 1) * 512],
                                                                  in_=trp_bf[ro:ro + 64, 0:512]), [trp_b], [omTB[qt]])

                                  return (3, later)

                              steps += attn_steps(KTt, KB_, QTt, QB_, 0, 96, lambda kt, h=h: Vm[:, kt, h, 0:65], VB, 64, sc_m,
                                                  qt, acc, accB, pts, after)
                          side = []
                          if h == 7 and INTERLEAVE:
                              S.add("sp", lambda e: [e.dma_start(out=Ct[:], in_=tab_d[2]),
                                                     e.dma_start(out=St[:], in_=tab_d[3])],
                                    writes=[Btab], dsem=dtab, ndma=2)
                              side = diff_proj_tasks(0)
                          run_steps(steps, side)

                      if stage == 2.2:
                          dump("omT", omT[:], [128, 4, SQ], BF16, omTB)
                          raise _Stop()
                      if not INTERLEAVE:
                          S.add("sp", lambda e: [e.dma_start(out=Ct[:], in_=tab_d[2]), e.dma_start(out=St[:], in_=tab_d[3])],
                                writes=[Btab], dsem=dtab, ndma=2)
                      f1 = S.fence()
                      odTB = [Buf("odT%d" % i, f1) for i in range(4)]
                      Vd = Vbuf[:, 0:16 * 4 * 130].rearrange("p (t h d) -> p t h d", t=16, h=4)
                      sc_d = 64.0 ** -0.5
                      for g in range(2):
                          (wvd,), wvB = wload([(w_in_v[:, :, V_OFF + g * 512:V_OFF + (g + 1) * 512], 8, 512)])
                          dve(lambda e: e.memset(Vd[:, :, :, 128:129], 1.0), [], [VB])
                          for t in range(NT):
                              bk, bb = sc_banks.next()
                              for c in range(8):
                                  mm(bk[:, :], hT[:, c, t * 128:(t + 1) * 128], wvd[:, c, :], c == 0, c == 7,
                                     [hTB[t // 4], wvB], [bb])
                              copy_alt(Vd[:, t, :, 0:128], bk[:, :].rearrange("p (h d) -> p h d", h=4), [bb], [VB])
                          for hh in range(4):
                              h = g * 4 + hh
                              QTt, QB_, KTt, KB_ = QK[h % 2]
                              if not INTERLEAVE:
                                  for fn, _b in diff_proj_tasks(h):
                                      fn()
                              steps = []
                              for qt in range(NB):
                                  a1, a1B = accs[0]
                                  a2, a2B = accs[1]

                                  def after1(a1=a1, a1B=a1B):
                                      accv = a1[:].rearrange("p (i n) -> p i n", i=4)
                                      st, sbf = stat_rot.next()
                                      dve(lambda e: e.reciprocal(out=st[:, 0:4], in_=accv[:, :, 128]), [a1B], [sbf])
                                      dve(lambda e: e.tensor_tensor(out=o1n[:], in0=accv[:, :, 0:128],
                                                                    in1=st[:, 0:4].unsqueeze(2).broadcast_to([128, 4, 128]),
                                                                    op=ALU.mult), [a1B, sbf], [o1nB])

                                  def after2(h=h, qt=qt, a2=a2, a2B=a2B):
                                      accv = a2[:].rearrange("p (i n) -> p i n", i=4)
                                      st, sbf = stat_rot.next()
                                      st2, sbf2 = stat_rot.next()
                                      dve(lambda e: e.reciprocal(out=st[:, 0:4], in_=accv[:, :, 128]), [a2B], [sbf])
                                      dve(lambda e: e.tensor_scalar(out=st2[:, 0:4], in0=st[:, 0:4], scalar1=neglam,
                                                                    scalar2=None, op0=ALU.mult), [sbf, Blam], [sbf2])
                                      dve(lambda e: e.tensor_tensor(out=otmp[:], in0=accv[:, :, 0:128],
                                                                    in1=st2[:, 0:4].unsqueeze(2).broadcast_to([128, 4, 128]),
                                                                    op=ALU.mult), [a2B, sbf2], [otmpB])
                                      dve(lambda e: e.tensor_tensor(out=odf[:], in0=o1n[:], in1=otmp[:], op=ALU.add),
                                          [o1nB, otmpB], [odfB])
                                      dve(lambda e: e.tensor_tensor(out=osq[:], in0=odf[:], in1=odf[:], op=ALU.mult),
                                          [odfB], [osqB])
                                      st3, sbf3 = stat_rot.next()
                                      st4, sbf4 = stat_rot.next()
                                      st5, sbf5 = stat_rot.next()
                                      dve(lambda e: e.tensor_reduce(out=st3[:, 0:4], in_=osq[:], axis=AX.X, op=ALU.add),
                                          [osqB], [sbf3])
                                      dve(lambda e: e.tensor_scalar(out=st4[:, 0:4], in0=st3[:, 0:4], scalar1=1.0 / 128,
                                                                    scalar2=EPS, op0=ALU.mult, op1=ALU.add), [sbf3], [sbf4])
                                      act(st5[:, 0:4], st4[:, 0:4], AF.Ln, [sbf4], [sbf5])
                                      act(st3[:, 0:4], st5[:, 0:4], AF.Exp, [sbf5], [sbf3], scale=-0.5)
                                      dve(lambda e: e.scalar_tensor_tensor(
                                          out=odn[:], in0=odf[:], scalar=1.0 - LAM_INIT,
                                          in1=st3[:, 0:4].unsqueeze(2).broadcast_to([128, 4, 128]),
                                          op0=ALU.mult, op1=ALU.mult), [odfB, sbf3], [odnB])
                                      def later():
                                          for i in range(4):
                                              tr(trp_bf[:, i * 128:(i + 1) * 128], odn[:, i, :], [odnB], [trp_b])
                                          dve(lambda e: e.tensor_scalar(out=odT[:, h, qt * 512:(qt + 1) * 512],
                                                                        in0=trp_bf[:, 0:512], scalar1=vecs[:, 21:22],
                                                                        scalar2=None, op0=ALU.mult),
                                              [trp_b, Bconst], [odTB[qt]])

                                      return (8, later)

                                  K2t, K2B = K2s[h % 2]
                                  steps += attn_steps(KTt, KB_, QTt, QB_, 0, 128, lambda kt, hh=hh: Vd[:, kt, hh, 0:129], VB,
                                                      128, sc_d, qt, a1, a1B, pts, after1)
                                  steps += attn_steps(K2t, K2B, QTt, QB_, 0, 128, lambda kt, hh=hh: Vd[:, kt, hh, 0:129], VB,
                                                      128, sc_d, qt, a2, a2B, pts, after2)
                              run_steps(steps, diff_proj_tasks(h + 1) if (INTERLEAVE and h + 1 < 8) else [])

                  if stage == 2.3:
                      dump("odT", odT[:], [128, 8, SQ], BF16, odTB)
                      raise _Stop()
                  CD = ExitStack()
                  CD.__enter__()
                  mgT = sbt(CD, "mgT", [128, 8, SQ], BF16)
                  mgB = [Buf("mgT%d" % i, S.fence()) for i in range(4)]
                  with ExitStack() as C:
                      fC = S.fence()
                      tmps = Rot([tuple((sbt(C, "mc%d_%d" % (k, i), [128, 512], F32), Buf("mc%d_%d" % (k, i), fC))
                                        for k in range(4)) for i in range(2)])
                      for j in range(8):
                          (wga, wgb, wod, wom), wcB = wload([
                              (w_in_v[:, :, GA_OFF + j * 128:GA_OFF + (j + 1) * 128], 8, 128),
                              (w_in_v[:, :, GB_OFF + j * 128:GB_OFF + (j + 1) * 128], 8, 128),
                              (wview(w_od_d)[:, :, j * 128:(j + 1) * 128], 8, 128),
                              (wview(w_om_d)[:, :, j * 128:(j + 1) * 128], 4, 128)])
                          for tb in range(NB):
                              cs = slice(tb * 512, (tb + 1) * 512)
                              (sa, saB), (sb_, sbB), (m1, m1B), (m2, m2B) = tmps.next()
                              ga, gaB = gen_banks.next()
                              for c in range(8):
                                  mm(ga, wga[:, c, :], hT[:, c, cs], c == 0, c == 7, [wcB, hTB[tb]], [gaB])
                              act(sa[:], ga, AF.Sigmoid, [gaB, Bconst], [saB], bias=vecs[:, j:j + 1])
                              gb, gbB = gen_banks.next()
                              for c in range(8):
                                  mm(gb, wgb[:, c, :], hT[:, c, cs], c == 0, c == 7, [wcB, hTB[tb]], [gbB])
                              act(sb_[:], gb, AF.Sigmoid, [gbB, Bconst], [sbB], bias=vecs[:, 8 + j:8 + j + 1])
                              oa, oaB = gen_banks.next()
                              for c in range(8):
                                  mm(oa, wod[:, c, :], odT[:, c, cs], c == 0, c == 7, [wcB, odTB[tb]], [oaB])
                              dve(lambda e, m1=m1, sa=sa, oa=oa: e.tensor_tensor(out=m1[:], in0=oa, in1=sa[:], op=ALU.mult),
                                  [oaB, saB], [m1B])
                              ob, obB = gen_banks.next()
                              for c in range(4):
                                  mm(ob, wom[:, c, :], omT[:, c, cs], c == 0, c == 3, [wcB, omTB[tb]], [obB])
                              dve(lambda e, m2=m2, sb_=sb_, ob=ob: e.tensor_tensor(out=m2[:], in0=ob, in1=sb_[:], op=ALU.mult),
                                  [obB, sbB], [m2B])
                              dve(lambda e, m1=m1, m2=m2, j=j, cs=cs: e.tensor_tensor(out=mgT[:, j, cs], in0=m1[:], in1=m2[:],
                                                                                    op=ALU.add), [m1B, m2B], [mgB[tb]])
              if stage == 3:
                  dump("mgT", mgT[:], [128, 8, SQ], BF16, mgB)
                  raise _Stop()
              DG = ExitStack()
              DG.__enter__()
              x1 = sbt(DG, "x1", [128, NT, D], F32)
              fD = S.fence()
              x1B = [Buf("x1_%d" % i, fD) for i in range(NT)]
              EG = ExitStack()
              EG.__enter__()
              h2T = sbt(EG, "h2T", [128, 8, SQ], BF16)
              h2B = [Buf("h2T%d" % i, S.fence()) for i in range(4)]
              with ExitStack() as Dp:
                  xsl = Rot([(sbt(Dp, "xsD%d" % i, [128, D], F32), Buf("xsD%d" % i, fD), nsem("xsD%d" % i)) for i in range(2)])
                  nE1a, nE1b, nE2 = norm_T(Dp, "nE", lambda t: (x1[:, t, :], x1B[t]), g_ffn_d, h2T, h2B, external=True)
                  wo = []
                  for nh in range(2):
                      (wv_,), wb_ = wload([(wview(w_out_d)[:, :, nh * 512:(nh + 1) * 512], 8, 512)])
                      wo.append((wv_, wb_))
                  qE = []
                  for t in range(NT):
                      xt, xb, ds = xsl.next()
                      S.add("sp", lambda e, xt=xt, t=t, s=s: e.dma_start(out=xt[:], in_=x_d[s, t * 128:(t + 1) * 128, :]),
                            writes=[xb], dsem=ds)
                      for nh in range(2):
                          bk, bb = gen_banks.next()
                          for c in range(8):
                              mm(bk, mgT[:, c, t * 128:(t + 1) * 128], wo[nh][0][:, c, :], c == 0, c == 7,
                                 [mgB[t // 4], wo[nh][1]], [bb])
                          dve(lambda e, bk=bk, xt=xt, t=t, nh=nh: e.tensor_tensor(
                              out=x1[:, t, nh * 512:(nh + 1) * 512], in0=bk, in1=xt[:, nh * 512:(nh + 1) * 512], op=ALU.add),
                              [bb, xb], [x1B[t]])
                      sa = nE1a(t)
                      if len(qE) >= 2:
                          nE2(t - 2, *qE.pop(0))
                      qE.append(nE1b(*sa))
                  nE2(NT - 2, *qE.pop(0))
                  nE2(NT - 1, *qE.pop(0))
              CD.__exit__(None, None, None)
              if stage == 4:
                  dump("x1", x1[:], [128, NT, D], F32, x1B)
                  raise _Stop()
              with ExitStack() as Fp:
                  fF = S.fence()
                  hid = sbt(Fp, "hid", [128, NF, 1024], BF16)
                  hidB = [Buf("hid%d" % i, fF) for i in range(2)]
                  sgs = Rot([(sbt(Fp, "sg%d" % i, [128, 512], F32), Buf("sg%d" % i, fF)) for i in range(3)])
                  for half in range(2):
                      for fb in range(11):
                          (wg, wu), wfB = wload([(wview(w_g_d)[:, :, fb * 256:(fb + 1) * 256], 8, 256),
                                                 (wview(w_u_d)[:, :, fb * 256:(fb + 1) * 256], 8, 256)])
                          for fc in range(2):
                              f = fb * 2 + fc
                              for tbh in range(2):
                                  tb = half * 2 + tbh
                                  cs = slice(tb * 512, (tb + 1) * 512)
                                  gk, gkB = gen_banks.next()
                                  for c in range(8):
                                      mm(gk, wg[:, c, fc * 128:(fc + 1) * 128], h2T[:, c, cs], c == 0, c == 7, [wfB, h2B[tb]],
                                         [gkB])
                                  sg, sgB = sgs.next()
                                  act(sg[:], gk, AF.Silu, [gkB], [sgB])
                                  uk, ukB = gen_banks.next()
                                  for c in range(8):
                                      mm(uk, wu[:, c, fc * 128:(fc + 1) * 128], h2T[:, c, cs], c == 0, c == 7, [wfB, h2B[tb]],
                                         [ukB])
                                  dve(lambda e, sg=sg, uk=uk, f=f, tbh=tbh: e.tensor_tensor(
                                      out=hid[:, f, tbh * 512:(tbh + 1) * 512], in0=uk, in1=sg[:], op=ALU.mult),
                                      [ukB, sgB], [hidB[tbh]])
                      wdv = w_d_d.rearrange("(f p) n -> p f n", p=128)
                      for nq in range(4):
                          (wd,), wdB = wload([(wdv[:, :, nq * 256:(nq + 1) * 256], NF, 256)])
                          for tl in range(8):
                              t = half * 8 + tl
                              bk, bb = gen_banks.next()
                              for f in range(NF):
                                  mm(bk[:, 0:256], hid[:, f, tl * 128:(tl + 1) * 128], wd[:, f, :], f == 0, f == NF - 1,
                                     [hidB[tl // 4], wdB], [bb])
                              dve(lambda e, bk=bk, t=t, nq=nq: e.tensor_tensor(
                                  out=x1[:, t, nq * 256:(nq + 1) * 256], in0=bk[:, 0:256], in1=x1[:, t, nq * 256:(nq + 1) * 256],
                                  op=ALU.add), [bb, x1B[t]], [x1B[t]])
              if stage == 5:
                  dump("x2", x1[:], [128, NT, D], F32, x1B)
                  raise _Stop()
              with ExitStack() as Gp:
                  nG1a, nG1b, nG2 = norm_T(Gp, "nG", lambda t: (x1[:, t, :], x1B[t]), g_ple_d, h2T, h2B, external=True)
                  fG = S.fence()
                  pT = sbt(Gp, "pT", [128, 2, SQ], BF16)
                  pTB = [Buf("pT%d" % i, fG) for i in range(4)]
                  pin = Rot([(sbt(Gp, "pin%d" % i, [128, 256], F32), Buf("pin%d" % i, fG), nsem("pin%d" % i)) for i in range(2)])
                  pbf = Rot([(sbt(Gp, "pbf%d" % i, [128, 256], BF16), Buf("pbf%d" % i, fG)) for i in range(2)])
                  gfb = sbt(Gp, "gfb", [128, D], F32)
                  Bbb = Buf("bbc", fG)
                  S.add("sp", lambda e: e.dma_start(out=gfb[:], in_=g_fin_d.partition_broadcast(128)),
                        writes=[Bbb], dsem=nsem("bbc"))
                  bpl = sbt(Gp, "bpl", [128, D], BF16)
                  e0 = sbt(Gp, "e0", [128, 128], BF16)
                  Bbp = Buf("bpl", fG)
                  dve(lambda e: e.memset(bpl[:], 0.0), [], [Bbp])
                  dve(lambda e: e.memset(e0[:], 0.0), [], [Bbp])
                  dve(lambda e: e.memset(e0[0:1, :], 1.0), [], [Bbp])
                  S.add("pool", lambda e: e.dma_start(out=bpl[0:1, :], in_=b_ple_d.rearrange("(o n) -> o n", o=1)),
                        writes=[Bbp], dsem=nsem("bpl"))
                  pk_bank = allbanks[0]

                  def p_tile(t):
                      pt_, pb_, ds = pin.next()
                      S.add("sp", lambda e: e.dma_start(out=pt_[:], in_=p_d[s, t * 128:(t + 1) * 128, :]),
                            writes=[pb_], dsem=ds)
                      pf, pfB = pbf.next()
                      dve(lambda e: e.tensor_copy(out=pf[:], in_=pt_[:]), [pb_], [pfB])
                      pbk = pk_bank[0].bitcast(BF16)
                      for c in range(2):
                          tr(pbk[:, c * 128:(c + 1) * 128], pf[:, c * 128:(c + 1) * 128], [pfB], [pk_bank[1]])
                      copy_alt(pT[:, :, t * 128:(t + 1) * 128], pbk[:, 0:256].rearrange("p (c t) -> p c t", c=2),
                               [pk_bank[1]], [pTB[t // 4]])

                  norm_drive(nG1a, nG1b, nG2, extra=p_tile)
                  wpg = []
                  for nh in range(2):
                      (wv_,), wb_ = wload([(wview(w_pg_d)[:, :, nh * 512:(nh + 1) * 512], 8, 512)])
                      wpg.append((wv_, wb_))
                  (wpl,), wplB = wload([(wview(w_pl_d), 2, 1024)])
                  gtm = Rot([tuple((sbt(Gp, "gt%d_%d" % (k, i), [128, 512], F32), Buf("gt%d_%d" % (k, i), fG))
                                   for k in range(3)) for i in range(2)])
                  junk = sbt(Gp, "junkG", [128, D], BF16)
                  Bj = Buf("junkG", fG)
                  ysl = Rot([(sbt(Gp, "ys%d" % i, [128, D], F32), Buf("ys%d" % i, fG), nsem("ys%d" % i)) for i in range(2)])
                  def final_norm(t):
                      st, sbf = stat_rot.next()
                      act(junk[:], x1[:, t, :], AF.Square, [x1B[t]], [Bj, sbf], accum_out=st[:, 0:1])
                      rstd_pool(st, sbf)
                      yt, yb, ysem = ysl.next()
                      dve(lambda e: e.scalar_tensor_tensor(out=yt[:], in0=x1[:, t, :], scalar=st[:, 2:3], in1=gfb[:],
                                                           op0=ALU.mult, op1=ALU.mult), [x1B[t], sbf, Bbb], [yb])
                      S.add("sp", lambda e: e.dma_start(out=y_d[s, t * 128:(t + 1) * 128, :], in_=yt[:]),
                            reads=[yb], dsem=ysem, store=True)

                  for t in range(NT):
                      ts_ = slice(t * 128, (t + 1) * 128)
                      for nh in range(2):
                          ns = slice(nh * 512, (nh + 1) * 512)
                          (g1, g1B), (g2, g2B), (g3, g3B) = gtm.next()
                          bk, bb = gen_banks.next()
                          for c in range(8):
                              mm(bk, h2T[:, c, ts_], wpg[nh][0][:, c, :], c == 0, False, [h2B[t // 4], wpg[nh][1]], [bb])
                          mm(bk, e0[:], bpl[:, ns], False, True, [Bbp], [bb])
                          act(g2[:], bk, AF.Sigmoid, [bb], [g2B])
                          pk, pkB = gen_banks.next()
                          for c in range(2):
                              mm(pk, pT[:, c, ts_], wpl[:, c, ns], c == 0, c == 1, [pTB[t // 4], wplB], [pkB])
                          dve(lambda e, g3=g3, g2=g2, pk=pk: e.tensor_tensor(out=g3[:], in0=pk, in1=g2[:], op=ALU.mult),
                              [pkB, g2B], [g3B])
                          dve(lambda e, g3=g3, t=t, ns=ns: e.tensor_tensor(out=x1[:, t, ns], in0=x1[:, t, ns], in1=g3[:],
                                                                          op=ALU.add), [g3B, x1B[t]], [x1B[t]])
                      if t > 0:
                          final_norm(t - 1)
                  final_norm(NT - 1)
              EG.__exit__(None, None, None)
              DG.__exit__(None, None, None)
          except _Stop:
            pass

        for s_ in range(nseq):
            one_seq(s_)

        fin = S.add("sp", lambda e: e.nop(), reads=[])
        fin.deps = list(S.stores)
        with nc.Block() as block:
            S.emit_all(block, esem)
    return nc


def _rope_tables():
    pos = np.arange(SQ, dtype=np.float32)
    tabs = np.zeros((4, 128, SQ), np.float32)
    tabs[0] = 1.0
    tabs[2] = 1.0
    inv = (np.float32(10000.0) ** (-(np.arange(0, 32, 2, dtype=np.float32) / np.float32(32)))).astype(np.float32)
    ang = (pos[:, None] * inv[None, :]).astype(np.float32)
    c, sn = np.cos(ang).astype(np.float32).T, np.sin(ang).astype(np.float32).T
    tabs[0, 64:80], tabs[0, 80:96] = c, c
    tabs[1, 64:80], tabs[1, 80:96] = sn, sn
    inv = (np.float32(500000.0) ** (-(np.arange(0, 16, 2, dtype=np.float32) / np.float32(16)))).astype(np.float32)
    ang = (pos[:, None] * inv[None, :]).astype(np.float32)
    c, sn = np.cos(ang).astype(np.float32).T, np.sin(ang).astype(np.float32).T
    for r0 in (0, 64):
        tabs[2, r0:r0 + 8], tabs[2, r0 + 8:r0 + 16] = c, c
        tabs[3, r0:r0 + 8], tabs[3, r0 + 8:r0 + 16] = sn, sn
    return tabs


def _perm_mats():
    pd = np.zeros((128, 128), np.float32)
    for r0 in (0, 64):
        for r in range(8):
            pd[r0 + r + 8, r0 + r] = -1.0
            pd[r0 + r, r0 + r + 8] = 1.0
    pm = np.zeros((128, 128), np.float32)
    for r in range(16):
        pm[64 + r + 16, 64 + r] = -1.0
        pm[64 + r, 64 + r + 16] = 1.0
    return pd.astype(ml_dtypes.bfloat16), pm.astype(ml_dtypes.bfloat16)


_CACHE = {}


def _get_nc():
    if "nc" not in _CACHE:
        _CACHE["nc"] = build_program()
    return _CACHE["nc"]


def kernel(x, p, attn_norm, w_in, b_gate, lam_q1, lam_k1, lam_q2, lam_k2, diff_subln, w_o_diff, q_norm, w_uq,
           kv_norm, w_ukv, w_o_mla, w_out, ffn_norm, w_ffn_gate, w_ffn_up, w_ffn_down, ple_norm, w_ple_gate,
           b_ple_gate, w_ple, final_norm):
    f = lambda a: np.ascontiguousarray(np.asarray(a, dtype=np.float32))
    x = f(x)
    p = f(p)[0]
    B = x.shape[0]
    nseq = B // NCORES
    w_ukv_ = f(w_ukv)[0].reshape(256, 8, 2, 64)
    vecs = np.zeros((128, 24), np.float32)
    bg = f(b_gate)[0]
    vecs[:, 0:8] = bg[0].reshape(8, 128).T
    vecs[:, 8:16] = bg[1].reshape(8, 128).T
    vecs[:, 16:19] = f(q_norm)[0].reshape(3, 128).T
    vecs[:, 19:21] = f(kv_norm)[0].reshape(2, 128).T
    vecs[:, 21] = f(diff_subln)[0]
    pd, pm = _perm_mats()
    shared = {
        "w_in": f(w_in)[0], "w_o_diff": f(w_o_diff)[0], "w_uq": f(w_uq)[0],
        "w_ukv_kn": np.ascontiguousarray(w_ukv_[:, :, 0, :].reshape(256, 512)),
        "w_ukv_v": np.ascontiguousarray(w_ukv_[:, :, 1, :].reshape(256, 512)),
        "w_o_mla": f(w_o_mla)[0], "w_out": f(w_out)[0], "w_ffn_gate": f(w_ffn_gate)[0], "w_ffn_up": f(w_ffn_up)[0],
        "w_ffn_down": f(w_ffn_down)[0], "w_ple_gate": f(w_ple_gate)[0], "w_ple": f(w_ple)[0],
        "attn_norm": f(attn_norm)[0], "ffn_norm": f(ffn_norm)[0], "ple_norm": f(ple_norm)[0],
        "final_norm": f(final_norm), "b_ple_gate": f(b_ple_gate)[0],
        "lam_q1": f(lam_q1)[0], "lam_k1": f(lam_k1)[0], "lam_q2": f(lam_q2)[0], "lam_k2": f(lam_k2)[0],
        "vecs": vecs, "ident": np.eye(128, dtype=np.float32).astype(ml_dtypes.bfloat16),
        "perm_diff": pd, "perm_mla": pm, "rope_tabs": _rope_tables(),
    }
    nc = _get_nc()
    in_maps = []
    for c in range(NCORES):
        m = dict(shared)
        m["x"] = x[c * nseq:(c + 1) * nseq]
        m["p"] = p[c * nseq:(c + 1) * nseq]
        in_maps.append(m)
    res = run_bass_kernel_spmd(nc, in_maps, core_ids=list(range(NCORES)))
    return np.concatenate([r["y"] for r in res.results], axis=0).astype(np.float32)
```

```python
import math
import os
from contextlib import ExitStack

import numpy as np
import ml_dtypes

import concourse.bass as bass
import concourse.mybir as mybir
from concourse.bass_utils import run_bass_kernel_spmd

F32 = mybir.dt.float32
BF16 = mybir.dt.bfloat16
AF = mybir.ActivationFunctionType
ALU = mybir.AluOpType
AX = mybir.AxisListType

NCORES = 8
SEQ_PER_CORE = 2
SQ = 2048
D = 1024
NT = 16
NB = 4
FF = 2816
NF = 22
EPS = 1e-6
Q_OFF, K_OFF, V_OFF, CQ_OFF, CKV_OFF, KR_OFF, GA_OFF, GB_OFF = 0, 1024, 2048, 3072, 3456, 3712, 3744, 4768
IN_COLS = 5792
WS = 6144
ARENA_BYTES = 207872
LAM_INIT = 0.8 - 0.6 * math.exp(0.0)

ENGS = ("pe", "act", "dve", "pool", "sp")


class Buf:
    __slots__ = ("name", "w", "r", "init", "psum")

    def __init__(self, name, init=(), psum=False):
        self.name = name
        self.w = None
        self.r = []
        self.init = list(init)
        self.psum = psum


class DmaSem:
    __slots__ = ("h", "count")

    def __init__(self, h):
        self.h = h
        self.count = 0


class Op:
    __slots__ = ("eng", "emit", "deps", "dsem", "dval", "ndma", "sig", "sigval", "pos")

    def __init__(self, eng, emit):
        self.eng = eng
        self.emit = emit
        self.deps = []
        self.dsem = None
        self.dval = 0
        self.ndma = 0
        self.sig = False
        self.sigval = 0


class Sched:
    def __init__(self):
        self.streams = {e: [] for e in ENGS}
        self.last_compute = {e: None for e in ENGS}
        self.stores = []

    def fence(self):
        return [o for o in self.last_compute.values() if o is not None] + list(self.stores)

    def add(self, eng, emit, reads=(), writes=(), dsem=None, ndma=1, store=False):
        op = Op(eng, emit)
        deps = {}

        def dep(o, raw):
            if o is None:
                return
            if o.dsem is None and o.eng == eng:
                if eng == "pe":
                    return
            deps[id(o)] = o

        for b in reads:
            dep(b.w, True)
            for o in b.init:
                dep(o, True)
            if b.psum:
                for o in b.r:
                    if o.eng != eng:
                        dep(o, True)
        for b in writes:
            dep(b.w, False)
            for o in b.r:
                dep(o, False)
            for o in b.init:
                dep(o, True)
            b.init = []
        latest = {}
        dl = []
        for o in deps.values():
            if o.dsem is not None:
                dl.append(o)
            elif o.eng not in latest or latest[o.eng].pos < o.pos:
                latest[o.eng] = o
        op.deps = dl + list(latest.values())
        for b in reads:
            b.r.append(op)
        for b in writes:
            b.w = op
            b.r = []
        if dsem is not None:
            op.dsem = dsem
            op.ndma = ndma
            dsem.count += 16 * ndma
            op.dval = dsem.count
            if store:
                self.stores.append(op)
        else:
            self.last_compute[eng] = op
        op.pos = len(self.streams[eng])
        self.streams[eng].append(op)
        return op

    def emit_all(self, block, esem):
        for e in ENGS:
            for op in self.streams[e]:
                for d in op.deps:
                    if d.dsem is None:
                        d.sig = True
        for e in ENGS:
            c = 0
            for op in self.streams[e]:
                if op.sig:
                    c += 1
                    op.sigval = c
        streams = self.streams

        def run(eng_name, eng):
            waited = {}
            for op in streams[eng_name]:
                for d in op.deps:
                    if d.dsem is not None:
                        key, h, v = id(d.dsem), d.dsem.h, d.dval
                    else:
                        key, h, v = d.eng, esem[d.eng], d.sigval
                    if waited.get(key, 0) >= v:
                        continue
                    waited[key] = v
                    eng.wait_ge(h, v)
                res = op.emit(eng)
                if op.dsem is not None:
                    if not isinstance(res, (list, tuple)):
                        res = [res]
                    assert len(res) == op.ndma
                    for r in res:
                        r.then_inc(op.dsem.h, 16)
                elif op.sig:
                    if isinstance(res, (list, tuple)):
                        res = res[-1]
                    res.then_inc(esem[eng_name], 1)

        @block.tensor
        def _(eng):
            run("pe", eng)

        @block.scalar
        def _(eng):
            run("act", eng)

        @block.vector
        def _(eng):
            run("dve", eng)

        @block.gpsimd
        def _(eng):
            run("pool", eng)

        @block.sync
        def _(eng):
            run("sp", eng)


class Rot:
    def __init__(self, items):
        self.items = items
        self.i = 0

    def next(self):
        it = self.items[self.i % len(self.items)]
        self.i += 1
        return it


class _Stop(Exception):
    pass


def build_program(nseq=SEQ_PER_CORE, stage=99):
    nc = bass.Bass("TRN2", target_bir_lowering=False)
    dbg_outs = {}

    def din(name, shape, dt=F32):
        return nc.dram_tensor(name, list(shape), dt, kind="ExternalInput").ap()

    x_d = din("x", [nseq, SQ, D])
    p_d = din("p", [nseq, SQ, 256])
    y_d = nc.dram_tensor("y", [nseq, SQ, D], F32, kind="ExternalOutput").ap()
    w_in_d = din("w_in", [D, IN_COLS])
    w_od_d = din("w_o_diff", [1024, D])
    w_uq_d = din("w_uq", [384, 768])
    w_kn_d = din("w_ukv_kn", [256, 512])
    w_v_d = din("w_ukv_v", [256, 512])
    w_om_d = din("w_o_mla", [512, D])
    w_out_d = din("w_out", [D, D])
    w_g_d = din("w_ffn_gate", [D, FF])
    w_u_d = din("w_ffn_up", [D, FF])
    w_d_d = din("w_ffn_down", [FF, D])
    w_pg_d = din("w_ple_gate", [D, D])
    w_pl_d = din("w_ple", [256, D])
    g_attn_d = din("attn_norm", [D])
    g_ffn_d = din("ffn_norm", [D])
    g_ple_d = din("ple_norm", [D])
    g_fin_d = din("final_norm", [D])
    b_ple_d = din("b_ple_gate", [D])
    lam_d = [din(n, [64]) for n in ("lam_q1", "lam_k1", "lam_q2", "lam_k2")]
    vecs_d = din("vecs", [128, 24])
    ident_d = din("ident", [128, 128], BF16)
    pd_d = din("perm_diff", [128, 128], BF16)
    pm_d = din("perm_mla", [128, 128], BF16)
    tab_d = din("rope_tabs", [4, 128, SQ])

    def wview(w):
        return w.rearrange("(c p) n -> p c n", p=128)

    w_in_v = wview(w_in_d)

    S = Sched()
    G = ExitStack()
    with G:
        arena = G.enter_context(nc.sbuf_tensor("arena", [128, ARENA_BYTES // 2], BF16))
        abase = nc.lookup_mloc(arena).addr
        free_list = [[abase, abase + ARENA_BYTES]]
        _uid = [0]

        def a_alloc(nbytes):
            nbytes = (nbytes + 63) // 64 * 64
            for iv in free_list:
                if iv[1] - iv[0] >= nbytes:
                    off = iv[0]
                    iv[0] += nbytes
                    if iv[0] == iv[1]:
                        free_list.remove(iv)
                    return off, nbytes
            raise RuntimeError("SBUF arena full: need %d, free %s" % (nbytes, free_list))

        def a_free(off, nbytes):
            free_list.append([off, off + nbytes])
            free_list.sort()
            i = 0
            while i + 1 < len(free_list):
                if free_list[i][1] == free_list[i + 1][0]:
                    free_list[i][1] = free_list[i + 1][1]
                    del free_list[i + 1]
                else:
                    i += 1

        def sbt(es, name, shape, dt):
            n = 1
            for d in shape[1:]:
                n *= d
            nbytes = n * (4 if dt == F32 else 2)
            off, nb = a_alloc(nbytes)
            _uid[0] += 1
            t = nc.alloc_sbuf_tensor_at("%s_%d" % (name, _uid[0]), list(shape), dt, offset=off)
            es.callback(a_free, off, nb)
            return t

        def sem(name):
            return G.enter_context(nc.semaphore(name))

        esem = {e: sem("s_" + e) for e in ("pe", "act", "dve", "pool")}
        _ds = [0]

        def dsem():
            _ds[0] += 1
            return DmaSem(sem("d%d" % _ds[0]))

        accs = []
        for i in range(2):
            t = G.enter_context(nc.psum_tensor("acc%d" % i, [128, 1024], F32))
            accs.append((t, Buf("acc%d" % i, psum=True)))
        banks = []
        for i in range(4):
            t = G.enter_context(nc.psum_tensor("bank%d" % i, [128, 512], F32))
            banks.append((t, Buf("bank%d" % i, psum=True)))
        allbanks = []
        for (t, b) in accs:
            allbanks.append((t[:, 0:512], b))
        for (t, b) in banks:
            allbanks.append((t[:], b))
        gen_banks = Rot([allbanks[0], allbanks[1], (banks[0][0][:], banks[0][1]), (banks[1][0][:], banks[1][1]),
                         (banks[2][0][:], banks[2][1])])
        trp_t, trp_b = banks[3]
        trp_bf = trp_t[:].bitcast(BF16)

        ident = sbt(G, "ident", [128, 128], BF16)
        ones = sbt(G, "ones", [128, 128], BF16)
        pdm = sbt(G, "pdm", [128, 128], BF16)
        pmm = sbt(G, "pmm", [128, 128], BF16)
        vecs = sbt(G, "vecs_s", [128, 24], F32)
        lamt = sbt(G, "lamt", [128, 4, 64], F32)
        lsm = sbt(G, "lsm", [128, 8], F32)
        Bconst = Buf("const")
        Blam = Buf("lam")
        dconst = dsem()
        S.add("sp", lambda e: [e.dma_start(out=ident[:], in_=ident_d), e.dma_start(out=pdm[:], in_=pd_d),
                               e.dma_start(out=pmm[:], in_=pm_d), e.dma_start(out=vecs[:], in_=vecs_d)]
              + [e.dma_start(out=lamt[:, i, :], in_=lam_d[i].partition_broadcast(128)) for i in range(4)],
              writes=[Bconst, Blam], dsem=dconst, ndma=8)
        S.add("dve", lambda e: e.memset(ones[:], 1.0), writes=[Bconst])
        negmask = sbt(G, "negmask", [128, 1], F32)
        S.add("dve", lambda e: e.memset(negmask[0:64, :], 0.0), writes=[Bconst])
        S.add("dve", lambda e: e.memset(negmask[64:128, :], -30000.0), writes=[Bconst])
        mhalf = sbt(G, "mhalf", [128, 1], F32)
        S.add("dve", lambda e: e.memset(mhalf[:], -0.5), writes=[Bconst])

        def rstd_pool(st, sbf):
            S.add("dve", lambda e: e.tensor_scalar(out=st[:, 1:2], in0=st[:, 0:1], scalar1=1.0 / D, scalar2=EPS,
                                                   op0=ALU.mult, op1=ALU.add), reads=[sbf], writes=[sbf])
            S.add("pool", lambda e: e.tensor_tensor(out=st[:, 2:3], in0=st[:, 1:2], in1=mhalf[:], op=ALU.pow),
                  reads=[sbf, Bconst], writes=[sbf])
        lprod = sbt(G, "lprod", [128, 2, 64], F32)
        S.add("dve", lambda e: e.tensor_tensor(out=lprod[:, 0, :], in0=lamt[:, 0, :], in1=lamt[:, 1, :], op=ALU.mult),
              reads=[Blam], writes=[Blam])
        S.add("dve", lambda e: e.tensor_tensor(out=lprod[:, 1, :], in0=lamt[:, 2, :], in1=lamt[:, 3, :], op=ALU.mult),
              reads=[Blam], writes=[Blam])
        S.add("dve", lambda e: e.tensor_reduce(out=lsm[:, 0:2], in_=lprod[:], axis=AX.X, op=ALU.add),
              reads=[Blam], writes=[Blam])
        S.add("act", lambda e: e.activation(out=lsm[:, 2:4], in_=lsm[:, 0:2], func=AF.Exp), reads=[Blam], writes=[Blam])
        S.add("dve", lambda e: e.tensor_tensor(out=lsm[:, 4:5], in0=lsm[:, 3:4], in1=lsm[:, 2:3], op=ALU.subtract),
              reads=[Blam], writes=[Blam])
        S.add("dve", lambda e: e.tensor_scalar(out=lsm[:, 5:6], in0=lsm[:, 4:5], scalar1=-LAM_INIT, scalar2=None,
                                               op0=ALU.add), reads=[Blam], writes=[Blam])
        neglam = lsm[:, 5:6]

        wslots = []
        for i in range(3):
            t = sbt(G, "wslot%d" % i, [128, WS], BF16)
            wslots.append((t, Buf("wslot%d" % i), dsem()))
        wrot = Rot(wslots)

        def wload(parts):
            t, b, ds = wrot.next()
            views = []
            off = 0
            pairs = []
            for part in parts:
                src_, C, N = part[0], part[1], part[2]
                v = t[:, off:off + C * N].rearrange("p (c n) -> p c n", c=C)
                views.append(v)
                if len(part) > 3:
                    n = src_.shape[2]
                    flat = t[:, off:off + C * N]
                    S.add("dve", lambda e, flat=flat: e.memset(flat, 0.0), writes=[b])
                    pairs.append((v[:, :, part[3]:part[3] + n], src_))
                else:
                    pairs.append((v, src_))
                off += C * N
            assert off <= WS
            S.add("pool", lambda e: [e.dma_start(out=v, in_=s) for (v, s) in pairs], writes=[b], dsem=ds,
                  ndma=len(pairs))
            return views, b

        stat = sbt(G, "stat", [128, 64], F32)
        stat_rot = Rot([(stat[:, i * 4:(i + 1) * 4], Buf("stat%d" % i)) for i in range(16)])

        _nsems = {}

        def nsem(key):
            if key not in _nsems:
                _nsems[key] = dsem()
            return _nsems[key]

        def dump(name, ap, shape, dt, bufs):
            d = nc.dram_tensor("dbg_" + name, list(shape), dt, kind="ExternalOutput").ap()
            dbg_outs[name] = d
            S.add("sp", lambda e: e.dma_start(out=d, in_=ap), reads=bufs, dsem=nsem("dbg_" + name), store=True)

        def mm(out, lhsT, rhs, start, stop, reads, writes, **kw):
            return S.add("pe", lambda e: e.matmul(out, lhsT=lhsT, rhs=rhs, start=start, stop=stop, **kw),
                         reads=reads, writes=writes)

        def tr(out, in_, reads, writes):
            return S.add("pe", lambda e: e.transpose(out=out, in_=in_, identity=ident[:]), reads=list(reads) + [Bconst],
                         writes=writes)

        def act(out, in_, func, reads, writes, **kw):
            return S.add("act", lambda e: e.activation(out=out, in_=in_, func=func, **kw), reads=reads, writes=writes)

        def dve(fn, reads, writes):
            return S.add("dve", fn, reads=reads, writes=writes)

        _alt = [0]

        def copy_alt(out, in_, reads, writes):
            _alt[0] += 1
            if _alt[0] % 2:
                return act(out, in_, AF.Copy, reads, writes)
            return dve(lambda e: e.tensor_copy(out=out, in_=in_), reads, writes)

        def norm_T(es, tag, get_x, g_d, dstT, dstB, external=False):
            gbc = sbt(es, tag + "_gbc", [128, D], F32)
            Bg = Buf(tag + "_gbc", S.fence())
            dg = nsem("gbc_" + tag)
            S.add("sp", lambda e: e.dma_start(out=gbc[:], in_=g_d.partition_broadcast(128)), writes=[Bg], dsem=dg)
            junk = sbt(es, tag + "_junk", [128, D], BF16)
            Bj = Buf(tag + "_junk", S.fence())
            hns = Rot([(sbt(es, tag + "_hn%d" % i, [128, D], BF16), Buf(tag + "_hn%d" % i, S.fence())) for i in range(3)])
            trps = Rot([(trp_bf, trp_b), (banks[2][0][:].bitcast(BF16), banks[2][1])])
            def stage1a(t):
                xa, xb = get_x(t)
                st, sbf = stat_rot.next()
                act(junk[:], xa, AF.Square, [xb], [Bj, sbf], accum_out=st[:, 0:1])
                rstd_pool(st, sbf)
                return xa, xb, st, sbf

            def stage1b(xa, xb, st, sbf):
                hn, hb = hns.next()
                dve(lambda e: e.scalar_tensor_tensor(out=hn[:], in0=xa, scalar=st[:, 2:3], in1=gbc[:],
                                                     op0=ALU.mult, op1=ALU.mult), [xb, sbf, Bg], [hb])
                return hn, hb

            def stage2(t, hn, hb):
                tb_, tbB = trps.next()
                for c in range(8):
                    tr(tb_[:, c * 128:(c + 1) * 128], hn[:, c * 128:(c + 1) * 128], [hb], [tbB])
                copy_alt(dstT[:, :, t * 128:(t + 1) * 128], tb_.rearrange("p (c t) -> p c t", c=8), [tbB],
                         [dstB[t // 4]])

            if external:
                return stage1a, stage1b, stage2
            norm_drive(stage1a, stage1b, stage2)

        def norm_drive(s1a, s1b, s2, extra=None):
            q = [s1b(*s1a(0)), s1b(*s1a(1))]
            for t in range(NT):
                sa = s1a(t + 2) if t + 2 < NT else None
                if extra is not None:
                    extra(t)
                s2(t, *q.pop(0))
                if sa is not None:
                    q.append(s1b(*sa))

        def rope(Aps, Ab, r0, r1, perm, Ct, St, Btab, tb, dst, dstB, tmp):
            (qbf, qbfB), (t1, t1B), (t2, t2B), (Bps, BpB) = tmp
            cs = slice(tb * 512, (tb + 1) * 512)
            p0, p1 = (0, 128) if r0 > 0 else (r0, r1)
            lvl = int(os.environ.get("KROPE", "9"))
            act(qbf[p0:p1, :], Aps[p0:p1, :], AF.Copy, [Ab], [qbfB])
            if lvl >= 2:
                mm(Bps[p0:p1, :], perm[p0:p1, p0:p1], qbf[p0:p1, :], True, True, [qbfB, Bconst], [BpB])
            if lvl >= 3:
                dve(lambda e: e.tensor_tensor(out=t1[r0:r1, :], in0=Aps[r0:r1, :], in1=Ct[r0:r1, cs], op=ALU.mult),
                    [Ab] + ([] if os.environ.get("KNOTAB") else [Btab]), [t1B])
            if lvl >= 4:
                dve(lambda e: e.tensor_tensor(out=t2[r0:r1, :], in0=Bps[r0:r1, :], in1=St[r0:r1, cs], op=ALU.mult),
                    [BpB, Btab], [t2B])
            if lvl >= 5:
                dve(lambda e: e.tensor_tensor(out=dst[r0:r1, :], in0=t1[r0:r1, :], in1=t2[r0:r1, :], op=ALU.add),
                    [t1B, t2B], [dstB])

        sc_banks = Rot([banks[0], banks[1]])
        NDUMMY = int(os.environ.get("KDUMMY", "0"))
        att_banks = Rot([banks[0], banks[1], banks[2]] if NDUMMY == 0 else [banks[0], banks[1]])
        LOOK = 2 if NDUMMY == 0 else 1
        INTERLEAVE = os.environ.get("KNOINT", "") == ""

        def attn_steps(KT, KB, QT, QB, r0, r1, Vfn, VB, dv, scale, qt, acc, accB, pts, after):
            steps = []
            nk = 4 * qt + 4
            first = {0: True, 1: True}
            accv = acc[:].rearrange("p (i n) -> p i n", i=4)
            for kt in range(nk):
                j = kt - 4 * qt
                q0 = 128 * j if j > 0 else 0
                sct, scb = att_banks.next()
                pt, ptb = pts.next()

                def s_fn(kt=kt, q0=q0, sct=sct, scb=scb):
                    mm(sct[:, q0:512], KT[r0:r1, kt * 128:(kt + 1) * 128], QT[r0:r1, qt * 512 + q0:(qt + 1) * 512],
                       True, True, [KB[kt // 4], QB[qt]], [scb])

                avs = []
                for i in range(max(j, 0), 4):
                    bk = i // 2
                    avs.append((i, first[bk]))
                    first[bk] = False

                def rest_fn(kt=kt, j=j, q0=q0, sct=sct, scb=scb, pt=pt, ptb=ptb, avs=avs, last=(kt == nk - 1)):
                    for dmy in range(NDUMMY):
                        mm(banks[2][0][:, :], ones[:], QT[:, 0:512], True, True, [Bconst, QB], [banks[2][1]])
                    if j >= 0:
                        act(pt[:, q0:q0 + 64], sct[:, q0:q0 + 64], AF.Exp, [scb, Bconst], [ptb], scale=scale,
                            bias=negmask[:, 0:1])
                        act(pt[:, q0 + 64:512], sct[:, q0 + 64:512], AF.Exp, [scb], [ptb], scale=scale)
                    else:
                        act(pt[:, q0:512], sct[:, q0:512], AF.Exp, [scb], [ptb], scale=scale)
                    if os.environ.get("KAV", "") == "dense":
                        vv = Vfn(kt)
                        for bki in range(2):
                            mm(acc[0:dv, bki * 512 + q0:(bki + 1) * 512], vv[:, 0:dv] if bki == 0 else ones[:, 0:dv],
                               pt[:, q0:512], kt == 0, True, [ptb, VB, Bconst], [accB], skip_group_check=True)
                    else:
                      for (i, st) in avs:
                        mm(accv[:, i, 0:dv + 1], pt[:, i * 128:(i + 1) * 128], Vfn(kt), st, True, [ptb, VB], [accB],
                           skip_group_check=True)
                    if last:
                        return after()
                    return None

                steps.append((s_fn, rest_fn))
            return steps

        def run_steps(steps, side=()):
            side = list(side)
            busy = [False]

            def run_side():
                fn, b = side.pop(0)
                fn()
                busy[0] = b

            if not steps:
                while side:
                    run_side()
                return
            per = -(-len(side) // len(steps)) if side else 0
            pending = []
            for i in range(min(LOOK, len(steps))):
                steps[i][0]()
            for i, (s_fn, rest_fn) in enumerate(steps):
                if i + LOOK < len(steps):
                    steps[i + LOOK][0]()
                while pending and pending[0][0] <= i and not busy[0]:
                    pending.pop(0)[1]()
                d = rest_fn()
                if d is not None:
                    pending.append((i + d[0], d[1]))
                for _ in range(per):
                    if side:
                        run_side()
            while side:
                run_side()
            for (_, fn) in pending:
                fn()

        def one_seq(s):
          try:
              ABC = ExitStack()
              with ABC:
                  hT = sbt(ABC, "hT", [128, 8, SQ], BF16)
                  hTB = [Buf("hT%d" % i, S.fence()) for i in range(4)]
                  odT = sbt(ABC, "odT", [128, 8, SQ], BF16)
                  omT = sbt(ABC, "omT", [128, 4, SQ], BF16)
                  omTB = [Buf("omT%d" % i, S.fence()) for i in range(4)]
                  with ExitStack() as A:
                      xsl = Rot([(sbt(A, "xsA%d" % i, [128, D], F32), Buf("xsA%d" % i, S.fence()), nsem("xsA%d" % i))
                                 for i in range(6)])

                      def get_x(t, s=s, xsl=xsl):
                          xt, xb, ds = xsl.next()
                          S.add("sp", lambda e: e.dma_start(out=xt[:], in_=x_d[s, t * 128:(t + 1) * 128, :]), writes=[xb],
                                dsem=ds)
                          return xt[:], xb

                      norm_T(A, "nA", get_x, g_attn_d, hT, hTB)
                  if stage == 1:
                      dump("hT", hT[:], [128, 8, SQ], BF16, hTB)
                      raise _Stop()

                  with ExitStack() as Bx:
                      Ct = sbt(Bx, "Ct", [128, SQ], F32)
                      St = sbt(Bx, "St", [128, SQ], F32)
                      Btab = Buf("tab", S.fence())
                      dtab = nsem("tab")
                      S.add("sp", lambda e: [e.dma_start(out=Ct[:], in_=tab_d[0]), e.dma_start(out=St[:], in_=tab_d[1])],
                            writes=[Btab], dsem=dtab, ndma=2)
                      Vbuf = sbt(Bx, "Vbuf", [128, 8704], BF16)
                      QK = [(sbt(Bx, "QT%d" % i, [128, SQ], BF16), [Buf("QT%d_%d" % (i, k), S.fence()) for k in range(4)],
                             sbt(Bx, "KT%d" % i, [128, SQ], BF16), [Buf("KT%d_%d" % (i, k), S.fence()) for k in range(4)])
                            for i in range(2)]
                      K2s = [(sbt(Bx, "K2_%d" % i, [128, SQ], BF16), [Buf("K2_%d_%d" % (i, k), S.fence()) for k in range(4)])
                             for i in range(2)]
                      pts = Rot([(sbt(Bx, "pt%d" % i, [128, 512], BF16), Buf("pt%d" % i, S.fence())) for i in range(4)])
                      f0 = S.fence()
                      rtmp = ((sbt(Bx, "qbf", [128, 512], BF16), Buf("qbf", f0)),
                              (sbt(Bx, "rt1", [128, 512], F32), Buf("rt1", f0)),
                              (sbt(Bx, "rt2", [128, 512], F32), Buf("rt2", f0)),
                              (banks[2][0], banks[2][1]))
                      o1n = sbt(Bx, "o1n", [128, 4, 128], F32)
                      o1nB = Buf("o1n", f0)
                      otmp = sbt(Bx, "otmp", [128, 4, 128], F32)
                      otmpB = Buf("otmp", f0)
                      odf = sbt(Bx, "odf", [128, 4, 128], F32)
                      odfB = Buf("odf", f0)
                      osq = sbt(Bx, "osq", [128, 4, 128], F32)
                      osqB = Buf("osq", f0)
                      odn = sbt(Bx, "odn", [128, 4, 128], BF16)
                      odnB = Buf("odn", f0)
                      omn = sbt(Bx, "omn", [128, 4, 64], BF16)
                      omnB = Buf("omn", f0)

                      latT = odT
                      latB = [Buf("lat%d" % i, f0) for i in range(4)]
                      latf = [Vbuf[:, j * 1024:(j + 1) * 1024].bitcast(F32) for j in range(5)]
                      sqs = [Vbuf[:, 5120 + j * 512:5120 + (j + 1) * 512] for j in range(5)]
                      ltBs = [Buf("lattmp%d" % j, f0) for j in range(5)]
                      rsq = sbt(Bx, "rsq", [128, 512], F32)
                      rsqB = Buf("rsq", f0)
                      rstd = sbt(Bx, "rstdl", [128, 512], F32)
                      rstdB = Buf("rstdl", f0)
                      (wl, wkr), wlB = wload([(w_in_v[:, :, CQ_OFF:CQ_OFF + 640], 8, 640),
                                                  (w_in_v[:, :, KR_OFF:KR_OFF + 32], 8, 128, 64) if os.environ.get("KDBG", "") != "krplain"
                                                  else (w_in_v[:, :, KR_OFF - 96:KR_OFF + 32], 8, 128)])
                      for tb in range(NB):
                          cs = slice(tb * 512, (tb + 1) * 512)
                          for j in range(6):
                              bk, bb = sc_banks.next()
                              if j < 5:
                                  c0 = j * 128
                                  for c in range(8):
                                      mm(bk[:, :], wl[:, c, c0:c0 + 128], hT[:, c, cs], c == 0, c == 7, [wlB, hTB[tb]], [bb])
                                  act(latf[j], bk[:, :], AF.Copy, [bb], [ltBs[j]])
                                  act(sqs[j], bk[:, :], AF.Square, [bb], [ltBs[j]])
                              elif os.environ.get("KDBG", "") != "skipkr":
                                  for c in range(8):
                                      mm(bk[:, :], wkr[:, c, :], hT[:, c, cs], c == 0, c == 7, [wlB, hTB[tb]], [bb])
                                  rope(bk, bb, 0, 128, pmm, Ct, St, Btab, tb, latT[:, 5, cs], latB[tb], rtmp)
                          for (js, nrm, vcol) in (((0, 1, 2), 384.0, 8 + 8), ((3, 4), 256.0, 8 + 8 + 3)):
                              rk, rb = banks[2]
                              for n, j in enumerate(js):
                                  mm(rk[:, :], ones[:], sqs[j], n == 0, n == len(js) - 1, [ltBs[j], Bconst], [rb])
                              act(rsq[:], rk[:, :], AF.Sqrt, [rb], [rsqB], scale=1.0 / nrm, bias=EPS)
                              dve(lambda e: e.reciprocal(out=rstd[:], in_=rsq[:]), [rsqB], [rstdB])
                              for n, j in enumerate(js):
                                  dve(lambda e, j=j, n=n, vcol=vcol, cs=cs: e.scalar_tensor_tensor(
                                      out=latT[:, j, cs], in0=latf[j], scalar=vecs[:, vcol + n:vcol + n + 1], in1=rstd[:],
                                      op0=ALU.mult, op1=ALU.mult), [ltBs[j], rstdB, Bconst], [latB[tb]])
                      if stage == 2.1:
                          dump("latT", odT[:, 0:5, :], [128, 5, SQ], BF16, latB)
                          if os.environ.get("KDBG", "") not in ("skipkr", "nodumpkr"):
                              dump("kr", odT[64:96, 5, :], [32, SQ], BF16, latB)
                          raise _Stop()
                      (wuq, wkn, wv), wmB = wload([(wview(w_uq_d), 3, 768), (wview(w_kn_d), 2, 512), (wview(w_v_d), 2, 512)])
                      VB = Buf("V", S.fence() + [b.w for b in ltBs] + [o for b in ltBs for o in b.r])
                      Vm = Vbuf[:, 0:16 * 8 * 66].rearrange("p (t h d) -> p t h d", t=16, h=8)
                      dve(lambda e: e.memset(Vm[:, :, :, 64:65], 1.0), [], [VB])
                      for t in range(NT):
                          bk, bb = sc_banks.next()
                          for c in range(2):
                              mm(bk[:, :], latT[:, 3 + c, t * 128:(t + 1) * 128], wv[:, c, :], c == 0, c == 1,
                                 [latB[t // 4], wmB], [bb])
                          copy_alt(Vm[:, t, :, 0:64], bk[:, :].rearrange("p (h d) -> p h d", h=8), [bb], [VB])
                      trp_f = trp_t[:, :]

                      def diff_proj_tasks(h):
                          (wq, wk), wqB = wload([(w_in_v[:, :, Q_OFF + h * 128:Q_OFF + (h + 1) * 128], 8, 128),
                                                 (w_in_v[:, :, K_OFF + h * 128:K_OFF + (h + 1) * 128], 8, 128)])
                          QTt, QB_, KTt, KB_ = QK[h % 2]
                          K2t, K2B = K2s[h % 2]
                          (qbf, qbfB), (t1, t1B), (t2, t2B), _ = rtmp
                          tasks = []
                          if h < 2:
                              dve(lambda e: e.memset(KTt[64:128, :], 0.0), [], KB_)
                              dve(lambda e: e.memset(K2t[0:64, :], 0.0), [], K2B)
                          for tb in range(NB):
                              cs = slice(tb * 512, (tb + 1) * 512)
                              for (wx, dT, dB) in ((wk, None, None), (wq, QTt, QB_[tb])):
                                  def pm(c, wx=wx, cs=cs, tb=tb):
                                      mm(trp_f, wx[:, c, :], hT[:, c, cs], c == 0, c == 7, [wqB, hTB[tb]], [trp_b])

                                  def p1b(cs=cs):
                                      dve(lambda e: e.tensor_copy(out=qbf[:], in_=trp_f), [trp_b], [qbfB])
                                      dve(lambda e: e.tensor_tensor(out=t1[:], in0=trp_f, in1=Ct[:, cs], op=ALU.mult),
                                          [trp_b, Btab], [t1B])

                                  def p2(dT=dT, dB=dB, cs=cs, tb=tb):
                                      b2, b2B = trp_t, trp_b
                                      mm(b2[:, :], pdm[:, :], qbf[:], True, True, [qbfB, Bconst], [b2B])
                                      dve(lambda e: e.tensor_tensor(out=t2[:], in0=b2[:, :], in1=St[:, cs], op=ALU.mult),
                                          [b2B, Btab], [t2B])
                                      if dT is None:
                                          dve(lambda e: e.tensor_tensor(out=KTt[0:64, cs], in0=t1[0:64, :], in1=t2[0:64, :],
                                                                        op=ALU.add), [t1B, t2B], [KB_[tb]])
                                          dve(lambda e: e.tensor_tensor(out=K2t[64:128, cs], in0=t1[64:128, :],
                                                                        in1=t2[64:128, :], op=ALU.add), [t1B, t2B], [K2B[tb]])
                                      else:
                                          dve(lambda e: e.tensor_tensor(out=dT[:, cs], in0=t1[:], in1=t2[:], op=ALU.add),
                                              [t1B, t2B], [dB])

                                  tasks += [((lambda c=c, pm=pm: pm(c)), True) for c in range(8)]
                                  tasks += [(p1b, False), (p2, False)]
                          return tasks

                      sc_m = 96.0 ** -0.5
                      for h in range(8):
                          QTt, QB_, KTt, KB_ = QK[h % 2]
                          for tb in range(NB):
                              cs = slice(tb * 512, (tb + 1) * 512)
                              bk, bb = sc_banks.next()
                              for c in range(2):
                                  mm(bk[0:64, :], wkn[:, c, h * 64:(h + 1) * 64], latT[:, 3 + c, cs], c == 0, c == 1,
                                     [wmB, latB[tb]], [bb])
                              copy_alt(KTt[0:64, cs], bk[0:64, :], [bb], [KB_[tb]])
                              dve(lambda e, KTt=KTt, cs=cs: e.tensor_copy(out=KTt[64:96, cs], in_=latT[64:96, 5, cs]),
                                  [latB[tb]], [KB_[tb]])
                              bk, bb = sc_banks.next()
                              for c in range(3):
                                  mm(bk[0:96, :], wuq[:, c, h * 96:(h + 1) * 96], latT[:, c, cs], c == 0, c == 2,
                                     [wmB, latB[tb]], [bb])
                              rope(bk, bb, 0, 96, pmm, Ct, St, Btab, tb, QTt[:, cs], QB_[tb], rtmp)
                          steps = []
                          for qt in range(NB):
                              acc, accB = accs[(h * 4 + qt) % 2]

                              def after(h=h, qt=qt, acc=acc, accB=accB):
                                  accv = acc[:].rearrange("p (i n) -> p i n", i=4)
                                  st, sbf = stat_rot.next()
                                  dve(lambda e: e.reciprocal(out=st[:, 0:4], in_=accv[:, :, 64]), [accB], [sbf])
                                  dve(lambda e: e.tensor_tensor(out=omn[:], in0=accv[:, :, 0:64],
                                                                in1=st[:, 0:4].unsqueeze(2).broadcast_to([128, 4, 64]),
                                                                op=ALU.mult), [accB, sbf], [omnB])
                                  ro = 64 * (h % 2)

                                  def later():
                                      for i in range(4):
                                          tr(trp_bf[ro:ro + 64, i * 128:(i + 1) * 128], omn[:, i, :], [omnB], [trp_b])
                                      dve(lambda e: e.tensor_copy(out=omT[ro:ro + 64, h // 2, qt * 512:(qt + 1) * 512],
                                                                  in_=trp_bf[ro:ro + 64, 0:512]), [trp_b], [omTB[qt]])

                                  return (3, later)

                              steps += attn_steps(KTt, KB_, QTt, QB_, 0, 96, lambda kt, h=h: Vm[:, kt, h, 0:65], VB, 64, sc_m,
                                                  qt, acc, accB, pts, after)
                          side = []
                          if h == 7 and INTERLEAVE:
                              S.add("sp", lambda e: [e.dma_start(out=Ct[:], in_=tab_d[2]),
                                                     e.dma_start(out=St[:], in_=tab_d[3])],
                                    writes=[Btab], dsem=dtab, ndma=2)
                              side = diff_proj_tasks(0)
                          run_steps(steps, side)

                      if stage == 2.2:
                          dump("omT", omT[:], [128, 4, SQ], BF16, omTB)
                          raise _Stop()
                      if not INTERLEAVE:
                          S.add("sp", lambda e: [e.dma_start(out=Ct[:], in_=tab_d[2]), e.dma_start(out=St[:], in_=tab_d[3])],
                                writes=[Btab], dsem=dtab, ndma=2)
                      f1 = S.fence()
                      odTB = [Buf("odT%d" % i, f1) for i in range(4)]
                      Vd = Vbuf[:, 0:16 * 4 * 130].rearrange("p (t h d) -> p t h d", t=16, h=4)
                      sc_d = 64.0 ** -0.5
                      for g in range(2):
                          (wvd,), wvB = wload([(w_in_v[:, :, V_OFF + g * 512:V_OFF + (g + 1) * 512], 8, 512)])
                          dve(lambda e: e.memset(Vd[:, :, :, 128:129], 1.0), [], [VB])
                          for t in range(NT):
                              bk, bb = sc_banks.next()
                              for c in range(8):
                                  mm(bk[:, :], hT[:, c, t * 128:(t + 1) * 128], wvd[:, c, :], c == 0, c == 7,
                                     [hTB[t // 4], wvB], [bb])
                              copy_alt(Vd[:, t, :, 0:128], bk[:, :].rearrange("p (h d) -> p h d", h=4), [bb], [VB])
                          for hh in range(4):
                              h = g * 4 + hh
                              QTt, QB_, KTt, KB_ = QK[h % 2]
                              if not INTERLEAVE:
                                  for fn, _b in diff_proj_tasks(h):
                                      fn()
                              steps = []
                              for qt in range(NB):
                                  a1, a1B = accs[0]
                                  a2, a2B = accs[1]

                                  def after1(a1=a1, a1B=a1B):
                                      accv = a1[:].rearrange("p (i n) -> p i n", i=4)
                                      st, sbf = stat_rot.next()
                                      dve(lambda e: e.reciprocal(out=st[:, 0:4], in_=accv[:, :, 128]), [a1B], [sbf])
                                      dve(lambda e: e.tensor_tensor(out=o1n[:], in0=accv[:, :, 0:128],
                                                                    in1=st[:, 0:4].unsqueeze(2).broadcast_to([128, 4, 128]),
                                                                    op=ALU.mult), [a1B, sbf], [o1nB])

                                  def after2(h=h, qt=qt, a2=a2, a2B=a2B):
                                      accv = a2[:].rearrange("p (i n) -> p i n", i=4)
                                      st, sbf = stat_rot.next()
                                      st2, sbf2 = stat_rot.next()
                                      dve(lambda e: e.reciprocal(out=st[:, 0:4], in_=accv[:, :, 128]), [a2B], [sbf])
                                      dve(lambda e: e.tensor_scalar(out=st2[:, 0:4], in0=st[:, 0:4], scalar1=neglam,
                                                                    scalar2=None, op0=ALU.mult), [sbf, Blam], [sbf2])
                                      dve(lambda e: e.tensor_tensor(out=otmp[:], in0=accv[:, :, 0:128],
                                                                    in1=st2[:, 0:4].unsqueeze(2).broadcast_to([128, 4, 128]),
                                                                    op=ALU.mult), [a2B, sbf2], [otmpB])
                                      dve(lambda e: e.tensor_tensor(out=odf[:], in0=o1n[:], in1=otmp[:], op=ALU.add),
                                          [o1nB, otmpB], [odfB])
                                      dve(lambda e: e.tensor_tensor(out=osq[:], in0=odf[:], in1=odf[:], op=ALU.mult),
                                          [odfB], [osqB])
                                      st3, sbf3 = stat_rot.next()
                                      st4, sbf4 = stat_rot.next()
                                      st5, sbf5 = stat_rot.next()
                                      dve(lambda e: e.tensor_reduce(out=st3[:, 0:4], in_=osq[:], axis=AX.X, op=ALU.add),
                                          [osqB], [sbf3])
                                      dve(lambda e: e.tensor_scalar(out=st4[:, 0:4], in0=st3[:, 0:4], scalar1=1.0 / 128,
                                                                    scalar2=EPS, op0=ALU.mult, op1=ALU.add), [sbf3], [sbf4])
                                      act(st5[:, 0:4], st4[:, 0:4], AF.Ln, [sbf4], [sbf5])
                                      act(st3[:, 0:4], st5[:, 0:4], AF.Exp, [sbf5], [sbf3], scale=-0.5)
                                      dve(lambda e: e.scalar_tensor_tensor(
                                          out=odn[:], in0=odf[:], scalar=1.0 - LAM_INIT,
                                          in1=st3[:, 0:4].unsqueeze(2).broadcast_to([128, 4, 128]),
                                          op0=ALU.mult, op1=ALU.mult), [odfB, sbf3], [odnB])
                                      def later():
                                          for i in range(4):
                                              tr(trp_bf[:, i * 128:(i + 1) * 128], odn[:, i, :], [odnB], [trp_b])
                                          dve(lambda e: e.tensor_scalar(out=odT[:, h, qt * 512:(qt + 1) * 512],
                                                                        in0=trp_bf[:, 0:512], scalar1=vecs[:, 21:22],
                                                                        scalar2=None, op0=ALU.mult),
                                              [trp_b, Bconst], [odTB[qt]])

                                      return (8, later)

                                  K2t, K2B = K2s[h % 2]
                                  steps += attn_steps(KTt, KB_, QTt, QB_, 0, 128, lambda kt, hh=hh: Vd[:, kt, hh, 0:129], VB,
                                                      128, sc_d, qt, a1, a1B, pts, after1)
                                  steps += attn_steps(K2t, K2B, QTt, QB_, 0, 128, lambda kt, hh=hh: Vd[:, kt, hh, 0:129], VB,
                                                      128, sc_d, qt, a2, a2B, pts, after2)
                              run_steps(steps, diff_proj_tasks(h + 1) if (INTERLEAVE and h + 1 < 8) else [])

                  if stage == 2.3:
                      dump("odT", odT[:], [128, 8, SQ], BF16, odTB)
                      raise _Stop()
                  CD = ExitStack()
                  CD.__enter__()
                  mgT = sbt(CD, "mgT", [128, 8, SQ], BF16)
                  mgB = [Buf("mgT%d" % i, S.fence()) for i in range(4)]
                  with ExitStack() as C:
                      fC = S.fence()
                      tmps = Rot([tuple((sbt(C, "mc%d_%d" % (k, i), [128, 512], F32), Buf("mc%d_%d" % (k, i), fC))
                                        for k in range(4)) for i in range(2)])
                      for j in range(8):
                          (wga, wgb, wod, wom), wcB = wload([
                              (w_in_v[:, :, GA_OFF + j * 128:GA_OFF + (j + 1) * 128], 8, 128),
                              (w_in_v[:, :, GB_OFF + j * 128:GB_OFF + (j + 1) * 128], 8, 128),
                              (wview(w_od_d)[:, :, j * 128:(j + 1) * 128], 8, 128),
                              (wview(w_om_d)[:, :, j * 128:(j + 1) * 128], 4, 128)])
                          for tb in range(NB):
                              cs = slice(tb * 512, (tb + 1) * 512)
                              (sa, saB), (sb_, sbB), (m1, m1B), (m2, m2B) = tmps.next()
                              ga, gaB = gen_banks.next()
                              for c in range(8):
                                  mm(ga, wga[:, c, :], hT[:, c, cs], c == 0, c == 7, [wcB, hTB[tb]], [gaB])
                              act(sa[:], ga, AF.Sigmoid, [gaB, Bconst], [saB], bias=vecs[:, j:j + 1])
                              gb, gbB = gen_banks.next()
                              for c in range(8):
                                  mm(gb, wgb[:, c, :], hT[:, c, cs], c == 0, c == 7, [wcB, hTB[tb]], [gbB])
                              act(sb_[:], gb, AF.Sigmoid, [gbB, Bconst], [sbB], bias=vecs[:, 8 + j:8 + j + 1])
                              oa, oaB = gen_banks.next()
                              for c in range(8):
                                  mm(oa, wod[:, c, :], odT[:, c, cs], c == 0, c == 7, [wcB, odTB[tb]], [oaB])
                              dve(lambda e, m1=m1, sa=sa, oa=oa: e.tensor_tensor(out=m1[:], in0=oa, in1=sa[:], op=ALU.mult),
                                  [oaB, saB], [m1B])
                              ob, obB = gen_banks.next()
                              for c in range(4):
                                  mm(ob, wom[:, c, :], omT[:, c, cs], c == 0, c == 3, [wcB, omTB[tb]], [obB])
                              dve(lambda e, m2=m2, sb_=sb_, ob=ob: e.tensor_tensor(out=m2[:], in0=ob, in1=sb_[:], op=ALU.mult),
                                  [obB, sbB], [m2B])
                              dve(lambda e, m1=m1, m2=m2, j=j, cs=cs: e.tensor_tensor(out=mgT[:, j, cs], in0=m1[:], in1=m2[:],
                                                                                    op=ALU.add), [m1B, m2B], [mgB[tb]])
              if stage == 3:
                  dump("mgT", mgT[:], [128, 8, SQ], BF16, mgB)
                  raise _Stop()
              DG = ExitStack()
              DG.__enter__()
              x1 = sbt(DG, "x1", [128, NT, D], F32)
              fD = S.fence()
              x1B = [Buf("x1_%d" % i, fD) for i in range(NT)]
              EG = ExitStack()
              EG.__enter__()
              h2T = sbt(EG, "h2T", [128, 8, SQ], BF16)
              h2B = [Buf("h2T%d" % i, S.fence()) for i in range(4)]
              with ExitStack() as Dp:
                  xsl = Rot([(sbt(Dp, "xsD%d" % i, [128, D], F32), Buf("xsD%d" % i, fD), nsem("xsD%d" % i)) for i in range(2)])
                  nE1a, nE1b, nE2 = norm_T(Dp, "nE", lambda t: (x1[:, t, :], x1B[t]), g_ffn_d, h2T, h2B, external=True)
                  wo = []
                  for nh in range(2):
                      (wv_,), wb_ = wload([(wview(w_out_d)[:, :, nh * 512:(nh + 1) * 512], 8, 512)])
                      wo.append((wv_, wb_))
                  qE = []
                  for t in range(NT):
                      xt, xb, ds = xsl.next()
                      S.add("sp", lambda e, xt=xt, t=t, s=s: e.dma_start(out=xt[:], in_=x_d[s, t * 128:(t + 1) * 128, :]),
                            writes=[xb], dsem=ds)
                      for nh in range(2):
                          bk, bb = gen_banks.next()
                          for c in range(8):
                              mm(bk, mgT[:, c, t * 128:(t + 1) * 128], wo[nh][0][:, c, :], c == 0, c == 7,
                                 [mgB[t // 4], wo[nh][1]], [bb])
                          dve(lambda e, bk=bk, xt=xt, t=t, nh=nh: e.tensor_tensor(
                              out=x1[:, t, nh * 512:(nh + 1) * 512], in0=bk, in1=xt[:, nh * 512:(nh + 1) * 512], op=ALU.add),
                              [bb, xb], [x1B[t]])
                      sa = nE1a(t)
                      if len(qE) >= 2:
                          nE2(t - 2, *qE.pop(0))
                      qE.append(nE1b(*sa))
                  nE2(NT - 2, *qE.pop(0))
                  nE2(NT - 1, *qE.pop(0))
              CD.__exit__(None, None, None)
              if stage == 4:
                  dump("x1", x1[:], [128, NT, D], F32, x1B)
                  raise _Stop()
              with ExitStack() as Fp:
                  fF = S.fence()
                  hid = sbt(Fp, "hid", [128, NF, 1024], BF16)
                  hidB = [Buf("hid%d" % i, fF) for i in range(2)]
                  sgs = Rot([(sbt(Fp, "sg%d" % i, [128, 512], F32), Buf("sg%d" % i, fF)) for i in range(3)])
                  for half in range(2):
                      for fb in range(11):
                          (wg, wu), wfB = wload([(wview(w_g_d)[:, :, fb * 256:(fb + 1) * 256], 8, 256),
                                                 (wview(w_u_d)[:, :, fb * 256:(fb + 1) * 256], 8, 256)])
                          for fc in range(2):
                              f = fb * 2 + fc
                              for tbh in range(2):
                                  tb = half * 2 + tbh
                                  cs = slice(tb * 512, (tb + 1) * 512)
                                  gk, gkB = gen_banks.next()
                                  for c in range(8):
                                      mm(gk, wg[:, c, fc * 128:(fc + 1) * 128], h2T[:, c, cs], c == 0, c == 7, [wfB, h2B[tb]],
                                         [gkB])
                                  sg, sgB = sgs.next()
                                  act(sg[:], gk, AF.Silu, [gkB], [sgB])
                                  uk, ukB = gen_banks.next()
                                  for c in range(8):
                                      mm(uk, wu[:, c, fc * 128:(fc + 1) * 128], h2T[:, c, cs], c == 0, c == 7, [wfB, h2B[tb]],
                                         [ukB])
                                  dve(lambda e, sg=sg, uk=uk, f=f, tbh=tbh: e.tensor_tensor(
                                      out=hid[:, f, tbh * 512:(tbh + 1) * 512], in0=uk, in1=sg[:], op=ALU.mult),
                                      [ukB, sgB], [hidB[tbh]])
                      wdv = w_d_d.rearrange("(f p) n -> p f n", p=128)
                      for nq in range(4):
                          (wd,), wdB = wload([(wdv[:, :, nq * 256:(nq + 1) * 256], NF, 256)])
                          for tl in range(8):
                              t = half * 8 + tl
                              bk, bb = gen_banks.next()
                              for f in range(NF):
                                  mm(bk[:, 0:256], hid[:, f, tl * 128:(tl + 1) * 128], wd[:, f, :], f == 0, f == NF - 1,
                                     [hidB[tl // 4], wdB], [bb])
                              dve(lambda e, bk=bk, t=t, nq=nq: e.tensor_tensor(
                                  out=x1[:, t, nq * 256:(nq + 1) * 256], in0=bk[:, 0:256], in1=x1[:, t, nq * 256:(nq + 1) * 256],
                                  op=ALU.add), [bb, x1B[t]], [x1B[t]])
              if stage == 5:
                  dump("x2", x1[:], [128, NT, D], F32, x1B)
                  raise _Stop()
              with ExitStack() as Gp:
                  nG1a, nG1b, nG2 = norm_T(Gp, "nG", lambda t: (x1[:, t, :], x1B[t]), g_ple_d, h2T, h2B, external=True)
                  fG = S.fence()
                  pT = sbt(Gp, "pT", [128, 2, SQ], BF16)
                  pTB = [Buf("pT%d" % i, fG) for i in range(4)]
                  pin = Rot([(sbt(Gp, "pin%d" % i, [128, 256], F32), Buf("pin%d" % i, fG), nsem("pin%d" % i)) for i in range(2)])
                  pbf = Rot([(sbt(Gp, "pbf%d" % i, [128, 256], BF16), Buf("pbf%d" % i, fG)) for i in range(2)])
                  gfb = sbt(Gp, "gfb", [128, D], F32)
                  Bbb = Buf("bbc", fG)
                  S.add("sp", lambda e: e.dma_start(out=gfb[:], in_=g_fin_d.partition_broadcast(128)),
                        writes=[Bbb], dsem=nsem("bbc"))
                  bpl = sbt(Gp, "bpl", [128, D], BF16)
                  e0 = sbt(Gp, "e0", [128, 128], BF16)
                  Bbp = Buf("bpl", fG)
                  dve(lambda e: e.memset(bpl[:], 0.0), [], [Bbp])
                  dve(lambda e: e.memset(e0[:], 0.0), [], [Bbp])
                  dve(lambda e: e.memset(e0[0:1, :], 1.0), [], [Bbp])
                  S.add("pool", lambda e: e.dma_start(out=bpl[0:1, :], in_=b_ple_d.rearrange("(o n) -> o n", o=1)),
                        writes=[Bbp], dsem=nsem("bpl"))
                  pk_bank = allbanks[0]

                  def p_tile(t):
                      pt_, pb_, ds = pin.next()
                      S.add("sp", lambda e: e.dma_start(out=pt_[:], in_=p_d[s, t * 128:(t + 1) * 128, :]),
                            writes=[pb_], dsem=ds)
                      pf, pfB = pbf.next()
                      dve(lambda e: e.tensor_copy(out=pf[:], in_=pt_[:]), [pb_], [pfB])
                      pbk = pk_bank[0].bitcast(BF16)
                      for c in range(2):
                          tr(pbk[:, c * 128:(c + 1) * 128], pf[:, c * 128:(c + 1) * 128], [pfB], [pk_bank[1]])
                      copy_alt(pT[:, :, t * 128:(t + 1) * 128], pbk[:, 0:256].rearrange("p (c t) -> p c t", c=2),
                               [pk_bank[1]], [pTB[t // 4]])

                  norm_drive(nG1a, nG1b, nG2, extra=p_tile)
                  wpg = []
                  for nh in range(2):
                      (wv_,), wb_ = wload([(wview(w_pg_d)[:, :, nh * 512:(nh + 1) * 512], 8, 512)])
                      wpg.append((wv_, wb_))
                  (wpl,), wplB = wload([(wview(w_pl_d), 2, 1024)])
                  gtm = Rot([tuple((sbt(Gp, "gt%d_%d" % (k, i), [128, 512], F32), Buf("gt%d_%d" % (k, i), fG))
                                   for k in range(3)) for i in range(2)])
                  junk = sbt(Gp, "junkG", [128, D], BF16)
                  Bj = Buf("junkG", fG)
                  ysl = Rot([(sbt(Gp, "ys%d" % i, [128, D], F32), Buf("ys%d" % i, fG), nsem("ys%d" % i)) for i in range(2)])
                  def final_norm(t):
                      st, sbf = stat_rot.next()
                      act(junk[:], x1[:, t, :], AF.Square, [x1B[t]], [Bj, sbf], accum_out=st[:, 0:1])
                      rstd_pool(st, sbf)
                      yt, yb, ysem = ysl.next()
                      dve(lambda e: e.scalar_tensor_tensor(out=yt[:], in0=x1[:, t, :], scalar=st[:, 2:3], in1=gfb[:],
                                                           op0=ALU.mult, op1=ALU.mult), [x1B[t], sbf, Bbb], [yb])
                      S.add("sp", lambda e: e.dma_start(out=y_d[s, t * 128:(t + 1) * 128, :], in_=yt[:]),
                            reads=[yb], dsem=ysem, store=True)

                  for t in range(NT):
                      ts_ = slice(t * 128, (t + 1) * 128)
                      for nh in range(2):
                          ns = slice(nh * 512, (nh + 1) * 512)
                          (g1, g1B), (g2, g2B), (g3, g3B) = gtm.next()
                          bk, bb = gen_banks.next()
                          for c in range(8):
                              mm(bk, h2T[:, c, ts_], wpg[nh][0][:, c, :], c == 0, False, [h2B[t // 4], wpg[nh][1]], [bb])
                          mm(bk, e0[:], bpl[:, ns], False, True, [Bbp], [bb])
                          act(g2[:], bk, AF.Sigmoid, [bb], [g2B])
                          pk, pkB = gen_banks.next()
                          for c in range(2):
                              mm(pk, pT[:, c, ts_], wpl[:, c, ns], c == 0, c == 1, [pTB[t // 4], wplB], [pkB])
                          dve(lambda e, g3=g3, g2=g2, pk=pk: e.tensor_tensor(out=g3[:], in0=pk, in1=g2[:], op=ALU.mult),
                              [pkB, g2B], [g3B])
                          dve(lambda e, g3=g3, t=t, ns=ns: e.tensor_tensor(out=x1[:, t, ns], in0=x1[:, t, ns], in1=g3[:],
                                                                          op=ALU.add), [g3B, x1B[t]], [x1B[t]])
                      if t > 0:
                          final_norm(t - 1)
                  final_norm(NT - 1)
              EG.__exit__(None, None, None)
              DG.__exit__(None, None, None)
          except _Stop:
            pass

        for s_ in range(nseq):
            one_seq(s_)

        fin = S.add("sp", lambda e: e.nop(), reads=[])
        fin.deps = list(S.stores)
        with nc.Block() as block:
            S.emit_all(block, esem)
    return nc


def _rope_tables():
    pos = np.arange(SQ, dtype=np.float32)
    tabs = np.zeros((4, 128, SQ), np.float32)
    tabs[0] = 1.0
    tabs[2] = 1.0
    inv = (np.float32(10000.0) ** (-(np.arange(0, 32, 2, dtype=np.float32) / np.float32(32)))).astype(np.float32)
    ang = (pos[:, None] * inv[None, :]).astype(np.float32)
    c, sn = np.cos(ang).astype(np.float32).T, np.sin(ang).astype(np.float32).T
    tabs[0, 64:80], tabs[0, 80:96] = c, c
    tabs[1, 64:80], tabs[1, 80:96] = sn, sn
    inv = (np.float32(500000.0) ** (-(np.arange(0, 16, 2, dtype=np.float32) / np.float32(16)))).astype(np.float32)
    ang = (pos[:, None] * inv[None, :]).astype(np.float32)
    c, sn = np.cos(ang).astype(np.float32).T, np.sin(ang).astype(np.float32).T
    for r0 in (0, 64):
        tabs[2, r0:r0 + 8], tabs[2, r0 + 8:r0 + 16] = c, c
        tabs[3, r0:r0 + 8], tabs[3, r0 + 8:r0 + 16] = sn, sn
    return tabs


def _perm_mats():
    pd = np.zeros((128, 128), np.float32)
    for r0 in (0, 64):
        for r in range(8):
            pd[r0 + r + 8, r0 + r] = -1.0
            pd[r0 + r, r0 + r + 8] = 1.0
    pm = np.zeros((128, 128), np.float32)
    for r in range(16):
        pm[64 + r + 16, 64 + r] = -1.0
        pm[64 + r, 64 + r + 16] = 1.0
    return pd.astype(ml_dtypes.bfloat16), pm.astype(ml_dtypes.bfloat16)


_CACHE = {}


def _get_nc():
    if "nc" not in _CACHE:
        _CACHE["nc"] = build_program()
    return _CACHE["nc"]


def kernel(x, p, attn_norm, w_in, b_gate, lam_q1, lam_k1, lam_q2, lam_k2, diff_subln, w_o_diff, q_norm, w_uq,
           kv_norm, w_ukv, w_o_mla, w_out, ffn_norm, w_ffn_gate, w_ffn_up, w_ffn_down, ple_norm, w_ple_gate,
           b_ple_gate, w_ple, final_norm):
    f = lambda a: np.ascontiguousarray(np.asarray(a, dtype=np.float32))
    x = f(x)
    p = f(p)[0]
    B = x.shape[0]
    nseq = B // NCORES
    w_ukv_ = f(w_ukv)[0].reshape(256, 8, 2, 64)
    vecs = np.zeros((128, 24), np.float32)
    bg = f(b_gate)[0]
    vecs[:, 0:8] = bg[0].reshape(8, 128).T
    vecs[:, 8:16] = bg[1].reshape(8, 128).T
    vecs[:, 16:19] = f(q_norm)[0].reshape(3, 128).T
    vecs[:, 19:21] = f(kv_norm)[0].reshape(2, 128).T
    vecs[:, 21] = f(diff_subln)[0]
    pd, pm = _perm_mats()
    shared = {
        "w_in": f(w_in)[0], "w_o_diff": f(w_o_diff)[0], "w_uq": f(w_uq)[0],
        "w_ukv_kn": np.ascontiguousarray(w_ukv_[:, :, 0, :].reshape(256, 512)),
        "w_ukv_v": np.ascontiguousarray(w_ukv_[:, :, 1, :].reshape(256, 512)),
        "w_o_mla": f(w_o_mla)[0], "w_out": f(w_out)[0], "w_ffn_gate": f(w_ffn_gate)[0], "w_ffn_up": f(w_ffn_up)[0],
        "w_ffn_down": f(w_ffn_down)[0], "w_ple_gate": f(w_ple_gate)[0], "w_ple": f(w_ple)[0],
        "attn_norm": f(attn_norm)[0], "ffn_norm": f(ffn_norm)[0], "ple_norm": f(ple_norm)[0],
        "final_norm": f(final_norm), "b_ple_gate": f(b_ple_gate)[0],
        "lam_q1": f(lam_q1)[0], "lam_k1": f(lam_k1)[0], "lam_q2": f(lam_q2)[0], "lam_k2": f(lam_k2)[0],
        "vecs": vecs, "ident": np.eye(128, dtype=np.float32).astype(ml_dtypes.bfloat16),
        "perm_diff": pd, "perm_mla": pm, "rope_tabs": _rope_tables(),
    }
    nc = _get_nc()
    in_maps = []
    for c in range(NCORES):
        m = dict(shared)
        m["x"] = x[c * nseq:(c + 1) * nseq]
        m["p"] = p[c * nseq:(c + 1) * nseq]
        in_maps.append(m)
    res = run_bass_kernel_spmd(nc, in_maps, core_ids=list(range(NCORES)))
    return np.concatenate([r["y"] for r in res.results], axis=0).astype(np.float32)
```

```python
import math
import os
from contextlib import ExitStack

import numpy as np
import ml_dtypes

import concourse.bass as bass
import concourse.mybir as mybir
from concourse.bass_utils import run_bass_kernel_spmd

F32 = mybir.dt.float32
BF16 = mybir.dt.bfloat16
AF = mybir.ActivationFunctionType
ALU = mybir.AluOpType
AX = mybir.AxisListType

NCORES = 8
SEQ_PER_CORE = 2
SQ = 2048
D = 1024
NT = 16
NB = 4
FF = 2816
NF = 22
EPS = 1e-6
Q_OFF, K_OFF, V_OFF, CQ_OFF, CKV_OFF, KR_OFF, GA_OFF, GB_OFF = 0, 1024, 2048, 3072, 3456, 3712, 3744, 4768
IN_COLS = 5792
WS = 6144
ARENA_BYTES = 207872
LAM_INIT = 0.8 - 0.6 * math.exp(0.0)

ENGS = ("pe", "act", "dve", "pool", "sp")


class Buf:
    __slots__ = ("name", "w", "r", "init", "psum")

    def __init__(self, name, init=(), psum=False):
        self.name = name
        self.w = None
        self.r = []
        self.init = list(init)
        self.psum = psum


class DmaSem:
    __slots__ = ("h", "count")

    def __init__(self, h):
        self.h = h
        self.count = 0


class Op:
    __slots__ = ("eng", "emit", "deps", "dsem", "dval", "ndma", "sig", "sigval", "pos")

    def __init__(self, eng, emit):
        self.eng = eng
        self.emit = emit
        self.deps = []
        self.dsem = None
        self.dval = 0
        self.ndma = 0
        self.sig = False
        self.sigval = 0


class Sched:
    def __init__(self):
        self.streams = {e: [] for e in ENGS}
        self.last_compute = {e: None for e in ENGS}
        self.stores = []

    def fence(self):
        return [o for o in self.last_compute.values() if o is not None] + list(self.stores)

    def add(self, eng, emit, reads=(), writes=(), dsem=None, ndma=1, store=False):
        op = Op(eng, emit)
        deps = {}

        def dep(o, raw):
            if o is None:
                return
            if o.dsem is None and o.eng == eng:
                if eng == "pe":
                    return
            deps[id(o)] = o

        for b in reads:
            dep(b.w, True)
            for o in b.init:
                dep(o, True)
            if b.psum:
                for o in b.r:
                    if o.eng != eng:
                        dep(o, True)
        for b in writes:
            dep(b.w, False)
            for o in b.r:
                dep(o, False)
            for o in b.init:
                dep(o, True)
            b.init = []
        latest = {}
        dl = []
        for o in deps.values():
            if o.dsem is not None:
                dl.append(o)
            elif o.eng not in latest or latest[o.eng].pos < o.pos:
                latest[o.eng] = o
        op.deps = dl + list(latest.values())
        for b in reads:
            b.r.append(op)
        for b in writes:
            b.w = op
            b.r = []
        if dsem is not None:
            op.dsem = dsem
            op.ndma = ndma
            dsem.count += 16 * ndma
            op.dval = dsem.count
            if store:
                self.stores.append(op)
        else:
            self.last_compute[eng] = op
        op.pos = len(self.streams[eng])
        self.streams[eng].append(op)
        return op

    def emit_all(self, block, esem):
        for e in ENGS:
            for op in self.streams[e]:
                for d in op.deps:
                    if d.dsem is None:
                        d.sig = True
        for e in ENGS:
            c = 0
            for op in self.streams[e]:
                if op.sig:
                    c += 1
                    op.sigval = c
        streams = self.streams

        def run(eng_name, eng):
            waited = {}
            for op in streams[eng_name]:
                for d in op.deps:
                    if d.dsem is not None:
                        key, h, v = id(d.dsem), d.dsem.h, d.dval
                    else:
                        key, h, v = d.eng, esem[d.eng], d.sigval
                    if waited.get(key, 0) >= v:
                        continue
                    waited[key] = v
                    eng.wait_ge(h, v)
                res = op.emit(eng)
                if op.dsem is not None:
                    if not isinstance(res, (list, tuple)):
                        res = [res]
                    assert len(res) == op.ndma
                    for r in res:
                        r.then_inc(op.dsem.h, 16)
                elif op.sig:
                    if isinstance(res, (list, tuple)):
                        res = res[-1]
                    res.then_inc(esem[eng_name], 1)

        @block.tensor
        def _(eng):
            run("pe", eng)

        @block.scalar
        def _(eng):
            run("act", eng)

        @block.vector
        def _(eng):
            run("dve", eng)

        @block.gpsimd
        def _(eng):
            run("pool", eng)

        @block.sync
        def _(eng):
            run("sp", eng)


class Rot:
    def __init__(self, items):
        self.items = items
        self.i = 0

    def next(self):
        it = self.items[self.i % len(self.items)]
        self.i += 1
        return it


class _Stop(Exception):
    pass


def build_program(nseq=SEQ_PER_CORE, stage=99):
    nc = bass.Bass("TRN2", target_bir_lowering=False)
    dbg_outs = {}

    def din(name, shape, dt=F32):
        return nc.dram_tensor(name, list(shape), dt, kind="ExternalInput").ap()

    x_d = din("x", [nseq, SQ, D])
    p_d = din("p", [nseq, SQ, 256])
    y_d = nc.dram_tensor("y", [nseq, SQ, D], F32, kind="ExternalOutput").ap()
    w_in_d = din("w_in", [D, IN_COLS])
    w_od_d = din("w_o_diff", [1024, D])
    w_uq_d = din("w_uq", [384, 768])
    w_kn_d = din("w_ukv_kn", [256, 512])
    w_v_d = din("w_ukv_v", [256, 512])
    w_om_d = din("w_o_mla", [512, D])
    w_out_d = din("w_out", [D, D])
    w_g_d = din("w_ffn_gate", [D, FF])
    w_u_d = din("w_ffn_up", [D, FF])
    w_d_d = din("w_ffn_down", [FF, D])
    w_pg_d = din("w_ple_gate", [D, D])
    w_pl_d = din("w_ple", [256, D])
    g_attn_d = din("attn_norm", [D])
    g_ffn_d = din("ffn_norm", [D])
    g_ple_d = din("ple_norm", [D])
    g_fin_d = din("final_norm", [D])
    b_ple_d = din("b_ple_gate", [D])
    lam_d = [din(n, [64]) for n in ("lam_q1", "lam_k1", "lam_q2", "lam_k2")]
    vecs_d = din("vecs", [128, 24])
    ident_d = din("ident", [128, 128], BF16)
    pd_d = din("perm_diff", [128, 128], BF16)
    pm_d = din("perm_mla", [128, 128], BF16)
    tab_d = din("rope_tabs", [4, 128, SQ])

    def wview(w):
        return w.rearrange("(c p) n -> p c n", p=128)

    w_in_v = wview(w_in_d)

    S = Sched()
    G = ExitStack()
    with G:
        arena = G.enter_context(nc.sbuf_tensor("arena", [128, ARENA_BYTES // 2], BF16))
        abase = nc.lookup_mloc(arena).addr
        free_list = [[abase, abase + ARENA_BYTES]]
        _uid = [0]

        def a_alloc(nbytes):
            nbytes = (nbytes + 63) // 64 * 64
            for iv in free_list:
                if iv[1] - iv[0] >= nbytes:
                    off = iv[0]
                    iv[0] += nbytes
                    if iv[0] == iv[1]:
                        free_list.remove(iv)
                    return off, nbytes
            raise RuntimeError("SBUF arena full: need %d, free %s" % (nbytes, free_list))

        def a_free(off, nbytes):
            free_list.append([off, off + nbytes])
            free_list.sort()
            i = 0
            while i + 1 < len(free_list):
                if free_list[i][1] == free_list[i + 1][0]:
                    free_list[i][1] = free_list[i + 1][1]
                    del free_list[i + 1]
                else:
                    i += 1

        def sbt(es, name, shape, dt):
            n = 1
            for d in shape[1:]:
                n *= d
            nbytes = n * (4 if dt == F32 else 2)
            off, nb = a_alloc(nbytes)
            _uid[0] += 1
            t = nc.alloc_sbuf_tensor_at("%s_%d" % (name, _uid[0]), list(shape), dt, offset=off)
            es.callback(a_free, off, nb)
            return t

        def sem(name):
            return G.enter_context(nc.semaphore(name))

        esem = {e: sem("s_" + e) for e in ("pe", "act", "dve", "pool")}
        _ds = [0]

        def dsem():
            _ds[0] += 1
            return DmaSem(sem("d%d" % _ds[0]))

        accs = []
        for i in range(2):
            t = G.enter_context(nc.psum_tensor("acc%d" % i, [128, 1024], F32))
            accs.append((t, Buf("acc%d" % i, psum=True)))
        banks = []
        for i in range(4):
            t = G.enter_context(nc.psum_tensor("bank%d" % i, [128, 512], F32))
            banks.append((t, Buf("bank%d" % i, psum=True)))
        allbanks = []
        for (t, b) in accs:
            allbanks.append((t[:, 0:512], b))
        for (t, b) in banks:
            allbanks.append((t[:], b))
        gen_banks = Rot([allbanks[0], allbanks[1], (banks[0][0][:], banks[0][1]), (banks[1][0][:], banks[1][1]),
                         (banks[2][0][:], banks[2][1])])
        trp_t, trp_b = banks[3]
        trp_bf = trp_t[:].bitcast(BF16)

        ident = sbt(G, "ident", [128, 128], BF16)
        ones = sbt(G, "ones", [128, 128], BF16)
        pdm = sbt(G, "pdm", [128, 128], BF16)
        pmm = sbt(G, "pmm", [128, 128], BF16)
        vecs = sbt(G, "vecs_s", [128, 24], F32)
        lamt = sbt(G, "lamt", [128, 4, 64], F32)
        lsm = sbt(G, "lsm", [128, 8], F32)
        Bconst = Buf("const")
        Blam = Buf("lam")
        dconst = dsem()
        S.add("sp", lambda e: [e.dma_start(out=ident[:], in_=ident_d), e.dma_start(out=pdm[:], in_=pd_d),
                               e.dma_start(out=pmm[:], in_=pm_d), e.dma_start(out=vecs[:], in_=vecs_d)]
              + [e.dma_start(out=lamt[:, i, :], in_=lam_d[i].partition_broadcast(128)) for i in range(4)],
              writes=[Bconst, Blam], dsem=dconst, ndma=8)
        S.add("dve", lambda e: e.memset(ones[:], 1.0), writes=[Bconst])
        mhalf = sbt(G, "mhalf", [128, 1], F32)
        S.add("dve", lambda e: e.memset(mhalf[:], -0.5), writes=[Bconst])

        def rstd_pool(st, sbf):
            S.add("dve", lambda e: e.tensor_scalar(out=st[:, 1:2], in0=st[:, 0:1], scalar1=1.0 / D, scalar2=EPS,
                                                   op0=ALU.mult, op1=ALU.add), reads=[sbf], writes=[sbf])
            S.add("pool", lambda e: e.tensor_tensor(out=st[:, 2:3], in0=st[:, 1:2], in1=mhalf[:], op=ALU.pow),
                  reads=[sbf, Bconst], writes=[sbf])
        lprod = sbt(G, "lprod", [128, 2, 64], F32)
        S.add("dve", lambda e: e.tensor_tensor(out=lprod[:, 0, :], in0=lamt[:, 0, :], in1=lamt[:, 1, :], op=ALU.mult),
              reads=[Blam], writes=[Blam])
        S.add("dve", lambda e: e.tensor_tensor(out=lprod[:, 1, :], in0=lamt[:, 2, :], in1=lamt[:, 3, :], op=ALU.mult),
              reads=[Blam], writes=[Blam])
        S.add("dve", lambda e: e.tensor_reduce(out=lsm[:, 0:2], in_=lprod[:], axis=AX.X, op=ALU.add),
              reads=[Blam], writes=[Blam])
        S.add("act", lambda e: e.activation(out=lsm[:, 2:4], in_=lsm[:, 0:2], func=AF.Exp), reads=[Blam], writes=[Blam])
        S.add("dve", lambda e: e.tensor_tensor(out=lsm[:, 4:5], in0=lsm[:, 3:4], in1=lsm[:, 2:3], op=ALU.subtract),
              reads=[Blam], writes=[Blam])
        S.add("dve", lambda e: e.tensor_scalar(out=lsm[:, 5:6], in0=lsm[:, 4:5], scalar1=-LAM_INIT, scalar2=None,
                                               op0=ALU.add), reads=[Blam], writes=[Blam])
        neglam = lsm[:, 5:6]

        wslots = []
        for i in range(3):
            t = sbt(G, "wslot%d" % i, [128, WS], BF16)
            wslots.append((t, Buf("wslot%d" % i), dsem()))
        wrot = Rot(wslots)

        def wload(parts):
            t, b, ds = wrot.next()
            views = []
            off = 0
            pairs = []
            for part in parts:
                src_, C, N = part[0], part[1], part[2]
                v = t[:, off:off + C * N].rearrange("p (c n) -> p c n", c=C)
                views.append(v)
                if len(part) > 3:
                    n = src_.shape[2]
                    flat = t[:, off:off + C * N]
                    S.add("dve", lambda e, flat=flat: e.memset(flat, 0.0), writes=[b])
                    pairs.append((v[:, :, part[3]:part[3] + n], src_))
                else:
                    pairs.append((v, src_))
                off += C * N
            assert off <= WS
            S.add("pool", lambda e: [e.dma_start(out=v, in_=s) for (v, s) in pairs], writes=[b], dsem=ds,
                  ndma=len(pairs))
            return views, b

        stat = sbt(G, "stat", [128, 64], F32)
        stat_rot = Rot([(stat[:, i * 4:(i + 1) * 4], Buf("stat%d" % i)) for i in range(16)])

        _nsems = {}

        def nsem(key):
            if key not in _nsems:
                _nsems[key] = dsem()
            return _nsems[key]

        def dump(name, ap, shape, dt, bufs):
            d = nc.dram_tensor("dbg_" + name, list(shape), dt, kind="ExternalOutput").ap()
            dbg_outs[name] = d
            S.add("sp", lambda e: e.dma_start(out=d, in_=ap), reads=bufs, dsem=nsem("dbg_" + name), store=True)

        def mm(out, lhsT, rhs, start, stop, reads, writes, **kw):
            return S.add("pe", lambda e: e.matmul(out, lhsT=lhsT, rhs=rhs, start=start, stop=stop, **kw),
                         reads=reads, writes=writes)

        def tr(out, in_, reads, writes):
            return S.add("pe", lambda e: e.transpose(out=out, in_=in_, identity=ident[:]), reads=list(reads) + [Bconst],
                         writes=writes)

        def act(out, in_, func, reads, writes, **kw):
            return S.add("act", lambda e: e.activation(out=out, in_=in_, func=func, **kw), reads=reads, writes=writes)

        def dve(fn, reads, writes):
            return S.add("dve", fn, reads=reads, writes=writes)

        _alt = [0]

        def copy_alt(out, in_, reads, writes):
            _alt[0] += 1
            if _alt[0] % 2:
                return act(out, in_, AF.Copy, reads, writes)
            return dve(lambda e: e.tensor_copy(out=out, in_=in_), reads, writes)

        def norm_T(es, tag, get_x, g_d, dstT, dstB, external=False):
            gbc = sbt(es, tag + "_gbc", [128, D], F32)
            Bg = Buf(tag + "_gbc", S.fence())
            dg = nsem("gbc_" + tag)
            S.add("sp", lambda e: e.dma_start(out=gbc[:], in_=g_d.partition_broadcast(128)), writes=[Bg], dsem=dg)
            junk = sbt(es, tag + "_junk", [128, D], BF16)
            Bj = Buf(tag + "_junk", S.fence())
            hns = Rot([(sbt(es, tag + "_hn%d" % i, [128, D], BF16), Buf(tag + "_hn%d" % i, S.fence())) for i in range(3)])
            trps = Rot([(trp_bf, trp_b), (banks[2][0][:].bitcast(BF16), banks[2][1])])
            def stage1a(t):
                xa, xb = get_x(t)
                st, sbf = stat_rot.next()
                act(junk[:], xa, AF.Square, [xb], [Bj, sbf], accum_out=st[:, 0:1])
                rstd_pool(st, sbf)
                return xa, xb, st, sbf

            def stage1b(xa, xb, st, sbf):
                hn, hb = hns.next()
                dve(lambda e: e.scalar_tensor_tensor(out=hn[:], in0=xa, scalar=st[:, 2:3], in1=gbc[:],
                                                     op0=ALU.mult, op1=ALU.mult), [xb, sbf, Bg], [hb])
                return hn, hb

            def stage2(t, hn, hb):
                tb_, tbB = trps.next()
                for c in range(8):
                    tr(tb_[:, c * 128:(c + 1) * 128], hn[:, c * 128:(c + 1) * 128], [hb], [tbB])
                copy_alt(dstT[:, :, t * 128:(t + 1) * 128], tb_.rearrange("p (c t) -> p c t", c=8), [tbB],
                         [dstB[t // 4]])

            if external:
                return stage1a, stage1b, stage2
            norm_drive(stage1a, stage1b, stage2)

        def norm_drive(s1a, s1b, s2, extra=None):
            q = [s1b(*s1a(0)), s1b(*s1a(1))]
            for t in range(NT):
                sa = s1a(t + 2) if t + 2 < NT else None
                if extra is not None:
                    extra(t)
                s2(t, *q.pop(0))
                if sa is not None:
                    q.append(s1b(*sa))

        def rope(Aps, Ab, r0, r1, perm, Ct, St, Btab, tb, dst, dstB, tmp):
            (qbf, qbfB), (t1, t1B), (t2, t2B), (Bps, BpB) = tmp
            cs = slice(tb * 512, (tb + 1) * 512)
            p0, p1 = (0, 128) if r0 > 0 else (r0, r1)
            lvl = int(os.environ.get("KROPE", "9"))
            act(qbf[p0:p1, :], Aps[p0:p1, :], AF.Copy, [Ab], [qbfB])
            if lvl >= 2:
                mm(Bps[p0:p1, :], perm[p0:p1, p0:p1], qbf[p0:p1, :], True, True, [qbfB, Bconst], [BpB])
            if lvl >= 3:
                dve(lambda e: e.tensor_tensor(out=t1[r0:r1, :], in0=Aps[r0:r1, :], in1=Ct[r0:r1, cs], op=ALU.mult),
                    [Ab] + ([] if os.environ.get("KNOTAB") else [Btab]), [t1B])
            if lvl >= 4:
                dve(lambda e: e.tensor_tensor(out=t2[r0:r1, :], in0=Bps[r0:r1, :], in1=St[r0:r1, cs], op=ALU.mult),
                    [BpB, Btab], [t2B])
            if lvl >= 5:
                dve(lambda e: e.tensor_tensor(out=dst[r0:r1, :], in0=t1[r0:r1, :], in1=t2[r0:r1, :], op=ALU.add),
                    [t1B, t2B], [dstB])

        sc_banks = Rot([banks[0], banks[1]])
        NDUMMY = int(os.environ.get("KDUMMY", "0"))
        att_banks = Rot([banks[0], banks[1], banks[2]] if NDUMMY == 0 else [banks[0], banks[1]])
        LOOK = 2 if NDUMMY == 0 else 1
        INTERLEAVE = os.environ.get("KNOINT", "") == ""

        def attn_steps(KT, KB, QT, QB, r0, r1, Vfn, VB, dv, scale, qt, acc, accB, pts, after):
            steps = []
            nk = 4 * qt + 4
            first = {0: True, 1: True}
            accv = acc[:].rearrange("p (i n) -> p i n", i=4)
            for kt in range(nk):
                j = kt - 4 * qt
                q0 = 128 * j if j > 0 else 0
                sct, scb = att_banks.next()
                pt, ptb = pts.next()

                def s_fn(kt=kt, q0=q0, sct=sct, scb=scb):
                    mm(sct[:, q0:512], KT[r0:r1, kt * 128:(kt + 1) * 128], QT[r0:r1, qt * 512 + q0:(qt + 1) * 512],
                       True, True, [KB[kt // 4], QB[qt]], [scb])

                avs = []
                for i in range(max(j, 0), 4):
                    bk = i // 2
                    avs.append((i, first[bk]))
                    first[bk] = False

                def rest_fn(kt=kt, j=j, q0=q0, sct=sct, scb=scb, pt=pt, ptb=ptb, avs=avs, last=(kt == nk - 1)):
                    for dmy in range(NDUMMY):
                        mm(banks[2][0][:, :], ones[:], QT[:, 0:512], True, True, [Bconst, QB], [banks[2][1]])
                    act(pt[:, q0:512], sct[:, q0:512], AF.Exp, [scb], [ptb], scale=scale)
                    if j >= 0:
                        dve(lambda e: e.memset(pt[64:128, 128 * j:128 * j + 64], 0.0), [], [ptb])
                    if os.environ.get("KAV", "") == "dense":
                        vv = Vfn(kt)
                        for bki in range(2):
                            mm(acc[0:dv, bki * 512 + q0:(bki + 1) * 512], vv[:, 0:dv] if bki == 0 else ones[:, 0:dv],
                               pt[:, q0:512], kt == 0, True, [ptb, VB, Bconst], [accB], skip_group_check=True)
                    else:
                      for (i, st) in avs:
                        mm(accv[:, i, 0:dv + 1], pt[:, i * 128:(i + 1) * 128], Vfn(kt), st, True, [ptb, VB], [accB],
                           skip_group_check=True)
                    if last:
                        return after()
                    return None

                steps.append((s_fn, rest_fn))
            return steps

        def run_steps(steps, side=()):
            side = list(side)
            busy = [False]

            def run_side():
                fn, b = side.pop(0)
                fn()
                busy[0] = b

            if not steps:
                while side:
                    run_side()
                return
            per = -(-len(side) // len(steps)) if side else 0
            pending = []
            for i in range(min(LOOK, len(steps))):
                steps[i][0]()
            for i, (s_fn, rest_fn) in enumerate(steps):
                if i + LOOK < len(steps):
                    steps[i + LOOK][0]()
                while pending and pending[0][0] <= i and not busy[0]:
                    r = pending.pop(0)[1]()
                    if r is not None:
                        pending.append((i + r[0], r[1]))
                        pending.sort(key=lambda x: x[0])
                d = rest_fn()
                if d is not None:
                    pending.append((i + d[0], d[1]))
                for _ in range(per):
                    if side:
                        run_side()
            while side:
                run_side()
            while pending:
                r = pending.pop(0)[1]()
                if r is not None:
                    pending.append((0, r[1]))

        def one_seq(s):
          try:
              ABC = ExitStack()
              with ABC:
                  hT = sbt(ABC, "hT", [128, 8, SQ], BF16)
                  hTB = [Buf("hT%d" % i, S.fence()) for i in range(4)]
                  odT = sbt(ABC, "odT", [128, 8, SQ], BF16)
                  omT = sbt(ABC, "omT", [128, 4, SQ], BF16)
                  omTB = [Buf("omT%d" % i, S.fence()) for i in range(4)]
                  with ExitStack() as A:
                      xsl = Rot([(sbt(A, "xsA%d" % i, [128, D], F32), Buf("xsA%d" % i, S.fence()), nsem("xsA%d" % i))
                                 for i in range(6)])

                      def get_x(t, s=s, xsl=xsl):
                          xt, xb, ds = xsl.next()
                          S.add("sp", lambda e: e.dma_start(out=xt[:], in_=x_d[s, t * 128:(t + 1) * 128, :]), writes=[xb],
                                dsem=ds)
                          return xt[:], xb

                      norm_T(A, "nA", get_x, g_attn_d, hT, hTB)
                  if stage == 1:
                      dump("hT", hT[:], [128, 8, SQ], BF16, hTB)
                      raise _Stop()

                  with ExitStack() as Bx:
                      Ct = sbt(Bx, "Ct", [128, SQ], F32)
                      St = sbt(Bx, "St", [128, SQ], F32)
                      Btab = Buf("tab", S.fence())
                      dtab = nsem("tab")
                      S.add("sp", lambda e: [e.dma_start(out=Ct[:], in_=tab_d[0]), e.dma_start(out=St[:], in_=tab_d[1])],
                            writes=[Btab], dsem=dtab, ndma=2)
                      Vbuf = sbt(Bx, "Vbuf", [128, 8704], BF16)
                      QK = [(sbt(Bx, "QT%d" % i, [128, SQ], BF16), [Buf("QT%d_%d" % (i, k), S.fence()) for k in range(4)],
                             sbt(Bx, "KT%d" % i, [128, SQ], BF16), [Buf("KT%d_%d" % (i, k), S.fence()) for k in range(4)])
                            for i in range(2)]
                      K2s = [(sbt(Bx, "K2_%d" % i, [128, SQ], BF16), [Buf("K2_%d_%d" % (i, k), S.fence()) for k in range(4)])
                             for i in range(2)]
                      pts = Rot([(sbt(Bx, "pt%d" % i, [128, 512], BF16), Buf("pt%d" % i, S.fence())) for i in range(4)])
                      f0 = S.fence()
                      rtmp = ((sbt(Bx, "qbf", [128, 512], BF16), Buf("qbf", f0)),
                              (sbt(Bx, "rt1", [128, 512], F32), Buf("rt1", f0)),
                              (sbt(Bx, "rt2", [128, 512], F32), Buf("rt2", f0)),
                              (banks[2][0], banks[2][1]))
                      o1n = sbt(Bx, "o1n", [128, 4, 128], F32)
                      o1nB = Buf("o1n", f0)
                      otmp = sbt(Bx, "otmp", [128, 4, 128], F32)
                      otmpB = Buf("otmp", f0)
                      odf = sbt(Bx, "odf", [128, 4, 128], F32)
                      odfB = Buf("odf", f0)
                      osq = sbt(Bx, "osq", [128, 4, 128], F32)
                      osqB = Buf("osq", f0)
                      odn = sbt(Bx, "odn", [128, 4, 128], BF16)
                      odnB = Buf("odn", f0)
                      omn = sbt(Bx, "omn", [128, 4, 64], BF16)
                      omnB = Buf("omn", f0)

                      latT = odT
                      latB = [Buf("lat%d" % i, f0) for i in range(4)]
                      latf = [Vbuf[:, j * 1024:(j + 1) * 1024].bitcast(F32) for j in range(5)]
                      sqs = [Vbuf[:, 5120 + j * 512:5120 + (j + 1) * 512] for j in range(5)]
                      ltBs = [Buf("lattmp%d" % j, f0) for j in range(5)]
                      rsq = sbt(Bx, "rsq", [128, 512], F32)
                      rsqB = Buf("rsq", f0)
                      rstd = sbt(Bx, "rstdl", [128, 512], F32)
                      rstdB = Buf("rstdl", f0)
                      (wl, wkr), wlB = wload([(w_in_v[:, :, CQ_OFF:CQ_OFF + 640], 8, 640),
                                                  (w_in_v[:, :, KR_OFF:KR_OFF + 32], 8, 128, 64) if os.environ.get("KDBG", "") != "krplain"
                                                  else (w_in_v[:, :, KR_OFF - 96:KR_OFF + 32], 8, 128)])
                      for tb in range(NB):
                          cs = slice(tb * 512, (tb + 1) * 512)
                          for j in range(6):
                              bk, bb = sc_banks.next()
                              if j < 5:
                                  c0 = j * 128
                                  for c in range(8):
                                      mm(bk[:, :], wl[:, c, c0:c0 + 128], hT[:, c, cs], c == 0, c == 7, [wlB, hTB[tb]], [bb])
                                  act(latf[j], bk[:, :], AF.Copy, [bb], [ltBs[j]])
                                  act(sqs[j], bk[:, :], AF.Square, [bb], [ltBs[j]])
                              elif os.environ.get("KDBG", "") != "skipkr":
                                  for c in range(8):
                                      mm(bk[:, :], wkr[:, c, :], hT[:, c, cs], c == 0, c == 7, [wlB, hTB[tb]], [bb])
                                  rope(bk, bb, 0, 128, pmm, Ct, St, Btab, tb, latT[:, 5, cs], latB[tb], rtmp)
                          for (js, nrm, vcol) in (((0, 1, 2), 384.0, 8 + 8), ((3, 4), 256.0, 8 + 8 + 3)):
                              rk, rb = banks[2]
                              for n, j in enumerate(js):
                                  mm(rk[:, :], ones[:], sqs[j], n == 0, n == len(js) - 1, [ltBs[j], Bconst], [rb])
                              act(rsq[:], rk[:, :], AF.Sqrt, [rb], [rsqB], scale=1.0 / nrm, bias=EPS)
                              dve(lambda e: e.reciprocal(out=rstd[:], in_=rsq[:]), [rsqB], [rstdB])
                              for n, j in enumerate(js):
                                  dve(lambda e, j=j, n=n, vcol=vcol, cs=cs: e.scalar_tensor_tensor(
                                      out=latT[:, j, cs], in0=latf[j], scalar=vecs[:, vcol + n:vcol + n + 1], in1=rstd[:],
                                      op0=ALU.mult, op1=ALU.mult), [ltBs[j], rstdB, Bconst], [latB[tb]])
                      if stage == 2.1:
                          dump("latT", odT[:, 0:5, :], [128, 5, SQ], BF16, latB)
                          if os.environ.get("KDBG", "") not in ("skipkr", "nodumpkr"):
                              dump("kr", odT[64:96, 5, :], [32, SQ], BF16, latB)
                          raise _Stop()
                      (wuq, wkn, wv), wmB = wload([(wview(w_uq_d), 3, 768), (wview(w_kn_d), 2, 512), (wview(w_v_d), 2, 512)])
                      VB = Buf("V", S.fence() + [b.w for b in ltBs] + [o for b in ltBs for o in b.r])
                      Vm = Vbuf[:, 0:16 * 8 * 66].rearrange("p (t h d) -> p t h d", t=16, h=8)
                      dve(lambda e: e.memset(Vm[:, :, :, 64:65], 1.0), [], [VB])
                      for t in range(NT):
                          bk, bb = sc_banks.next()
                          for c in range(2):
                              mm(bk[:, :], latT[:, 3 + c, t * 128:(t + 1) * 128], wv[:, c, :], c == 0, c == 1,
                                 [latB[t // 4], wmB], [bb])
                          copy_alt(Vm[:, t, :, 0:64], bk[:, :].rearrange("p (h d) -> p h d", h=8), [bb], [VB])
                      trp_f = trp_t[:, :]

                      def diff_proj_tasks(h):
                          (wq, wk), wqB = wload([(w_in_v[:, :, Q_OFF + h * 128:Q_OFF + (h + 1) * 128], 8, 128),
                                                 (w_in_v[:, :, K_OFF + h * 128:K_OFF + (h + 1) * 128], 8, 128)])
                          QTt, QB_, KTt, KB_ = QK[h % 2]
                          K2t, K2B = K2s[h % 2]
                          (qbf, qbfB), (t1, t1B), (t2, t2B), _ = rtmp
                          tasks = []
                          if h < 2:
                              dve(lambda e: e.memset(KTt[64:128, :], 0.0), [], KB_)
                              dve(lambda e: e.memset(K2t[0:64, :], 0.0), [], K2B)
                          for tb in range(NB):
                              cs = slice(tb * 512, (tb + 1) * 512)
                              for (wx, dT, dB) in ((wk, None, None), (wq, QTt, QB_[tb])):
                                  def pm(c, wx=wx, cs=cs, tb=tb):
                                      mm(trp_f, wx[:, c, :], hT[:, c, cs], c == 0, c == 7, [wqB, hTB[tb]], [trp_b])

                                  def p1b(cs=cs):
                                      dve(lambda e: e.tensor_copy(out=qbf[:], in_=trp_f), [trp_b], [qbfB])
                                      dve(lambda e: e.tensor_tensor(out=t1[:], in0=trp_f, in1=Ct[:, cs], op=ALU.mult),
                                          [trp_b, Btab], [t1B])

                                  def p2(dT=dT, dB=dB, cs=cs, tb=tb):
                                      b2, b2B = trp_t, trp_b
                                      mm(b2[:, :], pdm[:, :], qbf[:], True, True, [qbfB, Bconst], [b2B])
                                      dve(lambda e: e.tensor_tensor(out=t2[:], in0=b2[:, :], in1=St[:, cs], op=ALU.mult),
                                          [b2B, Btab], [t2B])
                                      if dT is None:
                                          dve(lambda e: e.tensor_tensor(out=KTt[0:64, cs], in0=t1[0:64, :], in1=t2[0:64, :],
                                                                        op=ALU.add), [t1B, t2B], [KB_[tb]])
                                          dve(lambda e: e.tensor_tensor(out=K2t[64:128, cs], in0=t1[64:128, :],
                                                                        in1=t2[64:128, :], op=ALU.add), [t1B, t2B], [K2B[tb]])
                                      else:
                                          dve(lambda e: e.tensor_tensor(out=dT[:, cs], in0=t1[:], in1=t2[:], op=ALU.add),
                                              [t1B, t2B], [dB])

                                  tasks += [((lambda c=c, pm=pm: pm(c)), True) for c in range(8)]
                                  tasks += [(p1b, False), (p2, False)]
                          return tasks

                      sc_m = 96.0 ** -0.5
                      for h in range(8):
                          QTt, QB_, KTt, KB_ = QK[h % 2]
                          for tb in range(NB):
                              cs = slice(tb * 512, (tb + 1) * 512)
                              bk, bb = sc_banks.next()
                              for c in range(2):
                                  mm(bk[0:64, :], wkn[:, c, h * 64:(h + 1) * 64], latT[:, 3 + c, cs], c == 0, c == 1,
                                     [wmB, latB[tb]], [bb])
                              copy_alt(KTt[0:64, cs], bk[0:64, :], [bb], [KB_[tb]])
                              dve(lambda e, KTt=KTt, cs=cs: e.tensor_copy(out=KTt[64:96, cs], in_=latT[64:96, 5, cs]),
                                  [latB[tb]], [KB_[tb]])
                              bk, bb = sc_banks.next()
                              for c in range(3):
                                  mm(bk[0:96, :], wuq[:, c, h * 96:(h + 1) * 96], latT[:, c, cs], c == 0, c == 2,
                                     [wmB, latB[tb]], [bb])
                              rope(bk, bb, 0, 96, pmm, Ct, St, Btab, tb, QTt[:, cs], QB_[tb], rtmp)
                          steps = []
                          for qt in range(NB):
                              acc, accB = accs[(h * 4 + qt) % 2]

                              def after(h=h, qt=qt, acc=acc, accB=accB):
                                  accv = acc[:].rearrange("p (i n) -> p i n", i=4)
                                  st, sbf = stat_rot.next()
                                  dve(lambda e: e.reciprocal(out=st[:, 0:4], in_=accv[:, :, 64]), [accB], [sbf])
                                  dve(lambda e: e.tensor_tensor(out=omn[:], in0=accv[:, :, 0:64],
                                                                in1=st[:, 0:4].unsqueeze(2).broadcast_to([128, 4, 64]),
                                                                op=ALU.mult), [accB, sbf], [omnB])
                                  ro = 64 * (h % 2)

                                  def later():
                                      for i in range(4):
                                          tr(trp_bf[ro:ro + 64, i * 128:(i + 1) * 128], omn[:, i, :], [omnB], [trp_b])
                                      dve(lambda e: e.tensor_copy(out=omT[ro:ro + 64, h // 2, qt * 512:(qt + 1) * 512],
                                                                  in_=trp_bf[ro:ro + 64, 0:512]), [trp_b], [omTB[qt]])

                                  return (3, later)

                              steps += attn_steps(KTt, KB_, QTt, QB_, 0, 96, lambda kt, h=h: Vm[:, kt, h, 0:65], VB, 64, sc_m,
                                                  qt, acc, accB, pts, after)
                          side = []
                          if h == 7 and INTERLEAVE:
                              S.add("sp", lambda e: [e.dma_start(out=Ct[:], in_=tab_d[2]),
                                                     e.dma_start(out=St[:], in_=tab_d[3])],
                                    writes=[Btab], dsem=dtab, ndma=2)
                              side = diff_proj_tasks(0)
                          run_steps(steps, side)

                      if stage == 2.2:
                          dump("omT", omT[:], [128, 4, SQ], BF16, omTB)
                          raise _Stop()
                      if not INTERLEAVE:
                          S.add("sp", lambda e: [e.dma_start(out=Ct[:], in_=tab_d[2]), e.dma_start(out=St[:], in_=tab_d[3])],
                                writes=[Btab], dsem=dtab, ndma=2)
                      f1 = S.fence()
                      odTB = [Buf("odT%d" % i, f1) for i in range(4)]
                      Vd = Vbuf[:, 0:16 * 4 * 130].rearrange("p (t h d) -> p t h d", t=16, h=4)
                      sc_d = 64.0 ** -0.5
                      for g in range(2):
                          (wvd,), wvB = wload([(w_in_v[:, :, V_OFF + g * 512:V_OFF + (g + 1) * 512], 8, 512)])
                          dve(lambda e: e.memset(Vd[:, :, :, 128:129], 1.0), [], [VB])
                          for t in range(NT):
                              bk, bb = sc_banks.next()
                              for c in range(8):
                                  mm(bk[:, :], hT[:, c, t * 128:(t + 1) * 128], wvd[:, c, :], c == 0, c == 7,
                                     [hTB[t // 4], wvB], [bb])
                              copy_alt(Vd[:, t, :, 0:128], bk[:, :].rearrange("p (h d) -> p h d", h=4), [bb], [VB])
                          for hh in range(4):
                              h = g * 4 + hh
                              QTt, QB_, KTt, KB_ = QK[h % 2]
                              if not INTERLEAVE:
                                  for fn, _b in diff_proj_tasks(h):
                                      fn()
                              steps = []
                              for qt in range(NB):
                                  a1, a1B = accs[0]
                                  a2, a2B = accs[1]

                                  def after1(a1=a1, a1B=a1B):
                                      accv = a1[:].rearrange("p (i n) -> p i n", i=4)
                                      st, sbf = stat_rot.next()
                                      dve(lambda e: e.reciprocal(out=st[:, 0:4], in_=accv[:, :, 128]), [a1B], [sbf])
                                      dve(lambda e: e.tensor_tensor(out=o1n[:], in0=accv[:, :, 0:128],
                                                                    in1=st[:, 0:4].unsqueeze(2).broadcast_to([128, 4, 128]),
                                                                    op=ALU.mult), [a1B, sbf], [o1nB])

                                  def after2(h=h, qt=qt, a2=a2, a2B=a2B):
                                      accv = a2[:].rearrange("p (i n) -> p i n", i=4)
                                      st, sbf = stat_rot.next()
                                      st2, sbf2 = stat_rot.next()
                                      dve(lambda e: e.reciprocal(out=st[:, 0:4], in_=accv[:, :, 128]), [a2B], [sbf])
                                      dve(lambda e: e.tensor_scalar(out=st2[:, 0:4], in0=st[:, 0:4], scalar1=neglam,
                                                                    scalar2=None, op0=ALU.mult), [sbf, Blam], [sbf2])
                                      dve(lambda e: e.tensor_tensor(out=otmp[:], in0=accv[:, :, 0:128],
                                                                    in1=st2[:, 0:4].unsqueeze(2).broadcast_to([128, 4, 128]),
                                                                    op=ALU.mult), [a2B, sbf2], [otmpB])
                                      dve(lambda e: e.tensor_tensor(out=odf[:], in0=o1n[:], in1=otmp[:], op=ALU.add),
                                          [o1nB, otmpB], [odfB])
                                      dve(lambda e: e.tensor_tensor(out=osq[:], in0=odf[:], in1=odf[:], op=ALU.mult),
                                          [odfB], [osqB])
                                      st3, sbf3 = stat_rot.next()
                                      st4, sbf4 = stat_rot.next()
                                      st5, sbf5 = stat_rot.next()
                                      dve(lambda e: e.tensor_reduce(out=st3[:, 0:4], in_=osq[:], axis=AX.X, op=ALU.add),
                                          [osqB], [sbf3])
                                      dve(lambda e: e.tensor_scalar(out=st4[:, 0:4], in0=st3[:, 0:4], scalar1=1.0 / 128,
                                                                    scalar2=EPS, op0=ALU.mult, op1=ALU.add), [sbf3], [sbf4])
                                      def later():
                                          for i in range(4):
                                              tr(trp_bf[:, i * 128:(i + 1) * 128], odn[:, i, :], [odnB], [trp_b])
                                          dve(lambda e: e.tensor_scalar(out=odT[:, h, qt * 512:(qt + 1) * 512],
                                                                        in0=trp_bf[:, 0:512], scalar1=vecs[:, 21:22],
                                                                        scalar2=None, op0=ALU.mult),
                                              [trp_b, Bconst], [odTB[qt]])

                                      def later0():
                                          act(st5[:, 0:4], st4[:, 0:4], AF.Ln, [sbf4], [sbf5])
                                          act(st3[:, 0:4], st5[:, 0:4], AF.Exp, [sbf5], [sbf3], scale=-0.5)
                                          dve(lambda e: e.scalar_tensor_tensor(
                                              out=odn[:], in0=odf[:], scalar=1.0 - LAM_INIT,
                                              in1=st3[:, 0:4].unsqueeze(2).broadcast_to([128, 4, 128]),
                                              op0=ALU.mult, op1=ALU.mult), [odfB, sbf3], [odnB])
                                          return (3, later)

                                      return (5, later0)

                                  K2t, K2B = K2s[h % 2]
                                  steps += attn_steps(KTt, KB_, QTt, QB_, 0, 128, lambda kt, hh=hh: Vd[:, kt, hh, 0:129], VB,
                                                      128, sc_d, qt, a1, a1B, pts, after1)
                                  steps += attn_steps(K2t, K2B, QTt, QB_, 0, 128, lambda kt, hh=hh: Vd[:, kt, hh, 0:129], VB,
                                                      128, sc_d, qt, a2, a2B, pts, after2)
                              run_steps(steps, diff_proj_tasks(h + 1) if (INTERLEAVE and h + 1 < 8) else [])

                  if stage == 2.3:
                      dump("odT", odT[:], [128, 8, SQ], BF16, odTB)
                      raise _Stop()
                  CD = ExitStack()
                  CD.__enter__()
                  mgT = sbt(CD, "mgT", [128, 8, SQ], BF16)
                  mgB = [Buf("mgT%d" % i, S.fence()) for i in range(4)]
                  with ExitStack() as C:
                      fC = S.fence()
                      tmps = Rot([tuple((sbt(C, "mc%d_%d" % (k, i), [128, 512], F32), Buf("mc%d_%d" % (k, i), fC))
                                        for k in range(4)) for i in range(2)])
                      for j in range(8):
                          (wga, wgb, wod, wom), wcB = wload([
                              (w_in_v[:, :, GA_OFF + j * 128:GA_OFF + (j + 1) * 128], 8, 128),
                              (w_in_v[:, :, GB_OFF + j * 128:GB_OFF + (j + 1) * 128], 8, 128),
                              (wview(w_od_d)[:, :, j * 128:(j + 1) * 128], 8, 128),
                              (wview(w_om_d)[:, :, j * 128:(j + 1) * 128], 4, 128)])
                          for tb in range(NB):
                              cs = slice(tb * 512, (tb + 1) * 512)
                              (sa, saB), (sb_, sbB), (m1, m1B), (m2, m2B) = tmps.next()
                              ga, gaB = gen_banks.next()
                              for c in range(8):
                                  mm(ga, wga[:, c, :], hT[:, c, cs], c == 0, c == 7, [wcB, hTB[tb]], [gaB])
                              act(sa[:], ga, AF.Sigmoid, [gaB, Bconst], [saB], bias=vecs[:, j:j + 1])
                              gb, gbB = gen_banks.next()
                              for c in range(8):
                                  mm(gb, wgb[:, c, :], hT[:, c, cs], c == 0, c == 7, [wcB, hTB[tb]], [gbB])
                              act(sb_[:], gb, AF.Sigmoid, [gbB, Bconst], [sbB], bias=vecs[:, 8 + j:8 + j + 1])
                              oa, oaB = gen_banks.next()
                              for c in range(8):
                                  mm(oa, wod[:, c, :], odT[:, c, cs], c == 0, c == 7, [wcB, odTB[tb]], [oaB])
                              dve(lambda e, m1=m1, sa=sa, oa=oa: e.tensor_tensor(out=m1[:], in0=oa, in1=sa[:], op=ALU.mult),
                                  [oaB, saB], [m1B])
                              ob, obB = gen_banks.next()
                              for c in range(4):
                                  mm(ob, wom[:, c, :], omT[:, c, cs], c == 0, c == 3, [wcB, omTB[tb]], [obB])
                              dve(lambda e, m2=m2, sb_=sb_, ob=ob: e.tensor_tensor(out=m2[:], in0=ob, in1=sb_[:], op=ALU.mult),
                                  [obB, sbB], [m2B])
                              dve(lambda e, m1=m1, m2=m2, j=j, cs=cs: e.tensor_tensor(out=mgT[:, j, cs], in0=m1[:], in1=m2[:],
                                                                                    op=ALU.add), [m1B, m2B], [mgB[tb]])
              if stage == 3:
                  dump("mgT", mgT[:], [128, 8, SQ], BF16, mgB)
                  raise _Stop()
              DG = ExitStack()
              DG.__enter__()
              x1 = sbt(DG, "x1", [128, NT, D], F32)
              fD = S.fence()
              x1B = [Buf("x1_%d" % i, fD) for i in range(NT)]
              EG = ExitStack()
              EG.__enter__()
              h2T = sbt(EG, "h2T", [128, 8, SQ], BF16)
              h2B = [Buf("h2T%d" % i, S.fence()) for i in range(4)]
              with ExitStack() as Dp:
                  xsl = Rot([(sbt(Dp, "xsD%d" % i, [128, D], F32), Buf("xsD%d" % i, fD), nsem("xsD%d" % i)) for i in range(2)])
                  nE1a, nE1b, nE2 = norm_T(Dp, "nE", lambda t: (x1[:, t, :], x1B[t]), g_ffn_d, h2T, h2B, external=True)
                  wo = []
                  for nh in range(2):
                      (wv_,), wb_ = wload([(wview(w_out_d)[:, :, nh * 512:(nh + 1) * 512], 8, 512)])
                      wo.append((wv_, wb_))
                  qE = []
                  for t in range(NT):
                      xt, xb, ds = xsl.next()
                      S.add("sp", lambda e, xt=xt, t=t, s=s: e.dma_start(out=xt[:], in_=x_d[s, t * 128:(t + 1) * 128, :]),
                            writes=[xb], dsem=ds)
                      for nh in range(2):
                          bk, bb = gen_banks.next()
                          for c in range(8):
                              mm(bk, mgT[:, c, t * 128:(t + 1) * 128], wo[nh][0][:, c, :], c == 0, c == 7,
                                 [mgB[t // 4], wo[nh][1]], [bb])
                          dve(lambda e, bk=bk, xt=xt, t=t, nh=nh: e.tensor_tensor(
                              out=x1[:, t, nh * 512:(nh + 1) * 512], in0=bk, in1=xt[:, nh * 512:(nh + 1) * 512], op=ALU.add),
                              [bb, xb], [x1B[t]])
                      sa = nE1a(t)
                      if len(qE) >= 2:
                          nE2(t - 2, *qE.pop(0))
                      qE.append(nE1b(*sa))
                  nE2(NT - 2, *qE.pop(0))
                  nE2(NT - 1, *qE.pop(0))
              CD.__exit__(None, None, None)
              if stage == 4:
                  dump("x1", x1[:], [128, NT, D], F32, x1B)
                  raise _Stop()
              with ExitStack() as Fp:
                  fF = S.fence()
                  hid = sbt(Fp, "hid", [128, NF, 1024], BF16)
                  hidB = [Buf("hid%d" % i, fF) for i in range(2)]
                  sgs = Rot([(sbt(Fp, "sg%d" % i, [128, 512], F32), Buf("sg%d" % i, fF)) for i in range(3)])
                  for half in range(2):
                      for fb in range(11):
                          (wg, wu), wfB = wload([(wview(w_g_d)[:, :, fb * 256:(fb + 1) * 256], 8, 256),
                                                 (wview(w_u_d)[:, :, fb * 256:(fb + 1) * 256], 8, 256)])
                          for fc in range(2):
                              f = fb * 2 + fc
                              for tbh in range(2):
                                  tb = half * 2 + tbh
                                  cs = slice(tb * 512, (tb + 1) * 512)
                                  gk, gkB = gen_banks.next()
                                  for c in range(8):
                                      mm(gk, wg[:, c, fc * 128:(fc + 1) * 128], h2T[:, c, cs], c == 0, c == 7, [wfB, h2B[tb]],
                                         [gkB])
                                  sg, sgB = sgs.next()
                                  act(sg[:], gk, AF.Silu, [gkB], [sgB])
                                  uk, ukB = gen_banks.next()
                                  for c in range(8):
                                      mm(uk, wu[:, c, fc * 128:(fc + 1) * 128], h2T[:, c, cs], c == 0, c == 7, [wfB, h2B[tb]],
                                         [ukB])
                                  dve(lambda e, sg=sg, uk=uk, f=f, tbh=tbh: e.tensor_tensor(
                                      out=hid[:, f, tbh * 512:(tbh + 1) * 512], in0=uk, in1=sg[:], op=ALU.mult),
                                      [ukB, sgB], [hidB[tbh]])
                      wdv = w_d_d.rearrange("(f p) n -> p f n", p=128)
                      for nq in range(4):
                          (wd,), wdB = wload([(wdv[:, :, nq * 256:(nq + 1) * 256], NF, 256)])
                          for tl in range(8):
                              t = half * 8 + tl
                              bk, bb = gen_banks.next()
                              for f in range(NF):
                                  mm(bk[:, 0:256], hid[:, f, tl * 128:(tl + 1) * 128], wd[:, f, :], f == 0, f == NF - 1,
                                     [hidB[tl // 4], wdB], [bb])
                              dve(lambda e, bk=bk, t=t, nq=nq: e.tensor_tensor(
                                  out=x1[:, t, nq * 256:(nq + 1) * 256], in0=bk[:, 0:256], in1=x1[:, t, nq * 256:(nq + 1) * 256],
                                  op=ALU.add), [bb, x1B[t]], [x1B[t]])
              if stage == 5:
                  dump("x2", x1[:], [128, NT, D], F32, x1B)
                  raise _Stop()
              with ExitStack() as Gp:
                  nG1a, nG1b, nG2 = norm_T(Gp, "nG", lambda t: (x1[:, t, :], x1B[t]), g_ple_d, h2T, h2B, external=True)
                  fG = S.fence()
                  pT = sbt(Gp, "pT", [128, 2, SQ], BF16)
                  pTB = [Buf("pT%d" % i, fG) for i in range(4)]
                  pin = Rot([(sbt(Gp, "pin%d" % i, [128, 256], F32), Buf("pin%d" % i, fG), nsem("pin%d" % i)) for i in range(2)])
                  pbf = Rot([(sbt(Gp, "pbf%d" % i, [128, 256], BF16), Buf("pbf%d" % i, fG)) for i in range(2)])
                  gfb = sbt(Gp, "gfb", [128, D], F32)
                  Bbb = Buf("bbc", fG)
                  S.add("sp", lambda e: e.dma_start(out=gfb[:], in_=g_fin_d.partition_broadcast(128)),
                        writes=[Bbb], dsem=nsem("bbc"))
                  bpl = sbt(Gp, "bpl", [128, D], BF16)
                  e0 = sbt(Gp, "e0", [128, 128], BF16)
                  Bbp = Buf("bpl", fG)
                  dve(lambda e: e.memset(bpl[:], 0.0), [], [Bbp])
                  dve(lambda e: e.memset(e0[:], 0.0), [], [Bbp])
                  dve(lambda e: e.memset(e0[0:1, :], 1.0), [], [Bbp])
                  S.add("pool", lambda e: e.dma_start(out=bpl[0:1, :], in_=b_ple_d.rearrange("(o n) -> o n", o=1)),
                        writes=[Bbp], dsem=nsem("bpl"))
                  pk_bank = allbanks[0]

                  def p_tile(t):
                      pt_, pb_, ds = pin.next()
                      S.add("sp", lambda e: e.dma_start(out=pt_[:], in_=p_d[s, t * 128:(t + 1) * 128, :]),
                            writes=[pb_], dsem=ds)
                      pf, pfB = pbf.next()
                      dve(lambda e: e.tensor_copy(out=pf[:], in_=pt_[:]), [pb_], [pfB])
                      pbk = pk_bank[0].bitcast(BF16)
                      for c in range(2):
                          tr(pbk[:, c * 128:(c + 1) * 128], pf[:, c * 128:(c + 1) * 128], [pfB], [pk_bank[1]])
                      copy_alt(pT[:, :, t * 128:(t + 1) * 128], pbk[:, 0:256].rearrange("p (c t) -> p c t", c=2),
                               [pk_bank[1]], [pTB[t // 4]])

                  norm_drive(nG1a, nG1b, nG2, extra=p_tile)
                  wpg = []
                  for nh in range(2):
                      (wv_,), wb_ = wload([(wview(w_pg_d)[:, :, nh * 512:(nh + 1) * 512], 8, 512)])
                      wpg.append((wv_, wb_))
                  (wpl,), wplB = wload([(wview(w_pl_d), 2, 1024)])
                  gtm = Rot([tuple((sbt(Gp, "gt%d_%d" % (k, i), [128, 512], F32), Buf("gt%d_%d" % (k, i), fG))
                                   for k in range(3)) for i in range(2)])
                  junk = sbt(Gp, "junkG", [128, D], BF16)
                  Bj = Buf("junkG", fG)
                  ysl = Rot([(sbt(Gp, "ys%d" % i, [128, D], F32), Buf("ys%d" % i, fG), nsem("ys%d" % i)) for i in range(2)])
                  def final_norm(t):
                      st, sbf = stat_rot.next()
                      act(junk[:], x1[:, t, :], AF.Square, [x1B[t]], [Bj, sbf], accum_out=st[:, 0:1])
                      rstd_pool(st, sbf)
                      yt, yb, ysem = ysl.next()
                      dve(lambda e: e.scalar_tensor_tensor(out=yt[:], in0=x1[:, t, :], scalar=st[:, 2:3], in1=gfb[:],
                                                           op0=ALU.mult, op1=ALU.mult), [x1B[t], sbf, Bbb], [yb])
                      S.add("sp", lambda e: e.dma_start(out=y_d[s, t * 128:(t + 1) * 128, :], in_=yt[:]),
                            reads=[yb], dsem=ysem, store=True)

                  for t in range(NT):
                      ts_ = slice(t * 128, (t + 1) * 128)
                      for nh in range(2):
                          ns = slice(nh * 512, (nh + 1) * 512)
                          (g1, g1B), (g2, g2B), (g3, g3B) = gtm.next()
                          bk, bb = gen_banks.next()
                          for c in range(8):
                              mm(bk, h2T[:, c, ts_], wpg[nh][0][:, c, :], c == 0, False, [h2B[t // 4], wpg[nh][1]], [bb])
                          mm(bk, e0[:], bpl[:, ns], False, True, [Bbp], [bb])
                          act(g2[:], bk, AF.Sigmoid, [bb], [g2B])
                          pk, pkB = gen_banks.next()
                          for c in range(2):
                              mm(pk, pT[:, c, ts_], wpl[:, c, ns], c == 0, c == 1, [pTB[t // 4], wplB], [pkB])
                          dve(lambda e, g3=g3, g2=g2, pk=pk: e.tensor_tensor(out=g3[:], in0=pk, in1=g2[:], op=ALU.mult),
                              [pkB, g2B], [g3B])
                          dve(lambda e, g3=g3, t=t, ns=ns: e.tensor_tensor(out=x1[:, t, ns], in0=x1[:, t, ns], in1=g3[:],
                                                                          op=ALU.add), [g3B, x1B[t]], [x1B[t]])
                      if t > 0:
                          final_norm(t - 1)
                  final_norm(NT - 1)
              EG.__exit__(None, None, None)
              DG.__exit__(None, None, None)
          except _Stop:
            pass

        for s_ in range(nseq):
            one_seq(s_)

        fin = S.add("sp", lambda e: e.nop(), reads=[])
        fin.deps = list(S.stores)
        with nc.Block() as block:
            S.emit_all(block, esem)
    return nc


def _rope_tables():
    pos = np.arange(SQ, dtype=np.float32)
    tabs = np.zeros((4, 128, SQ), np.float32)
    tabs[0] = 1.0
    tabs[2] = 1.0
    inv = (np.float32(10000.0) ** (-(np.arange(0, 32, 2, dtype=np.float32) / np.float32(32)))).astype(np.float32)
    ang = (pos[:, None] * inv[None, :]).astype(np.float32)
    c, sn = np.cos(ang).astype(np.float32).T, np.sin(ang).astype(np.float32).T
    tabs[0, 64:80], tabs[0, 80:96] = c, c
    tabs[1, 64:80], tabs[1, 80:96] = sn, sn
    inv = (np.float32(500000.0) ** (-(np.arange(0, 16, 2, dtype=np.float32) / np.float32(16)))).astype(np.float32)
    ang = (pos[:, None] * inv[None, :]).astype(np.float32)
    c, sn = np.cos(ang).astype(np.float32).T, np.sin(ang).astype(np.float32).T
    for r0 in (0, 64):
        tabs[2, r0:r0 + 8], tabs[2, r0 + 8:r0 + 16] = c, c
        tabs[3, r0:r0 + 8], tabs[3, r0 + 8:r0 + 16] = sn, sn
    return tabs


def _perm_mats():
    pd = np.zeros((128, 128), np.float32)
    for r0 in (0, 64):
        for r in range(8):
            pd[r0 + r + 8, r0 + r] = -1.0
            pd[r0 + r, r0 + r + 8] = 1.0
    pm = np.zeros((128, 128), np.float32)
    for r in range(16):
        pm[64 + r + 16, 64 + r] = -1.0
        pm[64 + r, 64 + r + 16] = 1.0
    return pd.astype(ml_dtypes.bfloat16), pm.astype(ml_dtypes.bfloat16)


_CACHE = {}


def _get_nc():
    if "nc" not in _CACHE:
        _CACHE["nc"] = build_program()
    return _CACHE["nc"]


def kernel(x, p, attn_norm, w_in, b_gate, lam_q1, lam_k1, lam_q2, lam_k2, diff_subln, w_o_diff, q_norm, w_uq,
           kv_norm, w_ukv, w_o_mla, w_out, ffn_norm, w_ffn_gate, w_ffn_up, w_ffn_down, ple_norm, w_ple_gate,
           b_ple_gate, w_ple, final_norm):
    f = lambda a: np.ascontiguousarray(np.asarray(a, dtype=np.float32))
    x = f(x)
    p = f(p)[0]
    B = x.shape[0]
    nseq = B // NCORES
    w_ukv_ = f(w_ukv)[0].reshape(256, 8, 2, 64)
    vecs = np.zeros((128, 24), np.float32)
    bg = f(b_gate)[0]
    vecs[:, 0:8] = bg[0].reshape(8, 128).T
    vecs[:, 8:16] = bg[1].reshape(8, 128).T
    vecs[:, 16:19] = f(q_norm)[0].reshape(3, 128).T
    vecs[:, 19:21] = f(kv_norm)[0].reshape(2, 128).T
    vecs[:, 21] = f(diff_subln)[0]
    pd, pm = _perm_mats()
    shared = {
        "w_in": f(w_in)[0], "w_o_diff": f(w_o_diff)[0], "w_uq": f(w_uq)[0],
        "w_ukv_kn": np.ascontiguousarray(w_ukv_[:, :, 0, :].reshape(256, 512)),
        "w_ukv_v": np.ascontiguousarray(w_ukv_[:, :, 1, :].reshape(256, 512)),
        "w_o_mla": f(w_o_mla)[0], "w_out": f(w_out)[0], "w_ffn_gate": f(w_ffn_gate)[0], "w_ffn_up": f(w_ffn_up)[0],
        "w_ffn_down": f(w_ffn_down)[0], "w_ple_gate": f(w_ple_gate)[0], "w_ple": f(w_ple)[0],
        "attn_norm": f(attn_norm)[0], "ffn_norm": f(ffn_norm)[0], "ple_norm": f(ple_norm)[0],
        "final_norm": f(final_norm), "b_ple_gate": f(b_ple_gate)[0],
        "lam_q1": f(lam_q1)[0], "lam_k1": f(lam_k1)[0], "lam_q2": f(lam_q2)[0], "lam_k2": f(lam_k2)[0],
        "vecs": vecs, "ident": np.eye(128, dtype=np.float32).astype(ml_dtypes.bfloat16),
        "perm_diff": pd, "perm_mla": pm, "rope_tabs": _rope_tables(),
    }
    nc = _get_nc()
    in_maps = []
    for c in range(NCORES):
        m = dict(shared)
        m["x"] = x[c * nseq:(c + 1) * nseq]
        m["p"] = p[c * nseq:(c + 1) * nseq]
        in_maps.append(m)
    res = run_bass_kernel_spmd(nc, in_maps, core_ids=list(range(NCORES)))
    return np.concatenate([r["y"] for r in res.results], axis=0).astype(np.float32)
```

```python
import math
import os
from contextlib import ExitStack

import numpy as np
import ml_dtypes

import concourse.bass as bass
import concourse.mybir as mybir
from concourse.bass_utils import run_bass_kernel_spmd

F32 = mybir.dt.float32
BF16 = mybir.dt.bfloat16
AF = mybir.ActivationFunctionType
ALU = mybir.AluOpType
AX = mybir.AxisListType

NCORES = 8
SEQ_PER_CORE = 2
SQ = 2048
D = 1024
NT = 16
NB = 4
FF = 2816
NF = 22
EPS = 1e-6
Q_OFF, K_OFF, V_OFF, CQ_OFF, CKV_OFF, KR_OFF, GA_OFF, GB_OFF = 0, 1024, 2048, 3072, 3456, 3712, 3744, 4768
IN_COLS = 5792
WS = 6144
ARENA_BYTES = 207872
LAM_INIT = 0.8 - 0.6 * math.exp(0.0)

ENGS = ("pe", "act", "dve", "pool", "sp")


class Buf:
    __slots__ = ("name", "w", "r", "init", "psum")

    def __init__(self, name, init=(), psum=False):
        self.name = name
        self.w = None
        self.r = []
        self.init = list(init)
        self.psum = psum


class DmaSem:
    __slots__ = ("h", "count")

    def __init__(self, h):
        self.h = h
        self.count = 0


class Op:
    __slots__ = ("eng", "emit", "deps", "dsem", "dval", "ndma", "sig", "sigval", "pos")

    def __init__(self, eng, emit):
        self.eng = eng
        self.emit = emit
        self.deps = []
        self.dsem = None
        self.dval = 0
        self.ndma = 0
        self.sig = False
        self.sigval = 0


class Sched:
    def __init__(self):
        self.streams = {e: [] for e in ENGS}
        self.last_compute = {e: None for e in ENGS}
        self.stores = []

    def fence(self):
        return [o for o in self.last_compute.values() if o is not None] + list(self.stores)

    def add(self, eng, emit, reads=(), writes=(), dsem=None, ndma=1, store=False):
        op = Op(eng, emit)
        deps = {}

        def dep(o, raw):
            if o is None:
                return
            if o.dsem is None and o.eng == eng:
                if eng == "pe":
                    return
            deps[id(o)] = o

        for b in reads:
            dep(b.w, True)
            for o in b.init:
                dep(o, True)
            if b.psum:
                for o in b.r:
                    if o.eng != eng:
                        dep(o, True)
        for b in writes:
            dep(b.w, False)
            for o in b.r:
                dep(o, False)
            for o in b.init:
                dep(o, True)
            b.init = []
        latest = {}
        dl = []
        for o in deps.values():
            if o.dsem is not None:
                dl.append(o)
            elif o.eng not in latest or latest[o.eng].pos < o.pos:
                latest[o.eng] = o
        op.deps = dl + list(latest.values())
        for b in reads:
            b.r.append(op)
        for b in writes:
            b.w = op
            b.r = []
        if dsem is not None:
            op.dsem = dsem
            op.ndma = ndma
            dsem.count += 16 * ndma
            op.dval = dsem.count
            if store:
                self.stores.append(op)
        else:
            self.last_compute[eng] = op
        op.pos = len(self.streams[eng])
        self.streams[eng].append(op)
        return op

    def emit_all(self, block, esem):
        for e in ENGS:
            for op in self.streams[e]:
                for d in op.deps:
                    if d.dsem is None:
                        d.sig = True
        for e in ENGS:
            c = 0
            for op in self.streams[e]:
                if op.sig:
                    c += 1
                    op.sigval = c
        streams = self.streams

        def run(eng_name, eng):
            waited = {}
            for op in streams[eng_name]:
                for d in op.deps:
                    if d.dsem is not None:
                        key, h, v = id(d.dsem), d.dsem.h, d.dval
                    else:
                        key, h, v = d.eng, esem[d.eng], d.sigval
                    if waited.get(key, 0) >= v:
                        continue
                    waited[key] = v
                    eng.wait_ge(h, v)
                res = op.emit(eng)
                if op.dsem is not None:
                    if not isinstance(res, (list, tuple)):
                        res = [res]
                    assert len(res) == op.ndma
                    for r in res:
                        r.then_inc(op.dsem.h, 16)
                elif op.sig:
                    if isinstance(res, (list, tuple)):
                        res = res[-1]
                    res.then_inc(esem[eng_name], 1)

        @block.tensor
        def _(eng):
            run("pe", eng)

        @block.scalar
        def _(eng):
            run("act", eng)

        @block.vector
        def _(eng):
            run("dve", eng)

        @block.gpsimd
        def _(eng):
            run("pool", eng)

        @block.sync
        def _(eng):
            run("sp", eng)


class Rot:
    def __init__(self, items):
        self.items = items
        self.i = 0

    def next(self):
        it = self.items[self.i % len(self.items)]
        self.i += 1
        return it


class _Stop(Exception):
    pass


def build_program(nseq=SEQ_PER_CORE, stage=99):
    nc = bass.Bass("TRN2", target_bir_lowering=False)
    dbg_outs = {}

    def din(name, shape, dt=F32):
        return nc.dram_tensor(name, list(shape), dt, kind="ExternalInput").ap()

    x_d = din("x", [nseq, SQ, D])
    p_d = din("p", [nseq, SQ, 256])
    y_d = nc.dram_tensor("y", [nseq, SQ, D], F32, kind="ExternalOutput").ap()
    w_in_d = din("w_in", [D, IN_COLS])
    w_od_d = din("w_o_diff", [1024, D])
    w_uq_d = din("w_uq", [384, 768])
    w_kn_d = din("w_ukv_kn", [256, 512])
    w_v_d = din("w_ukv_v", [256, 512])
    w_om_d = din("w_o_mla", [512, D])
    w_out_d = din("w_out", [D, D])
    w_g_d = din("w_ffn_gate", [D, FF])
    w_u_d = din("w_ffn_up", [D, FF])
    w_d_d = din("w_ffn_down", [FF, D])
    w_pg_d = din("w_ple_gate", [D, D])
    w_pl_d = din("w_ple", [256, D])
    g_attn_d = din("attn_norm", [D])
    g_ffn_d = din("ffn_norm", [D])
    g_ple_d = din("ple_norm", [D])
    g_fin_d = din("final_norm", [D])
    b_ple_d = din("b_ple_gate", [D])
    lam_d = [din(n, [64]) for n in ("lam_q1", "lam_k1", "lam_q2", "lam_k2")]
    vecs_d = din("vecs", [128, 24])
    ident_d = din("ident", [128, 128], BF16)
    pd_d = din("perm_diff", [128, 128], BF16)
    pm_d = din("perm_mla", [128, 128], BF16)
    tab_d = din("rope_tabs", [4, 128, SQ])

    def wview(w):
        return w.rearrange("(c p) n -> p c n", p=128)

    w_in_v = wview(w_in_d)

    S = Sched()
    G = ExitStack()
    with G:
        arena = G.enter_context(nc.sbuf_tensor("arena", [128, ARENA_BYTES // 2], BF16))
        abase = nc.lookup_mloc(arena).addr
        free_list = [[abase, abase + ARENA_BYTES]]
        _uid = [0]

        def a_alloc(nbytes):
            nbytes = (nbytes + 63) // 64 * 64
            for iv in free_list:
                if iv[1] - iv[0] >= nbytes:
                    off = iv[0]
                    iv[0] += nbytes
                    if iv[0] == iv[1]:
                        free_list.remove(iv)
                    return off, nbytes
            raise RuntimeError("SBUF arena full: need %d, free %s" % (nbytes, free_list))

        def a_free(off, nbytes):
            free_list.append([off, off + nbytes])
            free_list.sort()
            i = 0
            while i + 1 < len(free_list):
                if free_list[i][1] == free_list[i + 1][0]:
                    free_list[i][1] = free_list[i + 1][1]
                    del free_list[i + 1]
                else:
                    i += 1

        def sbt(es, name, shape, dt):
            n = 1
            for d in shape[1:]:
                n *= d
            nbytes = n * (4 if dt == F32 else 2)
            off, nb = a_alloc(nbytes)
            _uid[0] += 1
            t = nc.alloc_sbuf_tensor_at("%s_%d" % (name, _uid[0]), list(shape), dt, offset=off)
            es.callback(a_free, off, nb)
            return t

        def sem(name):
            return G.enter_context(nc.semaphore(name))

        esem = {e: sem("s_" + e) for e in ("pe", "act", "dve", "pool")}
        _ds = [0]

        def dsem():
            _ds[0] += 1
            return DmaSem(sem("d%d" % _ds[0]))

        accs = []
        for i in range(2):
            t = G.enter_context(nc.psum_tensor("acc%d" % i, [128, 1024], F32))
            accs.append((t, Buf("acc%d" % i, psum=True)))
        banks = []
        for i in range(4):
            t = G.enter_context(nc.psum_tensor("bank%d" % i, [128, 512], F32))
            banks.append((t, Buf("bank%d" % i, psum=True)))
        allbanks = []
        for (t, b) in accs:
            allbanks.append((t[:, 0:512], b))
        for (t, b) in banks:
            allbanks.append((t[:], b))
        gen_banks = Rot([allbanks[0], allbanks[1], (banks[0][0][:], banks[0][1]), (banks[1][0][:], banks[1][1]),
                         (banks[2][0][:], banks[2][1])])
        trp_t, trp_b = banks[3]
        trp_bf = trp_t[:].bitcast(BF16)

        ident = sbt(G, "ident", [128, 128], BF16)
        ones = sbt(G, "ones", [128, 128], BF16)
        pdm = sbt(G, "pdm", [128, 128], BF16)
        pmm = sbt(G, "pmm", [128, 128], BF16)
        vecs = sbt(G, "vecs_s", [128, 24], F32)
        lamt = sbt(G, "lamt", [128, 4, 64], F32)
        lsm = sbt(G, "lsm", [128, 8], F32)
        Bconst = Buf("const")
        Blam = Buf("lam")
        dconst = dsem()
        S.add("sp", lambda e: [e.dma_start(out=ident[:], in_=ident_d), e.dma_start(out=pdm[:], in_=pd_d),
                               e.dma_start(out=pmm[:], in_=pm_d), e.dma_start(out=vecs[:], in_=vecs_d)]
              + [e.dma_start(out=lamt[:, i, :], in_=lam_d[i].partition_broadcast(128)) for i in range(4)],
              writes=[Bconst, Blam], dsem=dconst, ndma=8)
        S.add("dve", lambda e: e.memset(ones[:], 1.0), writes=[Bconst])
        negmask = sbt(G, "negmask", [128, 1], F32)
        S.add("dve", lambda e: e.memset(negmask[0:64, :], 0.0), writes=[Bconst])
        S.add("dve", lambda e: e.memset(negmask[64:128, :], -30000.0), writes=[Bconst])
        mhalf = sbt(G, "mhalf", [128, 1], F32)
        S.add("dve", lambda e: e.memset(mhalf[:], -0.5), writes=[Bconst])

        def rstd_pool(st, sbf):
            S.add("dve", lambda e: e.tensor_scalar(out=st[:, 1:2], in0=st[:, 0:1], scalar1=1.0 / D, scalar2=EPS,
                                                   op0=ALU.mult, op1=ALU.add), reads=[sbf], writes=[sbf])
            S.add("pool", lambda e: e.tensor_tensor(out=st[:, 2:3], in0=st[:, 1:2], in1=mhalf[:], op=ALU.pow),
                  reads=[sbf, Bconst], writes=[sbf])
        lprod = sbt(G, "lprod", [128, 2, 64], F32)
        S.add("dve", lambda e: e.tensor_tensor(out=lprod[:, 0, :], in0=lamt[:, 0, :], in1=lamt[:, 1, :], op=ALU.mult),
              reads=[Blam], writes=[Blam])
        S.add("dve", lambda e: e.tensor_tensor(out=lprod[:, 1, :], in0=lamt[:, 2, :], in1=lamt[:, 3, :], op=ALU.mult),
              reads=[Blam], writes=[Blam])
        S.add("dve", lambda e: e.tensor_reduce(out=lsm[:, 0:2], in_=lprod[:], axis=AX.X, op=ALU.add),
              reads=[Blam], writes=[Blam])
        S.add("act", lambda e: e.activation(out=lsm[:, 2:4], in_=lsm[:, 0:2], func=AF.Exp), reads=[Blam], writes=[Blam])
        S.add("dve", lambda e: e.tensor_tensor(out=lsm[:, 4:5], in0=lsm[:, 3:4], in1=lsm[:, 2:3], op=ALU.subtract),
              reads=[Blam], writes=[Blam])
        S.add("dve", lambda e: e.tensor_scalar(out=lsm[:, 5:6], in0=lsm[:, 4:5], scalar1=-LAM_INIT, scalar2=None,
                                               op0=ALU.add), reads=[Blam], writes=[Blam])
        neglam = lsm[:, 5:6]

        wslots = []
        for i in range(3):
            t = sbt(G, "wslot%d" % i, [128, WS], BF16)
            wslots.append((t, Buf("wslot%d" % i), dsem()))
        wrot = Rot(wslots)

        def wload(parts):
            t, b, ds = wrot.next()
            views = []
            off = 0
            pairs = []
            for part in parts:
                src_, C, N = part[0], part[1], part[2]
                v = t[:, off:off + C * N].rearrange("p (c n) -> p c n", c=C)
                views.append(v)
                if len(part) > 3:
                    n = src_.shape[2]
                    flat = t[:, off:off + C * N]
                    S.add("dve", lambda e, flat=flat: e.memset(flat, 0.0), writes=[b])
                    pairs.append((v[:, :, part[3]:part[3] + n], src_))
                else:
                    pairs.append((v, src_))
                off += C * N
            assert off <= WS
            S.add("pool", lambda e: [e.dma_start(out=v, in_=s) for (v, s) in pairs], writes=[b], dsem=ds,
                  ndma=len(pairs))
            return views, b

        stat = sbt(G, "stat", [128, 64], F32)
        stat_rot = Rot([(stat[:, i * 4:(i + 1) * 4], Buf("stat%d" % i)) for i in range(16)])

        _nsems = {}

        def nsem(key):
            if key not in _nsems:
                _nsems[key] = dsem()
            return _nsems[key]

        def dump(name, ap, shape, dt, bufs):
            d = nc.dram_tensor("dbg_" + name, list(shape), dt, kind="ExternalOutput").ap()
            dbg_outs[name] = d
            S.add("sp", lambda e: e.dma_start(out=d, in_=ap), reads=bufs, dsem=nsem("dbg_" + name), store=True)

        def mm(out, lhsT, rhs, start, stop, reads, writes, **kw):
            return S.add("pe", lambda e: e.matmul(out, lhsT=lhsT, rhs=rhs, start=start, stop=stop, **kw),
                         reads=reads, writes=writes)

        def tr(out, in_, reads, writes):
            return S.add("pe", lambda e: e.transpose(out=out, in_=in_, identity=ident[:]), reads=list(reads) + [Bconst],
                         writes=writes)

        def act(out, in_, func, reads, writes, **kw):
            return S.add("act", lambda e: e.activation(out=out, in_=in_, func=func, **kw), reads=reads, writes=writes)

        def dve(fn, reads, writes):
            return S.add("dve", fn, reads=reads, writes=writes)

        _alt = [0]

        def copy_alt(out, in_, reads, writes):
            _alt[0] += 1
            if _alt[0] % 2:
                return act(out, in_, AF.Copy, reads, writes)
            return dve(lambda e: e.tensor_copy(out=out, in_=in_), reads, writes)

        def norm_T(es, tag, get_x, g_d, dstT, dstB, external=False):
            gbc = sbt(es, tag + "_gbc", [128, D], F32)
            Bg = Buf(tag + "_gbc", S.fence())
            dg = nsem("gbc_" + tag)
            S.add("sp", lambda e: e.dma_start(out=gbc[:], in_=g_d.partition_broadcast(128)), writes=[Bg], dsem=dg)
            junk = sbt(es, tag + "_junk", [128, D], BF16)
            Bj = Buf(tag + "_junk", S.fence())
            hns = Rot([(sbt(es, tag + "_hn%d" % i, [128, D], BF16), Buf(tag + "_hn%d" % i, S.fence())) for i in range(3)])
            trps = Rot([(trp_bf, trp_b), (banks[2][0][:].bitcast(BF16), banks[2][1])])
            def stage1a(t):
                xa, xb = get_x(t)
                st, sbf = stat_rot.next()
                act(junk[:], xa, AF.Square, [xb], [Bj, sbf], accum_out=st[:, 0:1])
                rstd_pool(st, sbf)
                return xa, xb, st, sbf

            def stage1b(xa, xb, st, sbf):
                hn, hb = hns.next()
                dve(lambda e: e.scalar_tensor_tensor(out=hn[:], in0=xa, scalar=st[:, 2:3], in1=gbc[:],
                                                     op0=ALU.mult, op1=ALU.mult), [xb, sbf, Bg], [hb])
                return hn, hb

            def stage2(t, hn, hb):
                tb_, tbB = trps.next()
                for c in range(8):
                    tr(tb_[:, c * 128:(c + 1) * 128], hn[:, c * 128:(c + 1) * 128], [hb], [tbB])
                copy_alt(dstT[:, :, t * 128:(t + 1) * 128], tb_.rearrange("p (c t) -> p c t", c=8), [tbB],
                         [dstB[t // 4]])

            if external:
                return stage1a, stage1b, stage2
            norm_drive(stage1a, stage1b, stage2)

        def norm_drive(s1a, s1b, s2, extra=None):
            q = [s1b(*s1a(0)), s1b(*s1a(1))]
            for t in range(NT):
                sa = s1a(t + 2) if t + 2 < NT else None
                if extra is not None:
                    extra(t)
                s2(t, *q.pop(0))
                if sa is not None:
                    q.append(s1b(*sa))

        def rope(Aps, Ab, r0, r1, perm, Ct, St, Btab, tb, dst, dstB, tmp):
            (qbf, qbfB), (t1, t1B), (t2, t2B), (Bps, BpB) = tmp
            cs = slice(tb * 512, (tb + 1) * 512)
            p0, p1 = (0, 128) if r0 > 0 else (r0, r1)
            lvl = int(os.environ.get("KROPE", "9"))
            act(qbf[p0:p1, :], Aps[p0:p1, :], AF.Copy, [Ab], [qbfB])
            if lvl >= 2:
                mm(Bps[p0:p1, :], perm[p0:p1, p0:p1], qbf[p0:p1, :], True, True, [qbfB, Bconst], [BpB])
            if lvl >= 3:
                dve(lambda e: e.tensor_tensor(out=t1[r0:r1, :], in0=Aps[r0:r1, :], in1=Ct[r0:r1, cs], op=ALU.mult),
                    [Ab] + ([] if os.environ.get("KNOTAB") else [Btab]), [t1B])
            if lvl >= 4:
                dve(lambda e: e.tensor_tensor(out=t2[r0:r1, :], in0=Bps[r0:r1, :], in1=St[r0:r1, cs], op=ALU.mult),
                    [BpB, Btab], [t2B])
            if lvl >= 5:
                dve(lambda e: e.tensor_tensor(out=dst[r0:r1, :], in0=t1[r0:r1, :], in1=t2[r0:r1, :], op=ALU.add),
                    [t1B, t2B], [dstB])

        sc_banks = Rot([banks[0], banks[1]])
        NDUMMY = int(os.environ.get("KDUMMY", "0"))
        att_banks = Rot([banks[0], banks[1], banks[2]] if NDUMMY == 0 else [banks[0], banks[1]])
        LOOK = 2 if NDUMMY == 0 else 1
        INTERLEAVE = os.environ.get("KNOINT", "") == ""

        def attn_steps(KT, KB, QT, QB, r0, r1, Vfn, VB, dv, scale, qt, acc, accB, pts, after):
            steps = []
            nk = 4 * qt + 4
            first = {0: True, 1: True}
            accv = acc[:].rearrange("p (i n) -> p i n", i=4)
            for kt in range(nk):
                j = kt - 4 * qt
                q0 = 128 * j if j > 0 else 0
                sct, scb = att_banks.next()
                pt, ptb = pts.next()

                def s_fn(kt=kt, q0=q0, sct=sct, scb=scb):
                    mm(sct[:, q0:512], KT[r0:r1, kt * 128:(kt + 1) * 128], QT[r0:r1, qt * 512 + q0:(qt + 1) * 512],
                       True, True, [KB[kt // 4], QB[qt]], [scb])

                avs = []
                for i in range(max(j, 0), 4):
                    bk = i // 2
                    avs.append((i, first[bk]))
                    first[bk] = False

                def rest_fn(kt=kt, j=j, q0=q0, sct=sct, scb=scb, pt=pt, ptb=ptb, avs=avs, last=(kt == nk - 1)):
                    for dmy in range(NDUMMY):
                        mm(banks[2][0][:, :], ones[:], QT[:, 0:512], True, True, [Bconst, QB], [banks[2][1]])
                    if j >= 0:
                        act(pt[:, q0:q0 + 64], sct[:, q0:q0 + 64], AF.Exp, [scb, Bconst], [ptb], scale=scale,
                            bias=negmask[:, 0:1])
                        act(pt[:, q0 + 64:512], sct[:, q0 + 64:512], AF.Exp, [scb], [ptb], scale=scale)
                    else:
                        act(pt[:, q0:512], sct[:, q0:512], AF.Exp, [scb], [ptb], scale=scale)
                    if os.environ.get("KAV", "") == "dense":
                        vv = Vfn(kt)
                        for bki in range(2):
                            mm(acc[0:dv, bki * 512 + q0:(bki + 1) * 512], vv[:, 0:dv] if bki == 0 else ones[:, 0:dv],
                               pt[:, q0:512], kt == 0, True, [ptb, VB, Bconst], [accB], skip_group_check=True)
                    else:
                      for (i, st) in avs:
                        mm(accv[:, i, 0:dv + 1], pt[:, i * 128:(i + 1) * 128], Vfn(kt), st, True, [ptb, VB], [accB],
                           skip_group_check=True)
                    if last:
                        return after()
                    return None

                steps.append((s_fn, rest_fn))
            return steps

        def run_steps(steps, side=()):
            side = list(side)
            busy = [False]

            def run_side():
                fn, b = side.pop(0)
                fn()
                busy[0] = b

            if not steps:
                while side:
                    run_side()
                return
            per = -(-len(side) // len(steps)) if side else 0
            pending = []
            for i in range(min(LOOK, len(steps))):
                steps[i][0]()
            for i, (s_fn, rest_fn) in enumerate(steps):
                if i + LOOK < len(steps):
                    steps[i + LOOK][0]()
                while pending and pending[0][0] <= i and not busy[0]:
                    r = pending.pop(0)[1]()
                    if r is not None:
                        pending.append((i + r[0], r[1]))
                        pending.sort(key=lambda x: x[0])
                d = rest_fn()
                if d is not None:
                    pending.append((i + d[0], d[1]))
                for _ in range(per):
                    if side:
                        run_side()
            while side:
                run_side()
            while pending:
                r = pending.pop(0)[1]()
                if r is not None:
                    pending.append((0, r[1]))

        def one_seq(s):
          try:
              ABC = ExitStack()
              with ABC:
                  hT = sbt(ABC, "hT", [128, 8, SQ], BF16)
                  hTB = [Buf("hT%d" % i, S.fence()) for i in range(4)]
                  odT = sbt(ABC, "odT", [128, 8, SQ], BF16)
                  omT = sbt(ABC, "omT", [128, 4, SQ], BF16)
                  omTB = [Buf("omT%d" % i, S.fence()) for i in range(4)]
                  with ExitStack() as A:
                      xsl = Rot([(sbt(A, "xsA%d" % i, [128, D], F32), Buf("xsA%d" % i, S.fence()), nsem("xsA%d" % i))
                                 for i in range(6)])

                      def get_x(t, s=s, xsl=xsl):
                          xt, xb, ds = xsl.next()
                          S.add("sp", lambda e: e.dma_start(out=xt[:], in_=x_d[s, t * 128:(t + 1) * 128, :]), writes=[xb],
                                dsem=ds)
                          return xt[:], xb

                      norm_T(A, "nA", get_x, g_attn_d, hT, hTB)
                  if stage == 1:
                      dump("hT", hT[:], [128, 8, SQ], BF16, hTB)
                      raise _Stop()

                  with ExitStack() as Bx:
                      Ct = sbt(Bx, "Ct", [128, SQ], F32)
                      St = sbt(Bx, "St", [128, SQ], F32)
                      Btab = Buf("tab", S.fence())
                      dtab = nsem("tab")
                      S.add("sp", lambda e: [e.dma_start(out=Ct[:], in_=tab_d[0]), e.dma_start(out=St[:], in_=tab_d[1])],
                            writes=[Btab], dsem=dtab, ndma=2)
                      Vbuf = sbt(Bx, "Vbuf", [128, 8704], BF16)
                      QK = [(sbt(Bx, "QT%d" % i, [128, SQ], BF16), [Buf("QT%d_%d" % (i, k), S.fence()) for k in range(4)],
                             sbt(Bx, "KT%d" % i, [128, SQ], BF16), [Buf("KT%d_%d" % (i, k), S.fence()) for k in range(4)])
                            for i in range(2)]
                      K2s = [(sbt(Bx, "K2_%d" % i, [128, SQ], BF16), [Buf("K2_%d_%d" % (i, k), S.fence()) for k in range(4)])
                             for i in range(2)]
                      pts = Rot([(sbt(Bx, "pt%d" % i, [128, 512], BF16), Buf("pt%d" % i, S.fence())) for i in range(4)])
                      f0 = S.fence()
                      rtmp = ((sbt(Bx, "qbf", [128, 512], BF16), Buf("qbf", f0)),
                              (sbt(Bx, "rt1", [128, 512], F32), Buf("rt1", f0)),
                              (sbt(Bx, "rt2", [128, 512], F32), Buf("rt2", f0)),
                              (banks[2][0], banks[2][1]))
                      o1n = sbt(Bx, "o1n", [128, 4, 128], F32)
                      o1nB = Buf("o1n", f0)
                      otmp = sbt(Bx, "otmp", [128, 4, 128], F32)
                      otmpB = Buf("otmp", f0)
                      odf = sbt(Bx, "odf", [128, 4, 128], F32)
                      odfB = Buf("odf", f0)
                      osq = sbt(Bx, "osq", [128, 4, 128], F32)
                      osqB = Buf("osq", f0)
                      odn = sbt(Bx, "odn", [128, 4, 128], BF16)
                      odnB = Buf("odn", f0)
                      omn = sbt(Bx, "omn", [128, 4, 64], BF16)
                      omnB = Buf("omn", f0)

                      latT = odT
                      latB = [Buf("lat%d" % i, f0) for i in range(4)]
                      latf = [Vbuf[:, j * 1024:(j + 1) * 1024].bitcast(F32) for j in range(5)]
                      sqs = [Vbuf[:, 5120 + j * 512:5120 + (j + 1) * 512] for j in range(5)]
                      ltBs = [Buf("lattmp%d" % j, f0) for j in range(5)]
                      rsq = sbt(Bx, "rsq", [128, 512], F32)
                      rsqB = Buf("rsq", f0)
                      rstd = sbt(Bx, "rstdl", [128, 512], F32)
                      rstdB = Buf("rstdl", f0)
                      (wl, wkr), wlB = wload([(w_in_v[:, :, CQ_OFF:CQ_OFF + 640], 8, 640),
                                                  (w_in_v[:, :, KR_OFF:KR_OFF + 32], 8, 128, 64) if os.environ.get("KDBG", "") != "krplain"
                                                  else (w_in_v[:, :, KR_OFF - 96:KR_OFF + 32], 8, 128)])
                      for tb in range(NB):
                          cs = slice(tb * 512, (tb + 1) * 512)
                          for j in range(6):
                              bk, bb = sc_banks.next()
                              if j < 5:
                                  c0 = j * 128
                                  for c in range(8):
                                      mm(bk[:, :], wl[:, c, c0:c0 + 128], hT[:, c, cs], c == 0, c == 7, [wlB, hTB[tb]], [bb])
                                  act(latf[j], bk[:, :], AF.Copy, [bb], [ltBs[j]])
                                  act(sqs[j], bk[:, :], AF.Square, [bb], [ltBs[j]])
                              elif os.environ.get("KDBG", "") != "skipkr":
                                  for c in range(8):
                                      mm(bk[:, :], wkr[:, c, :], hT[:, c, cs], c == 0, c == 7, [wlB, hTB[tb]], [bb])
                                  rope(bk, bb, 0, 128, pmm, Ct, St, Btab, tb, latT[:, 5, cs], latB[tb], rtmp)
                          for (js, nrm, vcol) in (((0, 1, 2), 384.0, 8 + 8), ((3, 4), 256.0, 8 + 8 + 3)):
                              rk, rb = banks[2]
                              for n, j in enumerate(js):
                                  mm(rk[:, :], ones[:], sqs[j], n == 0, n == len(js) - 1, [ltBs[j], Bconst], [rb])
                              act(rsq[:], rk[:, :], AF.Sqrt, [rb], [rsqB], scale=1.0 / nrm, bias=EPS)
                              dve(lambda e: e.reciprocal(out=rstd[:], in_=rsq[:]), [rsqB], [rstdB])
                              for n, j in enumerate(js):
                                  dve(lambda e, j=j, n=n, vcol=vcol, cs=cs: e.scalar_tensor_tensor(
                                      out=latT[:, j, cs], in0=latf[j], scalar=vecs[:, vcol + n:vcol + n + 1], in1=rstd[:],
                                      op0=ALU.mult, op1=ALU.mult), [ltBs[j], rstdB, Bconst], [latB[tb]])
                      if stage == 2.1:
                          dump("latT", odT[:, 0:5, :], [128, 5, SQ], BF16, latB)
                          if os.environ.get("KDBG", "") not in ("skipkr", "nodumpkr"):
                              dump("kr", odT[64:96, 5, :], [32, SQ], BF16, latB)
                          raise _Stop()
                      (wuq, wkn, wv), wmB = wload([(wview(w_uq_d), 3, 768), (wview(w_kn_d), 2, 512), (wview(w_v_d), 2, 512)])
                      VB = Buf("V", S.fence() + [b.w for b in ltBs] + [o for b in ltBs for o in b.r])
                      Vm = Vbuf[:, 0:16 * 8 * 66].rearrange("p (t h d) -> p t h d", t=16, h=8)
                      dve(lambda e: e.memset(Vm[:, :, :, 64:65], 1.0), [], [VB])
                      for t in range(NT):
                          bk, bb = sc_banks.next()
                          for c in range(2):
                              mm(bk[:, :], latT[:, 3 + c, t * 128:(t + 1) * 128], wv[:, c, :], c == 0, c == 1,
                                 [latB[t // 4], wmB], [bb])
                          copy_alt(Vm[:, t, :, 0:64], bk[:, :].rearrange("p (h d) -> p h d", h=8), [bb], [VB])
                      trp_f = trp_t[:, :]

                      def diff_proj_tasks(h):
                          (wq, wk), wqB = wload([(w_in_v[:, :, Q_OFF + h * 128:Q_OFF + (h + 1) * 128], 8, 128),
                                                 (w_in_v[:, :, K_OFF + h * 128:K_OFF + (h + 1) * 128], 8, 128)])
                          QTt, QB_, KTt, KB_ = QK[h % 2]
                          K2t, K2B = K2s[h % 2]
                          (qbf, qbfB), (t1, t1B), (t2, t2B), _ = rtmp
                          tasks = []
                          if h < 2:
                              dve(lambda e: e.memset(KTt[64:128, :], 0.0), [], KB_)
                              dve(lambda e: e.memset(K2t[0:64, :], 0.0), [], K2B)
                          for tb in range(NB):
                              cs = slice(tb * 512, (tb + 1) * 512)
                              for (wx, dT, dB) in ((wk, None, None), (wq, QTt, QB_[tb])):
                                  def pm(c, wx=wx, cs=cs, tb=tb):
                                      mm(trp_f, wx[:, c, :], hT[:, c, cs], c == 0, c == 7, [wqB, hTB[tb]], [trp_b])

                                  def p1b(cs=cs):
                                      dve(lambda e: e.tensor_copy(out=qbf[:], in_=trp_f), [trp_b], [qbfB])
                                      dve(lambda e: e.tensor_tensor(out=t1[:], in0=trp_f, in1=Ct[:, cs], op=ALU.mult),
                                          [trp_b, Btab], [t1B])

                                  def p2(dT=dT, dB=dB, cs=cs, tb=tb):
                                      b2, b2B = trp_t, trp_b
                                      mm(b2[:, :], pdm[:, :], qbf[:], True, True, [qbfB, Bconst], [b2B])
                                      dve(lambda e: e.tensor_tensor(out=t2[:], in0=b2[:, :], in1=St[:, cs], op=ALU.mult),
                                          [b2B, Btab], [t2B])
                                      if dT is None:
                                          dve(lambda e: e.tensor_tensor(out=KTt[0:64, cs], in0=t1[0:64, :], in1=t2[0:64, :],
                                                                        op=ALU.add), [t1B, t2B], [KB_[tb]])
                                          dve(lambda e: e.tensor_tensor(out=K2t[64:128, cs], in0=t1[64:128, :],
                                                                        in1=t2[64:128, :], op=ALU.add), [t1B, t2B], [K2B[tb]])
                                      else:
                                          dve(lambda e: e.tensor_tensor(out=dT[:, cs], in0=t1[:], in1=t2[:], op=ALU.add),
                                              [t1B, t2B], [dB])

                                  tasks += [((lambda c=c, pm=pm: pm(c)), True) for c in range(8)]
                                  tasks += [(p1b, False), (p2, False)]
                          return tasks

                      sc_m = 96.0 ** -0.5
                      for h in range(8):
                          QTt, QB_, KTt, KB_ = QK[h % 2]
                          for tb in range(NB):
                              cs = slice(tb * 512, (tb + 1) * 512)
                              bk, bb = sc_banks.next()
                              for c in range(2):
                                  mm(bk[0:64, :], wkn[:, c, h * 64:(h + 1) * 64], latT[:, 3 + c, cs], c == 0, c == 1,
                                     [wmB, latB[tb]], [bb])
                              copy_alt(KTt[0:64, cs], bk[0:64, :], [bb], [KB_[tb]])
                              dve(lambda e, KTt=KTt, cs=cs: e.tensor_copy(out=KTt[64:96, cs], in_=latT[64:96, 5, cs]),
                                  [latB[tb]], [KB_[tb]])
                              bk, bb = sc_banks.next()
                              for c in range(3):
                                  mm(bk[0:96, :], wuq[:, c, h * 96:(h + 1) * 96], latT[:, c, cs], c == 0, c == 2,
                                     [wmB, latB[tb]], [bb])
                              rope(bk, bb, 0, 96, pmm, Ct, St, Btab, tb, QTt[:, cs], QB_[tb], rtmp)
                          steps = []
                          for qt in range(NB):
                              acc, accB = accs[(h * 4 + qt) % 2]

                              def after(h=h, qt=qt, acc=acc, accB=accB):
                                  accv = acc[:].rearrange("p (i n) -> p i n", i=4)
                                  st, sbf = stat_rot.next()
                                  dve(lambda e: e.reciprocal(out=st[:, 0:4], in_=accv[:, :, 64]), [accB], [sbf])
                                  dve(lambda e: e.tensor_tensor(out=omn[:], in0=accv[:, :, 0:64],
                                                                in1=st[:, 0:4].unsqueeze(2).broadcast_to([128, 4, 64]),
                                                                op=ALU.mult), [accB, sbf], [omnB])
                                  ro = 64 * (h % 2)

                                  def later():
                                      for i in range(4):
                                          tr(trp_bf[ro:ro + 64, i * 128:(i + 1) * 128], omn[:, i, :], [omnB], [trp_b])
                                      dve(lambda e: e.tensor_copy(out=omT[ro:ro + 64, h // 2, qt * 512:(qt + 1) * 512],
                                                                  in_=trp_bf[ro:ro + 64, 0:512]), [trp_b], [omTB[qt]])

                                  return (3, later)

                              steps += attn_steps(KTt, KB_, QTt, QB_, 0, 96, lambda kt, h=h: Vm[:, kt, h, 0:65], VB, 64, sc_m,
                                                  qt, acc, accB, pts, after)
                          side = []
                          if h == 7 and INTERLEAVE:
                              S.add("sp", lambda e: [e.dma_start(out=Ct[:], in_=tab_d[2]),
                                                     e.dma_start(out=St[:], in_=tab_d[3])],
                                    writes=[Btab], dsem=dtab, ndma=2)
                              side = diff_proj_tasks(0)
                          run_steps(steps, side)

                      if stage == 2.2:
                          dump("omT", omT[:], [128, 4, SQ], BF16, omTB)
                          raise _Stop()
                      if not INTERLEAVE:
                          S.add("sp", lambda e: [e.dma_start(out=Ct[:], in_=tab_d[2]), e.dma_start(out=St[:], in_=tab_d[3])],
                                writes=[Btab], dsem=dtab, ndma=2)
                      f1 = S.fence()
                      odTB = [Buf("odT%d" % i, f1) for i in range(4)]
                      Vd = Vbuf[:, 0:16 * 4 * 130].rearrange("p (t h d) -> p t h d", t=16, h=4)
                      sc_d = 64.0 ** -0.5
                      for g in range(2):
                          (wvd,), wvB = wload([(w_in_v[:, :, V_OFF + g * 512:V_OFF + (g + 1) * 512], 8, 512)])
                          dve(lambda e: e.memset(Vd[:, :, :, 128:129], 1.0), [], [VB])
                          for t in range(NT):
                              bk, bb = sc_banks.next()
                              for c in range(8):
                                  mm(bk[:, :], hT[:, c, t * 128:(t + 1) * 128], wvd[:, c, :], c == 0, c == 7,
                                     [hTB[t // 4], wvB], [bb])
                              copy_alt(Vd[:, t, :, 0:128], bk[:, :].rearrange("p (h d) -> p h d", h=4), [bb], [VB])
                          for hh in range(4):
                              h = g * 4 + hh
                              QTt, QB_, KTt, KB_ = QK[h % 2]
                              if not INTERLEAVE:
                                  for fn, _b in diff_proj_tasks(h):
                                      fn()
                              steps = []
                              for qt in range(NB):
                                  a1, a1B = accs[0]
                                  a2, a2B = accs[1]

                                  def after1(a1=a1, a1B=a1B):
                                      accv = a1[:].rearrange("p (i n) -> p i n", i=4)
                                      st, sbf = stat_rot.next()
                                      dve(lambda e: e.reciprocal(out=st[:, 0:4], in_=accv[:, :, 128]), [a1B], [sbf])
                                      dve(lambda e: e.tensor_tensor(out=o1n[:], in0=accv[:, :, 0:128],
                                                                    in1=st[:, 0:4].unsqueeze(2).broadcast_to([128, 4, 128]),
                                                                    op=ALU.mult), [a1B, sbf], [o1nB])

                                  def after2(h=h, qt=qt, a2=a2, a2B=a2B):
                                      accv = a2[:].rearrange("p (i n) -> p i n", i=4)
                                      st, sbf = stat_rot.next()
                                      st2, sbf2 = stat_rot.next()
                                      dve(lambda e: e.reciprocal(out=st[:, 0:4], in_=accv[:, :, 128]), [a2B], [sbf])
                                      dve(lambda e: e.tensor_scalar(out=st2[:, 0:4], in0=st[:, 0:4], scalar1=neglam,
                                                                    scalar2=None, op0=ALU.mult), [sbf, Blam], [sbf2])
                                      dve(lambda e: e.tensor_tensor(out=otmp[:], in0=accv[:, :, 0:128],
                                                                    in1=st2[:, 0:4].unsqueeze(2).broadcast_to([128, 4, 128]),
                                                                    op=ALU.mult), [a2B, sbf2], [otmpB])
                                      dve(lambda e: e.tensor_tensor(out=odf[:], in0=o1n[:], in1=otmp[:], op=ALU.add),
                                          [o1nB, otmpB], [odfB])
                                      dve(lambda e: e.tensor_tensor(out=osq[:], in0=odf[:], in1=odf[:], op=ALU.mult),
                                          [odfB], [osqB])
                                      st3, sbf3 = stat_rot.next()
                                      st4, sbf4 = stat_rot.next()
                                      st5, sbf5 = stat_rot.next()
                                      dve(lambda e: e.tensor_reduce(out=st3[:, 0:4], in_=osq[:], axis=AX.X, op=ALU.add),
                                          [osqB], [sbf3])
                                      dve(lambda e: e.tensor_scalar(out=st4[:, 0:4], in0=st3[:, 0:4], scalar1=1.0 / 128,
                                                                    scalar2=EPS, op0=ALU.mult, op1=ALU.add), [sbf3], [sbf4])
                                      def later():
                                          for i in range(4):
                                              tr(trp_bf[:, i * 128:(i + 1) * 128], odn[:, i, :], [odnB], [trp_b])
                                          dve(lambda e: e.tensor_scalar(out=odT[:, h, qt * 512:(qt + 1) * 512],
                                                                        in0=trp_bf[:, 0:512], scalar1=vecs[:, 21:22],
                                                                        scalar2=None, op0=ALU.mult),
                                              [trp_b, Bconst], [odTB[qt]])

                                      def later0():
                                          act(st5[:, 0:4], st4[:, 0:4], AF.Ln, [sbf4], [sbf5])
                                          act(st3[:, 0:4], st5[:, 0:4], AF.Exp, [sbf5], [sbf3], scale=-0.5)
                                          dve(lambda e: e.scalar_tensor_tensor(
                                              out=odn[:], in0=odf[:], scalar=1.0 - LAM_INIT,
                                              in1=st3[:, 0:4].unsqueeze(2).broadcast_to([128, 4, 128]),
                                              op0=ALU.mult, op1=ALU.mult), [odfB, sbf3], [odnB])
                                          return (3, later)

                                      return (5, later0)

                                  K2t, K2B = K2s[h % 2]
                                  steps += attn_steps(KTt, KB_, QTt, QB_, 0, 128, lambda kt, hh=hh: Vd[:, kt, hh, 0:129], VB,
                                                      128, sc_d, qt, a1, a1B, pts, after1)
                                  steps += attn_steps(K2t, K2B, QTt, QB_, 0, 128, lambda kt, hh=hh: Vd[:, kt, hh, 0:129], VB,
                                                      128, sc_d, qt, a2, a2B, pts, after2)
                              run_steps(steps, diff_proj_tasks(h + 1) if (INTERLEAVE and h + 1 < 8) else [])

                  if stage == 2.3:
                      dump("odT", odT[:], [128, 8, SQ], BF16, odTB)
                      raise _Stop()
                  CD = ExitStack()
                  CD.__enter__()
                  mgT = sbt(CD, "mgT", [128, 8, SQ], BF16)
                  mgB = [Buf("mgT%d" % i, S.fence()) for i in range(4)]
                  with ExitStack() as C:
                      fC = S.fence()
                      tmps = Rot([tuple((sbt(C, "mc%d_%d" % (k, i), [128, 512], F32), Buf("mc%d_%d" % (k, i), fC))
                                        for k in range(4)) for i in range(2)])
                      for j in range(8):
                          (wga, wgb, wod, wom), wcB = wload([
                              (w_in_v[:, :, GA_OFF + j * 128:GA_OFF + (j + 1) * 128], 8, 128),
                              (w_in_v[:, :, GB_OFF + j * 128:GB_OFF + (j + 1) * 128], 8, 128),
                              (wview(w_od_d)[:, :, j * 128:(j + 1) * 128], 8, 128),
                              (wview(w_om_d)[:, :, j * 128:(j + 1) * 128], 4, 128)])
                          for tb in range(NB):
                              cs = slice(tb * 512, (tb + 1) * 512)
                              (sa, saB), (sb_, sbB), (m1, m1B), (m2, m2B) = tmps.next()
                              ga, gaB = gen_banks.next()
                              for c in range(8):
                                  mm(ga, wga[:, c, :], hT[:, c, cs], c == 0, c == 7, [wcB, hTB[tb]], [gaB])
                              act(sa[:], ga, AF.Sigmoid, [gaB, Bconst], [saB], bias=vecs[:, j:j + 1])
                              gb, gbB = gen_banks.next()
                              for c in range(8):
                                  mm(gb, wgb[:, c, :], hT[:, c, cs], c == 0, c == 7, [wcB, hTB[tb]], [gbB])
                              act(sb_[:], gb, AF.Sigmoid, [gbB, Bconst], [sbB], bias=vecs[:, 8 + j:8 + j + 1])
                              oa, oaB = gen_banks.next()
                              for c in range(8):
                                  mm(oa, wod[:, c, :], odT[:, c, cs], c == 0, c == 7, [wcB, odTB[tb]], [oaB])
                              dve(lambda e, m1=m1, sa=sa, oa=oa: e.tensor_tensor(out=m1[:], in0=oa, in1=sa[:], op=ALU.mult),
                                  [oaB, saB], [m1B])
                              ob, obB = gen_banks.next()
                              for c in range(4):
                                  mm(ob, wom[:, c, :], omT[:, c, cs], c == 0, c == 3, [wcB, omTB[tb]], [obB])
                              dve(lambda e, m2=m2, sb_=sb_, ob=ob: e.tensor_tensor(out=m2[:], in0=ob, in1=sb_[:], op=ALU.mult),
                                  [obB, sbB], [m2B])
                              dve(lambda e, m1=m1, m2=m2, j=j, cs=cs: e.tensor_tensor(out=mgT[:, j, cs], in0=m1[:], in1=m2[:],
                                                                                    op=ALU.add), [m1B, m2B], [mgB[tb]])
              if stage == 3:
                  dump("mgT", mgT[:], [128, 8, SQ], BF16, mgB)
                  raise _Stop()
              DG = ExitStack()
              DG.__enter__()
              x1 = sbt(DG, "x1", [128, NT, D], F32)
              fD = S.fence()
              x1B = [Buf("x1_%d" % i, fD) for i in range(NT)]
              EG = ExitStack()
              EG.__enter__()
              h2T = sbt(EG, "h2T", [128, 8, SQ], BF16)
              h2B = [Buf("h2T%d" % i, S.fence()) for i in range(4)]
              with ExitStack() as Dp:
                  xsl = Rot([(sbt(Dp, "xsD%d" % i, [128, D], F32), Buf("xsD%d" % i, fD), nsem("xsD%d" % i)) for i in range(2)])
                  nE1a, nE1b, nE2 = norm_T(Dp, "nE", lambda t: (x1[:, t, :], x1B[t]), g_ffn_d, h2T, h2B, external=True)
                  wo = []
                  for nh in range(2):
                      (wv_,), wb_ = wload([(wview(w_out_d)[:, :, nh * 512:(nh + 1) * 512], 8, 512)])
                      wo.append((wv_, wb_))
                  qE = []
                  for t in range(NT):
                      xt, xb, ds = xsl.next()
                      S.add("sp", lambda e, xt=xt, t=t, s=s: e.dma_start(out=xt[:], in_=x_d[s, t * 128:(t + 1) * 128, :]),
                            writes=[xb], dsem=ds)
                      for nh in range(2):
                          bk, bb = gen_banks.next()
                          for c in range(8):
                              mm(bk, mgT[:, c, t * 128:(t + 1) * 128], wo[nh][0][:, c, :], c == 0, c == 7,
                                 [mgB[t // 4], wo[nh][1]], [bb])
                          dve(lambda e, bk=bk, xt=xt, t=t, nh=nh: e.tensor_tensor(
                              out=x1[:, t, nh * 512:(nh + 1) * 512], in0=bk, in1=xt[:, nh * 512:(nh + 1) * 512], op=ALU.add),
                              [bb, xb], [x1B[t]])
                      sa = nE1a(t)
                      if len(qE) >= 2:
                          nE2(t - 2, *qE.pop(0))
                      qE.append(nE1b(*sa))
                  nE2(NT - 2, *qE.pop(0))
                  nE2(NT - 1, *qE.pop(0))
              CD.__exit__(None, None, None)
              if stage == 4:
                  dump("x1", x1[:], [128, NT, D], F32, x1B)
                  raise _Stop()
              with ExitStack() as Fp:
                  fF = S.fence()
                  hid = sbt(Fp, "hid", [128, NF, 1024], BF16)
                  hidB = [Buf("hid%d" % i, fF) for i in range(2)]
                  sgs = Rot([(sbt(Fp, "sg%d" % i, [128, 512], F32), Buf("sg%d" % i, fF)) for i in range(3)])
                  for half in range(2):
                      for fb in range(11):
                          (wg, wu), wfB = wload([(wview(w_g_d)[:, :, fb * 256:(fb + 1) * 256], 8, 256),
                                                 (wview(w_u_d)[:, :, fb * 256:(fb + 1) * 256], 8, 256)])
                          for fc in range(2):
                              f = fb * 2 + fc
                              for tbh in range(2):
                                  tb = half * 2 + tbh
                                  cs = slice(tb * 512, (tb + 1) * 512)
                                  gk, gkB = gen_banks.next()
                                  for c in range(8):
                                      mm(gk, wg[:, c, fc * 128:(fc + 1) * 128], h2T[:, c, cs], c == 0, c == 7, [wfB, h2B[tb]],
                                         [gkB])
                                  sg, sgB = sgs.next()
                                  act(sg[:], gk, AF.Silu, [gkB], [sgB])
                                  uk, ukB = gen_banks.next()
                                  for c in range(8):
                                      mm(uk, wu[:, c, fc * 128:(fc + 1) * 128], h2T[:, c, cs], c == 0, c == 7, [wfB, h2B[tb]],
                                         [ukB])
                                  dve(lambda e, sg=sg, uk=uk, f=f, tbh=tbh: e.tensor_tensor(
                                      out=hid[:, f, tbh * 512:(tbh + 1) * 512], in0=uk, in1=sg[:], op=ALU.mult),
                                      [ukB, sgB], [hidB[tbh]])
                      wdv = w_d_d.rearrange("(f p) n -> p f n", p=128)
                      for nq in range(4):
                          (wd,), wdB = wload([(wdv[:, :, nq * 256:(nq + 1) * 256], NF, 256)])
                          for tl in range(8):
                              t = half * 8 + tl
                              bk, bb = gen_banks.next()
                              for f in range(NF):
                                  mm(bk[:, 0:256], hid[:, f, tl * 128:(tl + 1) * 128], wd[:, f, :], f == 0, f == NF - 1,
                                     [hidB[tl // 4], wdB], [bb])
                              dve(lambda e, bk=bk, t=t, nq=nq: e.tensor_tensor(
                                  out=x1[:, t, nq * 256:(nq + 1) * 256], in0=bk[:, 0:256], in1=x1[:, t, nq * 256:(nq + 1) * 256],
                                  op=ALU.add), [bb, x1B[t]], [x1B[t]])
              if stage == 5:
                  dump("x2", x1[:], [128, NT, D], F32, x1B)
                  raise _Stop()
              with ExitStack() as Gp:
                  nG1a, nG1b, nG2 = norm_T(Gp, "nG", lambda t: (x1[:, t, :], x1B[t]), g_ple_d, h2T, h2B, external=True)
                  fG = S.fence()
                  pT = sbt(Gp, "pT", [128, 2, SQ], BF16)
                  pTB = [Buf("pT%d" % i, fG) for i in range(4)]
                  pin = Rot([(sbt(Gp, "pin%d" % i, [128, 256], F32), Buf("pin%d" % i, fG), nsem("pin%d" % i)) for i in range(2)])
                  pbf = Rot([(sbt(Gp, "pbf%d" % i, [128, 256], BF16), Buf("pbf%d" % i, fG)) for i in range(2)])
                  gfb = sbt(Gp, "gfb", [128, D], F32)
                  Bbb = Buf("bbc", fG)
                  S.add("sp", lambda e: e.dma_start(out=gfb[:], in_=g_fin_d.partition_broadcast(128)),
                        writes=[Bbb], dsem=nsem("bbc"))
                  bpl = sbt(Gp, "bpl", [128, D], BF16)
                  e0 = sbt(Gp, "e0", [128, 128], BF16)
                  Bbp = Buf("bpl", fG)
                  dve(lambda e: e.memset(bpl[:], 0.0), [], [Bbp])
                  dve(lambda e: e.memset(e0[:], 0.0), [], [Bbp])
                  dve(lambda e: e.memset(e0[0:1, :], 1.0), [], [Bbp])
                  S.add("pool", lambda e: e.dma_start(out=bpl[0:1, :], in_=b_ple_d.rearrange("(o n) -> o n", o=1)),
                        writes=[Bbp], dsem=nsem("bpl"))
                  pk_bank = allbanks[0]

                  def p_tile(t):
                      pt_, pb_, ds = pin.next()
                      S.add("sp", lambda e: e.dma_start(out=pt_[:], in_=p_d[s, t * 128:(t + 1) * 128, :]),
                            writes=[pb_], dsem=ds)
                      pf, pfB = pbf.next()
                      dve(lambda e: e.tensor_copy(out=pf[:], in_=pt_[:]), [pb_], [pfB])
                      pbk = pk_bank[0].bitcast(BF16)
                      for c in range(2):
                          tr(pbk[:, c * 128:(c + 1) * 128], pf[:, c * 128:(c + 1) * 128], [pfB], [pk_bank[1]])
                      copy_alt(pT[:, :, t * 128:(t + 1) * 128], pbk[:, 0:256].rearrange("p (c t) -> p c t", c=2),
                               [pk_bank[1]], [pTB[t // 4]])

                  norm_drive(nG1a, nG1b, nG2, extra=p_tile)
                  wpg = []
                  for nh in range(2):
                      (wv_,), wb_ = wload([(wview(w_pg_d)[:, :, nh * 512:(nh + 1) * 512], 8, 512)])
                      wpg.append((wv_, wb_))
                  (wpl,), wplB = wload([(wview(w_pl_d), 2, 1024)])
                  gtm = Rot([tuple((sbt(Gp, "gt%d_%d" % (k, i), [128, 512], F32), Buf("gt%d_%d" % (k, i), fG))
                                   for k in range(3)) for i in range(2)])
                  junk = sbt(Gp, "junkG", [128, D], BF16)
                  Bj = Buf("junkG", fG)
                  ysl = Rot([(sbt(Gp, "ys%d" % i, [128, D], F32), Buf("ys%d" % i, fG), nsem("ys%d" % i)) for i in range(2)])
                  def final_norm(t):
                      st, sbf = stat_rot.next()
                      act(junk[:], x1[:, t, :], AF.Square, [x1B[t]], [Bj, sbf], accum_out=st[:, 0:1])
                      rstd_pool(st, sbf)
                      yt, yb, ysem = ysl.next()
                      dve(lambda e: e.scalar_tensor_tensor(out=yt[:], in0=x1[:, t, :], scalar=st[:, 2:3], in1=gfb[:],
                                                           op0=ALU.mult, op1=ALU.mult), [x1B[t], sbf, Bbb], [yb])
                      S.add("sp", lambda e: e.dma_start(out=y_d[s, t * 128:(t + 1) * 128, :], in_=yt[:]),
                            reads=[yb], dsem=ysem, store=True)

                  for t in range(NT):
                      ts_ = slice(t * 128, (t + 1) * 128)
                      for nh in range(2):
                          ns = slice(nh * 512, (nh + 1) * 512)
                          (g1, g1B), (g2, g2B), (g3, g3B) = gtm.next()
                          bk, bb = gen_banks.next()
                          for c in range(8):
                              mm(bk, h2T[:, c, ts_], wpg[nh][0][:, c, :], c == 0, False, [h2B[t // 4], wpg[nh][1]], [bb])
                          mm(bk, e0[:], bpl[:, ns], False, True, [Bbp], [bb])
                          act(g2[:], bk, AF.Sigmoid, [bb], [g2B])
                          pk, pkB = gen_banks.next()
                          for c in range(2):
                              mm(pk, pT[:, c, ts_], wpl[:, c, ns], c == 0, c == 1, [pTB[t // 4], wplB], [pkB])
                          dve(lambda e, g3=g3, g2=g2, pk=pk: e.tensor_tensor(out=g3[:], in0=pk, in1=g2[:], op=ALU.mult),
                              [pkB, g2B], [g3B])
                          dve(lambda e, g3=g3, t=t, ns=ns: e.tensor_tensor(out=x1[:, t, ns], in0=x1[:, t, ns], in1=g3[:],
                                                                          op=ALU.add), [g3B, x1B[t]], [x1B[t]])
                      if t > 0:
                          final_norm(t - 1)
                  final_norm(NT - 1)
              EG.__exit__(None, None, None)
              DG.__exit__(None, None, None)
          except _Stop:
            pass

        for s_ in range(nseq):
            one_seq(s_)

        fin = S.add("sp", lambda e: e.nop(), reads=[])
        fin.deps = list(S.stores)
        with nc.Block() as block:
            S.emit_all(block, esem)
    return nc


def _rope_tables():
    pos = np.arange(SQ, dtype=np.float32)
    tabs = np.zeros((4, 128, SQ), np.float32)
    tabs[0] = 1.0
    tabs[2] = 1.0
    inv = (np.float32(10000.0) ** (-(np.arange(0, 32, 2, dtype=np.float32) / np.float32(32)))).astype(np.float32)
    ang = (pos[:, None] * inv[None, :]).astype(np.float32)
    c, sn = np.cos(ang).astype(np.float32).T, np.sin(ang).astype(np.float32).T
    tabs[0, 64:80], tabs[0, 80:96] = c, c
    tabs[1, 64:80], tabs[1, 80:96] = sn, sn
    inv = (np.float32(500000.0) ** (-(np.arange(0, 16, 2, dtype=np.float32) / np.float32(16)))).astype(np.float32)
    ang = (pos[:, None] * inv[None, :]).astype(np.float32)
    c, sn = np.cos(ang).astype(np.float32).T, np.sin(ang).astype(np.float32).T
    for r0 in (0, 64):
        tabs[2, r0:r0 + 8], tabs[2, r0 + 8:r0 + 16] = c, c
        tabs[3, r0:r0 + 8], tabs[3, r0 + 8:r0 + 16] = sn, sn
    return tabs


def _perm_mats():
    pd = np.zeros((128, 128), np.float32)
    for r0 in (0, 64):
        for r in range(8):
            pd[r0 + r + 8, r0 + r] = -1.0
            pd[r0 + r, r0 + r + 8] = 1.0
    pm = np.zeros((128, 128), np.float32)
    for r in range(16):
        pm[64 + r + 16, 64 + r] = -1.0
        pm[64 + r, 64 + r + 16] = 1.0
    return pd.astype(ml_dtypes.bfloat16), pm.astype(ml_dtypes.bfloat16)


_CACHE = {}


def _get_nc():
    if "nc" not in _CACHE:
        _CACHE["nc"] = build_program()
    return _CACHE["nc"]


def kernel(x, p, attn_norm, w_in, b_gate, lam_q1, lam_k1, lam_q2, lam_k2, diff_subln, w_o_diff, q_norm, w_uq,
           kv_norm, w_ukv, w_o_mla, w_out, ffn_norm, w_ffn_gate, w_ffn_up, w_ffn_down, ple_norm, w_ple_gate,
           b_ple_gate, w_ple, final_norm):
    f = lambda a: np.ascontiguousarray(np.asarray(a, dtype=np.float32))
    x = f(x)
    p = f(p)[0]
    B = x.shape[0]
    nseq = B // NCORES
    w_ukv_ = f(w_ukv)[0].reshape(256, 8, 2, 64)
    vecs = np.zeros((128, 24), np.float32)
    bg = f(b_gate)[0]
    vecs[:, 0:8] = bg[0].reshape(8, 128).T
    vecs[:, 8:16] = bg[1].reshape(8, 128).T
    vecs[:, 16:19] = f(q_norm)[0].reshape(3, 128).T
    vecs[:, 19:21] = f(kv_norm)[0].reshape(2, 128).T
    vecs[:, 21] = f(diff_subln)[0]
    pd, pm = _perm_mats()
    shared = {
        "w_in": f(w_in)[0], "w_o_diff": f(w_o_diff)[0], "w_uq": f(w_uq)[0],
        "w_ukv_kn": np.ascontiguousarray(w_ukv_[:, :, 0, :].reshape(256, 512)),
        "w_ukv_v": np.ascontiguousarray(w_ukv_[:, :, 1, :].reshape(256, 512)),
        "w_o_mla": f(w_o_mla)[0], "w_out": f(w_out)[0], "w_ffn_gate": f(w_ffn_gate)[0], "w_ffn_up": f(w_ffn_up)[0],
        "w_ffn_down": f(w_ffn_down)[0], "w_ple_gate": f(w_ple_gate)[0], "w_ple": f(w_ple)[0],
        "attn_norm": f(attn_norm)[0], "ffn_norm": f(ffn_norm)[0], "ple_norm": f(ple_norm)[0],
        "final_norm": f(final_norm), "b_ple_gate": f(b_ple_gate)[0],
        "lam_q1": f(lam_q1)[0], "lam_k1": f(lam_k1)[0], "lam_q2": f(lam_q2)[0], "lam_k2": f(lam_k2)[0],
        "vecs": vecs, "ident": np.eye(128, dtype=np.float32).astype(ml_dtypes.bfloat16),
        "perm_diff": pd, "perm_mla": pm, "rope_tabs": _rope_tables(),
    }
    nc = _get_nc()
    in_maps = []
    for c in range(NCORES):
        m = dict(shared)
        m["x"] = x[c * nseq:(c + 1) * nseq]
        m["p"] = p[c * nseq:(c + 1) * nseq]
        in_maps.append(m)
    res = run_bass_kernel_spmd(nc, in_maps, core_ids=list(range(NCORES)))
    return np.concatenate([r["y"] for r in res.results], axis=0).astype(np.float32)
```
